# Optimizing a Trainium2 kernel written in Bass

```python
import math
import jax, jax.numpy as jnp
from jax import lax
import numpy as np

D_MODEL = 1024
BATCH = 8
SEQ = 2048
DEPTH = 4
DEC_BATCH = 128
DEC_SEQ = 8
PAST_LEN = 16384
PAGE_SIZE = 128

D_SSM = D_MODEL // 2
SSM_GROUP = 16
SSM_GROUPS = D_SSM // SSM_GROUP
SSM_STATE = 64
RET_HEADS = 4
RET_DK = D_MODEL // (2 * RET_HEADS)
RET_DV = 2 * RET_DK
RET_CHUNK = 128
ROPE_BASE = 10000.0
D_FF = ((8 * D_MODEL // 3 + 127) // 128) * 128
CONV_W = 3
EPS = 1e-6
QK_W = RET_HEADS * RET_DK
V_W = RET_HEADS * RET_DV
IN_COLS = D_SSM + 2 * QK_W + 2 * V_W + 2 * D_MODEL
SPLITS = (D_SSM, D_SSM + QK_W, D_SSM + 2 * QK_W, D_SSM + 2 * QK_W + V_W,
          D_SSM + 2 * QK_W + 2 * V_W, D_SSM + 2 * QK_W + 2 * V_W + D_MODEL)

kernel_name = "hybrid_s5_retention_convffn_step"


def rmsnorm(x, g):
    xf = x.astype(jnp.float32)
    y = xf * lax.rsqrt(jnp.mean(xf * xf, axis=-1, keepdims=True) + EPS)
    return (y * g.astype(jnp.float32)).astype(x.dtype)


def rotary(x, pos):
    half = x.shape[-1] // 2
    inv = ROPE_BASE ** (-jnp.arange(half, dtype=jnp.float32) / half)
    ang = pos[:, None] * inv[None, :]
    cos = jnp.cos(ang)[None, :, None, :]
    sin = jnp.sin(ang)[None, :, None, :]
    xf = x.astype(jnp.float32)
    x1, x2 = xf[..., :half], xf[..., half:]
    return jnp.concatenate([x1 * cos - x2 * sin, x1 * sin + x2 * cos], axis=-1)


def s5_scan(u, s0, lam_re, lam_im, log_dt, b_re, b_im, c_re, c_im, d):
    bsz, L, _ = u.shape
    uf = u.astype(jnp.float32)
    ug = uf.reshape(bsz, L, SSM_GROUPS, SSM_GROUP)
    lam = lax.complex(lam_re.astype(jnp.float32), lam_im.astype(jnp.float32))
    dt = jnp.exp(log_dt.astype(jnp.float32))[:, None]
    lam_bar = jnp.exp(lam * dt)
    b = lax.complex(b_re.astype(jnp.float32), b_im.astype(jnp.float32))
    b_bar = ((lam_bar - 1.0) / lam)[..., None] * b
    bu = jnp.einsum('gpc,blgc->blgp', b_bar, ug.astype(jnp.complex64))
    bu = bu.at[:, 0].add(lam_bar[None] * s0)
    a = jnp.broadcast_to(lam_bar, bu.shape)

    def combine(e1, e2):
        a1, b1 = e1
        a2, b2 = e2
        return a1 * a2, a2 * b1 + b2

    _, s = lax.associative_scan(combine, (a, bu), axis=1)
    y = (jnp.einsum('gcp,blgp->blgc', c_re.astype(jnp.float32), jnp.real(s))
         - jnp.einsum('gcp,blgp->blgc', c_im.astype(jnp.float32), jnp.imag(s)))
    y = y.reshape(bsz, L, D_SSM) + d.astype(jnp.float32) * uf
    return y.astype(u.dtype), s[:, -1]


def retention(q, k, v, r0):
    bsz, L, H, dk = q.shape
    dv = v.shape[-1]
    c = RET_CHUNK if L % RET_CHUNK == 0 else L
    n = L // c
    log_g = jnp.log1p(-jnp.exp2(-5.0 - jnp.arange(H, dtype=jnp.float32)))
    idx = jnp.arange(c, dtype=jnp.float32)
    rel = idx[:, None] - idx[None, :]
    intra = jnp.where(rel >= 0, jnp.exp(jnp.maximum(rel, 0.0)[None] * log_g[:, None, None]), 0.0)
    inner = jnp.exp((idx[None, :] + 1.0) * log_g[:, None])
    tail = jnp.exp((c - 1.0 - idx[None, :]) * log_g[:, None])
    chunk_decay = jnp.exp(c * log_g)

    def to_chunks(t):
        return t.reshape(bsz, n, c, H, t.shape[-1]).transpose(1, 0, 3, 2, 4)

    qc, kc, vc = to_chunks(q), to_chunks(k), to_chunks(v.astype(jnp.float32))

    def step(r, inp):
        qi, ki, vi = inp
        sc = jnp.einsum('bhid,bhjd->bhij', qi, ki) * intra[None]
        o = (jnp.einsum('bhij,bhjv->bhiv', sc, vi)
             + jnp.einsum('bhid,bhdv->bhiv', qi, r) * inner[None, :, :, None])
        r_new = (r * chunk_decay[None, :, None, None]
                 + jnp.einsum('bhjd,bhjv->bhdv', ki * tail[None, :, :, None], vi))
        return r_new, o

    r, o = lax.scan(step, r0, (qc, kc, vc))
    o = o.transpose(1, 0, 3, 2, 4).reshape(bsz, L, H, dv)
    return o, r


def decoder_layer(x, pos, ssm_s0, ret_s0, conv_buf,
                  norm_mix, w_in, lam_re, lam_im, log_dt, b_re, b_im, c_re, c_im, d,
                  w_glu, w_ssm_out, w_ret_out, w_o, norm_ffn, w_up, conv_w, conv_b, w_down):
    bsz, L, _ = x.shape
    h = rmsnorm(x, norm_mix)
    z = h @ w_in
    u, q, k, v, g_ret, g_a, g_b = jnp.split(z, SPLITS, axis=-1)
    ya, ssm_last = s5_scan(u, ssm_s0, lam_re, lam_im, log_dt, b_re, b_im, c_re, c_im, d)
    ya = jax.nn.gelu(ya)
    ya = ya * jax.nn.sigmoid(ya @ w_glu)
    ya = ya @ w_ssm_out
    qr = rotary(q.reshape(bsz, L, RET_HEADS, RET_DK), pos)
    kr = rotary(k.reshape(bsz, L, RET_HEADS, RET_DK), pos) * (RET_DK ** -0.5)
    o, ret_last = retention(qr, kr, v.reshape(bsz, L, RET_HEADS, RET_DV), ret_s0)
    o = o * lax.rsqrt(jnp.mean(o * o, axis=-1, keepdims=True) + EPS)
    o = jax.nn.silu(g_ret) * o.reshape(bsz, L, V_W).astype(x.dtype)
    yb = o @ w_ret_out
    mix = jax.nn.sigmoid(g_a) * ya + jax.nn.sigmoid(g_b) * yb
    x = x + mix @ w_o
    h2 = rmsnorm(x, norm_ffn)
    up = h2 @ w_up
    hp = jnp.concatenate([conv_buf.astype(up.dtype), up], axis=1)
    hc = conv_b
    for j in range(CONV_W):
        hc = hc + conv_w[j] * hp[:, j:j + L]
    new_buf = hp[:, L:]
    val, gate = jnp.split(hc, 2, axis=-1)
    x = x + (jax.nn.silu(gate) * val) @ w_down
    return x, ssm_last, ret_last, new_buf


def setup_inputs(seed: int = 0) -> dict:
    key = jax.random.key(seed)
    ks = iter(jax.random.split(key, 32))

    def nrm(shape, scale):
        return scale * jax.random.normal(next(ks), shape, jnp.float32)

    L = DEPTH
    G, P = SSM_GROUPS, SSM_STATE
    x_prompt = nrm((BATCH, SEQ, D_MODEL), 1.0)
    x_sample = nrm((DEC_BATCH, DEC_SEQ, D_MODEL), 1.0)
    state_ssm_re = nrm((L, DEC_BATCH, G, P), 0.3)
    state_ssm_im = nrm((L, DEC_BATCH, G, P), 0.3)
    state_ret = nrm((L, DEC_BATCH, RET_HEADS, RET_DK, RET_DV), 0.5)
    state_conv = nrm((L, DEC_BATCH, CONV_W - 1, 2 * D_FF), 1.0)
    norm_mix = 1.0 + nrm((L, D_MODEL), 0.02)
    w_in = nrm((L, D_MODEL, IN_COLS), D_MODEL ** -0.5)
    ssm_lam_re = -0.5 + nrm((L, G, P), 0.01)
    ssm_lam_im = math.pi * jnp.arange(P, dtype=jnp.float32) + nrm((L, G, P), 0.01)
    ssm_log_dt = jax.random.uniform(next(ks), (L, G), jnp.float32, math.log(1e-3), math.log(1e-1))
    ssm_b_re = nrm((L, G, P, SSM_GROUP), (2 * SSM_GROUP) ** -0.5)
    ssm_b_im = nrm((L, G, P, SSM_GROUP), (2 * SSM_GROUP) ** -0.5)
    ssm_c_re = nrm((L, G, SSM_GROUP, P), (2 * P) ** -0.5)
    ssm_c_im = nrm((L, G, SSM_GROUP, P), (2 * P) ** -0.5)
    ssm_d = nrm((L, D_SSM), 0.5)
    w_glu = nrm((L, D_SSM, D_SSM), D_SSM ** -0.5)
    w_ssm_out = nrm((L, D_SSM, D_MODEL), D_SSM ** -0.5)
    w_ret_out = nrm((L, V_W, D_MODEL), V_W ** -0.5)
    w_o = nrm((L, D_MODEL, D_MODEL), D_MODEL ** -0.5)
    norm_ffn = 1.0 + nrm((L, D_MODEL), 0.02)
    w_up = nrm((L, D_MODEL, 2 * D_FF), D_MODEL ** -0.5)
    conv_w = nrm((L, CONV_W, 2 * D_FF), CONV_W ** -0.5)
    conv_b = nrm((L, 2 * D_FF), 0.02)
    w_down = nrm((L, D_FF, D_MODEL), D_FF ** -0.5)
    norm_final = 1.0 + nrm((D_MODEL,), 0.02)
    return {"x_prompt": x_prompt, "x_sample": x_sample,
            "state_ssm_re": state_ssm_re, "state_ssm_im": state_ssm_im,
            "state_ret": state_ret, "state_conv": state_conv,
            "norm_mix": norm_mix, "w_in": w_in,
            "ssm_lam_re": ssm_lam_re, "ssm_lam_im": ssm_lam_im, "ssm_log_dt": ssm_log_dt,
            "ssm_b_re": ssm_b_re, "ssm_b_im": ssm_b_im, "ssm_c_re": ssm_c_re, "ssm_c_im": ssm_c_im,
            "ssm_d": ssm_d, "w_glu": w_glu, "w_ssm_out": w_ssm_out, "w_ret_out": w_ret_out,
            "w_o": w_o, "norm_ffn": norm_ffn, "w_up": w_up, "conv_w": conv_w, "conv_b": conv_b,
            "w_down": w_down, "norm_final": norm_final}


def reference(x_prompt, x_sample, state_ssm_re, state_ssm_im, state_ret, state_conv,
              norm_mix, w_in, ssm_lam_re, ssm_lam_im, ssm_log_dt, ssm_b_re, ssm_b_im,
              ssm_c_re, ssm_c_im, ssm_d, w_glu, w_ssm_out, w_ret_out, w_o, norm_ffn,
              w_up, conv_w, conv_b, w_down, norm_final):
    pos_p = jnp.arange(SEQ, dtype=jnp.float32)
    pos_s = PAST_LEN + jnp.arange(DEC_SEQ, dtype=jnp.float32)
    bp = x_prompt.shape[0]
    xp, xs = x_prompt, x_sample
    ssm_re_p, ssm_im_p, ret_p, conv_p = [], [], [], []
    ssm_re_s, ssm_im_s, ret_s, conv_s = [], [], [], []
    for l in range(DEPTH):
        lw = (norm_mix[l], w_in[l], ssm_lam_re[l], ssm_lam_im[l], ssm_log_dt[l],
              ssm_b_re[l], ssm_b_im[l], ssm_c_re[l], ssm_c_im[l], ssm_d[l],
              w_glu[l], w_ssm_out[l], w_ret_out[l], w_o[l], norm_ffn[l],
              w_up[l], conv_w[l], conv_b[l], w_down[l])
        s0_p = jnp.zeros((bp, SSM_GROUPS, SSM_STATE), jnp.complex64)
        r0_p = jnp.zeros((bp, RET_HEADS, RET_DK, RET_DV), jnp.float32)
        c0_p = jnp.zeros((bp, CONV_W - 1, 2 * D_FF), xp.dtype)
        xp, sp, rp, cp = decoder_layer(xp, pos_p, s0_p, r0_p, c0_p, *lw)
        ssm_re_p.append(jnp.real(sp).astype(state_ssm_re.dtype))
        ssm_im_p.append(jnp.imag(sp).astype(state_ssm_im.dtype))
        ret_p.append(rp.astype(state_ret.dtype))
        conv_p.append(cp.astype(state_conv.dtype))
        s0_s = lax.complex(state_ssm_re[l].astype(jnp.float32), state_ssm_im[l].astype(jnp.float32))
        r0_s = state_ret[l].astype(jnp.float32)
        xs, ss, rs, cs = decoder_layer(xs, pos_s, s0_s, r0_s, state_conv[l], *lw)
        ssm_re_s.append(jnp.real(ss).astype(state_ssm_re.dtype))
        ssm_im_s.append(jnp.imag(ss).astype(state_ssm_im.dtype))
        ret_s.append(rs.astype(state_ret.dtype))
        conv_s.append(cs.astype(state_conv.dtype))
    y_prompt = rmsnorm(xp, norm_final)
    y_sample = rmsnorm(xs, norm_final)
    return (y_prompt, y_sample,
            jnp.stack(ssm_re_p), jnp.stack(ssm_im_p), jnp.stack(ret_p), jnp.stack(conv_p),
            jnp.stack(ssm_re_s), jnp.stack(ssm_im_s), jnp.stack(ret_s), jnp.stack(conv_s))
```

```python
import math
from contextlib import ExitStack
import numpy as np
import ml_dtypes
import concourse.bass as bass
import concourse.mybir as mybir
from concourse.bass_utils import run_bass_kernel_spmd

F32 = mybir.dt.float32
BF16 = mybir.dt.bfloat16
ALU = mybir.AluOpType
AF = mybir.ActivationFunctionType

NCORES = 8
D = 1024
DEPTH = 4
SEQ = 2048
NS = 16
DS = 8
TOK = SEQ + NS * DS
G = 32
P = 64
DFF = 2816
NCH = 44
EPS = 1e-6
PAST = 16384
MAGIC = 12582912.0
TWO_PI = 2.0 * math.pi
LAYERS = DEPTH


class LT:
    __slots__ = ("w", "r", "key")

    def __init__(self):
        self.w = {}
        self.r = {}
        self.key = None


class Sched:
    ENGS = ("pe", "act", "dve", "pool", "sp")

    def __init__(self, nc, stack, n_dma):
        self.nc = nc
        self.sem = {}
        self.cnt = {}
        for e in self.ENGS:
            self.sem[e] = stack.enter_context(nc.semaphore("s_" + e))
            self.cnt[e] = 0
        self.n_dma = n_dma
        for i in range(n_dma):
            k = "d%d" % i
            self.sem[k] = stack.enter_context(nc.semaphore("s_" + k))
            self.cnt[k] = 0
        self.seen = {}
        self.prog = {e: [] for e in self.ENGS}
        self.rr = 0

    def _deps(self, eng, reads, writes):
        deps = {}

        def add(d, skip_same):
            for k, v in d.items():
                if skip_same and k == eng:
                    continue
                if deps.get(k, 0) < v:
                    deps[k] = v
        for t in reads:
            add(t.w, eng == "pe")
        for t in writes:
            add(t.w, eng == "pe")
            add(t.r, eng == "pe")
        waits = []
        for k, v in deps.items():
            if self.seen.get((eng, k), 0) >= v:
                continue
            self.seen[(eng, k)] = v
            waits.append((k, v))
        return waits

    def _mark(self, me, reads, writes):
        k, v = me
        for t in reads:
            if t.r.get(k, 0) < v:
                t.r[k] = v
        for t in writes:
            if t.w.get(k, 0) < v:
                t.w[k] = v

    def op(self, eng, fn, reads=(), writes=()):
        waits = self._deps(eng, reads, writes)
        self.cnt[eng] += 1
        self._mark((eng, self.cnt[eng]), reads, writes)
        self.prog[eng].append((waits, fn, (eng, 1)))

    def dma(self, q, fn, reads=(), writes=(), key=None):
        if key is None:
            lt = writes[0] if len(writes) else reads[0]
            if lt.key is None:
                lt.key = "d%d" % self.rr
                self.rr = (self.rr + 1) % self.n_dma
            key = lt.key
        waits = self._deps(q, reads, writes)
        self.cnt[key] += 16
        self._mark((key, self.cnt[key]), reads, writes)
        self.prog[q].append((waits, fn, (key, 16)))

    def barrier(self):
        for eng in self.ENGS:
            waits = []
            for k, v in self.cnt.items():
                if k == eng or v == 0:
                    continue
                if self.seen.get((eng, k), 0) >= v:
                    continue
                self.seen[(eng, k)] = v
                waits.append((k, v))
            if waits:
                self.prog[eng].append((waits, None, None))

    def final_wait(self, eng, tiles):
        waits = self._deps(eng, tiles, tiles)
        self.prog[eng].append((waits, None, None))

    def emit(self, block):
        def mk(ename):
            def body(e):
                for waits, fn, inc in self.prog[ename]:
                    for k, v in waits:
                        e.wait_ge(self.sem[k], v)
                    if fn is not None:
                        fn(e).then_inc(self.sem[inc[0]], inc[1])
            return body
        block.tensor(mk("pe"))
        block.scalar(mk("act"))
        block.vector(mk("dve"))
        block.gpsimd(mk("pool"))
        block.sync(mk("sp"))


class Carver:
    def __init__(self, ap2d):
        self.ap = ap2d
        self.off = 0
        self.n = ap2d.shape[1]

    def take(self, shape):
        n = 1
        for d in shape[1:]:
            n *= d
        assert self.off + n <= self.n, ("arena overflow", self.off, n, self.n)
        v = self.ap[:, self.off:self.off + n]
        self.off += n
        if len(shape) == 2:
            return v
        names = " ".join("d%d" % i for i in range(len(shape) - 1))
        kw = {"d%d" % i: shape[i + 1] for i in range(len(shape) - 1)}
        return v.rearrange("p (%s) -> p %s" % (names, names), **kw)


class Pool_:
    def __init__(self, items):
        self.items = items
        self.i = 0

    def get(self):
        it = self.items[self.i]
        self.i = (self.i + 1) % len(self.items)
        return it


def build(nlayers=LAYERS, skip=""):
    nc = bass.Bass("TRN2", target_bir_lowering=False)

    def IN(name, shape, dt=F32):
        return nc.dram_tensor(name, list(shape), dt, kind="ExternalInput").ap()

    def OUT(name, shape):
        return nc.dram_tensor(name, list(shape), F32, kind="ExternalOutput").ap()

    xT_in = IN("xT_in", [128, 8, TOK])
    w_in = IN("w_in", [DEPTH, D, 5632]); w_glu = IN("w_glu", [DEPTH, 512, 512]); w_sso = IN("w_ssm_out", [DEPTH, 512, D])
    w_ro = IN("w_ret_out", [DEPTH, D, D]); w_o = IN("w_o", [DEPTH, D, D]); w_up = IN("w_up", [DEPTH, D, 5632])
    w_dn = IN("w_down", [DEPTH, DFF, D])
    gmix = IN("gmix", [128, DEPTH, 8]); gffn = IN("gffn", [128, DEPTH, 8]); gfin = IN("gfin", [128, 8])
    convw = IN("convw", [128, DEPTH, 3, NCH]); convb = IN("convb", [128, DEPTH, NCH])
    lamr = IN("lamr", [128, DEPTH, G]); lami = IN("lami", [128, DEPTH, G]); logdt = IN("logdt", [128, DEPTH, G])
    bS_in = IN("bS", [128, DEPTH, G, 16]); bX_in = IN("bX", [128, DEPTH, G, 16])
    crD_in = IN("crD", [128, DEPTH, G, 16]); ciD_in = IN("ciD", [128, DEPTH, G, 16])
    dcol_in = IN("dcol", [128, DEPTH, G])
    s0S_in = IN("s0S", [128, DEPTH, G, NS]); s0X_in = IN("s0X", [128, DEPTH, G, NS])
    conv0_in = IN("conv0", [128, DEPTH, NCH, NS, 2])
    sret_in = IN("sret", [DEPTH, NS, 4, 128, 256])
    cf_in = IN("cf32", [128, CF_N]); cb_in = IN("cbf16", [128, CB_N], BF16)
    rope_in = IN("rope", [128, 17, 2, 64])

    yT_out = OUT("yT_out", [128, 8, TOK])
    ssmp_out = OUT("ssmp_out", [128, DEPTH, G]); ssms_out = OUT("ssms_out", [128, DEPTH, G, NS])
    retp_out = OUT("retp_out", [DEPTH, 4, 128, 256]); rets_out = OUT("rets_out", [DEPTH, NS, 4, 128, 256])
    convp_out = OUT("convp_out", [128, DEPTH, NCH, 2]); convs_out = OUT("convs_out", [128, DEPTH, NCH, NS, 2])
    dbg_out = OUT("dbg_out", [128, 1024]) if "G" in skip else None
    s5cb = nc.dram_tensor("s5cb", [DEPTH, 8, 128, 5 * 4 * 128], BF16, kind="Internal").ap()
    s5cf = nc.dram_tensor("s5cf", [DEPTH, 8, 128, 64 + 192], F32, kind="Internal").ap()
    rcar = nc.dram_tensor("rcar", [DEPTH, 4, 128, 256], F32, kind="Internal").ap()

    with ExitStack() as st:
        def sb(name, shape, dt=F32):
            return st.enter_context(nc.sbuf_tensor("sb_" + name, list(shape), dt))

        def pst(name, shape, dt=F32):
            return st.enter_context(nc.psum_tensor(name, list(shape), dt))

        S = Sched(nc, st, n_dma=24)
        block = st.enter_context(nc.Block())

        CAP = [None]

        def V(eng, method, reads, writes, *a, **kw):
            if CAP[0] is not None:
                CAP[0].append((eng, method, reads, writes, a, kw))
                return
            S.op(eng, lambda e: getattr(e, method)(*a, **kw), reads, writes)

        def DMA(q, out, in_, reads, writes, key=None):
            if CAP[0] is not None:
                CAP[0].append(("__dma__", q, out, in_, reads, writes))
                return
            S.dma(q, lambda e: e.dma_start(out=out, in_=in_), reads, writes, key)

        def emit_captured(item):
            if item[0] == "__dma__":
                _, q, out, in_, reads, writes = item
                DMA(q, out, in_, reads, writes)
            else:
                e_, m_, r_, w_, a_, kw_ = item
                V(e_, m_, r_, w_, *a_, **kw_)

        NH = 1152
        xT = sb("xT", [128, 8, NH]); Lx = [LT() for _ in range(3)]
        hT = sb("hT", [128, 8, NH], BF16); Lh = [LT() for _ in range(3)]
        HT = [[(0, 512), (512, 512)], [(0, 512), (512, 512), (1024, 128)]]
        GOFF = [0, 1024]

        cf = sb("cf", [128, CF_N]); Lcf = LT()
        cb = sb("cb", [128, CB_N], BF16); Lcb = LT()
        rope = sb("rope", [128, 17, 2, 64]); Lrope = LT()
        DMA("sp", cf[:], cf_in, [], [Lcf]); DMA("sp", cb[:], cb_in, [], [Lcb]); DMA("sp", rope[:], rope_in, [], [Lrope])

        def cfs(name):
            o, n = CF_OFF[name]
            return cf[:, o:o + n]

        def cbs(name):
            o, n = CB_OFF[name]
            return cb[:, o:o + n]
        ident_f = cfs("ident"); tmask = cfs("tmask"); pswap = cfs("pswap"); nvec = cfs("nvec"); kvec = cfs("kvec")
        sgn = cfs("sgn"); mre = cfs("mre"); nmre = cfs("nmre"); nmim = cfs("nmim"); epsc = cfs("eps")
        qsc = cfs("qsc"); ksc = cfs("ksc"); qksc = cfs("qksc"); gct = cfs("gct"); rowmask = cfs("rowmask"); cmask_p = cfs("cmask_p"); cmask_s = cfs("cmask_s")
        ident_b = cbs("ident"); ones_b = cbs("ones")

        small = {}
        for nm, src, shp in [("gmix", gmix, [128, DEPTH, 8]), ("gffn", gffn, [128, DEPTH, 8]), ("gfin", gfin, [128, 8]),
                             ("convw", convw, [128, DEPTH, 3, NCH]), ("convb", convb, [128, DEPTH, NCH]),
                             ("lamr", lamr, [128, DEPTH, G]), ("lami", lami, [128, DEPTH, G]), ("logdt", logdt, [128, DEPTH, G]),
                             ("dcol", dcol_in, [128, DEPTH, G])]:
            t = sb("sm_" + nm, shp); L = LT()
            DMA("sp", t[:], src, [], [L])
            small[nm] = (t, L)

        Wlast = sb("Wlast", [128, DEPTH, G]); Mclast = sb("Mclast", [128, DEPTH, G]); Mslast = sb("Mslast", [128, DEPTH, G]); Lcar = LT()
        convcar = sb("convcar", [128, DEPTH, NCH, 2]); Lccar = LT()
        ssm_p_sb = sb("ssm_p_sb", [128, DEPTH, G]); Lssp = LT()
        ssm_s_sb = sb("ssm_s_sb", [128, DEPTH, G, NS]); Lsss = LT()
        Lrcar = [LT() for _ in range(DEPTH)]

        psBig = pst("psbig", [128, 2048])
        psF = Pool_([(psBig[:, i * 512:(i + 1) * 512], LT()) for i in range(4)])
        psS = Pool_([(pst("pss%d" % i, [128, 512]), LT()) for i in range(2)])
        psB = Pool_([(pst("psb%d" % i, [128, 1024], BF16), LT()) for i in range(2)])
        t2k = Pool_([(sb("t2k%d" % i, [128, 512])[:], LT()) for i in range(5)])
        tb1k = Pool_([(sb("tb1k%d" % i, [128, 512], BF16)[:], LT()) for i in range(4)])
        rstd_p = Pool_([(sb("rstd%d" % i, [128, 512])[:], LT()) for i in range(2)])

        ARF_N = 7424
        ARB_N = 39700
        arena_f = sb("arena_f", [128, ARF_N]); arena_b = sb("arena_b", [128, ARB_N], BF16)

        def wload(dst, src, Ld):
            DMA("pool", dst, src, [], [Ld])

        def wview(w, l):
            return w[l].rearrange("(c p) n -> p c n", p=128)

        def rms_rstd(ti, t0, n, dscale):
            ps, Lp = psF.get()
            for c in range(8):
                sq, Lsq = tb1k.get()
                V("act", "activation", [Lx[ti]], [Lsq], out=sq[:, :n], in_=xT[:, c, t0:t0 + n], func=AF.Square)
                V("pe", "matmul", [Lsq, Lcb], [Lp], ps[:, :n], lhsT=ones_b, rhs=sq[:, :n], start=(c == 0), stop=(c == 7))
            r, Lr = rstd_p.get()
            V("act", "activation", [Lp, Lcf], [Lr], out=r[:, :n], in_=ps[:, :n], func=AF.Sqrt, bias=epsc[:, 0:1], scale=dscale)
            V("dve", "reciprocal", [Lr], [Lr], out=r[:, :n], in_=r[:, :n])
            return r, Lr

        def norm_to_h(l, which, tiles):
            gt, Lg = small[which]
            for ti, (t0, n) in enumerate(tiles):
                r, Lr = rms_rstd(ti, t0, n, 1.0 / D)
                for c in range(8):
                    eng = "dve"
                    V(eng, "scalar_tensor_tensor", [Lx[ti], Lg, Lr], [Lh[ti]], out=hT[:, c, t0:t0 + n], in0=xT[:, c, t0:t0 + n],
                      scalar=gt[:, l, c:c + 1], in1=r[:, :n], op0=ALU.mult, op1=ALU.mult)

        def proj_fm(ps, Lp, wt, Lw, col0, nk, act_fn, Lact, n):
            for kc in range(nk):
                V("pe", "matmul", [Lw] + Lact, [Lp], ps[:, :n], lhsT=wt[:, kc, col0:col0 + 128], rhs=act_fn(kc), start=(kc == 0), stop=(kc == nk - 1))

        def apply_wo(ti, t0, n, wo, Lwo, mix, Lmix):
            for oc in range(8):
                ps, Lp = psF.get()
                proj_fm(ps, Lp, wo, Lwo, oc * 128, 8, lambda kc: mix[:, kc, :n], [Lmix], n)
                V("dve", "tensor_tensor", [Lp, Lx[ti]], [Lx[ti]], out=xT[:, oc, t0:t0 + n], in0=xT[:, oc, t0:t0 + n], in1=ps[:, :n], op=ALU.add)

        def tiles_overlapping(tiles, a, b):
            return [ti for ti, (t0, n) in enumerate(tiles) if t0 < b and t0 + n > a]

        def range_reduce(dst, src, Lt, tmp, add_half_pi):
            if add_half_pi:
                V("dve", "tensor_scalar", [Lt], [Lt], out=dst, in0=src, scalar1=math.pi / 2, scalar2=None, op0=ALU.add)
                src = dst
            V("dve", "tensor_scalar", [Lt], [Lt], out=tmp, in0=src, scalar1=1.0 / TWO_PI, scalar2=MAGIC, op0=ALU.mult, op1=ALU.add)
            V("dve", "tensor_scalar", [Lt], [Lt], out=tmp, in0=tmp, scalar1=MAGIC, scalar2=TWO_PI, op0=ALU.subtract, op1=ALU.mult)
            V("dve", "tensor_tensor", [Lt], [Lt], out=dst, in0=src, in1=tmp, op=ALU.subtract)

        GG = 4
        NSUB = G // GG
        KP = 128
        S5C_L = {}

        def s5_setup(l):
            lamr_t, Ll1 = small["lamr"]; lami_t, Ll2 = small["lami"]; ldt_t, Ll3 = small["logdt"]; dcol_t, Ll4 = small["dcol"]
            Lsmall = [Ll1, Ll2, Ll3, Ll4, Lcf]
            cfA = Carver(arena_f[:, :]); cbA = Carver(arena_b[:, 36368:])
            s5f = [None] + [cfA.take([128, GG, 128]) for _ in range(5)]; Ls5f = [None] + [LT() for _ in range(5)]
            Pb = cbA.take([128, 3, GG, 128]); LPb = LT()
            Ab = cbA.take([128, 3, GG, 128]); LAb = LT()

            class SCtx:
                pass
            sctx = []
            for i in range(4):
                c = SCtx()
                c.s5sm = cfA.take([128, 16, GG]); c.Lsm = LT()
                c.Ep = cfA.take([128, 6, GG, 24]); c.LEp = LT()
                c.bsx = cfA.take([128, 3, GG, 16]); c.Lbsx = LT()
                c.bSt = cfA.take([128, 4, GG, 16]); c.LbSt = LT()
                c.tsm = cfA.take([128, GG, 32]); c.Ltsm = LT()
                sctx.append(c)

            def small_stage(c, sq):
                g0 = sq * GG
                gs = slice(g0, g0 + GG)
                s5sm, Lsm, Ep, LEp, bsx, Lbsx, bSt, LbSt = c.s5sm, c.Lsm, c.Ep, c.LEp, c.bsx, c.Lbsx, c.bSt, c.LbSt
                s5f = [c.tsm]; Ls5f = [c.Ltsm]
                sm = lambda i: s5sm[:, i, :]
                Er = Ep[:, 4]; Ei = Ep[:, 5]
                V("act", "activation", Lsmall, [Lsm], out=sm(0), in_=ldt_t[:, l, gs], func=AF.Exp)
                V("dve", "tensor_tensor", Lsmall + [Lsm], [Lsm], out=sm(1), in0=lamr_t[:, l, gs], in1=sm(0), op=ALU.mult)
                V("dve", "tensor_tensor", Lsmall + [Lsm], [Lsm], out=sm(2), in0=lami_t[:, l, gs], in1=sm(0), op=ALU.mult)
                nv = nvec.unsqueeze(1).to_broadcast([128, GG, 24])
                V("dve", "tensor_tensor", [Lsm, Lcf], [LEp], out=Ep[:, 0], in0=sm(2).unsqueeze(2).to_broadcast([128, GG, 24]), in1=nv, op=ALU.mult)
                V("dve", "tensor_tensor", [Lsm, Lcf], [LEp], out=Ep[:, 1], in0=sm(1).unsqueeze(2).to_broadcast([128, GG, 24]), in1=nv, op=ALU.mult)
                range_reduce(Ep[:, 2], Ep[:, 0], LEp, Ep[:, 3], False)
                V("act", "activation", [LEp], [LEp], out=Ep[:, 5], in_=Ep[:, 2], func=AF.Sin)
                range_reduce(Ep[:, 2], Ep[:, 0], LEp, Ep[:, 3], True)
                V("act", "activation", [LEp], [LEp], out=Ep[:, 4], in_=Ep[:, 2], func=AF.Sin)
                V("act", "activation", [LEp], [LEp], out=Ep[:, 1], in_=Ep[:, 1], func=AF.Exp)
                V("dve", "tensor_tensor", [LEp], [LEp], out=Ep[:, 4], in0=Ep[:, 4], in1=Ep[:, 1], op=ALU.mult)
                V("dve", "tensor_tensor", [LEp], [LEp], out=Ep[:, 5], in0=Ep[:, 5], in1=Ep[:, 1], op=ALU.mult)
                Er = Ep[:, 4]; Ei = Ep[:, 5]
                E1r = Er[:, :, 16]; E1i = Ei[:, :, 16]
                lr_ = lamr_t[:, l, gs]; li_ = lami_t[:, l, gs]
                RS = Lsmall + [Lsm, LEp]
                V("dve", "tensor_scalar", RS, [Lsm], out=sm(3), in0=E1r, scalar1=-1.0, scalar2=None, op0=ALU.add)
                V("dve", "tensor_tensor", RS, [Lsm], out=sm(4), in0=lr_, in1=lr_, op=ALU.mult)
                V("dve", "tensor_tensor", RS, [Lsm], out=sm(8), in0=li_, in1=li_, op=ALU.mult)
                V("dve", "tensor_tensor", RS, [Lsm], out=sm(4), in0=sm(4), in1=sm(8), op=ALU.add)
                V("dve", "reciprocal", RS, [Lsm], out=sm(5), in_=sm(4))
                V("dve", "tensor_tensor", RS, [Lsm], out=sm(6), in0=sm(3), in1=lr_, op=ALU.mult)
                V("dve", "tensor_tensor", RS, [Lsm], out=sm(8), in0=E1i, in1=li_, op=ALU.mult)
                V("dve", "tensor_tensor", RS, [Lsm], out=sm(6), in0=sm(6), in1=sm(8), op=ALU.add)
                V("dve", "tensor_tensor", RS, [Lsm], out=sm(6), in0=sm(6), in1=sm(5), op=ALU.mult)
                V("dve", "tensor_tensor", RS, [Lsm], out=sm(7), in0=E1i, in1=lr_, op=ALU.mult)
                V("dve", "tensor_tensor", RS, [Lsm], out=sm(8), in0=sm(3), in1=li_, op=ALU.mult)
                V("dve", "tensor_tensor", RS, [Lsm], out=sm(7), in0=sm(7), in1=sm(8), op=ALU.subtract)
                V("dve", "tensor_tensor", RS, [Lsm], out=sm(7), in0=sm(7), in1=sm(5), op=ALU.mult)
                V("act", "activation", RS, [Lsm], out=sm(10), in_=sm(1), func=AF.Exp, scale=8.0)
                V("dve", "tensor_scalar", RS, [Lsm], out=sm(11), in0=sm(2), scalar1=8.0, scalar2=None, op0=ALU.mult)
                bc16 = lambda a: a.unsqueeze(2).to_broadcast([128, GG, 16])
                RB = [LbSt, Lsm, Lcf]
                V("dve", "tensor_scalar", RB, [Lbsx], out=bsx[:, 0], in0=bSt[:, 1], scalar1=sgn[:, 0:1], scalar2=None, op0=ALU.mult)
                t0_ = s5f[0][:, :, 0:16]; t1_ = s5f[0][:, :, 16:32]
                V("dve", "tensor_tensor", RB, [Ls5f[0]], out=t0_, in0=bSt[:, 0], in1=bc16(sm(6)), op=ALU.mult)
                V("dve", "tensor_tensor", RB + [Lbsx], [Ls5f[0]], out=t1_, in0=bsx[:, 0], in1=bc16(sm(7)), op=ALU.mult)
                V("dve", "tensor_tensor", [Ls5f[0]], [Lbsx], out=bsx[:, 1], in0=t0_, in1=t1_, op=ALU.add)
                V("dve", "tensor_tensor", RB + [Lbsx], [Ls5f[0]], out=t0_, in0=bsx[:, 0], in1=bc16(sm(6)), op=ALU.mult)
                V("dve", "tensor_tensor", RB, [Ls5f[0]], out=t1_, in0=bSt[:, 0], in1=bc16(sm(7)), op=ALU.mult)
                V("dve", "tensor_tensor", [Ls5f[0]], [Lbsx], out=bsx[:, 2], in0=t0_, in1=t1_, op=ALU.subtract)

            def big_stage(c, sq):
                g0 = sq * GG
                gs = slice(g0, g0 + GG)
                s5sm, Lsm, Ep, LEp, bsx, Lbsx, bSt, LbSt = c.s5sm, c.Lsm, c.Ep, c.LEp, c.bsx, c.Lbsx, c.bSt, c.LbSt
                sm = lambda i: s5sm[:, i, :]
                Er = Ep[:, 4]; Ei = Ep[:, 5]
                Ls5c = LT()
                bc16 = lambda a: a.unsqueeze(2).to_broadcast([128, GG, 16])
                v4 = lambda a: a.rearrange("p g (j c) -> p g j c", c=16)
                bj = lambda a: a.unsqueeze(2).to_broadcast([128, GG, 8, 16])
                ej = lambda a: a.unsqueeze(3).to_broadcast([128, GG, 8, 16])
                bs_ = bsx[:, 1]; bx_ = bsx[:, 2]
                RP = [LEp, Lbsx]

                def cplx(dst_bf, e_r, e_i, sign_mode, Ldst):
                    a_, b_ = (e_r, e_i) if sign_mode == 0 else (e_i, e_r)
                    V("dve", "tensor_tensor", RP, [Ls5f[1]], out=v4(s5f[1]), in0=ej(a_), in1=bj(bs_), op=ALU.mult)
                    V("dve", "tensor_tensor", RP, [Ls5f[2]], out=v4(s5f[2]), in0=ej(b_), in1=bj(bx_), op=ALU.mult)
                    V("dve", "tensor_tensor", [Ls5f[1], Ls5f[2]], [Ldst], out=dst_bf, in0=s5f[1], in1=s5f[2],
                      op=(ALU.add if sign_mode == 0 else ALU.subtract))
                Pp, LPp = tb1k.get(); Ppt, LPpt = tb1k.get()
                g3 = lambda a: a.rearrange("p (g n) -> p g n", g=GG)
                cplx(g3(Pp), Er[:, :, 0:8], Ei[:, :, 0:8], 0, LPp)
                cplx(g3(Ppt), Er[:, :, 0:8], Ei[:, :, 0:8], 1, LPpt)
                cplx(Pb[:, 0], Er[:, :, 8:16], Ei[:, :, 8:16], 0, LPb)
                ECr = Er[:, :, 16:24]; ECi = Ei[:, :, 16:24]
                crD_ = bSt[:, 2]; ciD_ = bSt[:, 3]
                RC = [LEp, LbSt]
                V("dve", "tensor_tensor", RC, [Ls5f[1]], out=v4(s5f[1]), in0=ej(ECr), in1=bj(crD_), op=ALU.mult)
                V("dve", "tensor_tensor", RC, [Ls5f[2]], out=v4(s5f[2]), in0=ej(ECi), in1=bj(ciD_), op=ALU.mult)
                V("dve", "tensor_tensor", [Ls5f[1], Ls5f[2]], [Ls5f[3]], out=s5f[3], in0=s5f[1], in1=s5f[2], op=ALU.subtract)
                V("dve", "tensor_tensor", RC, [Ls5f[1]], out=v4(s5f[1]), in0=ej(ECi), in1=bj(crD_), op=ALU.mult)
                V("dve", "tensor_tensor", RC, [Ls5f[2]], out=v4(s5f[2]), in0=ej(ECr), in1=bj(ciD_), op=ALU.mult)
                V("dve", "tensor_tensor", [Ls5f[1], Ls5f[2]], [Ls5f[4]], out=s5f[4], in0=s5f[1], in1=s5f[2], op=ALU.add)
                V("dve", "tensor_scalar", [Ls5f[3], Lcf], [Ls5f[1]], out=s5f[1], in0=s5f[3], scalar1=mre[:, 0:1], scalar2=None, op0=ALU.mult)
                V("dve", "scalar_tensor_tensor", [Ls5f[4], Ls5f[1], Lcf], [LPb], out=Pb[:, 1], in0=s5f[4], scalar=nmim[:, 0:1], in1=s5f[1], op0=ALU.mult, op1=ALU.add)
                V("dve", "tensor_scalar", [Ls5f[4], Lcf], [Ls5f[2]], out=s5f[2], in0=s5f[4], scalar1=nmre[:, 0:1], scalar2=None, op0=ALU.mult)
                V("dve", "scalar_tensor_tensor", [Ls5f[3], Ls5f[2], Lcf], [LPb], out=Pb[:, 2], in0=s5f[3], scalar=nmim[:, 0:1], in1=s5f[2], op0=ALU.mult, op1=ALU.add)
                ps, Lp = psS.get()
                for g in range(GG):
                    V("pe", "matmul", [LPb], [Lp], ps[:, g * 128:(g + 1) * 128], lhsT=Pb[:, 0, g, :], rhs=Pb[:, 1, g, :], start=True, stop=True)
                V("dve", "tensor_tensor", [Lp, Lcf], [Ls5f[5]], out=s5f[5], in0=ps[:].rearrange("p (g n) -> p g n", g=GG),
                  in1=tmask.unsqueeze(1).to_broadcast([128, GG, 128]), op=ALU.mult)
                for g in range(GG):
                    V("dve", "scalar_tensor_tensor", [Ls5f[5], Lcf, Ll4], [LAb], out=Ab[:, 2, g, :], in0=ident_f, scalar=dcol_t[:, l, g0 + g:g0 + g + 1],
                      in1=s5f[5][:, g, :], op0=ALU.mult, op1=ALU.add)
                pb_, Lpb_ = psB.get()
                for i, (Px, LPx) in enumerate(((Pp, LPp), (Ppt, LPpt))):
                    for g in range(GG):
                        V("pe", "transpose", [LPx, Lcb], [Lpb_], out=pb_[:, (i * GG + g) * 128:(i * GG + g + 1) * 128], in_=Px[:, g * 128:(g + 1) * 128], identity=ident_b)
                V("act", "activation", [Lpb_], [LAb], out=Ab[:, 0:2].rearrange("p a g n -> p (a g n)"), in_=pb_[:, 0:2 * GG * 128], func=AF.Copy)
                DMA("sp", s5cb[l, sq, :, 0:2 * GG * 128], Pb[:, 1:3].rearrange("p a g n -> p (a g n)"), [LPb], [Ls5c])
                DMA("sp", s5cb[l, sq, :, 2 * GG * 128:5 * GG * 128], Ab.rearrange("p a g n -> p (a g n)"), [LAb], [Ls5c])
                DMA("sp", s5cf[l, sq, :, 0:64], s5sm.rearrange("p a g -> p (a g)"), [Lsm], [Ls5c])
                DMA("sp", s5cf[l, sq, :, 64:256], Ep[:, 4:6].rearrange("p a g n -> p (a g n)"), [LEp], [Ls5c])
                S5C_L[(l, sq)] = Ls5c

            for grp in range(NSUB // 4):
                subs = [grp * 4 + i for i in range(4)]
                lists = []
                for i, sq in enumerate(subs):
                    c = sctx[i]
                    for i2, src in enumerate([bS_in, bX_in, crD_in, ciD_in]):
                        DMA("sp", c.bSt[:, i2], src[:, l, sq * GG:(sq + 1) * GG, :], [], [c.LbSt])
                    prev_cap = CAP[0]
                    CAP[0] = []
                    small_stage(c, sq)
                    lists.append(CAP[0]); CAP[0] = prev_cap
                for k in range(max(len(x) for x in lists)):
                    for lst in lists:
                        if k < len(lst):
                            e_, m_, r_, w_, a_, kw_ = lst[k]
                            V(e_, m_, r_, w_, *a_, **kw_)
                for i, sq in enumerate(subs):
                    big_stage(sctx[i], sq)

        def s5_phase(l, hi, tiles, yaT, Lya):
            S.barrier()
            has_s = (hi == 1)
            KT = KP + (NS if has_s else 0)
            kbase = hi * KP
            PT = [(0, KP)] + ([(KP, NS)] if has_s else [])
            lamr_t, Ll1 = small["lamr"]; lami_t, Ll2 = small["lami"]; ldt_t, Ll3 = small["logdt"]; dcol_t, Ll4 = small["dcol"]
            Lsmall = [Ll1, Ll2, Ll3, Ll4, Lcf]
            cfA = Carver(arena_f[:, :]); cbA = Carver(arena_b[:, 5120:])

            class Ctx:
                pass
            ctxs = []
            for u in range(2):
                b = Ctx()
                b.s5sm = cfA.take([128, 16, GG]); b.Lsm = LT()
                b.Ep = cfA.take([128, 2, GG, 24]); b.LEp = LT()
                b.cosT = cfA.take([128, GG, KP]); b.sinT = cfA.take([128, GG, KP]); b.Ltab = LT()
                b.Yb = cfA.take([128, GG, KP]); b.LY = LT()
                b.Wb = cfA.take([128, GG, KP]); b.LW = LT()
                b.s0t = cfA.take([128, 2, GG, NS]); b.Ls0 = LT()
                b.sfin = cfA.take([128, 4, GG, NS]); b.Lsfin = LT()
                b.ctmp = cfA.take([128, GG, 2]); b.Lctmp = LT()
                b.PbC = cbA.take([128, 2, GG, 128]); b.LPb = LT()
                b.Ab = cbA.take([128, 3, GG, 128]); b.LAb = LT()
                b.Mc = cbA.take([128, GG, KP + NS]); b.Ms = cbA.take([128, GG, KP + NS]); b.LM = LT()
                b.yapt = cbA.take([128, 2, 8, GG * 16]); b.Lyapt = LT()
                ctxs.append(b)
            Uq_p = Pool_([(cbA.take([128, GG, KP + NS]), LT()) for i in range(4)])
            upt_p = Pool_([(cbA.take([128, 2, GG, 8, 16]), LT()) for i in range(4)])
            wu = Pool_([(cbA.take([128, 8, GG * 16]), LT()) for i in range(4)])

            def stage_u(sq):
                g0 = sq * GG
                wut, Lwu = wu.get()
                wload(wut, wview(w_in, l)[:, :, g0 * 16:g0 * 16 + GG * 16], Lwu)
                Uq, LU = Uq_p.get()
                upt, Lupt = upt_p.get()
                for m, (k0, nk) in enumerate(PT):
                    t0 = k0 * 8
                    tl = tiles_overlapping(tiles, t0, t0 + nk * 8)
                    ps, Lp = psF.get()
                    for j in range(8):
                        for c in range(8):
                            lh = hT[:, c, t0:t0 + nk * 8].rearrange("p (k j) -> p k j", j=8)[:, :, j]
                            V("pe", "matmul", [Lh[t] for t in tl] + [Lwu], [Lp], ps[:nk, j * 64:(j + 1) * 64], lhsT=lh, rhs=wut[:, c, :],
                              start=(c == 0), stop=(c == 7))
                    V("act", "activation", [Lp], [Lupt], out=upt[:nk, m].rearrange("p g j c -> p j g c"),
                      in_=ps[:nk, :].rearrange("p (j g c) -> p j g c", j=8, g=GG), func=AF.Copy)
                    pb_, Lpb_ = psB.get()
                    for g in range(GG):
                        V("pe", "transpose", [Lupt, Lcb], [Lpb_], out=pb_[:, g * 128:g * 128 + nk], in_=upt[:nk, m, g].rearrange("p j c -> p (j c)"),
                          identity=ident_b[:nk, :nk])
                    V("dve", "tensor_copy", [Lpb_], [LU], out=Uq[:, :, k0:k0 + nk], in_=pb_[:, 0:GG * 128].rearrange("p (g n) -> p g n", g=GG)[:, :, :nk])
                return Uq, LU


            f2 = lambda a: a.rearrange("p g k -> p (g k)")

            def stepL(b):
                sq = b.sq
                Lc_ = S5C_L[(l, sq)]
                DMA("sp", b.PbC.rearrange("p a g n -> p (a g n)"), s5cb[l, sq, :, 0:2 * GG * 128], [Lc_], [b.LPb])
                DMA("sp", b.Ab.rearrange("p a g n -> p (a g n)"), s5cb[l, sq, :, 2 * GG * 128:5 * GG * 128], [Lc_], [b.LAb])
                DMA("sp", b.s5sm.rearrange("p a g -> p (a g)"), s5cf[l, sq, :, 0:64], [Lc_], [b.Lsm])
                DMA("sp", b.Ep.rearrange("p a g n -> p (a g n)"), s5cf[l, sq, :, 64:256], [Lc_], [b.LEp])
                if has_s:
                    DMA("sp", b.s0t[:, 0], s0S_in[:, l, b.gs, :], [], [b.Ls0]); DMA("sp", b.s0t[:, 1], s0X_in[:, l, b.gs, :], [], [b.Ls0])

            def stepT(b):
                s5sm, Lsm, Yb, Wb, LY, sinT, cosT, Ltab = b.s5sm, b.Lsm, b.Yb, b.Wb, b.LY, b.sinT, b.cosT, b.Ltab
                kv = kvec[:, kbase:kbase + KP].unsqueeze(1).to_broadcast([128, GG, KP])
                V("dve", "tensor_tensor", [Lsm, Lcf], [LY], out=Yb, in0=s5sm[:, 11, :].unsqueeze(2).to_broadcast([128, GG, KP]), in1=kv, op=ALU.mult)
                range_reduce(Wb, Yb, LY, sinT, False)
                V("act", "activation", [LY], [Ltab], out=sinT, in_=Wb, func=AF.Sin)
                V("act", "activation", [LY], [LY], out=Yb, in_=Wb, func=AF.Abs)
                V("act", "activation", [LY, Lcf], [Ltab], out=cosT, in_=Yb, func=AF.Sin, scale=-1.0, bias=cfs("halfpi")[:, 0:1])

            def stepX(b):
                s5sm, Lsm, Yb, Wb, LY, LW, sinT, cosT, Ltab = b.s5sm, b.Lsm, b.Yb, b.Wb, b.LY, b.LW, b.sinT, b.cosT, b.Ltab
                Ab, LAb, Uq, LU = b.Ab, b.LAb, b.Uq, b.LU
                psx, Lpx = psF.get(); psxt, Lpxt = psF.get()
                for g in range(GG):
                    V("pe", "matmul", [LAb, LU], [Lpx], psx[:, g * KP:(g + 1) * KP], lhsT=Ab[:, 0, g, :], rhs=Uq[:, g, 0:KP], start=True, stop=True)
                    V("pe", "matmul", [LAb, LU], [Lpxt], psxt[:, g * KP:(g + 1) * KP], lhsT=Ab[:, 1, g, :], rhs=Uq[:, g, 0:KP], start=True, stop=True)
                V("dve", "tensor_tensor", [Lpx, Ltab], [LY], out=f2(Yb), in0=psx[:], in1=f2(cosT), op=ALU.mult)
                V("dve", "tensor_tensor", [Lpxt, Ltab], [LW], out=f2(Wb), in0=psxt[:], in1=f2(sinT), op=ALU.mult)
                V("dve", "tensor_tensor", [LY, LW], [LY], out=f2(Yb), in0=f2(Yb), in1=f2(Wb), op=ALU.add)
                if hi == 1:
                    V("dve", "tensor_tensor", [Lsm, Lcar], [b.Lctmp], out=b.ctmp[:, :, 0], in0=s5sm[:, 10, :], in1=Wlast[:, l, b.gs], op=ALU.mult)
                    V("dve", "tensor_tensor", [LY, b.Lctmp], [LY], out=Yb[:, :, 0], in0=Yb[:, :, 0], in1=b.ctmp[:, :, 0], op=ALU.add)

            def stepS(b):
                s5sm, Lsm, Yb, Wb, LY, LW, sinT, cosT, Ltab = b.s5sm, b.Lsm, b.Yb, b.Wb, b.LY, b.LW, b.sinT, b.cosT, b.Ltab
                Ab, LAb, Uq, LU, Mc, Ms, LM = b.Ab, b.LAb, b.Uq, b.LU, b.Mc, b.Ms, b.LM
                s0t, Ls0, sfin, Lsfin, LEp, gs = b.s0t, b.Ls0, b.sfin, b.Lsfin, b.LEp, b.gs
                Er = b.Ep[:, 0]; Ei = b.Ep[:, 1]
                for g in range(GG):
                    V("dve", "tensor_tensor_scan", [LY, Lsm, LW], [LW], out=Wb[:, g, :], data0=s5sm[:, 10, g:g + 1].to_broadcast([128, KP]), data1=Yb[:, g, :],
                      initial=0.0, op0=ALU.mult, op1=ALU.add)
                if hi == 0:
                    V("dve", "memset", [], [LM], Mc[:, :, 0:1], 0.0)
                    V("dve", "memset", [], [LM], Ms[:, :, 0:1], 0.0)
                else:
                    V("dve", "tensor_copy", [Lcar], [LM], out=Mc[:, :, 0], in_=Mclast[:, l, gs])
                    V("dve", "tensor_copy", [Lcar], [LM], out=Ms[:, :, 0], in_=Mslast[:, l, gs])
                V("dve", "tensor_tensor", [LW, Ltab], [LM], out=Mc[:, :, 1:KP], in0=Wb[:, :, 0:KP - 1], in1=cosT[:, :, 0:KP - 1], op=ALU.mult)
                V("dve", "tensor_tensor", [LW, Ltab], [LM], out=Ms[:, :, 1:KP], in0=Wb[:, :, 0:KP - 1], in1=sinT[:, :, 0:KP - 1], op=ALU.mult)
                if hi == 0:
                    V("dve", "tensor_copy", [LW], [Lcar], out=Wlast[:, l, gs], in_=Wb[:, :, KP - 1])
                    V("dve", "tensor_tensor", [LW, Ltab], [Lcar], out=Mclast[:, l, gs], in0=Wb[:, :, KP - 1], in1=cosT[:, :, KP - 1], op=ALU.mult)
                    V("dve", "tensor_tensor", [LW, Ltab], [Lcar], out=Mslast[:, l, gs], in0=Wb[:, :, KP - 1], in1=sinT[:, :, KP - 1], op=ALU.mult)
                else:
                    V("dve", "memset", [], [LM], Ms[:, :, KP:KT], 0.0)
                    V("act", "activation", [Ls0], [LM], out=Mc[:, :, KP:KT], in_=s0t[:, 0], func=AF.Copy)
                    V("dve", "tensor_tensor", [LW, Ltab], [Lsfin], out=sfin[:, 0, :, 0], in0=Wb[:, :, KP - 1], in1=cosT[:, :, KP - 1], op=ALU.mult)
                    V("dve", "tensor_tensor", [LW, Ltab], [Lsfin], out=sfin[:, 1, :, 0], in0=Wb[:, :, KP - 1], in1=sinT[:, :, KP - 1], op=ALU.mult)
                    ps, Lp = psF.get()
                    V("pe", "matmul", [Lsfin, Lcf], [Lp], ps[:, 0:GG], lhsT=pswap, rhs=sfin[:, 1, :, 0], start=True, stop=True)
                    V("dve", "tensor_tensor", [Lp, Lsfin], [Lssp], out=ssm_p_sb[:, l, gs], in0=ps[:, 0:GG], in1=sfin[:, 0, :, 0], op=ALU.add)
                    psx, Lpx = psF.get()
                    for g in range(GG):
                        V("pe", "matmul", [LAb, LU], [Lpx], psx[:, g * NS:(g + 1) * NS], lhsT=Ab[:, 0, g, :], rhs=Uq[:, g, KP:KT], start=True, stop=True)
                    Lr8 = Er[:, :, 23]; Li8 = Ei[:, :, 23]
                    bns = lambda a: a.unsqueeze(2).to_broadcast([128, GG, NS])
                    V("dve", "tensor_tensor", [Ls0, LEp], [Lsfin], out=sfin[:, 2], in0=s0t[:, 0], in1=bns(Lr8), op=ALU.mult)
                    V("dve", "scalar_tensor_tensor", [Ls0, LEp, Lcf], [Lsfin], out=sfin[:, 3], in0=s0t[:, 1], scalar=sgn[:, 0:1], in1=bns(Li8), op0=ALU.mult, op1=ALU.mult)
                    V("dve", "tensor_tensor", [Lsfin], [Lsfin], out=sfin[:, 2], in0=sfin[:, 2], in1=sfin[:, 3], op=ALU.add)
                    V("dve", "tensor_tensor", [Lsfin, Lpx], [Lsss], out=ssm_s_sb[:, l, gs, :], in0=sfin[:, 2], in1=psx[:, 0:GG * NS].rearrange("p (g s) -> p g s", g=GG), op=ALU.add)

            def stepY(b):
                Ab, LAb, Uq, LU, Mc, Ms, LM, PbC, LPb, yapt, Lyapt, g0 = b.Ab, b.LAb, b.Uq, b.LU, b.Mc, b.Ms, b.LM, b.PbC, b.LPb, b.yapt, b.Lyapt, b.g0
                for m, (k0, nk) in enumerate(PT):
                    ps, Lp = psF.get()
                    for g in range(GG):
                        o_ = ps[:nk, g * 128:(g + 1) * 128]
                        V("pe", "matmul", [LU, LAb], [Lp], o_, lhsT=Uq[:, g, k0:k0 + nk], rhs=Ab[:, 2, g, :], start=True, stop=False)
                        V("pe", "matmul", [LM, LPb], [Lp], o_, lhsT=Mc[:, g, k0:k0 + nk], rhs=PbC[:, 0, g, :], start=False, stop=False)
                        V("pe", "matmul", [LM, LPb], [Lp], o_, lhsT=Ms[:, g, k0:k0 + nk], rhs=PbC[:, 1, g, :], start=False, stop=True)
                    ta, La_ = t2k.get(); tb_, Lb_ = t2k.get()
                    V("act", "activation", [Lp], [La_], out=ta[:nk], in_=ps[:nk], func=AF.Square)
                    V("dve", "tensor_scalar", [La_], [La_], out=ta[:nk], in0=ta[:nk], scalar1=0.044715, scalar2=1.0, op0=ALU.mult, op1=ALU.add)
                    V("dve", "tensor_tensor", [La_, Lp], [La_], out=ta[:nk], in0=ta[:nk], in1=ps[:nk], op=ALU.mult)
                    V("act", "activation", [La_], [Lb_], out=tb_[:nk], in_=ta[:nk], func=AF.Sigmoid, scale=1.5957691216)
                    V("dve", "tensor_tensor", [Lb_, Lp], [Lyapt], out=yapt[:nk, m].rearrange("p t (g c) -> p g t c", g=GG),
                      in0=tb_[:nk].rearrange("p (g t c) -> p g t c", g=GG, t=8), in1=ps[:nk].rearrange("p (g t c) -> p g t c", g=GG, t=8), op=ALU.mult)
                    pb_, Lpb_ = psB.get()
                    for t in range(8):
                        V("pe", "transpose", [Lyapt, Lcb], [Lpb_], out=pb_[0:GG * 16, t * 128:t * 128 + nk], in_=yapt[:nk, m, t, :], identity=ident_b[:nk, :nk])
                    tok0 = k0 * 8
                    tl = tiles_overlapping(tiles, tok0, tok0 + nk * 8)
                    cq = (g0 * 16) // 128; p0 = (g0 * 16) % 128
                    V("dve", "tensor_copy", [Lpb_], [Lya[t] for t in tl], out=yaT[p0:p0 + GG * 16, cq, tok0:tok0 + nk * 8].rearrange("p (k j) -> p j k", j=8),
                      in_=pb_[0:GG * 16, :].rearrange("p (t k) -> p t k", t=8)[:, :, :nk])

            pairs = [(2 * i, 2 * i + 1) for i in range(NSUB // 2)]
            Ubuf = {0: stage_u(0), 1: stage_u(1)}
            for pi, pr in enumerate(pairs):
                for u, sq in enumerate(pr):
                    b = ctxs[u]
                    b.sq = sq; b.g0 = sq * GG; b.gs = slice(sq * GG, sq * GG + GG)
                    b.Uq, b.LU = Ubuf.pop(sq)
                if pi + 1 < len(pairs):
                    for sq2 in pairs[pi + 1]:
                        Ubuf[sq2] = stage_u(sq2)
                for step in (stepL, stepT, stepX, stepS, stepY):
                    for u in range(2):
                        step(ctxs[u])
            if hi == 1:
                DMA("sp", ssmp_out[:, l, :], ssm_p_sb[:, l, :], [Lssp], [])
                DMA("sp", ssms_out[:, l], ssm_s_sb[:, l], [Lsss], [])

        def phase_B(l, tiles, yaT, Lya, wo, Lwo):
            S.barrier()
            cbA = Carver(arena_b[:, 5120 + 8192:])
            wgl = cbA.take([128, 4, 512]); Lwgl = LT()
            wso = cbA.take([128, 4, 1024]); Lwso = LT()
            wga = cbA.take([128, 8, 1024]); Lwga = LT()
            mix = cbA.take([128, 8, 512]); Lmix = LT()
            ya2 = cbA.take([128, 4, 512]); Ly2 = LT()
            wload(wgl, w_glu[l].rearrange("(c p) n -> p c n", p=128), Lwgl)
            wload(wso, w_sso[l].rearrange("(c p) n -> p c n", p=128), Lwso)
            GA0 = 3584
            wload(wga, wview(w_in, l)[:, :, GA0:GA0 + 1024], Lwga)
            wload(wo, wview(w_o, l), Lwo)
            for ti, (t0, n) in enumerate(tiles):
                for oc in range(4):
                    ps, Lp = psF.get()
                    proj_fm(ps, Lp, wgl, Lwgl, oc * 128, 4, lambda kc: yaT[:, kc, t0:t0 + n], [Lya[ti]], n)
                    sg, Lsg = tb1k.get()
                    V("act", "activation", [Lp], [Lsg], out=sg[:, :n], in_=ps[:, :n], func=AF.Sigmoid)
                    V("dve", "tensor_tensor", [Lsg, Lya[ti]], [Ly2], out=ya2[:, oc, :n], in0=sg[:, :n], in1=yaT[:, oc, t0:t0 + n], op=ALU.mult)
                for oc in range(8):
                    ps, Lp = psF.get(); ps2, Lp2 = psF.get()
                    proj_fm(ps, Lp, wso, Lwso, oc * 128, 4, lambda kc: ya2[:, kc, :n], [Ly2], n)
                    proj_fm(ps2, Lp2, wga, Lwga, oc * 128, 8, lambda kc: hT[:, kc, t0:t0 + n], [Lh[ti]], n)
                    sg, Lsg = t2k.get()
                    V("act", "activation", [Lp2], [Lsg], out=sg[:, :n], in_=ps2[:, :n], func=AF.Sigmoid)
                    V("dve", "tensor_tensor", [Lsg, Lp], [Lmix], out=mix[:, oc, :n], in0=sg[:, :n], in1=ps[:, :n], op=ALU.mult)
                if "B" not in skip:
                    apply_wo(ti, t0, n, wo, Lwo, mix, Lmix)

        def phase_C(l, hi, tiles, oT, LoT):
            S.barrier()
            cfA = Carver(arena_f[:, :])
            cbY = Carver(arena_b[:, 0:5120])
            cbW = Carver(arena_b[:, 5120:5120 + 8192])
            cbA = Carver(arena_b[:, 5120 + 8192 + 9216:])
            r_st = cfA.take([128, 4, 256]); Lr_st = LT()
            orw_p = Pool_([(cfA.take([128, 4, 2, 128]), LT()) for i in range(1)])
            qf_p = Pool_([(cfA.take([128, 4, 256]), LT()) for i in range(1)])
            r0_p = Pool_([(cfA.take([128, 4, 256]), LT()) for i in range(2)])
            rn_p = Pool_([(cfA.take([128, 4, 256]), LT()) for i in range(2)])
            wq4 = cbA.take([128, 8, 2048]); Lwq4 = [LT() for _ in range(4)]
            r_bf = cbY.take([128, 4, 256]); Lr_bf = LT()
            qk_p = Pool_([(cbY.take([128, 4, 2, 128]), LT()) for i in range(2)])
            qkT_p = Pool_([(cbY.take([128, 4, 2, 128]), LT()) for i in range(1)])
            sc_p = Pool_([(cbY.take([128, 4, 128]), LT()) for i in range(2)])
            v_p = Pool_([(cbW.take([128, 4, 256]), LT()) for i in range(2)])
            sq_p = Pool_([(cbW.take([128, 1024]), LT()) for i in range(2)])
            r0b_p = Pool_([(cbW.take([128, 4, 256]), LT()) for i in range(2)])
            km_p = Pool_([(cbW.take([128, 4, 128]), LT()) for i in range(2)])
            goff = GOFF[hi]
            wv_ = wview(w_in, l)
            for hh in range(4):
                wload(wq4[:, :, hh * 512:hh * 512 + 128], wv_[:, :, 512 + hh * 128:512 + (hh + 1) * 128], Lwq4[hh])
                wload(wq4[:, :, hh * 512 + 128:hh * 512 + 256], wv_[:, :, 1024 + hh * 128:1024 + (hh + 1) * 128], Lwq4[hh])
                wload(wq4[:, :, hh * 512 + 256:hh * 512 + 512], wv_[:, :, 1536 + hh * 256:1536 + (hh + 1) * 256], Lwq4[hh])
            if hi == 0:
                V("dve", "memset", [], [Lr_st], r_st, 0.0)
            else:
                DMA("sp", r_st, rcar[l].rearrange("h d v -> d h v"), [Lrcar[l]], [Lr_st])
            V("act", "activation", [Lr_st], [Lr_bf], out=r_bf, in_=r_st, func=AF.Copy)
            psQ = psBig[:, :]
            LQ = [it[1] for it in psF.items]
            blocks = []
            for ti, (t0, n) in enumerate(tiles):
                for b in range(n // 128):
                    blocks.append((ti, t0 + b * 128))
            for (ti, t0) in blocks:
                tg = goff + t0
                is_s = (tg >= SEQ)
                blk = 16 if is_s else tg // 128
                kind = 1 if is_s else 0
                for hh in range(4):
                    for c in range(8):
                        V("pe", "matmul", [Lh[ti], Lwq4[hh]], [LQ[hh]], psQ[:, hh * 512:(hh + 1) * 512], lhsT=hT[:, c, t0:t0 + 128], rhs=wq4[:, c, hh * 512:(hh + 1) * 512],
                          start=(c == 0), stop=(c == 7))
                qf, Lqf = qf_p.get()
                vt, Lv = v_p.get()
                for hh in range(4):
                    V("act", "activation", [LQ[hh]], [Lqf], out=qf[:, hh, :], in_=psQ[:, hh * 512:hh * 512 + 256], func=AF.Copy)
                    V("act", "activation", [LQ[hh]], [Lv], out=vt[:, hh, :], in_=psQ[:, hh * 512 + 256:hh * 512 + 512], func=AF.Copy)
                x1 = qf.rearrange("p h (a f d) -> p h a f d", a=2, f=2)[:, :, :, 0, :]
                x2 = qf.rearrange("p h (a f d) -> p h a f d", a=2, f=2)[:, :, :, 1, :]
                cs_ = rope[:, blk, 0, :].unsqueeze(1).unsqueeze(1).to_broadcast([128, 4, 2, 64])
                sn_ = rope[:, blk, 1, :].unsqueeze(1).unsqueeze(1).to_broadcast([128, 4, 2, 64])
                pr = [t2k.get() for _ in range(4)]
                v4 = lambda a: a.rearrange("p (h a d) -> p h a d", h=4, a=2)
                V("dve", "tensor_tensor", [Lqf, Lrope], [pr[0][1]], out=v4(pr[0][0]), in0=x1, in1=cs_, op=ALU.mult)
                V("dve", "tensor_tensor", [Lqf, Lrope], [pr[1][1]], out=v4(pr[1][0]), in0=x2, in1=sn_, op=ALU.mult)
                V("dve", "tensor_tensor", [Lqf, Lrope], [pr[2][1]], out=v4(pr[2][0]), in0=x1, in1=sn_, op=ALU.mult)
                V("dve", "tensor_tensor", [Lqf, Lrope], [pr[3][1]], out=v4(pr[3][0]), in0=x2, in1=cs_, op=ALU.mult)
                V("dve", "tensor_tensor", [pr[0][1], pr[1][1]], [pr[0][1]], out=pr[0][0], in0=pr[0][0], in1=pr[1][0], op=ALU.subtract)
                V("dve", "tensor_tensor", [pr[2][1], pr[3][1]], [pr[2][1]], out=pr[2][0], in0=pr[2][0], in1=pr[3][0], op=ALU.add)
                qk, Lqk = qk_p.get()
                sct = qksc.rearrange("p (k h a) -> p k h a", k=2, h=4)[:, kind].unsqueeze(3).to_broadcast([128, 4, 2, 64])
                V("dve", "tensor_tensor", [pr[0][1], Lcf], [Lqk], out=qk[:, :, :, 0:64], in0=v4(pr[0][0]), in1=sct, op=ALU.mult)
                V("dve", "tensor_tensor", [pr[2][1], Lcf], [Lqk], out=qk[:, :, :, 64:128], in0=v4(pr[2][0]), in1=sct, op=ALU.mult)
                pb_, Lpb_ = psB.get()
                for hh in range(4):
                    for a in range(2):
                        V("pe", "transpose", [Lqk, Lcb], [Lpb_], out=pb_[:, (hh * 2 + a) * 128:(hh * 2 + a + 1) * 128], in_=qk[:, hh, a, :], identity=ident_b)
                qkT, LqkT = qkT_p.get()
                V("dve", "tensor_copy", [Lpb_], [LqkT], out=qkT.rearrange("p h a n -> p (h a n)"), in_=pb_[:, 0:1024])
                ps2, Lp2 = psF.get()
                for hh in range(4):
                    V("pe", "matmul", [LqkT], [Lp2], ps2[:, hh * 128:(hh + 1) * 128], lhsT=qkT[:, hh, 1, :], rhs=qkT[:, hh, 0, :], start=True, stop=True)
                sc, Lsc = sc_p.get()
                mk = (cmask_s if is_s else cmask_p).unsqueeze(1).to_broadcast([128, 4, 128])
                V("dve", "tensor_tensor", [Lp2, Lcf], [Lsc], out=sc, in0=ps2[:, :].rearrange("p (h n) -> p h n", h=4), in1=mk, op=ALU.mult)
                po = [psF.get(), psF.get()]
                orw, Lor = orw_p.get()
                for hh in range(4):
                    pso, Lpo = po[hh // 2]
                    for e_ in range(2):
                        o_ = pso[:, ((hh % 2) * 2 + e_) * 128:((hh % 2) * 2 + e_ + 1) * 128]
                        V("pe", "matmul", [Lv, Lsc], [Lpo], o_, lhsT=vt[:, hh, e_ * 128:(e_ + 1) * 128], rhs=sc[:, hh, :], start=True, stop=is_s)
                        if not is_s:
                            V("pe", "matmul", [Lr_bf, LqkT], [Lpo], o_, lhsT=r_bf[:, hh, e_ * 128:(e_ + 1) * 128], rhs=qkT[:, hh, 0, :], start=False, stop=True)
                orf = orw.rearrange("p h e n -> p (h e n)")
                for i2 in range(2):
                    V("act", "activation", [po[i2][1]], [Lor], out=orf[:, i2 * 512:(i2 + 1) * 512], in_=po[i2][0][:, :], func=AF.Copy)
                if not is_s:
                    pd = [psS.get(), psS.get()]
                    for hh in range(4):
                        psd, Lpd = pd[hh // 2]
                        V("pe", "matmul", [Lqk, Lv], [Lpd], psd[:, (hh % 2) * 256:(hh % 2 + 1) * 256], lhsT=qk[:, hh, 1, :], rhs=vt[:, hh, :], start=True, stop=True)
                    for i2 in range(2):
                        rv = r_st[:, i2 * 2:i2 * 2 + 2, :].rearrange("p h v -> p (h v)")
                        V("dve", "tensor_tensor", [pd[i2][1], Lr_st], [Lr_st], out=rv, in0=rv, in1=pd[i2][0][:, :], op=ALU.add)
                    gtab = gct.rearrange("p (k h) -> p k h", k=2)[:, 0].unsqueeze(2).to_broadcast([128, 4, 256])
                    V("dve", "tensor_tensor", [Lr_st, Lcf], [Lr_st], out=r_st, in0=r_st, in1=gtab, op=ALU.mult)
                    V("act", "activation", [Lr_st], [Lr_bf], out=r_bf, in_=r_st, func=AF.Copy)
                    if tg == 1024 - 128:
                        DMA("sp", rcar[l].rearrange("h d v -> d h v"), r_st, [Lr_st], [Lrcar[l]])
                    if tg == SEQ - 128:
                        DMA("sp", retp_out[l].rearrange("h d v -> d h v"), r_st, [Lr_st], [])
                else:
                    pin = [psF.get(), psF.get()]
                    gtab = gct.rearrange("p (k h) -> p k h", k=2)[:, 1].unsqueeze(2).to_broadcast([128, 4, 256])
                    def _ld_r0(sx):
                        r0x, Lr0x = r0_p.get()
                        DMA("sp", r0x, sret_in[l, sx].rearrange("h d v -> d h v"), [], [Lr0x])
                        return r0x, Lr0x
                    r0_next = _ld_r0(0)
                    for s_ in range(NS):
                        r0, Lr0 = r0_next
                        if s_ + 1 < NS:
                            r0_next = _ld_r0(s_ + 1)
                        r0b, Lr0b = r0b_p.get()
                        V("act", "activation", [Lr0], [Lr0b], out=r0b, in_=r0, func=AF.Copy)
                        for hh in range(4):
                            psi, Lpi = pin[hh // 2]
                            for e_ in range(2):
                                c0_ = ((hh % 2) * 2 + e_) * 128 + s_ * 8
                                V("pe", "matmul", [Lr0b, LqkT], [Lpi], psi[:, c0_:c0_ + 8], lhsT=r0b[:, hh, e_ * 128:(e_ + 1) * 128],
                                  rhs=qkT[:, hh, 0, s_ * 8:s_ * 8 + 8], start=True, stop=True)
                        km, Lkm = km_p.get()
                        V("dve", "tensor_scalar", [Lqk, Lcf], [Lkm], out=km, in0=qk[:, :, 1, :], scalar1=rowmask[:, s_:s_ + 1], scalar2=None, op0=ALU.mult)
                        pd = [psS.get(), psS.get()]
                        for hh in range(4):
                            psd, Lpd = pd[hh // 2]
                            V("pe", "matmul", [Lkm, Lv], [Lpd], psd[:, (hh % 2) * 256:(hh % 2 + 1) * 256], lhsT=km[:, hh, :], rhs=vt[:, hh, :], start=True, stop=True)
                        rn, Lrn = rn_p.get()
                        for i2 in range(2):
                            V("dve", "tensor_tensor", [pd[i2][1], Lr0], [Lrn], out=rn[:, i2 * 2:i2 * 2 + 2, :].rearrange("p h v -> p (h v)"),
                              in0=r0[:, i2 * 2:i2 * 2 + 2, :].rearrange("p h v -> p (h v)"), in1=pd[i2][0][:, :], op=ALU.add)
                        V("dve", "tensor_tensor", [Lrn, Lcf], [Lrn], out=rn, in0=rn, in1=gtab, op=ALU.mult)
                        DMA("sp", rets_out[l, s_].rearrange("h d v -> d h v"), rn, [Lrn], [])
                    for i2 in range(2):
                        tin, Ltin = t2k.get()
                        V("act", "activation", [pin[i2][1]], [Ltin], out=tin[:, :], in_=pin[i2][0][:, :], func=AF.Copy)
                        V("dve", "tensor_tensor", [Lor, Ltin], [Lor], out=orf[:, i2 * 512:(i2 + 1) * 512], in0=orf[:, i2 * 512:(i2 + 1) * 512], in1=tin[:, :], op=ALU.add)
                sq, Lsq = sq_p.get()
                V("dve", "tensor_tensor", [Lor], [Lsq], out=sq, in0=orf, in1=orf, op=ALU.mult)
                ps5, Lp5 = psF.get()
                for hh in range(4):
                    for e_ in range(2):
                        V("pe", "matmul", [Lsq, Lcb], [Lp5], ps5[:, hh * 128:(hh + 1) * 128], lhsT=ones_b, rhs=sq[:, (hh * 2 + e_) * 128:(hh * 2 + e_ + 1) * 128],
                          start=(e_ == 0), stop=(e_ == 1))
                rs, Lrs = t2k.get()
                V("act", "activation", [Lp5, Lcf], [Lrs], out=rs[:, :], in_=ps5[:, :], func=AF.Sqrt, bias=epsc[:, 0:1], scale=1.0 / 256)
                V("dve", "reciprocal", [Lrs], [Lrs], out=rs[:, :], in_=rs[:, :])
                V("dve", "tensor_tensor", [Lor, Lrs], [LoT[ti]], out=oT[:, :, t0:t0 + 128].rearrange("p (h e) n -> p h e n", h=4), in0=orw,
                  in1=rs[:, :].rearrange("p (h n) -> p h n", h=4).unsqueeze(2).to_broadcast([128, 4, 2, 128]), op=ALU.mult)

        def phase_D(l, tiles, oT, LoT, wo, Lwo):
            S.barrier()
            cbA = Carver(arena_b[:, 5120 + 8192 + 9216:])
            wX = cbA.take([128, 8, 1024]); LwX = LT()
            wY = cbA.take([128, 8, 1024]); LwY = LT()
            mix = arena_b[:, 0:4096].rearrange("p (c n) -> p c n", c=8); Lmix = LT()
            GR0 = 2560
            wload(wX, wview(w_in, l)[:, :, GR0:GR0 + 1024], LwX)
            wload(wY, wview(w_ro, l), LwY)
            wload(wo, wview(w_o, l), Lwo)
            for ti, (t0, n) in enumerate(tiles):
                for oc in range(8):
                    ps, Lp = psF.get()
                    proj_fm(ps, Lp, wX, LwX, oc * 128, 8, lambda kc: hT[:, kc, t0:t0 + n], [Lh[ti]], n)
                    sg, Lsg = tb1k.get()
                    V("act", "activation", [Lp], [Lsg], out=sg[:, :n], in_=ps[:, :n], func=AF.Silu)
                    o_ = oT[:, oc, t0:t0 + n]
                    V("dve", "tensor_tensor", [Lsg, LoT[ti]], [LoT[ti]], out=o_, in0=o_, in1=sg[:, :n], op=ALU.mult)
            GB0 = 4608
            wload(wX, wview(w_in, l)[:, :, GB0:GB0 + 1024], LwX)
            for ti, (t0, n) in enumerate(tiles):
                for oc in range(8):
                    ps, Lp = psF.get(); ps2, Lp2 = psF.get()
                    proj_fm(ps, Lp, wY, LwY, oc * 128, 8, lambda kc: oT[:, kc, t0:t0 + n], [LoT[ti]], n)
                    proj_fm(ps2, Lp2, wX, LwX, oc * 128, 8, lambda kc: hT[:, kc, t0:t0 + n], [Lh[ti]], n)
                    sg, Lsg = t2k.get()
                    V("act", "activation", [Lp2], [Lsg], out=sg[:, :n], in_=ps2[:, :n], func=AF.Sigmoid)
                    V("dve", "tensor_tensor", [Lsg, Lp], [Lmix], out=mix[:, oc, :n], in0=sg[:, :n], in1=ps[:, :n], op=ALU.mult)
                if "D" not in skip:
                    apply_wo(ti, t0, n, wo, Lwo, mix, Lmix)

        GF = 4
        EXTRA_K = [10]

        def ffn(l, hi, tiles, extra=None):
            S.barrier()
            cfA = Carver(arena_f[:, :]); cbA = Carver(arena_b[:, :])
            conv0 = cfA.take([128, NCH, NS, 2]); Lc0 = LT()
            convp_sb = cfA.take([128, NCH, 2]); Lcp = LT()
            convs_sb = cfA.take([128, NCH, NS, 2]); Lcs = LT()
            actT = cbA.take([128, GF, NH]); Lact = [LT() for _ in range(3)]
            wup_p = Pool_([(cbA.take([128, 8, 2 * GF * 128]), LT()) for i in range(2)])
            wdn_p = Pool_([(cbA.take([128, GF, D]), LT()) for i in range(2)])
            upb = [[(cbA.take([128, 514]), LT()) for a in range(2)] for i in range(GF)]
            Dg = cbA.take([128, GF, 2, 3, 128]); LDg = LT()
            ups = Pool_([(cbA.take([128, NS, 10]), LT()) for i in range(2)]) if hi == 1 else None
            cw, Lcw = small["convw"]; cbv, Lcbv = small["convb"]
            if hi == 1:
                DMA("sp", conv0, conv0_in[:, l], [], [Lc0])
            ngroups = (22 + GF - 1) // GF
            for gi in range(ngroups):
                c0 = gi * GF
                ng = min(GF, 22 - c0)
                wu_, Lwu_ = wup_p.get(); wd_, Lwd_ = wdn_p.get()
                wload(wu_[:, :, 0:ng * 128], wview(w_up, l)[:, :, c0 * 128:(c0 + ng) * 128], Lwu_)
                wload(wu_[:, :, GF * 128:GF * 128 + ng * 128], wview(w_up, l)[:, :, DFF + c0 * 128:DFF + (c0 + ng) * 128], Lwu_)
                wload(wd_[:, 0:ng, :], w_dn[l, c0 * 128:(c0 + ng) * 128, :].rearrange("(c p) n -> p c n", p=128), Lwd_)
                for cc in range(ng):
                    for a in range(2):
                        ch = c0 + cc + a * 22
                        for j in range(3):
                            V("dve", "tensor_scalar", [Lcw, Lcb], [LDg], out=Dg[:, cc, a, j, :], in0=ident_b, scalar1=cw[:, l, j, ch:ch + 1], scalar2=None, op0=ALU.mult)
                pend_conv = []
                pend_down = []
                resmap = {}

                def emit_up(ti, t0, n, cc, a):
                    is_s = (n == 128)
                    ch = c0 + cc + a * 22
                    ps, Lp = psF.get()
                    proj_fm(ps, Lp, wu_, Lwu_, a * GF * 128 + cc * 128, 8, lambda kc: hT[:, kc, t0:t0 + n], [Lh[ti]], n)
                    if not is_s:
                        ub, Lub = upb[cc][a]
                        if ti == 0:
                            if hi == 0:
                                V("dve", "memset", [], [Lub], ub[:, 0:2], 0.0)
                            else:
                                V("dve", "tensor_copy", [Lccar], [Lub], out=ub[:, 0:2], in_=convcar[:, l, ch, :])
                        else:
                            V("dve", "tensor_copy", [Lub], [Lub], out=ub[:, 0:2], in_=ub[:, 512:514])
                        V("act", "activation", [Lp], [Lub], out=ub[:, 2:514], in_=ps[:, :], func=AF.Copy)
                        if ti == 1:
                            if hi == 0:
                                V("act", "activation", [Lp], [Lccar], out=convcar[:, l, ch, :], in_=ps[:, 510:512], func=AF.Copy)
                            else:
                                V("act", "activation", [Lp], [Lcp], out=convp_sb[:, ch, :], in_=ps[:, 510:512], func=AF.Copy)
                        return (ub, Lub)
                    else:
                        us, Lus = ups.get()
                        V("dve", "tensor_copy", [Lc0], [Lus], out=us[:, :, 0:2], in_=conv0[:, ch])
                        V("act", "activation", [Lp], [Lus], out=us[:, :, 2:10], in_=ps[:, 0:128].rearrange("p (s j) -> p s j", j=8), func=AF.Copy)
                        V("act", "activation", [Lp], [Lcs], out=convs_sb[:, ch], in_=ps[:, 0:128].rearrange("p (s j) -> p s j", j=8)[:, :, 6:8], func=AF.Copy)
                        return (us, Lus)

                def emit_conv(ti, t0, n, cc, a, buf):
                    is_s = (n == 128)
                    ch = c0 + cc + a * 22
                    ub, Lub = buf
                    ps2, Lp2 = psF.get()
                    for j in range(3):
                        if not is_s:
                            V("pe", "matmul", [LDg, Lub], [Lp2], ps2[:, :], lhsT=Dg[:, cc, a, j, :], rhs=ub[:, j:j + 512], start=(j == 0), stop=(j == 2))
                        else:
                            V("pe", "matmul", [LDg, Lub], [Lp2], ps2[:, 0:128], lhsT=Dg[:, cc, a, j, :], rhs=ub[:, :, j:j + 8], start=(j == 0), stop=(j == 2))
                    resmap[(ti, cc, a)] = (ps2, Lp2, ch)
                    if a == 1:
                        (pv, Lpv, chv) = resmap.pop((ti, cc, 0)); (pg, Lpg, chg) = resmap.pop((ti, cc, 1))
                        sg, Lsg = t2k.get()
                        V("act", "activation", [Lpg, Lcbv], [Lsg], out=sg[:, :n], in_=pg[:, :n], func=AF.Silu, bias=cbv[:, l, chg:chg + 1])
                        V("dve", "scalar_tensor_tensor", [Lpv, Lsg, Lcbv], [Lact[ti]], out=actT[:, cc, t0:t0 + n], in0=pv[:, :n], scalar=cbv[:, l, chv:chv + 1], in1=sg[:, :n], op0=ALU.add, op1=ALU.mult)

                def emit_down(ti, t0, n):
                    for oc in range(8):
                        ps, Lp = psF.get()
                        for cc in range(ng):
                            V("pe", "matmul", [Lwd_, Lact[ti]], [Lp], ps[:, :n], lhsT=wd_[:, cc, oc * 128:(oc + 1) * 128], rhs=actT[:, cc, t0:t0 + n], start=(cc == 0), stop=(cc == ng - 1))
                        V("dve", "tensor_tensor", [Lp, Lx[ti]], [Lx[ti]], out=xT[:, oc, t0:t0 + n], in0=xT[:, oc, t0:t0 + n], in1=ps[:, :n], op=ALU.add)

                for ti, (t0, n) in enumerate(tiles):
                    ui = 0
                    for cc in range(ng):
                        for a in range(2):
                            buf = emit_up(ti, t0, n, cc, a)
                            if pend_conv:
                                emit_conv(*pend_conv.pop(0))
                            pend_conv.append((ti, t0, n, cc, a, buf))
                            if extra:
                                for _ in range(min(EXTRA_K[0], len(extra))):
                                    emit_captured(extra.pop(0))
                            if ui == 2 and pend_down:
                                emit_down(*pend_down.pop(0))
                            ui += 1
                    pend_down.append((ti, t0, n))
                while pend_conv:
                    emit_conv(*pend_conv.pop(0))
                while pend_down:
                    emit_down(*pend_down.pop(0))
            while extra:
                emit_captured(extra.pop(0))
            if hi == 1:
                DMA("sp", convp_out[:, l], convp_sb, [Lcp], [])
                DMA("sp", convs_out[:, l], convs_sb, [Lcs], [])
                S.final_wait("sp", [Lcp, Lcs])

        out_L = []
        for hi in range(2):
            tiles = HT[hi]
            goff = GOFF[hi]
            nh = sum(n for _, n in tiles)
            S.barrier()
            DMA("sp", xT[:, :, 0:nh], xT_in[:, :, goff:goff + nh], [], [Lx[i] for i in range(len(tiles))])
            yaT = arena_b[:, 0:4 * NH].rearrange("p (c n) -> p c n", c=4)
            wo = arena_b[:, 5120:5120 + 8192].rearrange("p (c n) -> p c n", c=8)
            oT = arena_b[:, 5120 + 8192:5120 + 8192 + 9216].rearrange("p (c n) -> p c n", c=8)
            for l in range(nlayers):
                Lya = [LT() for _ in range(3)]; Lwo = LT(); LoT = [LT() for _ in range(3)]
                norm_to_h(l, "gmix", tiles)
                if "a" not in skip:
                    if hi == 0 and l == 0:
                        S.barrier()
                        s5_setup(0)
                    s5_phase(l, hi, tiles, yaT, Lya)
                if "b" not in skip:
                    phase_B(l, tiles, yaT, Lya, wo, Lwo)
                if "c" not in skip:
                    phase_C(l, hi, tiles, oT, LoT)
                if "d" not in skip:
                    phase_D(l, tiles, oT, LoT, wo, Lwo)
                if "F" not in skip:
                    norm_to_h(l, "gffn", tiles)
                    extra = None
                    if hi == 0 and l + 1 < nlayers and "a" not in skip:
                        CAP[0] = []
                        s5_setup(l + 1)
                        extra = CAP[0]; CAP[0] = None
                        EXTRA_K[0] = len(extra) // 90 + 1
                    ffn(l, hi, tiles, extra)
            S.barrier()
            gt, Lg = small["gfin"]
            yo = arena_f[:, 0:4096].rearrange("p (c n) -> p c n", c=8); Lyo = LT()
            for ti, (t0, n) in enumerate(tiles):
                r, Lr = rms_rstd(ti, t0, n, 1.0 / D)
                for c in range(8):
                    V("dve", "scalar_tensor_tensor", [Lx[ti], Lg, Lr], [Lyo], out=yo[:, c, :n], in0=xT[:, c, t0:t0 + n], scalar=gt[:, c:c + 1], in1=r[:, :n], op0=ALU.mult, op1=ALU.mult)
                DMA("sp", yT_out[:, :, goff + t0:goff + t0 + n], yo[:, :, :n], [Lyo], [])
            out_L.append(Lyo)
            S.final_wait("sp", [Lyo])
        S.barrier()
        S.emit(block)
    return nc


def _mk_consts():
    cf = {}
    cf["ident"] = np.eye(128, dtype=np.float32)
    jc = np.arange(128) // 16
    tc_t = np.arange(128) // 16
    cf["tmask"] = (tc_t[None, :] >= jc[:, None]).astype(np.float32)
    ps = np.zeros((128, 128), np.float32)
    for p in range(64):
        ps[64 + p, p] = -1.0
        ps[p, 64 + p] = 1.0
    cf["pswap"] = ps
    nv = np.array([7, 6, 5, 4, 3, 2, 1, 0, -1, -2, -3, -4, -5, -6, -7, -8, 1, 2, 3, 4, 5, 6, 7, 8], np.float32)
    cf["nvec"] = np.broadcast_to(nv, (128, 24)).copy()
    cf["kvec"] = np.broadcast_to(np.arange(256, dtype=np.float32), (128, 256)).copy()
    top = (np.arange(128) < 64)
    cf["sgn"] = np.where(top, -1.0, 1.0).astype(np.float32)[:, None]
    cf["mre"] = top.astype(np.float32)[:, None]
    cf["nmre"] = -top.astype(np.float32)[:, None]
    cf["nmim"] = -(~top).astype(np.float32)[:, None]
    cf["eps"] = np.full((128, 1), EPS, np.float32)
    cf["halfpi"] = np.full((128, 1), math.pi / 2, np.float32)
    cf["zero"] = np.zeros((128, 1), np.float32)
    i = np.arange(128, dtype=np.float64)
    qs = np.zeros((128, 8)); ks = np.zeros((128, 8))
    for h in range(4):
        g = 1.0 - 2.0 ** (-5 - h)
        qs[:, h] = g ** (i + 1); ks[:, h] = (128 ** -0.5) * g ** (-(i + 1))
        qs[:, 4 + h] = g ** ((i % 8) + 1); ks[:, 4 + h] = (128 ** -0.5) * g ** (-((i % 8) + 1))
    cf["qsc"] = qs.astype(np.float32); cf["ksc"] = ks.astype(np.float32)
    qk_ = np.zeros((128, 2, 4, 2)); gc_ = np.zeros((128, 2, 4))
    for kd in range(2):
        for h in range(4):
            qk_[:, kd, h, 0] = qs[:, kd * 4 + h]; qk_[:, kd, h, 1] = ks[:, kd * 4 + h]
            gc_[:, kd, h] = (1.0 - 2.0 ** (-5 - h)) ** (128 if kd == 0 else 8)
    cf["qksc"] = qk_.reshape(128, 16).astype(np.float32); cf["gct"] = gc_.reshape(128, 8).astype(np.float32)
    rm = np.zeros((128, 16), np.float32)
    for s in range(16):
        rm[s * 8:(s + 1) * 8, s] = 1.0
    cf["rowmask"] = rm
    j = np.arange(128)
    cf["cmask_p"] = (j[:, None] <= j[None, :]).astype(np.float32)
    cf["cmask_s"] = ((j[:, None] <= j[None, :]) & ((j[:, None] // 8) == (j[None, :] // 8))).astype(np.float32)
    off = {}; o = 0; parts = []
    for k, v in cf.items():
        off[k] = (o, v.shape[1]); o += v.shape[1]; parts.append(v)
    cfa = np.ascontiguousarray(np.concatenate(parts, axis=1))
    cb = {"ident": np.eye(128, dtype=np.float32), "ones": np.ones((128, 128), np.float32)}
    offb = {}; o = 0; partsb = []
    for k, v in cb.items():
        offb[k] = (o, v.shape[1]); o += v.shape[1]; partsb.append(v)
    cba = np.ascontiguousarray(np.concatenate(partsb, axis=1)).astype(ml_dtypes.bfloat16)
    half = 64
    inv = (10000.0 ** (-np.arange(half, dtype=np.float32) / half)).astype(np.float32)
    rope = np.zeros((128, 17, 2, 64), np.float32)
    for b in range(17):
        if b < 16:
            pos = (b * 128 + np.arange(128)).astype(np.float32)
        else:
            pos = (PAST + (np.arange(128) % 8)).astype(np.float32)
        ang = (pos[:, None] * inv[None, :]).astype(np.float32)
        rope[:, b, 0, :] = np.cos(ang); rope[:, b, 1, :] = np.sin(ang)
    return cfa, off, cba, offb, rope


CF_ARR, CF_OFF, CB_ARR, CB_OFF, ROPE_ARR = _mk_consts()
CF_N = CF_ARR.shape[1]
CB_N = CB_ARR.shape[1]

_NC_CACHE = {}


def _stack(a, b):
    return np.ascontiguousarray(np.concatenate([a, b], axis=0))


def make_in_maps(inp):
    f = lambda a: np.ascontiguousarray(np.asarray(a, dtype=np.float32))
    shared = {}
    for k, src in [("w_in", "w_in"), ("w_glu", "w_glu"), ("w_ssm_out", "w_ssm_out"), ("w_ret_out", "w_ret_out"), ("w_o", "w_o"), ("w_up", "w_up"), ("w_down", "w_down")]:
        shared[k] = f(inp[src])
    pl = lambda v: np.ascontiguousarray(f(v).reshape(DEPTH, -1, 128).transpose(2, 0, 1))
    shared["gmix"] = pl(inp["norm_mix"]); shared["gffn"] = pl(inp["norm_ffn"])
    shared["gfin"] = np.ascontiguousarray(f(inp["norm_final"]).reshape(8, 128).T)
    shared["convw"] = np.ascontiguousarray(f(inp["conv_w"]).reshape(DEPTH, 3, NCH, 128).transpose(3, 0, 1, 2))
    shared["convb"] = np.ascontiguousarray(f(inp["conv_b"]).reshape(DEPTH, NCH, 128).transpose(2, 0, 1))
    lr = f(inp["ssm_lam_re"]).transpose(2, 0, 1)
    li = f(inp["ssm_lam_im"]).transpose(2, 0, 1)
    shared["lamr"] = _stack(lr, lr); shared["lami"] = _stack(li, li)
    shared["logdt"] = np.ascontiguousarray(np.broadcast_to(f(inp["ssm_log_dt"])[None], (128, DEPTH, G)))
    br = f(inp["ssm_b_re"]).transpose(2, 0, 1, 3)
    bi = f(inp["ssm_b_im"]).transpose(2, 0, 1, 3)
    shared["bS"] = _stack(br, bi); shared["bX"] = _stack(bi, br)
    cr = f(inp["ssm_c_re"]).transpose(3, 0, 1, 2)
    ci = f(inp["ssm_c_im"]).transpose(3, 0, 1, 2)
    shared["crD"] = _stack(cr, cr); shared["ciD"] = _stack(ci, ci)
    d = f(inp["ssm_d"]).reshape(DEPTH, G, 16)
    dc = d.transpose(2, 0, 1)
    shared["dcol"] = np.ascontiguousarray(np.tile(dc, (8, 1, 1)))
    shared["cf32"] = CF_ARR; shared["cbf16"] = CB_ARR; shared["rope"] = ROPE_ARR
    xp = f(inp["x_prompt"]); xs = f(inp["x_sample"])
    sre = f(inp["state_ssm_re"]); sim = f(inp["state_ssm_im"]); sret = f(inp["state_ret"]); scv = f(inp["state_conv"])
    maps = []
    for ci_ in range(NCORES):
        m = dict(shared)
        S0 = ci_ * NS
        xt = np.concatenate([xp[ci_], xs[S0:S0 + NS].reshape(NS * DS, D)], axis=0)
        m["xT_in"] = np.ascontiguousarray(xt.T.reshape(8, 128, TOK).transpose(1, 0, 2))
        a = sre[:, S0:S0 + NS].transpose(3, 0, 2, 1)
        b = sim[:, S0:S0 + NS].transpose(3, 0, 2, 1)
        m["s0S"] = _stack(a, b); m["s0X"] = _stack(b, a)
        cv = scv[:, S0:S0 + NS].reshape(DEPTH, NS, 2, NCH, 128).transpose(4, 0, 3, 1, 2)
        m["conv0"] = np.ascontiguousarray(cv)
        m["sret"] = np.ascontiguousarray(sret[:, S0:S0 + NS])
        maps.append(m)
    return maps


def kernel(**inputs):
    if "nc" not in _NC_CACHE:
        _NC_CACHE["nc"] = build()
    nc = _NC_CACHE["nc"]
    maps = make_in_maps(inputs)
    res = run_bass_kernel_spmd(nc, maps, core_ids=list(range(NCORES)))
    R = res.results
    if "dbg_out" in R[0]:
        _NC_CACHE["dbg"] = [np.asarray(r["dbg_out"]) for r in R]
    B = NCORES
    y_p = np.zeros((B, SEQ, D), np.float32); y_s = np.zeros((B * NS, DS, D), np.float32)
    sre_p = np.zeros((DEPTH, B, G, P), np.float32); sim_p = np.zeros_like(sre_p)
    ret_p = np.zeros((DEPTH, B, 4, 128, 256), np.float32)
    cv_p = np.zeros((DEPTH, B, 2, 2 * DFF), np.float32)
    sre_s = np.zeros((DEPTH, B * NS, G, P), np.float32); sim_s = np.zeros_like(sre_s)
    ret_s = np.zeros((DEPTH, B * NS, 4, 128, 256), np.float32)
    cv_s = np.zeros((DEPTH, B * NS, 2, 2 * DFF), np.float32)
    for c in range(B):
        r = R[c]
        yt = np.asarray(r["yT_out"]).transpose(1, 0, 2).reshape(D, TOK).T
        y_p[c] = yt[:SEQ]; y_s[c * NS:(c + 1) * NS] = yt[SEQ:].reshape(NS, DS, D)
        sp = np.asarray(r["ssmp_out"])
        sre_p[:, c] = sp[:64].transpose(1, 2, 0); sim_p[:, c] = sp[64:].transpose(1, 2, 0)
        ss = np.asarray(r["ssms_out"])
        sre_s[:, c * NS:(c + 1) * NS] = ss[:64].transpose(1, 3, 2, 0); sim_s[:, c * NS:(c + 1) * NS] = ss[64:].transpose(1, 3, 2, 0)
        ret_p[:, c] = np.asarray(r["retp_out"]); ret_s[:, c * NS:(c + 1) * NS] = np.asarray(r["rets_out"])
        cp = np.asarray(r["convp_out"])
        cv_p[:, c] = cp.transpose(1, 3, 2, 0).reshape(DEPTH, 2, 2 * DFF)
        cs = np.asarray(r["convs_out"])
        cv_s[:, c * NS:(c + 1) * NS] = cs.transpose(1, 3, 4, 2, 0).reshape(DEPTH, NS, 2, 2 * DFF)
    return (y_p, y_s, sre_p, sim_p, ret_p, cv_p, sre_s, sim_s, ret_s, cv_s)
```

```python
import math
from contextlib import ExitStack
import numpy as np
import ml_dtypes
import concourse.bass as bass
import concourse.mybir as mybir
from concourse.bass_utils import run_bass_kernel_spmd

F32 = mybir.dt.float32
BF16 = mybir.dt.bfloat16
ALU = mybir.AluOpType
AF = mybir.ActivationFunctionType

NCORES = 8
D = 1024
DEPTH = 4
SEQ = 2048
NS = 16
DS = 8
TOK = SEQ + NS * DS
G = 32
P = 64
DFF = 2816
NCH = 44
EPS = 1e-6
PAST = 16384
MAGIC = 12582912.0
TWO_PI = 2.0 * math.pi
LAYERS = DEPTH


class LT:
    __slots__ = ("w", "r", "key")

    def __init__(self):
        self.w = {}
        self.r = {}
        self.key = None


class Sched:
    ENGS = ("pe", "act", "dve", "pool", "sp")

    def __init__(self, nc, stack, n_dma):
        self.nc = nc
        self.sem = {}
        self.cnt = {}
        for e in self.ENGS:
            self.sem[e] = stack.enter_context(nc.semaphore("s_" + e))
            self.cnt[e] = 0
        self.n_dma = n_dma
        for i in range(n_dma):
            k = "d%d" % i
            self.sem[k] = stack.enter_context(nc.semaphore("s_" + k))
            self.cnt[k] = 0
        self.seen = {}
        self.prog = {e: [] for e in self.ENGS}
        self.rr = 0

    def _deps(self, eng, reads, writes):
        deps = {}

        def add(d, skip_same):
            for k, v in d.items():
                if skip_same and k == eng:
                    continue
                if deps.get(k, 0) < v:
                    deps[k] = v
        for t in reads:
            add(t.w, eng == "pe")
        for t in writes:
            add(t.w, True)
            add(t.r, True)
        waits = []
        for k, v in deps.items():
            if self.seen.get((eng, k), 0) >= v:
                continue
            self.seen[(eng, k)] = v
            waits.append((k, v))
        return waits

    def _mark(self, me, reads, writes):
        k, v = me
        for t in reads:
            if t.r.get(k, 0) < v:
                t.r[k] = v
        for t in writes:
            if t.w.get(k, 0) < v:
                t.w[k] = v

    def op(self, eng, fn, reads=(), writes=()):
        waits = self._deps(eng, reads, writes)
        self.cnt[eng] += 1
        self._mark((eng, self.cnt[eng]), reads, writes)
        self.prog[eng].append((waits, fn, (eng, 1)))

    def dma(self, q, fn, reads=(), writes=(), key=None):
        if key is None:
            lt = writes[0] if len(writes) else reads[0]
            if lt.key is None:
                lt.key = "d%d" % self.rr
                self.rr = (self.rr + 1) % self.n_dma
            key = lt.key
        waits = self._deps(q, reads, writes)
        self.cnt[key] += 16
        self._mark((key, self.cnt[key]), reads, writes)
        self.prog[q].append((waits, fn, (key, 16)))

    def barrier(self):
        for eng in self.ENGS:
            waits = []
            for k, v in self.cnt.items():
                if k == eng or v == 0:
                    continue
                if self.seen.get((eng, k), 0) >= v:
                    continue
                self.seen[(eng, k)] = v
                waits.append((k, v))
            if waits:
                self.prog[eng].append((waits, None, None))

    def final_wait(self, eng, tiles):
        waits = self._deps(eng, tiles, tiles)
        self.prog[eng].append((waits, None, None))

    def emit(self, block):
        def mk(ename):
            def body(e):
                for waits, fn, inc in self.prog[ename]:
                    for k, v in waits:
                        e.wait_ge(self.sem[k], v)
                    if fn is not None:
                        fn(e).then_inc(self.sem[inc[0]], inc[1])
            return body
        block.tensor(mk("pe"))
        block.scalar(mk("act"))
        block.vector(mk("dve"))
        block.gpsimd(mk("pool"))
        block.sync(mk("sp"))


class Carver:
    def __init__(self, ap2d):
        self.ap = ap2d
        self.off = 0
        self.n = ap2d.shape[1]

    def take(self, shape):
        n = 1
        for d in shape[1:]:
            n *= d
        assert self.off + n <= self.n, ("arena overflow", self.off, n, self.n)
        v = self.ap[:, self.off:self.off + n]
        self.off += n
        if len(shape) == 2:
            return v
        names = " ".join("d%d" % i for i in range(len(shape) - 1))
        kw = {"d%d" % i: shape[i + 1] for i in range(len(shape) - 1)}
        return v.rearrange("p (%s) -> p %s" % (names, names), **kw)


class Pool_:
    def __init__(self, items):
        self.items = items
        self.i = 0

    def get(self):
        it = self.items[self.i]
        self.i = (self.i + 1) % len(self.items)
        return it


def build(nlayers=LAYERS, skip=""):
    nc = bass.Bass("TRN2", target_bir_lowering=False)

    def IN(name, shape, dt=F32):
        return nc.dram_tensor(name, list(shape), dt, kind="ExternalInput").ap()

    def OUT(name, shape):
        return nc.dram_tensor(name, list(shape), F32, kind="ExternalOutput").ap()

    xT_in = IN("xT_in", [128, 8, TOK])
    w_in = IN("w_in", [DEPTH, D, 5632]); w_glu = IN("w_glu", [DEPTH, 512, 512]); w_sso = IN("w_ssm_out", [DEPTH, 512, D])
    w_ro = IN("w_ret_out", [DEPTH, D, D]); w_o = IN("w_o", [DEPTH, D, D]); w_up = IN("w_up", [DEPTH, D, 5632])
    w_dn = IN("w_down", [DEPTH, DFF, D])
    gmix = IN("gmix", [128, DEPTH, 8]); gffn = IN("gffn", [128, DEPTH, 8]); gfin = IN("gfin", [128, 8])
    convw = IN("convw", [128, DEPTH, 3, NCH]); convb = IN("convb", [128, DEPTH, NCH])
    lamr = IN("lamr", [128, DEPTH, G]); lami = IN("lami", [128, DEPTH, G]); logdt = IN("logdt", [128, DEPTH, G])
    bS_in = IN("bS", [128, DEPTH, G, 16]); bX_in = IN("bX", [128, DEPTH, G, 16])
    crD_in = IN("crD", [128, DEPTH, G, 16]); ciD_in = IN("ciD", [128, DEPTH, G, 16])
    dcol_in = IN("dcol", [128, DEPTH, G])
    s0S_in = IN("s0S", [128, DEPTH, G, NS]); s0X_in = IN("s0X", [128, DEPTH, G, NS])
    conv0_in = IN("conv0", [128, DEPTH, NCH, NS, 2])
    sret_in = IN("sret", [DEPTH, NS, 4, 128, 256])
    cf_in = IN("cf32", [128, CF_N]); cb_in = IN("cbf16", [128, CB_N], BF16)
    rope_in = IN("rope", [128, 17, 2, 64])

    yT_out = OUT("yT_out", [128, 8, TOK])
    ssmp_out = OUT("ssmp_out", [128, DEPTH, G]); ssms_out = OUT("ssms_out", [128, DEPTH, G, NS])
    retp_out = OUT("retp_out", [DEPTH, 4, 128, 256]); rets_out = OUT("rets_out", [DEPTH, NS, 4, 128, 256])
    convp_out = OUT("convp_out", [128, DEPTH, NCH, 2]); convs_out = OUT("convs_out", [128, DEPTH, NCH, NS, 2])
    dbg_out = OUT("dbg_out", [128, 1024]) if "G" in skip else None
    s5cb = nc.dram_tensor("s5cb", [DEPTH, 8, 128, 5 * 4 * 128], BF16, kind="Internal").ap()
    s5cf = nc.dram_tensor("s5cf", [DEPTH, 8, 128, 64 + 192], F32, kind="Internal").ap()
    rcar = nc.dram_tensor("rcar", [DEPTH, 4, 128, 256], F32, kind="Internal").ap()

    with ExitStack() as st:
        def sb(name, shape, dt=F32):
            return st.enter_context(nc.sbuf_tensor("sb_" + name, list(shape), dt))

        def pst(name, shape, dt=F32):
            return st.enter_context(nc.psum_tensor(name, list(shape), dt))

        S = Sched(nc, st, n_dma=24)
        block = st.enter_context(nc.Block())

        CAP = [None]

        def V(eng, method, reads, writes, *a, **kw):
            if CAP[0] is not None:
                CAP[0].append((eng, method, reads, writes, a, kw))
                return
            S.op(eng, lambda e: getattr(e, method)(*a, **kw), reads, writes)

        def DMA(q, out, in_, reads, writes, key=None):
            if CAP[0] is not None:
                CAP[0].append(("__dma__", q, out, in_, reads, writes))
                return
            S.dma(q, lambda e: e.dma_start(out=out, in_=in_), reads, writes, key)

        def emit_captured(item):
            if item[0] == "__dma__":
                _, q, out, in_, reads, writes = item
                DMA(q, out, in_, reads, writes)
            else:
                e_, m_, r_, w_, a_, kw_ = item
                V(e_, m_, r_, w_, *a_, **kw_)

        NH = 1152
        xT = sb("xT", [128, 8, NH]); Lx = [LT() for _ in range(3)]
        hT = sb("hT", [128, 8, NH], BF16); Lh = [LT() for _ in range(3)]
        HT = [[(0, 512), (512, 512)], [(0, 512), (512, 512), (1024, 128)]]
        GOFF = [0, 1024]

        cf = sb("cf", [128, CF_N]); Lcf = LT()
        cb = sb("cb", [128, CB_N], BF16); Lcb = LT()
        rope = sb("rope", [128, 17, 2, 64]); Lrope = LT()
        DMA("sp", cf[:], cf_in, [], [Lcf]); DMA("sp", cb[:], cb_in, [], [Lcb]); DMA("sp", rope[:], rope_in, [], [Lrope])

        def cfs(name):
            o, n = CF_OFF[name]
            return cf[:, o:o + n]

        def cbs(name):
            o, n = CB_OFF[name]
            return cb[:, o:o + n]
        ident_f = cfs("ident"); tmask = cfs("tmask"); pswap = cfs("pswap"); nvec = cfs("nvec"); kvec = cfs("kvec")
        sgn = cfs("sgn"); mre = cfs("mre"); nmre = cfs("nmre"); nmim = cfs("nmim"); epsc = cfs("eps")
        qsc = cfs("qsc"); ksc = cfs("ksc"); qksc = cfs("qksc"); gct = cfs("gct"); rowmask = cfs("rowmask"); cmask_p = cfs("cmask_p"); cmask_s = cfs("cmask_s")
        ident_b = cbs("ident"); ones_b = cbs("ones")

        small = {}
        for nm, src, shp in [("gmix", gmix, [128, DEPTH, 8]), ("gffn", gffn, [128, DEPTH, 8]), ("gfin", gfin, [128, 8]),
                             ("convw", convw, [128, DEPTH, 3, NCH]), ("convb", convb, [128, DEPTH, NCH]),
                             ("lamr", lamr, [128, DEPTH, G]), ("lami", lami, [128, DEPTH, G]), ("logdt", logdt, [128, DEPTH, G]),
                             ("dcol", dcol_in, [128, DEPTH, G])]:
            t = sb("sm_" + nm, shp); L = LT()
            DMA("sp", t[:], src, [], [L])
            small[nm] = (t, L)

        Wlast = sb("Wlast", [128, DEPTH, G]); Mclast = sb("Mclast", [128, DEPTH, G]); Mslast = sb("Mslast", [128, DEPTH, G]); Lcar = LT()
        convcar = sb("convcar", [128, DEPTH, NCH, 2]); Lccar = LT()
        ssm_p_sb = sb("ssm_p_sb", [128, DEPTH, G]); Lssp = LT()
        ssm_s_sb = sb("ssm_s_sb", [128, DEPTH, G, NS]); Lsss = LT()
        Lrcar = [LT() for _ in range(DEPTH)]

        psBig = pst("psbig", [128, 2048])
        psF = Pool_([(psBig[:, i * 512:(i + 1) * 512], LT()) for i in range(4)])
        psS = Pool_([(pst("pss%d" % i, [128, 512]), LT()) for i in range(2)])
        psB = Pool_([(pst("psb%d" % i, [128, 1024], BF16), LT()) for i in range(2)])
        t2k = Pool_([(sb("t2k%d" % i, [128, 512])[:], LT()) for i in range(5)])
        tb1k = Pool_([(sb("tb1k%d" % i, [128, 512], BF16)[:], LT()) for i in range(4)])
        rstd_p = Pool_([(sb("rstd%d" % i, [128, 512])[:], LT()) for i in range(2)])

        ARF_N = 7424
        ARB_N = 39700
        arena_f = sb("arena_f", [128, ARF_N]); arena_b = sb("arena_b", [128, ARB_N], BF16)

        def wload(dst, src, Ld):
            DMA("pool", dst, src, [], [Ld])

        def wview(w, l):
            return w[l].rearrange("(c p) n -> p c n", p=128)

        def rms_rstd(ti, t0, n, dscale):
            ps, Lp = psF.get()
            for c in range(8):
                sq, Lsq = tb1k.get()
                V("act", "activation", [Lx[ti]], [Lsq], out=sq[:, :n], in_=xT[:, c, t0:t0 + n], func=AF.Square)
                V("pe", "matmul", [Lsq, Lcb], [Lp], ps[:, :n], lhsT=ones_b, rhs=sq[:, :n], start=(c == 0), stop=(c == 7))
            r, Lr = rstd_p.get()
            V("act", "activation", [Lp, Lcf], [Lr], out=r[:, :n], in_=ps[:, :n], func=AF.Sqrt, bias=epsc[:, 0:1], scale=dscale)
            V("dve", "reciprocal", [Lr], [Lr], out=r[:, :n], in_=r[:, :n])
            return r, Lr

        def norm_to_h(l, which, tiles):
            gt, Lg = small[which]
            for ti, (t0, n) in enumerate(tiles):
                r, Lr = rms_rstd(ti, t0, n, 1.0 / D)
                for c in range(8):
                    eng = "dve"
                    V(eng, "scalar_tensor_tensor", [Lx[ti], Lg, Lr], [Lh[ti]], out=hT[:, c, t0:t0 + n], in0=xT[:, c, t0:t0 + n],
                      scalar=gt[:, l, c:c + 1], in1=r[:, :n], op0=ALU.mult, op1=ALU.mult)

        def proj_fm(ps, Lp, wt, Lw, col0, nk, act_fn, Lact, n):
            for kc in range(nk):
                V("pe", "matmul", [Lw] + Lact, [Lp], ps[:, :n], lhsT=wt[:, kc, col0:col0 + 128], rhs=act_fn(kc), start=(kc == 0), stop=(kc == nk - 1))

        def apply_wo(ti, t0, n, wo, Lwo, mix, Lmix):
            for oc in range(8):
                ps, Lp = psF.get()
                proj_fm(ps, Lp, wo, Lwo, oc * 128, 8, lambda kc: mix[:, kc, :n], [Lmix], n)
                V("dve", "tensor_tensor", [Lp, Lx[ti]], [Lx[ti]], out=xT[:, oc, t0:t0 + n], in0=xT[:, oc, t0:t0 + n], in1=ps[:, :n], op=ALU.add)

        def tiles_overlapping(tiles, a, b):
            return [ti for ti, (t0, n) in enumerate(tiles) if t0 < b and t0 + n > a]

        def range_reduce(dst, src, Lt, tmp, add_half_pi):
            if add_half_pi:
                V("dve", "tensor_scalar", [Lt], [Lt], out=dst, in0=src, scalar1=math.pi / 2, scalar2=None, op0=ALU.add)
                src = dst
            V("dve", "tensor_scalar", [Lt], [Lt], out=tmp, in0=src, scalar1=1.0 / TWO_PI, scalar2=MAGIC, op0=ALU.mult, op1=ALU.add)
            V("dve", "tensor_scalar", [Lt], [Lt], out=tmp, in0=tmp, scalar1=MAGIC, scalar2=TWO_PI, op0=ALU.subtract, op1=ALU.mult)
            V("dve", "tensor_tensor", [Lt], [Lt], out=dst, in0=src, in1=tmp, op=ALU.subtract)

        GG = 4
        NSUB = G // GG
        KP = 128
        S5C_L = {}

        def s5_setup(l):
            lamr_t, Ll1 = small["lamr"]; lami_t, Ll2 = small["lami"]; ldt_t, Ll3 = small["logdt"]; dcol_t, Ll4 = small["dcol"]
            Lsmall = [Ll1, Ll2, Ll3, Ll4, Lcf]
            cfA = Carver(arena_f[:, :]); cbA = Carver(arena_b[:, 36368:])
            s5f = [None] + [cfA.take([128, GG, 128]) for _ in range(5)]; Ls5f = [None] + [LT() for _ in range(5)]
            Pb = cbA.take([128, 3, GG, 128]); LPb = LT()
            Ab = cbA.take([128, 3, GG, 128]); LAb = LT()

            class SCtx:
                pass
            sctx = []
            for i in range(4):
                c = SCtx()
                c.s5sm = cfA.take([128, 16, GG]); c.Lsm = LT()
                c.Ep = cfA.take([128, 6, GG, 24]); c.LEp = LT()
                c.bsx = cfA.take([128, 3, GG, 16]); c.Lbsx = LT()
                c.bSt = cfA.take([128, 4, GG, 16]); c.LbSt = LT()
                c.tsm = cfA.take([128, GG, 32]); c.Ltsm = LT()
                sctx.append(c)

            def small_stage(c, sq):
                g0 = sq * GG
                gs = slice(g0, g0 + GG)
                s5sm, Lsm, Ep, LEp, bsx, Lbsx, bSt, LbSt = c.s5sm, c.Lsm, c.Ep, c.LEp, c.bsx, c.Lbsx, c.bSt, c.LbSt
                s5f = [c.tsm]; Ls5f = [c.Ltsm]
                sm = lambda i: s5sm[:, i, :]
                Er = Ep[:, 4]; Ei = Ep[:, 5]
                V("act", "activation", Lsmall, [Lsm], out=sm(0), in_=ldt_t[:, l, gs], func=AF.Exp)
                V("dve", "tensor_tensor", Lsmall + [Lsm], [Lsm], out=sm(1), in0=lamr_t[:, l, gs], in1=sm(0), op=ALU.mult)
                V("dve", "tensor_tensor", Lsmall + [Lsm], [Lsm], out=sm(2), in0=lami_t[:, l, gs], in1=sm(0), op=ALU.mult)
                nv = nvec.unsqueeze(1).to_broadcast([128, GG, 24])
                V("dve", "tensor_tensor", [Lsm, Lcf], [LEp], out=Ep[:, 0], in0=sm(2).unsqueeze(2).to_broadcast([128, GG, 24]), in1=nv, op=ALU.mult)
                V("dve", "tensor_tensor", [Lsm, Lcf], [LEp], out=Ep[:, 1], in0=sm(1).unsqueeze(2).to_broadcast([128, GG, 24]), in1=nv, op=ALU.mult)
                range_reduce(Ep[:, 2], Ep[:, 0], LEp, Ep[:, 3], False)
                V("act", "activation", [LEp], [LEp], out=Ep[:, 5], in_=Ep[:, 2], func=AF.Sin)
                range_reduce(Ep[:, 2], Ep[:, 0], LEp, Ep[:, 3], True)
                V("act", "activation", [LEp], [LEp], out=Ep[:, 4], in_=Ep[:, 2], func=AF.Sin)
                V("act", "activation", [LEp], [LEp], out=Ep[:, 1], in_=Ep[:, 1], func=AF.Exp)
                V("dve", "tensor_tensor", [LEp], [LEp], out=Ep[:, 4], in0=Ep[:, 4], in1=Ep[:, 1], op=ALU.mult)
                V("dve", "tensor_tensor", [LEp], [LEp], out=Ep[:, 5], in0=Ep[:, 5], in1=Ep[:, 1], op=ALU.mult)
                Er = Ep[:, 4]; Ei = Ep[:, 5]
                E1r = Er[:, :, 16]; E1i = Ei[:, :, 16]
                lr_ = lamr_t[:, l, gs]; li_ = lami_t[:, l, gs]
                RS = Lsmall + [Lsm, LEp]
                V("dve", "tensor_scalar", RS, [Lsm], out=sm(3), in0=E1r, scalar1=-1.0, scalar2=None, op0=ALU.add)
                V("dve", "tensor_tensor", RS, [Lsm], out=sm(4), in0=lr_, in1=lr_, op=ALU.mult)
                V("dve", "tensor_tensor", RS, [Lsm], out=sm(8), in0=li_, in1=li_, op=ALU.mult)
                V("dve", "tensor_tensor", RS, [Lsm], out=sm(4), in0=sm(4), in1=sm(8), op=ALU.add)
                V("dve", "reciprocal", RS, [Lsm], out=sm(5), in_=sm(4))
                V("dve", "tensor_tensor", RS, [Lsm], out=sm(6), in0=sm(3), in1=lr_, op=ALU.mult)
                V("dve", "tensor_tensor", RS, [Lsm], out=sm(8), in0=E1i, in1=li_, op=ALU.mult)
                V("dve", "tensor_tensor", RS, [Lsm], out=sm(6), in0=sm(6), in1=sm(8), op=ALU.add)
                V("dve", "tensor_tensor", RS, [Lsm], out=sm(6), in0=sm(6), in1=sm(5), op=ALU.mult)
                V("dve", "tensor_tensor", RS, [Lsm], out=sm(7), in0=E1i, in1=lr_, op=ALU.mult)
                V("dve", "tensor_tensor", RS, [Lsm], out=sm(8), in0=sm(3), in1=li_, op=ALU.mult)
                V("dve", "tensor_tensor", RS, [Lsm], out=sm(7), in0=sm(7), in1=sm(8), op=ALU.subtract)
                V("dve", "tensor_tensor", RS, [Lsm], out=sm(7), in0=sm(7), in1=sm(5), op=ALU.mult)
                V("act", "activation", RS, [Lsm], out=sm(10), in_=sm(1), func=AF.Exp, scale=8.0)
                V("dve", "tensor_scalar", RS, [Lsm], out=sm(11), in0=sm(2), scalar1=8.0, scalar2=None, op0=ALU.mult)
                bc16 = lambda a: a.unsqueeze(2).to_broadcast([128, GG, 16])
                RB = [LbSt, Lsm, Lcf]
                V("dve", "tensor_scalar", RB, [Lbsx], out=bsx[:, 0], in0=bSt[:, 1], scalar1=sgn[:, 0:1], scalar2=None, op0=ALU.mult)
                t0_ = s5f[0][:, :, 0:16]; t1_ = s5f[0][:, :, 16:32]
                V("dve", "tensor_tensor", RB, [Ls5f[0]], out=t0_, in0=bSt[:, 0], in1=bc16(sm(6)), op=ALU.mult)
                V("dve", "tensor_tensor", RB + [Lbsx], [Ls5f[0]], out=t1_, in0=bsx[:, 0], in1=bc16(sm(7)), op=ALU.mult)
                V("dve", "tensor_tensor", [Ls5f[0]], [Lbsx], out=bsx[:, 1], in0=t0_, in1=t1_, op=ALU.add)
                V("dve", "tensor_tensor", RB + [Lbsx], [Ls5f[0]], out=t0_, in0=bsx[:, 0], in1=bc16(sm(6)), op=ALU.mult)
                V("dve", "tensor_tensor", RB, [Ls5f[0]], out=t1_, in0=bSt[:, 0], in1=bc16(sm(7)), op=ALU.mult)
                V("dve", "tensor_tensor", [Ls5f[0]], [Lbsx], out=bsx[:, 2], in0=t0_, in1=t1_, op=ALU.subtract)

            def big_stage(c, sq):
                g0 = sq * GG
                gs = slice(g0, g0 + GG)
                s5sm, Lsm, Ep, LEp, bsx, Lbsx, bSt, LbSt = c.s5sm, c.Lsm, c.Ep, c.LEp, c.bsx, c.Lbsx, c.bSt, c.LbSt
                sm = lambda i: s5sm[:, i, :]
                Er = Ep[:, 4]; Ei = Ep[:, 5]
                Ls5c = LT()
                bc16 = lambda a: a.unsqueeze(2).to_broadcast([128, GG, 16])
                v4 = lambda a: a.rearrange("p g (j c) -> p g j c", c=16)
                bj = lambda a: a.unsqueeze(2).to_broadcast([128, GG, 8, 16])
                ej = lambda a: a.unsqueeze(3).to_broadcast([128, GG, 8, 16])
                bs_ = bsx[:, 1]; bx_ = bsx[:, 2]
                RP = [LEp, Lbsx]

                def cplx(dst_bf, e_r, e_i, sign_mode, Ldst):
                    a_, b_ = (e_r, e_i) if sign_mode == 0 else (e_i, e_r)
                    V("dve", "tensor_tensor", RP, [Ls5f[1]], out=v4(s5f[1]), in0=ej(a_), in1=bj(bs_), op=ALU.mult)
                    V("dve", "tensor_tensor", RP, [Ls5f[2]], out=v4(s5f[2]), in0=ej(b_), in1=bj(bx_), op=ALU.mult)
                    V("dve", "tensor_tensor", [Ls5f[1], Ls5f[2]], [Ldst], out=dst_bf, in0=s5f[1], in1=s5f[2],
                      op=(ALU.add if sign_mode == 0 else ALU.subtract))
                Pp, LPp = tb1k.get(); Ppt, LPpt = tb1k.get()
                g3 = lambda a: a.rearrange("p (g n) -> p g n", g=GG)
                cplx(g3(Pp), Er[:, :, 0:8], Ei[:, :, 0:8], 0, LPp)
                cplx(g3(Ppt), Er[:, :, 0:8], Ei[:, :, 0:8], 1, LPpt)
                cplx(Pb[:, 0], Er[:, :, 8:16], Ei[:, :, 8:16], 0, LPb)
                ECr = Er[:, :, 16:24]; ECi = Ei[:, :, 16:24]
                crD_ = bSt[:, 2]; ciD_ = bSt[:, 3]
                RC = [LEp, LbSt]
                V("dve", "tensor_tensor", RC, [Ls5f[1]], out=v4(s5f[1]), in0=ej(ECr), in1=bj(crD_), op=ALU.mult)
                V("dve", "tensor_tensor", RC, [Ls5f[2]], out=v4(s5f[2]), in0=ej(ECi), in1=bj(ciD_), op=ALU.mult)
                V("dve", "tensor_tensor", [Ls5f[1], Ls5f[2]], [Ls5f[3]], out=s5f[3], in0=s5f[1], in1=s5f[2], op=ALU.subtract)
                V("dve", "tensor_tensor", RC, [Ls5f[1]], out=v4(s5f[1]), in0=ej(ECi), in1=bj(crD_), op=ALU.mult)
                V("dve", "tensor_tensor", RC, [Ls5f[2]], out=v4(s5f[2]), in0=ej(ECr), in1=bj(ciD_), op=ALU.mult)
                V("dve", "tensor_tensor", [Ls5f[1], Ls5f[2]], [Ls5f[4]], out=s5f[4], in0=s5f[1], in1=s5f[2], op=ALU.add)
                V("dve", "tensor_scalar", [Ls5f[3], Lcf], [Ls5f[1]], out=s5f[1], in0=s5f[3], scalar1=mre[:, 0:1], scalar2=None, op0=ALU.mult)
                V("dve", "scalar_tensor_tensor", [Ls5f[4], Ls5f[1], Lcf], [LPb], out=Pb[:, 1], in0=s5f[4], scalar=nmim[:, 0:1], in1=s5f[1], op0=ALU.mult, op1=ALU.add)
                V("dve", "tensor_scalar", [Ls5f[4], Lcf], [Ls5f[2]], out=s5f[2], in0=s5f[4], scalar1=nmre[:, 0:1], scalar2=None, op0=ALU.mult)
                V("dve", "scalar_tensor_tensor", [Ls5f[3], Ls5f[2], Lcf], [LPb], out=Pb[:, 2], in0=s5f[3], scalar=nmim[:, 0:1], in1=s5f[2], op0=ALU.mult, op1=ALU.add)
                ps, Lp = psS.get()
                for g in range(GG):
                    V("pe", "matmul", [LPb], [Lp], ps[:, g * 128:(g + 1) * 128], lhsT=Pb[:, 0, g, :], rhs=Pb[:, 1, g, :], start=True, stop=True)
                V("dve", "tensor_tensor", [Lp, Lcf], [Ls5f[5]], out=s5f[5], in0=ps[:].rearrange("p (g n) -> p g n", g=GG),
                  in1=tmask.unsqueeze(1).to_broadcast([128, GG, 128]), op=ALU.mult)
                for g in range(GG):
                    V("dve", "scalar_tensor_tensor", [Ls5f[5], Lcf, Ll4], [LAb], out=Ab[:, 2, g, :], in0=ident_f, scalar=dcol_t[:, l, g0 + g:g0 + g + 1],
                      in1=s5f[5][:, g, :], op0=ALU.mult, op1=ALU.add)
                pb_, Lpb_ = psB.get()
                for i, (Px, LPx) in enumerate(((Pp, LPp), (Ppt, LPpt))):
                    for g in range(GG):
                        V("pe", "transpose", [LPx, Lcb], [Lpb_], out=pb_[:, (i * GG + g) * 128:(i * GG + g + 1) * 128], in_=Px[:, g * 128:(g + 1) * 128], identity=ident_b)
                V("act", "activation", [Lpb_], [LAb], out=Ab[:, 0:2].rearrange("p a g n -> p (a g n)"), in_=pb_[:, 0:2 * GG * 128], func=AF.Copy)
                DMA("sp", s5cb[l, sq, :, 0:2 * GG * 128], Pb[:, 1:3].rearrange("p a g n -> p (a g n)"), [LPb], [Ls5c])
                DMA("sp", s5cb[l, sq, :, 2 * GG * 128:5 * GG * 128], Ab.rearrange("p a g n -> p (a g n)"), [LAb], [Ls5c])
                DMA("sp", s5cf[l, sq, :, 0:64], s5sm.rearrange("p a g -> p (a g)"), [Lsm], [Ls5c])
                DMA("sp", s5cf[l, sq, :, 64:256], Ep[:, 4:6].rearrange("p a g n -> p (a g n)"), [LEp], [Ls5c])
                S5C_L[(l, sq)] = Ls5c

            for grp in range(NSUB // 4):
                subs = [grp * 4 + i for i in range(4)]
                lists = []
                for i, sq in enumerate(subs):
                    c = sctx[i]
                    for i2, src in enumerate([bS_in, bX_in, crD_in, ciD_in]):
                        DMA("sp", c.bSt[:, i2], src[:, l, sq * GG:(sq + 1) * GG, :], [], [c.LbSt])
                    prev_cap = CAP[0]
                    CAP[0] = []
                    small_stage(c, sq)
                    lists.append(CAP[0]); CAP[0] = prev_cap
                for k in range(max(len(x) for x in lists)):
                    for lst in lists:
                        if k < len(lst):
                            e_, m_, r_, w_, a_, kw_ = lst[k]
                            V(e_, m_, r_, w_, *a_, **kw_)
                for i, sq in enumerate(subs):
                    big_stage(sctx[i], sq)

        def s5_phase(l, hi, tiles, yaT, Lya):
            S.barrier()
            has_s = (hi == 1)
            KT = KP + (NS if has_s else 0)
            kbase = hi * KP
            PT = [(0, KP)] + ([(KP, NS)] if has_s else [])
            lamr_t, Ll1 = small["lamr"]; lami_t, Ll2 = small["lami"]; ldt_t, Ll3 = small["logdt"]; dcol_t, Ll4 = small["dcol"]
            Lsmall = [Ll1, Ll2, Ll3, Ll4, Lcf]
            cfA = Carver(arena_f[:, :]); cbA = Carver(arena_b[:, 5120:])

            class Ctx:
                pass
            ctxs = []
            for u in range(2):
                b = Ctx()
                b.s5sm = cfA.take([128, 16, GG]); b.Lsm = LT()
                b.Ep = cfA.take([128, 2, GG, 24]); b.LEp = LT()
                b.cosT = cfA.take([128, GG, KP]); b.sinT = cfA.take([128, GG, KP]); b.Ltab = LT()
                b.Yb = cfA.take([128, GG, KP]); b.LY = LT()
                b.Wb = cfA.take([128, GG, KP]); b.LW = LT()
                b.s0t = cfA.take([128, 2, GG, NS]); b.Ls0 = LT()
                b.sfin = cfA.take([128, 4, GG, NS]); b.Lsfin = LT()
                b.ctmp = cfA.take([128, GG, 2]); b.Lctmp = LT()
                b.PbC = cbA.take([128, 2, GG, 128]); b.LPb = LT()
                b.Ab = cbA.take([128, 3, GG, 128]); b.LAb = LT()
                b.Mc = cbA.take([128, GG, KP + NS]); b.Ms = cbA.take([128, GG, KP + NS]); b.LM = LT()
                b.yapt = cbA.take([128, 2, 8, GG * 16]); b.Lyapt = LT()
                ctxs.append(b)
            Uq_p = Pool_([(cbA.take([128, GG, KP + NS]), LT()) for i in range(4)])
            upt_p = Pool_([(cbA.take([128, 2, GG, 8, 16]), LT()) for i in range(4)])
            wu = Pool_([(cbA.take([128, 8, GG * 16]), LT()) for i in range(4)])

            def stage_u(sq):
                g0 = sq * GG
                wut, Lwu = wu.get()
                wload(wut, wview(w_in, l)[:, :, g0 * 16:g0 * 16 + GG * 16], Lwu)
                Uq, LU = Uq_p.get()
                upt, Lupt = upt_p.get()
                for m, (k0, nk) in enumerate(PT):
                    t0 = k0 * 8
                    tl = tiles_overlapping(tiles, t0, t0 + nk * 8)
                    ps, Lp = psF.get()
                    for j in range(8):
                        for c in range(8):
                            lh = hT[:, c, t0:t0 + nk * 8].rearrange("p (k j) -> p k j", j=8)[:, :, j]
                            V("pe", "matmul", [Lh[t] for t in tl] + [Lwu], [Lp], ps[:nk, j * 64:(j + 1) * 64], lhsT=lh, rhs=wut[:, c, :],
                              start=(c == 0), stop=(c == 7))
                    V("act", "activation", [Lp], [Lupt], out=upt[:nk, m].rearrange("p g j c -> p j g c"),
                      in_=ps[:nk, :].rearrange("p (j g c) -> p j g c", j=8, g=GG), func=AF.Copy)
                    pb_, Lpb_ = psB.get()
                    for g in range(GG):
                        V("pe", "transpose", [Lupt, Lcb], [Lpb_], out=pb_[:, g * 128:g * 128 + nk], in_=upt[:nk, m, g].rearrange("p j c -> p (j c)"),
                          identity=ident_b[:nk, :nk])
                    V("dve", "tensor_copy", [Lpb_], [LU], out=Uq[:, :, k0:k0 + nk], in_=pb_[:, 0:GG * 128].rearrange("p (g n) -> p g n", g=GG)[:, :, :nk])
                return Uq, LU


            f2 = lambda a: a.rearrange("p g k -> p (g k)")

            def stepL(b):
                sq = b.sq
                Lc_ = S5C_L[(l, sq)]
                DMA("sp", b.PbC.rearrange("p a g n -> p (a g n)"), s5cb[l, sq, :, 0:2 * GG * 128], [Lc_], [b.LPb])
                DMA("sp", b.Ab.rearrange("p a g n -> p (a g n)"), s5cb[l, sq, :, 2 * GG * 128:5 * GG * 128], [Lc_], [b.LAb])
                DMA("sp", b.s5sm.rearrange("p a g -> p (a g)"), s5cf[l, sq, :, 0:64], [Lc_], [b.Lsm])
                DMA("sp", b.Ep.rearrange("p a g n -> p (a g n)"), s5cf[l, sq, :, 64:256], [Lc_], [b.LEp])
                if has_s:
                    DMA("sp", b.s0t[:, 0], s0S_in[:, l, b.gs, :], [], [b.Ls0]); DMA("sp", b.s0t[:, 1], s0X_in[:, l, b.gs, :], [], [b.Ls0])

            def stepT(b):
                s5sm, Lsm, Yb, Wb, LY, sinT, cosT, Ltab = b.s5sm, b.Lsm, b.Yb, b.Wb, b.LY, b.sinT, b.cosT, b.Ltab
                kv = kvec[:, kbase:kbase + KP].unsqueeze(1).to_broadcast([128, GG, KP])
                V("dve", "tensor_tensor", [Lsm, Lcf], [LY], out=Yb, in0=s5sm[:, 11, :].unsqueeze(2).to_broadcast([128, GG, KP]), in1=kv, op=ALU.mult)
                range_reduce(Wb, Yb, LY, sinT, False)
                V("act", "activation", [LY], [Ltab], out=sinT, in_=Wb, func=AF.Sin)
                V("act", "activation", [LY], [LY], out=Yb, in_=Wb, func=AF.Abs)
                V("act", "activation", [LY, Lcf], [Ltab], out=cosT, in_=Yb, func=AF.Sin, scale=-1.0, bias=cfs("halfpi")[:, 0:1])

            def stepX(b):
                s5sm, Lsm, Yb, Wb, LY, LW, sinT, cosT, Ltab = b.s5sm, b.Lsm, b.Yb, b.Wb, b.LY, b.LW, b.sinT, b.cosT, b.Ltab
                Ab, LAb, Uq, LU = b.Ab, b.LAb, b.Uq, b.LU
                psx, Lpx = psF.get(); psxt, Lpxt = psF.get()
                for g in range(GG):
                    V("pe", "matmul", [LAb, LU], [Lpx], psx[:, g * KP:(g + 1) * KP], lhsT=Ab[:, 0, g, :], rhs=Uq[:, g, 0:KP], start=True, stop=True)
                    V("pe", "matmul", [LAb, LU], [Lpxt], psxt[:, g * KP:(g + 1) * KP], lhsT=Ab[:, 1, g, :], rhs=Uq[:, g, 0:KP], start=True, stop=True)
                V("dve", "tensor_tensor", [Lpx, Ltab], [LY], out=f2(Yb), in0=psx[:], in1=f2(cosT), op=ALU.mult)
                V("dve", "tensor_tensor", [Lpxt, Ltab], [LW], out=f2(Wb), in0=psxt[:], in1=f2(sinT), op=ALU.mult)
                V("dve", "tensor_tensor", [LY, LW], [LY], out=f2(Yb), in0=f2(Yb), in1=f2(Wb), op=ALU.add)
                if hi == 1:
                    V("dve", "tensor_tensor", [Lsm, Lcar], [b.Lctmp], out=b.ctmp[:, :, 0], in0=s5sm[:, 10, :], in1=Wlast[:, l, b.gs], op=ALU.mult)
                    V("dve", "tensor_tensor", [LY, b.Lctmp], [LY], out=Yb[:, :, 0], in0=Yb[:, :, 0], in1=b.ctmp[:, :, 0], op=ALU.add)

            def stepS(b):
                s5sm, Lsm, Yb, Wb, LY, LW, sinT, cosT, Ltab = b.s5sm, b.Lsm, b.Yb, b.Wb, b.LY, b.LW, b.sinT, b.cosT, b.Ltab
                Ab, LAb, Uq, LU, Mc, Ms, LM = b.Ab, b.LAb, b.Uq, b.LU, b.Mc, b.Ms, b.LM
                s0t, Ls0, sfin, Lsfin, LEp, gs = b.s0t, b.Ls0, b.sfin, b.Lsfin, b.LEp, b.gs
                Er = b.Ep[:, 0]; Ei = b.Ep[:, 1]
                for g in range(GG):
                    V("dve", "tensor_tensor_scan", [LY, Lsm, LW], [LW], out=Wb[:, g, :], data0=s5sm[:, 10, g:g + 1].to_broadcast([128, KP]), data1=Yb[:, g, :],
                      initial=0.0, op0=ALU.mult, op1=ALU.add)
                if hi == 0:
                    V("dve", "memset", [], [LM], Mc[:, :, 0:1], 0.0)
                    V("dve", "memset", [], [LM], Ms[:, :, 0:1], 0.0)
                else:
                    V("dve", "tensor_copy", [Lcar], [LM], out=Mc[:, :, 0], in_=Mclast[:, l, gs])
                    V("dve", "tensor_copy", [Lcar], [LM], out=Ms[:, :, 0], in_=Mslast[:, l, gs])
                V("dve", "tensor_tensor", [LW, Ltab], [LM], out=Mc[:, :, 1:KP], in0=Wb[:, :, 0:KP - 1], in1=cosT[:, :, 0:KP - 1], op=ALU.mult)
                V("dve", "tensor_tensor", [LW, Ltab], [LM], out=Ms[:, :, 1:KP], in0=Wb[:, :, 0:KP - 1], in1=sinT[:, :, 0:KP - 1], op=ALU.mult)
                if hi == 0:
                    V("dve", "tensor_copy", [LW], [Lcar], out=Wlast[:, l, gs], in_=Wb[:, :, KP - 1])
                    V("dve", "tensor_tensor", [LW, Ltab], [Lcar], out=Mclast[:, l, gs], in0=Wb[:, :, KP - 1], in1=cosT[:, :, KP - 1], op=ALU.mult)
                    V("dve", "tensor_tensor", [LW, Ltab], [Lcar], out=Mslast[:, l, gs], in0=Wb[:, :, KP - 1], in1=sinT[:, :, KP - 1], op=ALU.mult)
                else:
                    V("dve", "memset", [], [LM], Ms[:, :, KP:KT], 0.0)
                    V("act", "activation", [Ls0], [LM], out=Mc[:, :, KP:KT], in_=s0t[:, 0], func=AF.Copy)
                    V("dve", "tensor_tensor", [LW, Ltab], [Lsfin], out=sfin[:, 0, :, 0], in0=Wb[:, :, KP - 1], in1=cosT[:, :, KP - 1], op=ALU.mult)
                    V("dve", "tensor_tensor", [LW, Ltab], [Lsfin], out=sfin[:, 1, :, 0], in0=Wb[:, :, KP - 1], in1=sinT[:, :, KP - 1], op=ALU.mult)
                    ps, Lp = psF.get()
                    V("pe", "matmul", [Lsfin, Lcf], [Lp], ps[:, 0:GG], lhsT=pswap, rhs=sfin[:, 1, :, 0], start=True, stop=True)
                    V("dve", "tensor_tensor", [Lp, Lsfin], [Lssp], out=ssm_p_sb[:, l, gs], in0=ps[:, 0:GG], in1=sfin[:, 0, :, 0], op=ALU.add)
                    psx, Lpx = psF.get()
                    for g in range(GG):
                        V("pe", "matmul", [LAb, LU], [Lpx], psx[:, g * NS:(g + 1) * NS], lhsT=Ab[:, 0, g, :], rhs=Uq[:, g, KP:KT], start=True, stop=True)
                    Lr8 = Er[:, :, 23]; Li8 = Ei[:, :, 23]
                    bns = lambda a: a.unsqueeze(2).to_broadcast([128, GG, NS])
                    V("dve", "tensor_tensor", [Ls0, LEp], [Lsfin], out=sfin[:, 2], in0=s0t[:, 0], in1=bns(Lr8), op=ALU.mult)
                    V("dve", "scalar_tensor_tensor", [Ls0, LEp, Lcf], [Lsfin], out=sfin[:, 3], in0=s0t[:, 1], scalar=sgn[:, 0:1], in1=bns(Li8), op0=ALU.mult, op1=ALU.mult)
                    V("dve", "tensor_tensor", [Lsfin], [Lsfin], out=sfin[:, 2], in0=sfin[:, 2], in1=sfin[:, 3], op=ALU.add)
                    V("dve", "tensor_tensor", [Lsfin, Lpx], [Lsss], out=ssm_s_sb[:, l, gs, :], in0=sfin[:, 2], in1=psx[:, 0:GG * NS].rearrange("p (g s) -> p g s", g=GG), op=ALU.add)

            def stepY(b):
                Ab, LAb, Uq, LU, Mc, Ms, LM, PbC, LPb, yapt, Lyapt, g0 = b.Ab, b.LAb, b.Uq, b.LU, b.Mc, b.Ms, b.LM, b.PbC, b.LPb, b.yapt, b.Lyapt, b.g0
                for m, (k0, nk) in enumerate(PT):
                    ps, Lp = psF.get()
                    for g in range(GG):
                        o_ = ps[:nk, g * 128:(g + 1) * 128]
                        V("pe", "matmul", [LU, LAb], [Lp], o_, lhsT=Uq[:, g, k0:k0 + nk], rhs=Ab[:, 2, g, :], start=True, stop=False)
                        V("pe", "matmul", [LM, LPb], [Lp], o_, lhsT=Mc[:, g, k0:k0 + nk], rhs=PbC[:, 0, g, :], start=False, stop=False)
                        V("pe", "matmul", [LM, LPb], [Lp], o_, lhsT=Ms[:, g, k0:k0 + nk], rhs=PbC[:, 1, g, :], start=False, stop=True)
                    ta, La_ = t2k.get(); tb_, Lb_ = t2k.get()
                    V("act", "activation", [Lp], [La_], out=ta[:nk], in_=ps[:nk], func=AF.Square)
                    V("dve", "tensor_scalar", [La_], [La_], out=ta[:nk], in0=ta[:nk], scalar1=0.044715, scalar2=1.0, op0=ALU.mult, op1=ALU.add)
                    V("dve", "tensor_tensor", [La_, Lp], [La_], out=ta[:nk], in0=ta[:nk], in1=ps[:nk], op=ALU.mult)
                    V("act", "activation", [La_], [Lb_], out=tb_[:nk], in_=ta[:nk], func=AF.Sigmoid, scale=1.5957691216)
                    V("dve", "tensor_tensor", [Lb_, Lp], [Lyapt], out=yapt[:nk, m].rearrange("p t (g c) -> p g t c", g=GG),
                      in0=tb_[:nk].rearrange("p (g t c) -> p g t c", g=GG, t=8), in1=ps[:nk].rearrange("p (g t c) -> p g t c", g=GG, t=8), op=ALU.mult)
                    pb_, Lpb_ = psB.get()
                    for t in range(8):
                        V("pe", "transpose", [Lyapt, Lcb], [Lpb_], out=pb_[0:GG * 16, t * 128:t * 128 + nk], in_=yapt[:nk, m, t, :], identity=ident_b[:nk, :nk])
                    tok0 = k0 * 8
                    tl = tiles_overlapping(tiles, tok0, tok0 + nk * 8)
                    cq = (g0 * 16) // 128; p0 = (g0 * 16) % 128
                    V("dve", "tensor_copy", [Lpb_], [Lya[t] for t in tl], out=yaT[p0:p0 + GG * 16, cq, tok0:tok0 + nk * 8].rearrange("p (k j) -> p j k", j=8),
                      in_=pb_[0:GG * 16, :].rearrange("p (t k) -> p t k", t=8)[:, :, :nk])

            pairs = [(2 * i, 2 * i + 1) for i in range(NSUB // 2)]
            Ubuf = {0: stage_u(0), 1: stage_u(1)}
            for pi, pr in enumerate(pairs):
                for u, sq in enumerate(pr):
                    b = ctxs[u]
                    b.sq = sq; b.g0 = sq * GG; b.gs = slice(sq * GG, sq * GG + GG)
                    b.Uq, b.LU = Ubuf.pop(sq)
                if pi + 1 < len(pairs):
                    for sq2 in pairs[pi + 1]:
                        Ubuf[sq2] = stage_u(sq2)
                for step in (stepL, stepT, stepX, stepS, stepY):
                    for u in range(2):
                        step(ctxs[u])
            if hi == 1:
                DMA("sp", ssmp_out[:, l, :], ssm_p_sb[:, l, :], [Lssp], [])
                DMA("sp", ssms_out[:, l], ssm_s_sb[:, l], [Lsss], [])

        def phase_B(l, tiles, yaT, Lya, wo, Lwo):
            S.barrier()
            cbA = Carver(arena_b[:, 5120 + 8192:])
            wgl = cbA.take([128, 4, 512]); Lwgl = LT()
            wso = cbA.take([128, 4, 1024]); Lwso = LT()
            wga = cbA.take([128, 8, 1024]); Lwga = LT()
            mix = cbA.take([128, 8, 512]); Lmix = LT()
            ya2 = cbA.take([128, 4, 512]); Ly2 = LT()
            wload(wgl, w_glu[l].rearrange("(c p) n -> p c n", p=128), Lwgl)
            wload(wso, w_sso[l].rearrange("(c p) n -> p c n", p=128), Lwso)
            GA0 = 3584
            wload(wga, wview(w_in, l)[:, :, GA0:GA0 + 1024], Lwga)
            wload(wo, wview(w_o, l), Lwo)
            for ti, (t0, n) in enumerate(tiles):
                for oc in range(4):
                    ps, Lp = psF.get()
                    proj_fm(ps, Lp, wgl, Lwgl, oc * 128, 4, lambda kc: yaT[:, kc, t0:t0 + n], [Lya[ti]], n)
                    sg, Lsg = tb1k.get()
                    V("act", "activation", [Lp], [Lsg], out=sg[:, :n], in_=ps[:, :n], func=AF.Sigmoid)
                    V("dve", "tensor_tensor", [Lsg, Lya[ti]], [Ly2], out=ya2[:, oc, :n], in0=sg[:, :n], in1=yaT[:, oc, t0:t0 + n], op=ALU.mult)
                for oc in range(8):
                    ps, Lp = psF.get(); ps2, Lp2 = psF.get()
                    proj_fm(ps, Lp, wso, Lwso, oc * 128, 4, lambda kc: ya2[:, kc, :n], [Ly2], n)
                    proj_fm(ps2, Lp2, wga, Lwga, oc * 128, 8, lambda kc: hT[:, kc, t0:t0 + n], [Lh[ti]], n)
                    sg, Lsg = t2k.get()
                    V("act", "activation", [Lp2], [Lsg], out=sg[:, :n], in_=ps2[:, :n], func=AF.Sigmoid)
                    V("dve", "tensor_tensor", [Lsg, Lp], [Lmix], out=mix[:, oc, :n], in0=sg[:, :n], in1=ps[:, :n], op=ALU.mult)
                if "B" not in skip:
                    apply_wo(ti, t0, n, wo, Lwo, mix, Lmix)

        def phase_C(l, hi, tiles, oT, LoT):
            S.barrier()
            cfA = Carver(arena_f[:, :])
            cbY = Carver(arena_b[:, 0:5120])
            cbW = Carver(arena_b[:, 5120:5120 + 8192])
            cbA = Carver(arena_b[:, 5120 + 8192 + 9216:])
            r_st = cfA.take([128, 4, 256]); Lr_st = LT()
            orw_p = Pool_([(cfA.take([128, 4, 2, 128]), LT()) for i in range(1)])
            qf_p = Pool_([(cfA.take([128, 4, 256]), LT()) for i in range(1)])
            r0_p = Pool_([(cfA.take([128, 4, 256]), LT()) for i in range(2)])
            rn_p = Pool_([(cfA.take([128, 4, 256]), LT()) for i in range(2)])
            wq4 = cbA.take([128, 8, 2048]); Lwq4 = [LT() for _ in range(4)]
            r_bf = cbY.take([128, 4, 256]); Lr_bf = LT()
            qk_p = Pool_([(cbY.take([128, 4, 2, 128]), LT()) for i in range(2)])
            qkT_p = Pool_([(cbY.take([128, 4, 2, 128]), LT()) for i in range(1)])
            sc_p = Pool_([(cbY.take([128, 4, 128]), LT()) for i in range(2)])
            v_p = Pool_([(cbW.take([128, 4, 256]), LT()) for i in range(2)])
            sq_p = Pool_([(cbW.take([128, 1024]), LT()) for i in range(2)])
            r0b_p = Pool_([(cbW.take([128, 4, 256]), LT()) for i in range(2)])
            km_p = Pool_([(cbW.take([128, 4, 128]), LT()) for i in range(2)])
            goff = GOFF[hi]
            wv_ = wview(w_in, l)
            for hh in range(4):
                wload(wq4[:, :, hh * 512:hh * 512 + 128], wv_[:, :, 512 + hh * 128:512 + (hh + 1) * 128], Lwq4[hh])
                wload(wq4[:, :, hh * 512 + 128:hh * 512 + 256], wv_[:, :, 1024 + hh * 128:1024 + (hh + 1) * 128], Lwq4[hh])
                wload(wq4[:, :, hh * 512 + 256:hh * 512 + 512], wv_[:, :, 1536 + hh * 256:1536 + (hh + 1) * 256], Lwq4[hh])
            if hi == 0:
                V("dve", "memset", [], [Lr_st], r_st, 0.0)
            else:
                DMA("sp", r_st, rcar[l].rearrange("h d v -> d h v"), [Lrcar[l]], [Lr_st])
            V("act", "activation", [Lr_st], [Lr_bf], out=r_bf, in_=r_st, func=AF.Copy)
            psQ = psBig[:, :]
            LQ = [it[1] for it in psF.items]
            blocks = []
            for ti, (t0, n) in enumerate(tiles):
                for b in range(n // 128):
                    blocks.append((ti, t0 + b * 128))
            for (ti, t0) in blocks:
                tg = goff + t0
                is_s = (tg >= SEQ)
                blk = 16 if is_s else tg // 128
                kind = 1 if is_s else 0
                for hh in range(4):
                    for c in range(8):
                        V("pe", "matmul", [Lh[ti], Lwq4[hh]], [LQ[hh]], psQ[:, hh * 512:(hh + 1) * 512], lhsT=hT[:, c, t0:t0 + 128], rhs=wq4[:, c, hh * 512:(hh + 1) * 512],
                          start=(c == 0), stop=(c == 7))
                qf, Lqf = qf_p.get()
                vt, Lv = v_p.get()
                for hh in range(4):
                    V("act", "activation", [LQ[hh]], [Lqf], out=qf[:, hh, :], in_=psQ[:, hh * 512:hh * 512 + 256], func=AF.Copy)
                    V("act", "activation", [LQ[hh]], [Lv], out=vt[:, hh, :], in_=psQ[:, hh * 512 + 256:hh * 512 + 512], func=AF.Copy)
                x1 = qf.rearrange("p h (a f d) -> p h a f d", a=2, f=2)[:, :, :, 0, :]
                x2 = qf.rearrange("p h (a f d) -> p h a f d", a=2, f=2)[:, :, :, 1, :]
                cs_ = rope[:, blk, 0, :].unsqueeze(1).unsqueeze(1).to_broadcast([128, 4, 2, 64])
                sn_ = rope[:, blk, 1, :].unsqueeze(1).unsqueeze(1).to_broadcast([128, 4, 2, 64])
                pr = [t2k.get() for _ in range(4)]
                v4 = lambda a: a.rearrange("p (h a d) -> p h a d", h=4, a=2)
                V("dve", "tensor_tensor", [Lqf, Lrope], [pr[0][1]], out=v4(pr[0][0]), in0=x1, in1=cs_, op=ALU.mult)
                V("dve", "tensor_tensor", [Lqf, Lrope], [pr[1][1]], out=v4(pr[1][0]), in0=x2, in1=sn_, op=ALU.mult)
                V("dve", "tensor_tensor", [Lqf, Lrope], [pr[2][1]], out=v4(pr[2][0]), in0=x1, in1=sn_, op=ALU.mult)
                V("dve", "tensor_tensor", [Lqf, Lrope], [pr[3][1]], out=v4(pr[3][0]), in0=x2, in1=cs_, op=ALU.mult)
                V("dve", "tensor_tensor", [pr[0][1], pr[1][1]], [pr[0][1]], out=pr[0][0], in0=pr[0][0], in1=pr[1][0], op=ALU.subtract)
                V("dve", "tensor_tensor", [pr[2][1], pr[3][1]], [pr[2][1]], out=pr[2][0], in0=pr[2][0], in1=pr[3][0], op=ALU.add)
                qk, Lqk = qk_p.get()
                sct = qksc.rearrange("p (k h a) -> p k h a", k=2, h=4)[:, kind].unsqueeze(3).to_broadcast([128, 4, 2, 64])
                V("dve", "tensor_tensor", [pr[0][1], Lcf], [Lqk], out=qk[:, :, :, 0:64], in0=v4(pr[0][0]), in1=sct, op=ALU.mult)
                V("dve", "tensor_tensor", [pr[2][1], Lcf], [Lqk], out=qk[:, :, :, 64:128], in0=v4(pr[2][0]), in1=sct, op=ALU.mult)
                pb_, Lpb_ = psB.get()
                for hh in range(4):
                    for a in range(2):
                        V("pe", "transpose", [Lqk, Lcb], [Lpb_], out=pb_[:, (hh * 2 + a) * 128:(hh * 2 + a + 1) * 128], in_=qk[:, hh, a, :], identity=ident_b)
                qkT, LqkT = qkT_p.get()
                V("dve", "tensor_copy", [Lpb_], [LqkT], out=qkT.rearrange("p h a n -> p (h a n)"), in_=pb_[:, 0:1024])
                ps2, Lp2 = psF.get()
                for hh in range(4):
                    V("pe", "matmul", [LqkT], [Lp2], ps2[:, hh * 128:(hh + 1) * 128], lhsT=qkT[:, hh, 1, :], rhs=qkT[:, hh, 0, :], start=True, stop=True)
                sc, Lsc = sc_p.get()
                mk = (cmask_s if is_s else cmask_p).unsqueeze(1).to_broadcast([128, 4, 128])
                V("dve", "tensor_tensor", [Lp2, Lcf], [Lsc], out=sc, in0=ps2[:, :].rearrange("p (h n) -> p h n", h=4), in1=mk, op=ALU.mult)
                po = [psF.get(), psF.get()]
                orw, Lor = orw_p.get()
                for hh in range(4):
                    pso, Lpo = po[hh // 2]
                    for e_ in range(2):
                        o_ = pso[:, ((hh % 2) * 2 + e_) * 128:((hh % 2) * 2 + e_ + 1) * 128]
                        V("pe", "matmul", [Lv, Lsc], [Lpo], o_, lhsT=vt[:, hh, e_ * 128:(e_ + 1) * 128], rhs=sc[:, hh, :], start=True, stop=is_s)
                        if not is_s:
                            V("pe", "matmul", [Lr_bf, LqkT], [Lpo], o_, lhsT=r_bf[:, hh, e_ * 128:(e_ + 1) * 128], rhs=qkT[:, hh, 0, :], start=False, stop=True)
                orf = orw.rearrange("p h e n -> p (h e n)")
                for i2 in range(2):
                    V("act", "activation", [po[i2][1]], [Lor], out=orf[:, i2 * 512:(i2 + 1) * 512], in_=po[i2][0][:, :], func=AF.Copy)
                if not is_s:
                    pd = [psS.get(), psS.get()]
                    for hh in range(4):
                        psd, Lpd = pd[hh // 2]
                        V("pe", "matmul", [Lqk, Lv], [Lpd], psd[:, (hh % 2) * 256:(hh % 2 + 1) * 256], lhsT=qk[:, hh, 1, :], rhs=vt[:, hh, :], start=True, stop=True)
                    for i2 in range(2):
                        rv = r_st[:, i2 * 2:i2 * 2 + 2, :].rearrange("p h v -> p (h v)")
                        V("dve", "tensor_tensor", [pd[i2][1], Lr_st], [Lr_st], out=rv, in0=rv, in1=pd[i2][0][:, :], op=ALU.add)
                    gtab = gct.rearrange("p (k h) -> p k h", k=2)[:, 0].unsqueeze(2).to_broadcast([128, 4, 256])
                    V("dve", "tensor_tensor", [Lr_st, Lcf], [Lr_st], out=r_st, in0=r_st, in1=gtab, op=ALU.mult)
                    V("act", "activation", [Lr_st], [Lr_bf], out=r_bf, in_=r_st, func=AF.Copy)
                    if tg == 1024 - 128:
                        DMA("sp", rcar[l].rearrange("h d v -> d h v"), r_st, [Lr_st], [Lrcar[l]])
                    if tg == SEQ - 128:
                        DMA("sp", retp_out[l].rearrange("h d v -> d h v"), r_st, [Lr_st], [])
                else:
                    pin = [psF.get(), psF.get()]
                    gtab = gct.rearrange("p (k h) -> p k h", k=2)[:, 1].unsqueeze(2).to_broadcast([128, 4, 256])
                    def _ld_r0(sx):
                        r0x, Lr0x = r0_p.get()
                        DMA("sp", r0x, sret_in[l, sx].rearrange("h d v -> d h v"), [], [Lr0x])
                        return r0x, Lr0x
                    r0_next = _ld_r0(0)
                    for s_ in range(NS):
                        r0, Lr0 = r0_next
                        if s_ + 1 < NS:
                            r0_next = _ld_r0(s_ + 1)
                        r0b, Lr0b = r0b_p.get()
                        V("act", "activation", [Lr0], [Lr0b], out=r0b, in_=r0, func=AF.Copy)
                        for hh in range(4):
                            psi, Lpi = pin[hh // 2]
                            for e_ in range(2):
                                c0_ = ((hh % 2) * 2 + e_) * 128 + s_ * 8
                                V("pe", "matmul", [Lr0b, LqkT], [Lpi], psi[:, c0_:c0_ + 8], lhsT=r0b[:, hh, e_ * 128:(e_ + 1) * 128],
                                  rhs=qkT[:, hh, 0, s_ * 8:s_ * 8 + 8], start=True, stop=True)
                        km, Lkm = km_p.get()
                        V("dve", "tensor_scalar", [Lqk, Lcf], [Lkm], out=km, in0=qk[:, :, 1, :], scalar1=rowmask[:, s_:s_ + 1], scalar2=None, op0=ALU.mult)
                        pd = [psS.get(), psS.get()]
                        for hh in range(4):
                            psd, Lpd = pd[hh // 2]
                            V("pe", "matmul", [Lkm, Lv], [Lpd], psd[:, (hh % 2) * 256:(hh % 2 + 1) * 256], lhsT=km[:, hh, :], rhs=vt[:, hh, :], start=True, stop=True)
                        rn, Lrn = rn_p.get()
                        for i2 in range(2):
                            V("dve", "tensor_tensor", [pd[i2][1], Lr0], [Lrn], out=rn[:, i2 * 2:i2 * 2 + 2, :].rearrange("p h v -> p (h v)"),
                              in0=r0[:, i2 * 2:i2 * 2 + 2, :].rearrange("p h v -> p (h v)"), in1=pd[i2][0][:, :], op=ALU.add)
                        V("dve", "tensor_tensor", [Lrn, Lcf], [Lrn], out=rn, in0=rn, in1=gtab, op=ALU.mult)
                        DMA("sp", rets_out[l, s_].rearrange("h d v -> d h v"), rn, [Lrn], [])
                    for i2 in range(2):
                        tin, Ltin = t2k.get()
                        V("act", "activation", [pin[i2][1]], [Ltin], out=tin[:, :], in_=pin[i2][0][:, :], func=AF.Copy)
                        V("dve", "tensor_tensor", [Lor, Ltin], [Lor], out=orf[:, i2 * 512:(i2 + 1) * 512], in0=orf[:, i2 * 512:(i2 + 1) * 512], in1=tin[:, :], op=ALU.add)
                sq, Lsq = sq_p.get()
                V("dve", "tensor_tensor", [Lor], [Lsq], out=sq, in0=orf, in1=orf, op=ALU.mult)
                ps5, Lp5 = psF.get()
                for hh in range(4):
                    for e_ in range(2):
                        V("pe", "matmul", [Lsq, Lcb], [Lp5], ps5[:, hh * 128:(hh + 1) * 128], lhsT=ones_b, rhs=sq[:, (hh * 2 + e_) * 128:(hh * 2 + e_ + 1) * 128],
                          start=(e_ == 0), stop=(e_ == 1))
                rs, Lrs = t2k.get()
                V("act", "activation", [Lp5, Lcf], [Lrs], out=rs[:, :], in_=ps5[:, :], func=AF.Sqrt, bias=epsc[:, 0:1], scale=1.0 / 256)
                V("dve", "reciprocal", [Lrs], [Lrs], out=rs[:, :], in_=rs[:, :])
                V("dve", "tensor_tensor", [Lor, Lrs], [LoT[ti]], out=oT[:, :, t0:t0 + 128].rearrange("p (h e) n -> p h e n", h=4), in0=orw,
                  in1=rs[:, :].rearrange("p (h n) -> p h n", h=4).unsqueeze(2).to_broadcast([128, 4, 2, 128]), op=ALU.mult)

        def phase_D(l, tiles, oT, LoT, wo, Lwo):
            S.barrier()
            cbA = Carver(arena_b[:, 5120 + 8192 + 9216:])
            wX = cbA.take([128, 8, 1024]); LwX = LT()
            wY = cbA.take([128, 8, 1024]); LwY = LT()
            mix = arena_b[:, 0:4096].rearrange("p (c n) -> p c n", c=8); Lmix = LT()
            GR0 = 2560
            wload(wX, wview(w_in, l)[:, :, GR0:GR0 + 1024], LwX)
            wload(wY, wview(w_ro, l), LwY)
            wload(wo, wview(w_o, l), Lwo)
            for ti, (t0, n) in enumerate(tiles):
                for oc in range(8):
                    ps, Lp = psF.get()
                    proj_fm(ps, Lp, wX, LwX, oc * 128, 8, lambda kc: hT[:, kc, t0:t0 + n], [Lh[ti]], n)
                    sg, Lsg = tb1k.get()
                    V("act", "activation", [Lp], [Lsg], out=sg[:, :n], in_=ps[:, :n], func=AF.Silu)
                    o_ = oT[:, oc, t0:t0 + n]
                    V("dve", "tensor_tensor", [Lsg, LoT[ti]], [LoT[ti]], out=o_, in0=o_, in1=sg[:, :n], op=ALU.mult)
            GB0 = 4608
            wload(wX, wview(w_in, l)[:, :, GB0:GB0 + 1024], LwX)
            for ti, (t0, n) in enumerate(tiles):
                for oc in range(8):
                    ps, Lp = psF.get(); ps2, Lp2 = psF.get()
                    proj_fm(ps, Lp, wY, LwY, oc * 128, 8, lambda kc: oT[:, kc, t0:t0 + n], [LoT[ti]], n)
                    proj_fm(ps2, Lp2, wX, LwX, oc * 128, 8, lambda kc: hT[:, kc, t0:t0 + n], [Lh[ti]], n)
                    sg, Lsg = t2k.get()
                    V("act", "activation", [Lp2], [Lsg], out=sg[:, :n], in_=ps2[:, :n], func=AF.Sigmoid)
                    V("dve", "tensor_tensor", [Lsg, Lp], [Lmix], out=mix[:, oc, :n], in0=sg[:, :n], in1=ps[:, :n], op=ALU.mult)
                if "D" not in skip:
                    apply_wo(ti, t0, n, wo, Lwo, mix, Lmix)

        GF = 4
        EXTRA_K = [10]

        def ffn(l, hi, tiles, extra=None):
            S.barrier()
            cfA = Carver(arena_f[:, :]); cbA = Carver(arena_b[:, :])
            conv0 = cfA.take([128, NCH, NS, 2]); Lc0 = LT()
            convp_sb = cfA.take([128, NCH, 2]); Lcp = LT()
            convs_sb = cfA.take([128, NCH, NS, 2]); Lcs = LT()
            actT = cbA.take([128, GF, NH]); Lact = [LT() for _ in range(3)]
            wup_p = Pool_([(cbA.take([128, 8, 2 * GF * 128]), LT()) for i in range(2)])
            wdn_p = Pool_([(cbA.take([128, GF, D]), LT()) for i in range(2)])
            upb = [[(cbA.take([128, 514]), LT()) for a in range(2)] for i in range(GF)]
            Dg = cbA.take([128, GF, 2, 3, 128]); LDg = LT()
            ups = Pool_([(cbA.take([128, NS, 10]), LT()) for i in range(2)]) if hi == 1 else None
            cw, Lcw = small["convw"]; cbv, Lcbv = small["convb"]
            if hi == 1:
                DMA("sp", conv0, conv0_in[:, l], [], [Lc0])
            ngroups = (22 + GF - 1) // GF
            for gi in range(ngroups):
                c0 = gi * GF
                ng = min(GF, 22 - c0)
                wu_, Lwu_ = wup_p.get(); wd_, Lwd_ = wdn_p.get()
                wload(wu_[:, :, 0:ng * 128], wview(w_up, l)[:, :, c0 * 128:(c0 + ng) * 128], Lwu_)
                wload(wu_[:, :, GF * 128:GF * 128 + ng * 128], wview(w_up, l)[:, :, DFF + c0 * 128:DFF + (c0 + ng) * 128], Lwu_)
                wload(wd_[:, 0:ng, :], w_dn[l, c0 * 128:(c0 + ng) * 128, :].rearrange("(c p) n -> p c n", p=128), Lwd_)
                for cc in range(ng):
                    for a in range(2):
                        ch = c0 + cc + a * 22
                        for j in range(3):
                            V("dve", "tensor_scalar", [Lcw, Lcb], [LDg], out=Dg[:, cc, a, j, :], in0=ident_b, scalar1=cw[:, l, j, ch:ch + 1], scalar2=None, op0=ALU.mult)
                pend_conv = []
                pend_down = []
                resmap = {}

                def emit_up(ti, t0, n, cc, a):
                    is_s = (n == 128)
                    ch = c0 + cc + a * 22
                    ps, Lp = psF.get()
                    proj_fm(ps, Lp, wu_, Lwu_, a * GF * 128 + cc * 128, 8, lambda kc: hT[:, kc, t0:t0 + n], [Lh[ti]], n)
                    if not is_s:
                        ub, Lub = upb[cc][a]
                        if ti == 0:
                            if hi == 0:
                                V("dve", "memset", [], [Lub], ub[:, 0:2], 0.0)
                            else:
                                V("dve", "tensor_copy", [Lccar], [Lub], out=ub[:, 0:2], in_=convcar[:, l, ch, :])
                        else:
                            V("dve", "tensor_copy", [Lub], [Lub], out=ub[:, 0:2], in_=ub[:, 512:514])
                        V("act", "activation", [Lp], [Lub], out=ub[:, 2:514], in_=ps[:, :], func=AF.Copy)
                        if ti == 1:
                            if hi == 0:
                                V("act", "activation", [Lp], [Lccar], out=convcar[:, l, ch, :], in_=ps[:, 510:512], func=AF.Copy)
                            else:
                                V("act", "activation", [Lp], [Lcp], out=convp_sb[:, ch, :], in_=ps[:, 510:512], func=AF.Copy)
                        return (ub, Lub)
                    else:
                        us, Lus = ups.get()
                        V("dve", "tensor_copy", [Lc0], [Lus], out=us[:, :, 0:2], in_=conv0[:, ch])
                        V("act", "activation", [Lp], [Lus], out=us[:, :, 2:10], in_=ps[:, 0:128].rearrange("p (s j) -> p s j", j=8), func=AF.Copy)
                        V("act", "activation", [Lp], [Lcs], out=convs_sb[:, ch], in_=ps[:, 0:128].rearrange("p (s j) -> p s j", j=8)[:, :, 6:8], func=AF.Copy)
                        return (us, Lus)

                def emit_conv(ti, t0, n, cc, a, buf):
                    is_s = (n == 128)
                    ch = c0 + cc + a * 22
                    ub, Lub = buf
                    ps2, Lp2 = psF.get()
                    for j in range(3):
                        if not is_s:
                            V("pe", "matmul", [LDg, Lub], [Lp2], ps2[:, :], lhsT=Dg[:, cc, a, j, :], rhs=ub[:, j:j + 512], start=(j == 0), stop=(j == 2))
                        else:
                            V("pe", "matmul", [LDg, Lub], [Lp2], ps2[:, 0:128], lhsT=Dg[:, cc, a, j, :], rhs=ub[:, :, j:j + 8], start=(j == 0), stop=(j == 2))
                    resmap[(ti, cc, a)] = (ps2, Lp2, ch)
                    if a == 1:
                        (pv, Lpv, chv) = resmap.pop((ti, cc, 0)); (pg, Lpg, chg) = resmap.pop((ti, cc, 1))
                        sg, Lsg = t2k.get()
                        V("act", "activation", [Lpg, Lcbv], [Lsg], out=sg[:, :n], in_=pg[:, :n], func=AF.Silu, bias=cbv[:, l, chg:chg + 1])
                        V("dve", "scalar_tensor_tensor", [Lpv, Lsg, Lcbv], [Lact[ti]], out=actT[:, cc, t0:t0 + n], in0=pv[:, :n], scalar=cbv[:, l, chv:chv + 1], in1=sg[:, :n], op0=ALU.add, op1=ALU.mult)

                def emit_down(ti, t0, n):
                    for oc in range(8):
                        ps, Lp = psF.get()
                        for cc in range(ng):
                            V("pe", "matmul", [Lwd_, Lact[ti]], [Lp], ps[:, :n], lhsT=wd_[:, cc, oc * 128:(oc + 1) * 128], rhs=actT[:, cc, t0:t0 + n], start=(cc == 0), stop=(cc == ng - 1))
                        V("dve", "tensor_tensor", [Lp, Lx[ti]], [Lx[ti]], out=xT[:, oc, t0:t0 + n], in0=xT[:, oc, t0:t0 + n], in1=ps[:, :n], op=ALU.add)

                for ti, (t0, n) in enumerate(tiles):
                    ui = 0
                    for cc in range(ng):
                        for a in range(2):
                            buf = emit_up(ti, t0, n, cc, a)
                            if pend_conv:
                                emit_conv(*pend_conv.pop(0))
                            pend_conv.append((ti, t0, n, cc, a, buf))
                            if extra:
                                for _ in range(min(EXTRA_K[0], len(extra))):
                                    emit_captured(extra.pop(0))
                            if ui == 2 and pend_down:
                                emit_down(*pend_down.pop(0))
                            ui += 1
                    pend_down.append((ti, t0, n))
                while pend_conv:
                    emit_conv(*pend_conv.pop(0))
                while pend_down:
                    emit_down(*pend_down.pop(0))
            while extra:
                emit_captured(extra.pop(0))
            if hi == 1:
                DMA("sp", convp_out[:, l], convp_sb, [Lcp], [])
                DMA("sp", convs_out[:, l], convs_sb, [Lcs], [])
                S.final_wait("sp", [Lcp, Lcs])

        out_L = []
        for hi in range(2):
            tiles = HT[hi]
            goff = GOFF[hi]
            nh = sum(n for _, n in tiles)
            S.barrier()
            DMA("sp", xT[:, :, 0:nh], xT_in[:, :, goff:goff + nh], [], [Lx[i] for i in range(len(tiles))])
            yaT = arena_b[:, 0:4 * NH].rearrange("p (c n) -> p c n", c=4)
            wo = arena_b[:, 5120:5120 + 8192].rearrange("p (c n) -> p c n", c=8)
            oT = arena_b[:, 5120 + 8192:5120 + 8192 + 9216].rearrange("p (c n) -> p c n", c=8)
            for l in range(nlayers):
                Lya = [LT() for _ in range(3)]; Lwo = LT(); LoT = [LT() for _ in range(3)]
                norm_to_h(l, "gmix", tiles)
                if "a" not in skip:
                    if hi == 0 and l == 0:
                        S.barrier()
                        s5_setup(0)
                    s5_phase(l, hi, tiles, yaT, Lya)
                if "b" not in skip:
                    phase_B(l, tiles, yaT, Lya, wo, Lwo)
                if "c" not in skip:
                    phase_C(l, hi, tiles, oT, LoT)
                if "d" not in skip:
                    phase_D(l, tiles, oT, LoT, wo, Lwo)
                if "F" not in skip:
                    norm_to_h(l, "gffn", tiles)
                    extra = None
                    if hi == 0 and l + 1 < nlayers and "a" not in skip:
                        CAP[0] = []
                        s5_setup(l + 1)
                        extra = CAP[0]; CAP[0] = None
                        EXTRA_K[0] = len(extra) // 90 + 1
                    ffn(l, hi, tiles, extra)
            S.barrier()
            gt, Lg = small["gfin"]
            yo = arena_f[:, 0:4096].rearrange("p (c n) -> p c n", c=8); Lyo = LT()
            for ti, (t0, n) in enumerate(tiles):
                r, Lr = rms_rstd(ti, t0, n, 1.0 / D)
                for c in range(8):
                    V("dve", "scalar_tensor_tensor", [Lx[ti], Lg, Lr], [Lyo], out=yo[:, c, :n], in0=xT[:, c, t0:t0 + n], scalar=gt[:, c:c + 1], in1=r[:, :n], op0=ALU.mult, op1=ALU.mult)
                DMA("sp", yT_out[:, :, goff + t0:goff + t0 + n], yo[:, :, :n], [Lyo], [])
            out_L.append(Lyo)
            S.final_wait("sp", [Lyo])
        S.barrier()
        S.emit(block)
    return nc


def _mk_consts():
    cf = {}
    cf["ident"] = np.eye(128, dtype=np.float32)
    jc = np.arange(128) // 16
    tc_t = np.arange(128) // 16
    cf["tmask"] = (tc_t[None, :] >= jc[:, None]).astype(np.float32)
    ps = np.zeros((128, 128), np.float32)
    for p in range(64):
        ps[64 + p, p] = -1.0
        ps[p, 64 + p] = 1.0
    cf["pswap"] = ps
    nv = np.array([7, 6, 5, 4, 3, 2, 1, 0, -1, -2, -3, -4, -5, -6, -7, -8, 1, 2, 3, 4, 5, 6, 7, 8], np.float32)
    cf["nvec"] = np.broadcast_to(nv, (128, 24)).copy()
    cf["kvec"] = np.broadcast_to(np.arange(256, dtype=np.float32), (128, 256)).copy()
    top = (np.arange(128) < 64)
    cf["sgn"] = np.where(top, -1.0, 1.0).astype(np.float32)[:, None]
    cf["mre"] = top.astype(np.float32)[:, None]
    cf["nmre"] = -top.astype(np.float32)[:, None]
    cf["nmim"] = -(~top).astype(np.float32)[:, None]
    cf["eps"] = np.full((128, 1), EPS, np.float32)
    cf["halfpi"] = np.full((128, 1), math.pi / 2, np.float32)
    cf["zero"] = np.zeros((128, 1), np.float32)
    i = np.arange(128, dtype=np.float64)
    qs = np.zeros((128, 8)); ks = np.zeros((128, 8))
    for h in range(4):
        g = 1.0 - 2.0 ** (-5 - h)
        qs[:, h] = g ** (i + 1); ks[:, h] = (128 ** -0.5) * g ** (-(i + 1))
        qs[:, 4 + h] = g ** ((i % 8) + 1); ks[:, 4 + h] = (128 ** -0.5) * g ** (-((i % 8) + 1))
    cf["qsc"] = qs.astype(np.float32); cf["ksc"] = ks.astype(np.float32)
    qk_ = np.zeros((128, 2, 4, 2)); gc_ = np.zeros((128, 2, 4))
    for kd in range(2):
        for h in range(4):
            qk_[:, kd, h, 0] = qs[:, kd * 4 + h]; qk_[:, kd, h, 1] = ks[:, kd * 4 + h]
            gc_[:, kd, h] = (1.0 - 2.0 ** (-5 - h)) ** (128 if kd == 0 else 8)
    cf["qksc"] = qk_.reshape(128, 16).astype(np.float32); cf["gct"] = gc_.reshape(128, 8).astype(np.float32)
    rm = np.zeros((128, 16), np.float32)
    for s in range(16):
        rm[s * 8:(s + 1) * 8, s] = 1.0
    cf["rowmask"] = rm
    j = np.arange(128)
    cf["cmask_p"] = (j[:, None] <= j[None, :]).astype(np.float32)
    cf["cmask_s"] = ((j[:, None] <= j[None, :]) & ((j[:, None] // 8) == (j[None, :] // 8))).astype(np.float32)
    off = {}; o = 0; parts = []
    for k, v in cf.items():
        off[k] = (o, v.shape[1]); o += v.shape[1]; parts.append(v)
    cfa = np.ascontiguousarray(np.concatenate(parts, axis=1))
    cb = {"ident": np.eye(128, dtype=np.float32), "ones": np.ones((128, 128), np.float32)}
    offb = {}; o = 0; partsb = []
    for k, v in cb.items():
        offb[k] = (o, v.shape[1]); o += v.shape[1]; partsb.append(v)
    cba = np.ascontiguousarray(np.concatenate(partsb, axis=1)).astype(ml_dtypes.bfloat16)
    half = 64
    inv = (10000.0 ** (-np.arange(half, dtype=np.float32) / half)).astype(np.float32)
    rope = np.zeros((128, 17, 2, 64), np.float32)
    for b in range(17):
        if b < 16:
            pos = (b * 128 + np.arange(128)).astype(np.float32)
        else:
            pos = (PAST + (np.arange(128) % 8)).astype(np.float32)
        ang = (pos[:, None] * inv[None, :]).astype(np.float32)
        rope[:, b, 0, :] = np.cos(ang); rope[:, b, 1, :] = np.sin(ang)
    return cfa, off, cba, offb, rope


CF_ARR, CF_OFF, CB_ARR, CB_OFF, ROPE_ARR = _mk_consts()
CF_N = CF_ARR.shape[1]
CB_N = CB_ARR.shape[1]

_NC_CACHE = {}


def _stack(a, b):
    return np.ascontiguousarray(np.concatenate([a, b], axis=0))


def make_in_maps(inp):
    f = lambda a: np.ascontiguousarray(np.asarray(a, dtype=np.float32))
    shared = {}
    for k, src in [("w_in", "w_in"), ("w_glu", "w_glu"), ("w_ssm_out", "w_ssm_out"), ("w_ret_out", "w_ret_out"), ("w_o", "w_o"), ("w_up", "w_up"), ("w_down", "w_down")]:
        shared[k] = f(inp[src])
    pl = lambda v: np.ascontiguousarray(f(v).reshape(DEPTH, -1, 128).transpose(2, 0, 1))
    shared["gmix"] = pl(inp["norm_mix"]); shared["gffn"] = pl(inp["norm_ffn"])
    shared["gfin"] = np.ascontiguousarray(f(inp["norm_final"]).reshape(8, 128).T)
    shared["convw"] = np.ascontiguousarray(f(inp["conv_w"]).reshape(DEPTH, 3, NCH, 128).transpose(3, 0, 1, 2))
    shared["convb"] = np.ascontiguousarray(f(inp["conv_b"]).reshape(DEPTH, NCH, 128).transpose(2, 0, 1))
    lr = f(inp["ssm_lam_re"]).transpose(2, 0, 1)
    li = f(inp["ssm_lam_im"]).transpose(2, 0, 1)
    shared["lamr"] = _stack(lr, lr); shared["lami"] = _stack(li, li)
    shared["logdt"] = np.ascontiguousarray(np.broadcast_to(f(inp["ssm_log_dt"])[None], (128, DEPTH, G)))
    br = f(inp["ssm_b_re"]).transpose(2, 0, 1, 3)
    bi = f(inp["ssm_b_im"]).transpose(2, 0, 1, 3)
    shared["bS"] = _stack(br, bi); shared["bX"] = _stack(bi, br)
    cr = f(inp["ssm_c_re"]).transpose(3, 0, 1, 2)
    ci = f(inp["ssm_c_im"]).transpose(3, 0, 1, 2)
    shared["crD"] = _stack(cr, cr); shared["ciD"] = _stack(ci, ci)
    d = f(inp["ssm_d"]).reshape(DEPTH, G, 16)
    dc = d.transpose(2, 0, 1)
    shared["dcol"] = np.ascontiguousarray(np.tile(dc, (8, 1, 1)))
    shared["cf32"] = CF_ARR; shared["cbf16"] = CB_ARR; shared["rope"] = ROPE_ARR
    xp = f(inp["x_prompt"]); xs = f(inp["x_sample"])
    sre = f(inp["state_ssm_re"]); sim = f(inp["state_ssm_im"]); sret = f(inp["state_ret"]); scv = f(inp["state_conv"])
    maps = []
    for ci_ in range(NCORES):
        m = dict(shared)
        S0 = ci_ * NS
        xt = np.concatenate([xp[ci_], xs[S0:S0 + NS].reshape(NS * DS, D)], axis=0)
        m["xT_in"] = np.ascontiguousarray(xt.T.reshape(8, 128, TOK).transpose(1, 0, 2))
        a = sre[:, S0:S0 + NS].transpose(3, 0, 2, 1)
        b = sim[:, S0:S0 + NS].transpose(3, 0, 2, 1)
        m["s0S"] = _stack(a, b); m["s0X"] = _stack(b, a)
        cv = scv[:, S0:S0 + NS].reshape(DEPTH, NS, 2, NCH, 128).transpose(4, 0, 3, 1, 2)
        m["conv0"] = np.ascontiguousarray(cv)
        m["sret"] = np.ascontiguousarray(sret[:, S0:S0 + NS])
        maps.append(m)
    return maps


def kernel(**inputs):
    if "nc" not in _NC_CACHE:
        _NC_CACHE["nc"] = build()
    nc = _NC_CACHE["nc"]
    maps = make_in_maps(inputs)
    res = run_bass_kernel_spmd(nc, maps, core_ids=list(range(NCORES)))
    R = res.results
    if "dbg_out" in R[0]:
        _NC_CACHE["dbg"] = [np.asarray(r["dbg_out"]) for r in R]
    B = NCORES
    y_p = np.zeros((B, SEQ, D), np.float32); y_s = np.zeros((B * NS, DS, D), np.float32)
    sre_p = np.zeros((DEPTH, B, G, P), np.float32); sim_p = np.zeros_like(sre_p)
    ret_p = np.zeros((DEPTH, B, 4, 128, 256), np.float32)
    cv_p = np.zeros((DEPTH, B, 2, 2 * DFF), np.float32)
    sre_s = np.zeros((DEPTH, B * NS, G, P), np.float32); sim_s = np.zeros_like(sre_s)
    ret_s = np.zeros((DEPTH, B * NS, 4, 128, 256), np.float32)
    cv_s = np.zeros((DEPTH, B * NS, 2, 2 * DFF), np.float32)
    for c in range(B):
        r = R[c]
        yt = np.asarray(r["yT_out"]).transpose(1, 0, 2).reshape(D, TOK).T
        y_p[c] = yt[:SEQ]; y_s[c * NS:(c + 1) * NS] = yt[SEQ:].reshape(NS, DS, D)
        sp = np.asarray(r["ssmp_out"])
        sre_p[:, c] = sp[:64].transpose(1, 2, 0); sim_p[:, c] = sp[64:].transpose(1, 2, 0)
        ss = np.asarray(r["ssms_out"])
        sre_s[:, c * NS:(c + 1) * NS] = ss[:64].transpose(1, 3, 2, 0); sim_s[:, c * NS:(c + 1) * NS] = ss[64:].transpose(1, 3, 2, 0)
        ret_p[:, c] = np.asarray(r["retp_out"]); ret_s[:, c * NS:(c + 1) * NS] = np.asarray(r["rets_out"])
        cp = np.asarray(r["convp_out"])
        cv_p[:, c] = cp.transpose(1, 3, 2, 0).reshape(DEPTH, 2, 2 * DFF)
        cs = np.asarray(r["convs_out"])
        cv_s[:, c * NS:(c + 1) * NS] = cs.transpose(1, 3, 4, 2, 0).reshape(DEPTH, NS, 2, 2 * DFF)
    return (y_p, y_s, sre_p, sim_p, ret_p, cv_p, sre_s, sim_s, ret_s, cv_s)
```

```python
import math
from contextlib import ExitStack
import numpy as np
import ml_dtypes
import concourse.bass as bass
import concourse.mybir as mybir
from concourse.bass_utils import run_bass_kernel_spmd

F32 = mybir.dt.float32
BF16 = mybir.dt.bfloat16
ALU = mybir.AluOpType
AF = mybir.ActivationFunctionType

NCORES = 8
D = 1024
DEPTH = 4
SEQ = 2048
NS = 16
DS = 8
TOK = SEQ + NS * DS
G = 32
P = 64
DFF = 2816
NCH = 44
EPS = 1e-6
PAST = 16384
MAGIC = 12582912.0
TWO_PI = 2.0 * math.pi
LAYERS = DEPTH


class LT:
    __slots__ = ("w", "r", "key")

    def __init__(self):
        self.w = {}
        self.r = {}
        self.key = None


class Sched:
    ENGS = ("pe", "act", "dve", "pool", "sp")

    def __init__(self, nc, stack, n_dma):
        self.nc = nc
        self.sem = {}
        self.cnt = {}
        for e in self.ENGS:
            self.sem[e] = stack.enter_context(nc.semaphore("s_" + e))
            self.cnt[e] = 0
        self.n_dma = n_dma
        for i in range(n_dma):
            k = "d%d" % i
            self.sem[k] = stack.enter_context(nc.semaphore("s_" + k))
            self.cnt[k] = 0
        self.seen = {}
        self.prog = {e: [] for e in self.ENGS}
        self.rr = 0

    def _deps(self, eng, reads, writes):
        deps = {}

        def add(d, skip_same):
            for k, v in d.items():
                if skip_same and k == eng:
                    continue
                if deps.get(k, 0) < v:
                    deps[k] = v
        for t in reads:
            add(t.w, eng == "pe")
        for t in writes:
            add(t.w, True)
            add(t.r, True)
        waits = []
        for k, v in deps.items():
            if self.seen.get((eng, k), 0) >= v:
                continue
            self.seen[(eng, k)] = v
            waits.append((k, v))
        return waits

    def _mark(self, me, reads, writes):
        k, v = me
        for t in reads:
            if t.r.get(k, 0) < v:
                t.r[k] = v
        for t in writes:
            if t.w.get(k, 0) < v:
                t.w[k] = v

    def op(self, eng, fn, reads=(), writes=()):
        waits = self._deps(eng, reads, writes)
        self.cnt[eng] += 1
        self._mark((eng, self.cnt[eng]), reads, writes)
        self.prog[eng].append((waits, fn, (eng, 1)))

    def dma(self, q, fn, reads=(), writes=(), key=None):
        if key is None:
            lt = writes[0] if len(writes) else reads[0]
            if lt.key is None:
                lt.key = "d%d" % self.rr
                self.rr = (self.rr + 1) % self.n_dma
            key = lt.key
        waits = self._deps(q, reads, writes)
        self.cnt[key] += 16
        self._mark((key, self.cnt[key]), reads, writes)
        self.prog[q].append((waits, fn, (key, 16)))

    def barrier(self):
        for eng in self.ENGS:
            waits = []
            for k, v in self.cnt.items():
                if k == eng or v == 0:
                    continue
                if self.seen.get((eng, k), 0) >= v:
                    continue
                self.seen[(eng, k)] = v
                waits.append((k, v))
            if waits:
                self.prog[eng].append((waits, None, None))

    def final_wait(self, eng, tiles):
        waits = self._deps(eng, tiles, tiles)
        self.prog[eng].append((waits, None, None))

    def emit(self, block):
        def mk(ename):
            def body(e):
                for waits, fn, inc in self.prog[ename]:
                    for k, v in waits:
                        e.wait_ge(self.sem[k], v)
                    if fn is not None:
                        fn(e).then_inc(self.sem[inc[0]], inc[1])
            return body
        block.tensor(mk("pe"))
        block.scalar(mk("act"))
        block.vector(mk("dve"))
        block.gpsimd(mk("pool"))
        block.sync(mk("sp"))


class Carver:
    def __init__(self, ap2d):
        self.ap = ap2d
        self.off = 0
        self.n = ap2d.shape[1]

    def take(self, shape):
        n = 1
        for d in shape[1:]:
            n *= d
        assert self.off + n <= self.n, ("arena overflow", self.off, n, self.n)
        v = self.ap[:, self.off:self.off + n]
        self.off += n
        if len(shape) == 2:
            return v
        names = " ".join("d%d" % i for i in range(len(shape) - 1))
        kw = {"d%d" % i: shape[i + 1] for i in range(len(shape) - 1)}
        return v.rearrange("p (%s) -> p %s" % (names, names), **kw)


class Pool_:
    def __init__(self, items):
        self.items = items
        self.i = 0

    def get(self):
        it = self.items[self.i]
        self.i = (self.i + 1) % len(self.items)
        return it


def build(nlayers=LAYERS, skip=""):
    nc = bass.Bass("TRN2", target_bir_lowering=False)

    def IN(name, shape, dt=F32):
        return nc.dram_tensor(name, list(shape), dt, kind="ExternalInput").ap()

    def OUT(name, shape):
        return nc.dram_tensor(name, list(shape), F32, kind="ExternalOutput").ap()

    xT_in = IN("xT_in", [128, 8, TOK])
    w_in = IN("w_in", [DEPTH, D, 5632]); w_glu = IN("w_glu", [DEPTH, 512, 512]); w_sso = IN("w_ssm_out", [DEPTH, 512, D])
    w_ro = IN("w_ret_out", [DEPTH, D, D]); w_o = IN("w_o", [DEPTH, D, D]); w_up = IN("w_up", [DEPTH, D, 5632])
    w_dn = IN("w_down", [DEPTH, DFF, D])
    gmix = IN("gmix", [128, DEPTH, 8]); gffn = IN("gffn", [128, DEPTH, 8]); gfin = IN("gfin", [128, 8])
    convw = IN("convw", [128, DEPTH, 3, NCH]); convb = IN("convb", [128, DEPTH, NCH])
    lamr = IN("lamr", [128, DEPTH, G]); lami = IN("lami", [128, DEPTH, G]); logdt = IN("logdt", [128, DEPTH, G])
    bS_in = IN("bS", [128, DEPTH, G, 16]); bX_in = IN("bX", [128, DEPTH, G, 16])
    crD_in = IN("crD", [128, DEPTH, G, 16]); ciD_in = IN("ciD", [128, DEPTH, G, 16])
    dcol_in = IN("dcol", [128, DEPTH, G])
    s0S_in = IN("s0S", [128, DEPTH, G, NS]); s0X_in = IN("s0X", [128, DEPTH, G, NS])
    conv0_in = IN("conv0", [128, DEPTH, NCH, NS, 2])
    sret_in = IN("sret", [DEPTH, NS, 4, 128, 256])
    cf_in = IN("cf32", [128, CF_N]); cb_in = IN("cbf16", [128, CB_N], BF16)
    rope_in = IN("rope", [128, 17, 2, 64])

    yT_out = OUT("yT_out", [128, 8, TOK])
    ssmp_out = OUT("ssmp_out", [128, DEPTH, G]); ssms_out = OUT("ssms_out", [128, DEPTH, G, NS])
    retp_out = OUT("retp_out", [DEPTH, 4, 128, 256]); rets_out = OUT("rets_out", [DEPTH, NS, 4, 128, 256])
    convp_out = OUT("convp_out", [128, DEPTH, NCH, 2]); convs_out = OUT("convs_out", [128, DEPTH, NCH, NS, 2])
    dbg_out = OUT("dbg_out", [128, 1024]) if "G" in skip else None
    s5cb = nc.dram_tensor("s5cb", [DEPTH, 8, 128, 5 * 4 * 128], BF16, kind="Internal").ap()
    s5cf = nc.dram_tensor("s5cf", [DEPTH, 8, 128, 64 + 192], F32, kind="Internal").ap()
    rcar = nc.dram_tensor("rcar", [DEPTH, 4, 128, 256], F32, kind="Internal").ap()

    with ExitStack() as st:
        def sb(name, shape, dt=F32):
            return st.enter_context(nc.sbuf_tensor("sb_" + name, list(shape), dt))

        def pst(name, shape, dt=F32):
            return st.enter_context(nc.psum_tensor(name, list(shape), dt))

        S = Sched(nc, st, n_dma=24)
        block = st.enter_context(nc.Block())

        CAP = [None]

        def V(eng, method, reads, writes, *a, **kw):
            if CAP[0] is not None:
                CAP[0].append((eng, method, reads, writes, a, kw))
                return
            S.op(eng, lambda e: getattr(e, method)(*a, **kw), reads, writes)

        def DMA(q, out, in_, reads, writes, key=None):
            if CAP[0] is not None:
                CAP[0].append(("__dma__", q, out, in_, reads, writes))
                return
            S.dma(q, lambda e: e.dma_start(out=out, in_=in_), reads, writes, key)

        def emit_captured(item):
            if item[0] == "__dma__":
                _, q, out, in_, reads, writes = item
                DMA(q, out, in_, reads, writes)
            else:
                e_, m_, r_, w_, a_, kw_ = item
                V(e_, m_, r_, w_, *a_, **kw_)

        NH = 1152
        xT = sb("xT", [128, 8, NH]); Lx = [LT() for _ in range(3)]
        hT = sb("hT", [128, 8, NH], BF16); Lh = [LT() for _ in range(3)]
        HT = [[(0, 512), (512, 512)], [(0, 512), (512, 512), (1024, 128)]]
        GOFF = [0, 1024]

        cf = sb("cf", [128, CF_N]); Lcf = LT()
        cb = sb("cb", [128, CB_N], BF16); Lcb = LT()
        rope = sb("rope", [128, 17, 2, 64]); Lrope = LT()
        DMA("sp", cf[:], cf_in, [], [Lcf]); DMA("sp", cb[:], cb_in, [], [Lcb]); DMA("sp", rope[:], rope_in, [], [Lrope])

        def cfs(name):
            o, n = CF_OFF[name]
            return cf[:, o:o + n]

        def cbs(name):
            o, n = CB_OFF[name]
            return cb[:, o:o + n]
        ident_f = cfs("ident"); tmask = cfs("tmask"); pswap = cfs("pswap"); nvec = cfs("nvec"); kvec = cfs("kvec")
        sgn = cfs("sgn"); mre = cfs("mre"); nmre = cfs("nmre"); nmim = cfs("nmim"); epsc = cfs("eps")
        qsc = cfs("qsc"); ksc = cfs("ksc"); qksc = cfs("qksc"); gct = cfs("gct"); rowmask = cfs("rowmask"); cmask_p = cfs("cmask_p"); cmask_s = cfs("cmask_s")
        ident_b = cbs("ident"); ones_b = cbs("ones")

        small = {}
        for nm, src, shp in [("gmix", gmix, [128, DEPTH, 8]), ("gffn", gffn, [128, DEPTH, 8]), ("gfin", gfin, [128, 8]),
                             ("convw", convw, [128, DEPTH, 3, NCH]), ("convb", convb, [128, DEPTH, NCH]),
                             ("lamr", lamr, [128, DEPTH, G]), ("lami", lami, [128, DEPTH, G]), ("logdt", logdt, [128, DEPTH, G]),
                             ("dcol", dcol_in, [128, DEPTH, G])]:
            t = sb("sm_" + nm, shp); L = LT()
            DMA("sp", t[:], src, [], [L])
            small[nm] = (t, L)

        Wlast = sb("Wlast", [128, DEPTH, G]); Mclast = sb("Mclast", [128, DEPTH, G]); Mslast = sb("Mslast", [128, DEPTH, G]); Lcar = LT()
        convcar = sb("convcar", [128, DEPTH, NCH, 2]); Lccar = LT()
        ssm_p_sb = sb("ssm_p_sb", [128, DEPTH, G]); Lssp = LT()
        ssm_s_sb = sb("ssm_s_sb", [128, DEPTH, G, NS]); Lsss = LT()
        Lrcar = [LT() for _ in range(DEPTH)]

        psBig = pst("psbig", [128, 2048])
        psF = Pool_([(psBig[:, i * 512:(i + 1) * 512], LT()) for i in range(4)])
        psS = Pool_([(pst("pss%d" % i, [128, 512]), LT()) for i in range(2)])
        psB = Pool_([(pst("psb%d" % i, [128, 1024], BF16), LT()) for i in range(2)])
        t2k = Pool_([(sb("t2k%d" % i, [128, 512])[:], LT()) for i in range(5)])
        tb1k = Pool_([(sb("tb1k%d" % i, [128, 512], BF16)[:], LT()) for i in range(4)])
        rstd_p = Pool_([(sb("rstd%d" % i, [128, 512])[:], LT()) for i in range(2)])

        ARF_N = 7424
        ARB_N = 39700
        arena_f = sb("arena_f", [128, ARF_N]); arena_b = sb("arena_b", [128, ARB_N], BF16)

        def wload(dst, src, Ld):
            DMA("pool", dst, src, [], [Ld])

        def wview(w, l):
            return w[l].rearrange("(c p) n -> p c n", p=128)

        def rms_rstd(ti, t0, n, dscale):
            ps, Lp = psF.get()
            for c in range(8):
                sq, Lsq = tb1k.get()
                V("act", "activation", [Lx[ti]], [Lsq], out=sq[:, :n], in_=xT[:, c, t0:t0 + n], func=AF.Square)
                V("pe", "matmul", [Lsq, Lcb], [Lp], ps[:, :n], lhsT=ones_b, rhs=sq[:, :n], start=(c == 0), stop=(c == 7))
            r, Lr = rstd_p.get()
            V("act", "activation", [Lp, Lcf], [Lr], out=r[:, :n], in_=ps[:, :n], func=AF.Sqrt, bias=epsc[:, 0:1], scale=dscale)
            V("dve", "reciprocal", [Lr], [Lr], out=r[:, :n], in_=r[:, :n])
            return r, Lr

        def norm_to_h(l, which, tiles):
            gt, Lg = small[which]
            for ti, (t0, n) in enumerate(tiles):
                r, Lr = rms_rstd(ti, t0, n, 1.0 / D)
                for c in range(8):
                    eng = "dve"
                    V(eng, "scalar_tensor_tensor", [Lx[ti], Lg, Lr], [Lh[ti]], out=hT[:, c, t0:t0 + n], in0=xT[:, c, t0:t0 + n],
                      scalar=gt[:, l, c:c + 1], in1=r[:, :n], op0=ALU.mult, op1=ALU.mult)

        def proj_fm(ps, Lp, wt, Lw, col0, nk, act_fn, Lact, n):
            for kc in range(nk):
                V("pe", "matmul", [Lw] + Lact, [Lp], ps[:, :n], lhsT=wt[:, kc, col0:col0 + 128], rhs=act_fn(kc), start=(kc == 0), stop=(kc == nk - 1))

        def apply_wo(ti, t0, n, wo, Lwo, mix, Lmix):
            for oc in range(8):
                ps, Lp = psF.get()
                proj_fm(ps, Lp, wo, Lwo, oc * 128, 8, lambda kc: mix[:, kc, :n], [Lmix], n)
                V("dve", "tensor_tensor", [Lp, Lx[ti]], [Lx[ti]], out=xT[:, oc, t0:t0 + n], in0=xT[:, oc, t0:t0 + n], in1=ps[:, :n], op=ALU.add)

        def tiles_overlapping(tiles, a, b):
            return [ti for ti, (t0, n) in enumerate(tiles) if t0 < b and t0 + n > a]

        def range_reduce(dst, src, Lt, tmp, add_half_pi):
            if add_half_pi:
                V("dve", "tensor_scalar", [Lt], [Lt], out=dst, in0=src, scalar1=math.pi / 2, scalar2=None, op0=ALU.add)
                src = dst
            V("dve", "tensor_scalar", [Lt], [Lt], out=tmp, in0=src, scalar1=1.0 / TWO_PI, scalar2=MAGIC, op0=ALU.mult, op1=ALU.add)
            V("dve", "tensor_scalar", [Lt], [Lt], out=tmp, in0=tmp, scalar1=MAGIC, scalar2=TWO_PI, op0=ALU.subtract, op1=ALU.mult)
            V("dve", "tensor_tensor", [Lt], [Lt], out=dst, in0=src, in1=tmp, op=ALU.subtract)

        GG = 4
        NSUB = G // GG
        KP = 128
        S5C_L = {}

        def s5_setup(l):
            lamr_t, Ll1 = small["lamr"]; lami_t, Ll2 = small["lami"]; ldt_t, Ll3 = small["logdt"]; dcol_t, Ll4 = small["dcol"]
            Lsmall = [Ll1, Ll2, Ll3, Ll4, Lcf]
            cfA = Carver(arena_f[:, :]); cbA = Carver(arena_b[:, 36368:])
            s5f = [None] + [cfA.take([128, GG, 128]) for _ in range(5)]; Ls5f = [None] + [LT() for _ in range(5)]
            Pb = cbA.take([128, 3, GG, 128]); LPb = LT()
            Ab = cbA.take([128, 3, GG, 128]); LAb = LT()

            class SCtx:
                pass
            sctx = []
            for i in range(4):
                c = SCtx()
                c.s5sm = cfA.take([128, 16, GG]); c.Lsm = LT()
                c.Ep = cfA.take([128, 6, GG, 24]); c.LEp = LT()
                c.bsx = cfA.take([128, 3, GG, 16]); c.Lbsx = LT()
                c.bSt = cfA.take([128, 4, GG, 16]); c.LbSt = LT()
                c.tsm = cfA.take([128, GG, 32]); c.Ltsm = LT()
                sctx.append(c)

            def small_stage(c, sq):
                g0 = sq * GG
                gs = slice(g0, g0 + GG)
                s5sm, Lsm, Ep, LEp, bsx, Lbsx, bSt, LbSt = c.s5sm, c.Lsm, c.Ep, c.LEp, c.bsx, c.Lbsx, c.bSt, c.LbSt
                s5f = [c.tsm]; Ls5f = [c.Ltsm]
                sm = lambda i: s5sm[:, i, :]
                Er = Ep[:, 4]; Ei = Ep[:, 5]
                V("act", "activation", Lsmall, [Lsm], out=sm(0), in_=ldt_t[:, l, gs], func=AF.Exp)
                V("dve", "tensor_tensor", Lsmall + [Lsm], [Lsm], out=sm(1), in0=lamr_t[:, l, gs], in1=sm(0), op=ALU.mult)
                V("dve", "tensor_tensor", Lsmall + [Lsm], [Lsm], out=sm(2), in0=lami_t[:, l, gs], in1=sm(0), op=ALU.mult)
                nv = nvec.unsqueeze(1).to_broadcast([128, GG, 24])
                V("dve", "tensor_tensor", [Lsm, Lcf], [LEp], out=Ep[:, 0], in0=sm(2).unsqueeze(2).to_broadcast([128, GG, 24]), in1=nv, op=ALU.mult)
                V("dve", "tensor_tensor", [Lsm, Lcf], [LEp], out=Ep[:, 1], in0=sm(1).unsqueeze(2).to_broadcast([128, GG, 24]), in1=nv, op=ALU.mult)
                range_reduce(Ep[:, 2], Ep[:, 0], LEp, Ep[:, 3], False)
                V("act", "activation", [LEp], [LEp], out=Ep[:, 5], in_=Ep[:, 2], func=AF.Sin)
                range_reduce(Ep[:, 2], Ep[:, 0], LEp, Ep[:, 3], True)
                V("act", "activation", [LEp], [LEp], out=Ep[:, 4], in_=Ep[:, 2], func=AF.Sin)
                V("act", "activation", [LEp], [LEp], out=Ep[:, 1], in_=Ep[:, 1], func=AF.Exp)
                V("dve", "tensor_tensor", [LEp], [LEp], out=Ep[:, 4], in0=Ep[:, 4], in1=Ep[:, 1], op=ALU.mult)
                V("dve", "tensor_tensor", [LEp], [LEp], out=Ep[:, 5], in0=Ep[:, 5], in1=Ep[:, 1], op=ALU.mult)
                Er = Ep[:, 4]; Ei = Ep[:, 5]
                E1r = Er[:, :, 16]; E1i = Ei[:, :, 16]
                lr_ = lamr_t[:, l, gs]; li_ = lami_t[:, l, gs]
                RS = Lsmall + [Lsm, LEp]
                V("dve", "tensor_scalar", RS, [Lsm], out=sm(3), in0=E1r, scalar1=-1.0, scalar2=None, op0=ALU.add)
                V("dve", "tensor_tensor", RS, [Lsm], out=sm(4), in0=lr_, in1=lr_, op=ALU.mult)
                V("dve", "tensor_tensor", RS, [Lsm], out=sm(8), in0=li_, in1=li_, op=ALU.mult)
                V("dve", "tensor_tensor", RS, [Lsm], out=sm(4), in0=sm(4), in1=sm(8), op=ALU.add)
                V("dve", "reciprocal", RS, [Lsm], out=sm(5), in_=sm(4))
                V("dve", "tensor_tensor", RS, [Lsm], out=sm(6), in0=sm(3), in1=lr_, op=ALU.mult)
                V("dve", "tensor_tensor", RS, [Lsm], out=sm(8), in0=E1i, in1=li_, op=ALU.mult)
                V("dve", "tensor_tensor", RS, [Lsm], out=sm(6), in0=sm(6), in1=sm(8), op=ALU.add)
                V("dve", "tensor_tensor", RS, [Lsm], out=sm(6), in0=sm(6), in1=sm(5), op=ALU.mult)
                V("dve", "tensor_tensor", RS, [Lsm], out=sm(7), in0=E1i, in1=lr_, op=ALU.mult)
                V("dve", "tensor_tensor", RS, [Lsm], out=sm(8), in0=sm(3), in1=li_, op=ALU.mult)
                V("dve", "tensor_tensor", RS, [Lsm], out=sm(7), in0=sm(7), in1=sm(8), op=ALU.subtract)
                V("dve", "tensor_tensor", RS, [Lsm], out=sm(7), in0=sm(7), in1=sm(5), op=ALU.mult)
                V("act", "activation", RS, [Lsm], out=sm(10), in_=sm(1), func=AF.Exp, scale=8.0)
                V("dve", "tensor_scalar", RS, [Lsm], out=sm(11), in0=sm(2), scalar1=8.0, scalar2=None, op0=ALU.mult)
                bc16 = lambda a: a.unsqueeze(2).to_broadcast([128, GG, 16])
                RB = [LbSt, Lsm, Lcf]
                V("dve", "tensor_scalar", RB, [Lbsx], out=bsx[:, 0], in0=bSt[:, 1], scalar1=sgn[:, 0:1], scalar2=None, op0=ALU.mult)
                t0_ = s5f[0][:, :, 0:16]; t1_ = s5f[0][:, :, 16:32]
                V("dve", "tensor_tensor", RB, [Ls5f[0]], out=t0_, in0=bSt[:, 0], in1=bc16(sm(6)), op=ALU.mult)
                V("dve", "tensor_tensor", RB + [Lbsx], [Ls5f[0]], out=t1_, in0=bsx[:, 0], in1=bc16(sm(7)), op=ALU.mult)
                V("dve", "tensor_tensor", [Ls5f[0]], [Lbsx], out=bsx[:, 1], in0=t0_, in1=t1_, op=ALU.add)
                V("dve", "tensor_tensor", RB + [Lbsx], [Ls5f[0]], out=t0_, in0=bsx[:, 0], in1=bc16(sm(6)), op=ALU.mult)
                V("dve", "tensor_tensor", RB, [Ls5f[0]], out=t1_, in0=bSt[:, 0], in1=bc16(sm(7)), op=ALU.mult)
                V("dve", "tensor_tensor", [Ls5f[0]], [Lbsx], out=bsx[:, 2], in0=t0_, in1=t1_, op=ALU.subtract)

            def big_stage(c, sq):
                g0 = sq * GG
                gs = slice(g0, g0 + GG)
                s5sm, Lsm, Ep, LEp, bsx, Lbsx, bSt, LbSt = c.s5sm, c.Lsm, c.Ep, c.LEp, c.bsx, c.Lbsx, c.bSt, c.LbSt
                sm = lambda i: s5sm[:, i, :]
                Er = Ep[:, 4]; Ei = Ep[:, 5]
                Ls5c = LT()
                bc16 = lambda a: a.unsqueeze(2).to_broadcast([128, GG, 16])
                v4 = lambda a: a.rearrange("p g (j c) -> p g j c", c=16)
                bj = lambda a: a.unsqueeze(2).to_broadcast([128, GG, 8, 16])
                ej = lambda a: a.unsqueeze(3).to_broadcast([128, GG, 8, 16])
                bs_ = bsx[:, 1]; bx_ = bsx[:, 2]
                RP = [LEp, Lbsx]

                def cplx(dst_bf, e_r, e_i, sign_mode, Ldst):
                    a_, b_ = (e_r, e_i) if sign_mode == 0 else (e_i, e_r)
                    V("dve", "tensor_tensor", RP, [Ls5f[1]], out=v4(s5f[1]), in0=ej(a_), in1=bj(bs_), op=ALU.mult)
                    V("dve", "tensor_tensor", RP, [Ls5f[2]], out=v4(s5f[2]), in0=ej(b_), in1=bj(bx_), op=ALU.mult)
                    V("dve", "tensor_tensor", [Ls5f[1], Ls5f[2]], [Ldst], out=dst_bf, in0=s5f[1], in1=s5f[2],
                      op=(ALU.add if sign_mode == 0 else ALU.subtract))
                Pp, LPp = tb1k.get(); Ppt, LPpt = tb1k.get()
                g3 = lambda a: a.rearrange("p (g n) -> p g n", g=GG)
                cplx(g3(Pp), Er[:, :, 0:8], Ei[:, :, 0:8], 0, LPp)
                cplx(g3(Ppt), Er[:, :, 0:8], Ei[:, :, 0:8], 1, LPpt)
                cplx(Pb[:, 0], Er[:, :, 8:16], Ei[:, :, 8:16], 0, LPb)
                ECr = Er[:, :, 16:24]; ECi = Ei[:, :, 16:24]
                crD_ = bSt[:, 2]; ciD_ = bSt[:, 3]
                RC = [LEp, LbSt]
                V("dve", "tensor_tensor", RC, [Ls5f[1]], out=v4(s5f[1]), in0=ej(ECr), in1=bj(crD_), op=ALU.mult)
                V("dve", "tensor_tensor", RC, [Ls5f[2]], out=v4(s5f[2]), in0=ej(ECi), in1=bj(ciD_), op=ALU.mult)
                V("dve", "tensor_tensor", [Ls5f[1], Ls5f[2]], [Ls5f[3]], out=s5f[3], in0=s5f[1], in1=s5f[2], op=ALU.subtract)
                V("dve", "tensor_tensor", RC, [Ls5f[1]], out=v4(s5f[1]), in0=ej(ECi), in1=bj(crD_), op=ALU.mult)
                V("dve", "tensor_tensor", RC, [Ls5f[2]], out=v4(s5f[2]), in0=ej(ECr), in1=bj(ciD_), op=ALU.mult)
                V("dve", "tensor_tensor", [Ls5f[1], Ls5f[2]], [Ls5f[4]], out=s5f[4], in0=s5f[1], in1=s5f[2], op=ALU.add)
                V("dve", "tensor_scalar", [Ls5f[3], Lcf], [Ls5f[1]], out=s5f[1], in0=s5f[3], scalar1=mre[:, 0:1], scalar2=None, op0=ALU.mult)
                V("dve", "scalar_tensor_tensor", [Ls5f[4], Ls5f[1], Lcf], [LPb], out=Pb[:, 1], in0=s5f[4], scalar=nmim[:, 0:1], in1=s5f[1], op0=ALU.mult, op1=ALU.add)
                V("dve", "tensor_scalar", [Ls5f[4], Lcf], [Ls5f[2]], out=s5f[2], in0=s5f[4], scalar1=nmre[:, 0:1], scalar2=None, op0=ALU.mult)
                V("dve", "scalar_tensor_tensor", [Ls5f[3], Ls5f[2], Lcf], [LPb], out=Pb[:, 2], in0=s5f[3], scalar=nmim[:, 0:1], in1=s5f[2], op0=ALU.mult, op1=ALU.add)
                ps, Lp = psS.get()
                for g in range(GG):
                    V("pe", "matmul", [LPb], [Lp], ps[:, g * 128:(g + 1) * 128], lhsT=Pb[:, 0, g, :], rhs=Pb[:, 1, g, :], start=True, stop=True)
                V("dve", "tensor_tensor", [Lp, Lcf], [Ls5f[5]], out=s5f[5], in0=ps[:].rearrange("p (g n) -> p g n", g=GG),
                  in1=tmask.unsqueeze(1).to_broadcast([128, GG, 128]), op=ALU.mult)
                for g in range(GG):
                    V("dve", "scalar_tensor_tensor", [Ls5f[5], Lcf, Ll4], [LAb], out=Ab[:, 2, g, :], in0=ident_f, scalar=dcol_t[:, l, g0 + g:g0 + g + 1],
                      in1=s5f[5][:, g, :], op0=ALU.mult, op1=ALU.add)
                pb_, Lpb_ = psB.get()
                for i, (Px, LPx) in enumerate(((Pp, LPp), (Ppt, LPpt))):
                    for g in range(GG):
                        V("pe", "transpose", [LPx, Lcb], [Lpb_], out=pb_[:, (i * GG + g) * 128:(i * GG + g + 1) * 128], in_=Px[:, g * 128:(g + 1) * 128], identity=ident_b)
                V("act", "activation", [Lpb_], [LAb], out=Ab[:, 0:2].rearrange("p a g n -> p (a g n)"), in_=pb_[:, 0:2 * GG * 128], func=AF.Copy)
                DMA("sp", s5cb[l, sq, :, 0:2 * GG * 128], Pb[:, 1:3].rearrange("p a g n -> p (a g n)"), [LPb], [Ls5c])
                DMA("sp", s5cb[l, sq, :, 2 * GG * 128:5 * GG * 128], Ab.rearrange("p a g n -> p (a g n)"), [LAb], [Ls5c])
                DMA("sp", s5cf[l, sq, :, 0:64], s5sm.rearrange("p a g -> p (a g)"), [Lsm], [Ls5c])
                DMA("sp", s5cf[l, sq, :, 64:256], Ep[:, 4:6].rearrange("p a g n -> p (a g n)"), [LEp], [Ls5c])
                S5C_L[(l, sq)] = Ls5c

            for grp in range(NSUB // 4):
                subs = [grp * 4 + i for i in range(4)]
                lists = []
                for i, sq in enumerate(subs):
                    c = sctx[i]
                    for i2, src in enumerate([bS_in, bX_in, crD_in, ciD_in]):
                        DMA("sp", c.bSt[:, i2], src[:, l, sq * GG:(sq + 1) * GG, :], [], [c.LbSt])
                    prev_cap = CAP[0]
                    CAP[0] = []
                    small_stage(c, sq)
                    lists.append(CAP[0]); CAP[0] = prev_cap
                for k in range(max(len(x) for x in lists)):
                    for lst in lists:
                        if k < len(lst):
                            e_, m_, r_, w_, a_, kw_ = lst[k]
                            V(e_, m_, r_, w_, *a_, **kw_)
                for i, sq in enumerate(subs):
                    big_stage(sctx[i], sq)

        def s5_phase(l, hi, tiles, yaT, Lya):
            S.barrier()
            has_s = (hi == 1)
            KT = KP + (NS if has_s else 0)
            kbase = hi * KP
            PT = [(0, KP)] + ([(KP, NS)] if has_s else [])
            lamr_t, Ll1 = small["lamr"]; lami_t, Ll2 = small["lami"]; ldt_t, Ll3 = small["logdt"]; dcol_t, Ll4 = small["dcol"]
            Lsmall = [Ll1, Ll2, Ll3, Ll4, Lcf]
            cfA = Carver(arena_f[:, :]); cbA = Carver(arena_b[:, 5120:])

            class Ctx:
                pass
            ctxs = []
            for u in range(2):
                b = Ctx()
                b.s5sm = cfA.take([128, 16, GG]); b.Lsm = LT()
                b.Ep = cfA.take([128, 2, GG, 24]); b.LEp = LT()
                b.cosT = cfA.take([128, GG, KP]); b.sinT = cfA.take([128, GG, KP]); b.Ltab = LT()
                b.Yb = cfA.take([128, GG, KP]); b.LY = LT()
                b.Wb = cfA.take([128, GG, KP]); b.LW = LT()
                b.s0t = cfA.take([128, 2, GG, NS]); b.Ls0 = LT()
                b.sfin = cfA.take([128, 4, GG, NS]); b.Lsfin = LT()
                b.ctmp = cfA.take([128, GG, 2]); b.Lctmp = LT()
                b.PbC = cbA.take([128, 2, GG, 128]); b.LPb = LT()
                b.Ab = cbA.take([128, 3, GG, 128]); b.LAb = LT()
                b.Mc = cbA.take([128, GG, KP + NS]); b.Ms = cbA.take([128, GG, KP + NS]); b.LM = LT()
                b.yapt = cbA.take([128, 2, 8, GG * 16]); b.Lyapt = LT()
                ctxs.append(b)
            Uq_p = Pool_([(cbA.take([128, GG, KP + NS]), LT()) for i in range(4)])
            upt_p = Pool_([(cbA.take([128, 2, GG, 8, 16]), LT()) for i in range(4)])
            wu = Pool_([(cbA.take([128, 8, GG * 16]), LT()) for i in range(4)])

            def stage_u(sq):
                g0 = sq * GG
                wut, Lwu = wu.get()
                wload(wut, wview(w_in, l)[:, :, g0 * 16:g0 * 16 + GG * 16], Lwu)
                Uq, LU = Uq_p.get()
                upt, Lupt = upt_p.get()
                for m, (k0, nk) in enumerate(PT):
                    t0 = k0 * 8
                    tl = tiles_overlapping(tiles, t0, t0 + nk * 8)
                    ps, Lp = psF.get()
                    for j in range(8):
                        for c in range(8):
                            lh = hT[:, c, t0:t0 + nk * 8].rearrange("p (k j) -> p k j", j=8)[:, :, j]
                            V("pe", "matmul", [Lh[t] for t in tl] + [Lwu], [Lp], ps[:nk, j * 64:(j + 1) * 64], lhsT=lh, rhs=wut[:, c, :],
                              start=(c == 0), stop=(c == 7))
                    V("act", "activation", [Lp], [Lupt], out=upt[:nk, m].rearrange("p g j c -> p j g c"),
                      in_=ps[:nk, :].rearrange("p (j g c) -> p j g c", j=8, g=GG), func=AF.Copy)
                    pb_, Lpb_ = psB.get()
                    for g in range(GG):
                        V("pe", "transpose", [Lupt, Lcb], [Lpb_], out=pb_[:, g * 128:g * 128 + nk], in_=upt[:nk, m, g].rearrange("p j c -> p (j c)"),
                          identity=ident_b[:nk, :nk])
                    V("dve", "tensor_copy", [Lpb_], [LU], out=Uq[:, :, k0:k0 + nk], in_=pb_[:, 0:GG * 128].rearrange("p (g n) -> p g n", g=GG)[:, :, :nk])
                return Uq, LU


            f2 = lambda a: a.rearrange("p g k -> p (g k)")

            def stepL(b):
                sq = b.sq
                Lc_ = S5C_L[(l, sq)]
                DMA("sp", b.PbC.rearrange("p a g n -> p (a g n)"), s5cb[l, sq, :, 0:2 * GG * 128], [Lc_], [b.LPb])
                DMA("sp", b.Ab.rearrange("p a g n -> p (a g n)"), s5cb[l, sq, :, 2 * GG * 128:5 * GG * 128], [Lc_], [b.LAb])
                DMA("sp", b.s5sm.rearrange("p a g -> p (a g)"), s5cf[l, sq, :, 0:64], [Lc_], [b.Lsm])
                DMA("sp", b.Ep.rearrange("p a g n -> p (a g n)"), s5cf[l, sq, :, 64:256], [Lc_], [b.LEp])
                if has_s:
                    DMA("sp", b.s0t[:, 0], s0S_in[:, l, b.gs, :], [], [b.Ls0]); DMA("sp", b.s0t[:, 1], s0X_in[:, l, b.gs, :], [], [b.Ls0])

            def stepT(b):
                s5sm, Lsm, Yb, Wb, LY, sinT, cosT, Ltab = b.s5sm, b.Lsm, b.Yb, b.Wb, b.LY, b.sinT, b.cosT, b.Ltab
                kv = kvec[:, kbase:kbase + KP].unsqueeze(1).to_broadcast([128, GG, KP])
                V("dve", "tensor_tensor", [Lsm, Lcf], [LY], out=Yb, in0=s5sm[:, 11, :].unsqueeze(2).to_broadcast([128, GG, KP]), in1=kv, op=ALU.mult)
                range_reduce(Wb, Yb, LY, sinT, False)
                V("act", "activation", [LY], [Ltab], out=sinT, in_=Wb, func=AF.Sin)
                V("act", "activation", [LY], [LY], out=Yb, in_=Wb, func=AF.Abs)
                V("act", "activation", [LY, Lcf], [Ltab], out=cosT, in_=Yb, func=AF.Sin, scale=-1.0, bias=cfs("halfpi")[:, 0:1])

            def stepX(b):
                s5sm, Lsm, Yb, Wb, LY, LW, sinT, cosT, Ltab = b.s5sm, b.Lsm, b.Yb, b.Wb, b.LY, b.LW, b.sinT, b.cosT, b.Ltab
                Ab, LAb, Uq, LU = b.Ab, b.LAb, b.Uq, b.LU
                psx, Lpx = psF.get(); psxt, Lpxt = psF.get()
                for g in range(GG):
                    V("pe", "matmul", [LAb, LU], [Lpx], psx[:, g * KP:(g + 1) * KP], lhsT=Ab[:, 0, g, :], rhs=Uq[:, g, 0:KP], start=True, stop=True)
                    V("pe", "matmul", [LAb, LU], [Lpxt], psxt[:, g * KP:(g + 1) * KP], lhsT=Ab[:, 1, g, :], rhs=Uq[:, g, 0:KP], start=True, stop=True)
                V("dve", "tensor_tensor", [Lpx, Ltab], [LY], out=f2(Yb), in0=psx[:], in1=f2(cosT), op=ALU.mult)
                V("dve", "tensor_tensor", [Lpxt, Ltab], [LW], out=f2(Wb), in0=psxt[:], in1=f2(sinT), op=ALU.mult)
                V("dve", "tensor_tensor", [LY, LW], [LY], out=f2(Yb), in0=f2(Yb), in1=f2(Wb), op=ALU.add)
                if hi == 1:
                    V("dve", "tensor_tensor", [Lsm, Lcar], [b.Lctmp], out=b.ctmp[:, :, 0], in0=s5sm[:, 10, :], in1=Wlast[:, l, b.gs], op=ALU.mult)
                    V("dve", "tensor_tensor", [LY, b.Lctmp], [LY], out=Yb[:, :, 0], in0=Yb[:, :, 0], in1=b.ctmp[:, :, 0], op=ALU.add)

            def stepS(b):
                s5sm, Lsm, Yb, Wb, LY, LW, sinT, cosT, Ltab = b.s5sm, b.Lsm, b.Yb, b.Wb, b.LY, b.LW, b.sinT, b.cosT, b.Ltab
                Ab, LAb, Uq, LU, Mc, Ms, LM = b.Ab, b.LAb, b.Uq, b.LU, b.Mc, b.Ms, b.LM
                s0t, Ls0, sfin, Lsfin, LEp, gs = b.s0t, b.Ls0, b.sfin, b.Lsfin, b.LEp, b.gs
                Er = b.Ep[:, 0]; Ei = b.Ep[:, 1]
                for g in range(GG):
                    V("dve", "tensor_tensor_scan", [LY, Lsm, LW], [LW], out=Wb[:, g, :], data0=s5sm[:, 10, g:g + 1].to_broadcast([128, KP]), data1=Yb[:, g, :],
                      initial=0.0, op0=ALU.mult, op1=ALU.add)
                if hi == 0:
                    V("dve", "memset", [], [LM], Mc[:, :, 0:1], 0.0)
                    V("dve", "memset", [], [LM], Ms[:, :, 0:1], 0.0)
                else:
                    V("dve", "tensor_copy", [Lcar], [LM], out=Mc[:, :, 0], in_=Mclast[:, l, gs])
                    V("dve", "tensor_copy", [Lcar], [LM], out=Ms[:, :, 0], in_=Mslast[:, l, gs])
                V("dve", "tensor_tensor", [LW, Ltab], [LM], out=Mc[:, :, 1:KP], in0=Wb[:, :, 0:KP - 1], in1=cosT[:, :, 0:KP - 1], op=ALU.mult)
                V("dve", "tensor_tensor", [LW, Ltab], [LM], out=Ms[:, :, 1:KP], in0=Wb[:, :, 0:KP - 1], in1=sinT[:, :, 0:KP - 1], op=ALU.mult)
                if hi == 0:
                    V("dve", "tensor_copy", [LW], [Lcar], out=Wlast[:, l, gs], in_=Wb[:, :, KP - 1])
                    V("dve", "tensor_tensor", [LW, Ltab], [Lcar], out=Mclast[:, l, gs], in0=Wb[:, :, KP - 1], in1=cosT[:, :, KP - 1], op=ALU.mult)
                    V("dve", "tensor_tensor", [LW, Ltab], [Lcar], out=Mslast[:, l, gs], in0=Wb[:, :, KP - 1], in1=sinT[:, :, KP - 1], op=ALU.mult)
                else:
                    V("dve", "memset", [], [LM], Ms[:, :, KP:KT], 0.0)
                    V("act", "activation", [Ls0], [LM], out=Mc[:, :, KP:KT], in_=s0t[:, 0], func=AF.Copy)
                    V("dve", "tensor_tensor", [LW, Ltab], [Lsfin], out=sfin[:, 0, :, 0], in0=Wb[:, :, KP - 1], in1=cosT[:, :, KP - 1], op=ALU.mult)
                    V("dve", "tensor_tensor", [LW, Ltab], [Lsfin], out=sfin[:, 1, :, 0], in0=Wb[:, :, KP - 1], in1=sinT[:, :, KP - 1], op=ALU.mult)
                    ps, Lp = psF.get()
                    V("pe", "matmul", [Lsfin, Lcf], [Lp], ps[:, 0:GG], lhsT=pswap, rhs=sfin[:, 1, :, 0], start=True, stop=True)
                    V("dve", "tensor_tensor", [Lp, Lsfin], [Lssp], out=ssm_p_sb[:, l, gs], in0=ps[:, 0:GG], in1=sfin[:, 0, :, 0], op=ALU.add)
                    psx, Lpx = psF.get()
                    for g in range(GG):
                        V("pe", "matmul", [LAb, LU], [Lpx], psx[:, g * NS:(g + 1) * NS], lhsT=Ab[:, 0, g, :], rhs=Uq[:, g, KP:KT], start=True, stop=True)
                    Lr8 = Er[:, :, 23]; Li8 = Ei[:, :, 23]
                    bns = lambda a: a.unsqueeze(2).to_broadcast([128, GG, NS])
                    V("dve", "tensor_tensor", [Ls0, LEp], [Lsfin], out=sfin[:, 2], in0=s0t[:, 0], in1=bns(Lr8), op=ALU.mult)
                    V("dve", "scalar_tensor_tensor", [Ls0, LEp, Lcf], [Lsfin], out=sfin[:, 3], in0=s0t[:, 1], scalar=sgn[:, 0:1], in1=bns(Li8), op0=ALU.mult, op1=ALU.mult)
                    V("dve", "tensor_tensor", [Lsfin], [Lsfin], out=sfin[:, 2], in0=sfin[:, 2], in1=sfin[:, 3], op=ALU.add)
                    V("dve", "tensor_tensor", [Lsfin, Lpx], [Lsss], out=ssm_s_sb[:, l, gs, :], in0=sfin[:, 2], in1=psx[:, 0:GG * NS].rearrange("p (g s) -> p g s", g=GG), op=ALU.add)

            def stepY(b):
                Ab, LAb, Uq, LU, Mc, Ms, LM, PbC, LPb, yapt, Lyapt, g0 = b.Ab, b.LAb, b.Uq, b.LU, b.Mc, b.Ms, b.LM, b.PbC, b.LPb, b.yapt, b.Lyapt, b.g0
                for m, (k0, nk) in enumerate(PT):
                    ps, Lp = psF.get()
                    for g in range(GG):
                        o_ = ps[:nk, g * 128:(g + 1) * 128]
                        V("pe", "matmul", [LU, LAb], [Lp], o_, lhsT=Uq[:, g, k0:k0 + nk], rhs=Ab[:, 2, g, :], start=True, stop=False)
                        V("pe", "matmul", [LM, LPb], [Lp], o_, lhsT=Mc[:, g, k0:k0 + nk], rhs=PbC[:, 0, g, :], start=False, stop=False)
                        V("pe", "matmul", [LM, LPb], [Lp], o_, lhsT=Ms[:, g, k0:k0 + nk], rhs=PbC[:, 1, g, :], start=False, stop=True)
                    ta, La_ = t2k.get(); tb_, Lb_ = t2k.get()
                    V("act", "activation", [Lp], [La_], out=ta[:nk], in_=ps[:nk], func=AF.Square)
                    V("dve", "tensor_scalar", [La_], [La_], out=ta[:nk], in0=ta[:nk], scalar1=0.044715, scalar2=1.0, op0=ALU.mult, op1=ALU.add)
                    V("dve", "tensor_tensor", [La_, Lp], [La_], out=ta[:nk], in0=ta[:nk], in1=ps[:nk], op=ALU.mult)
                    V("act", "activation", [La_], [Lb_], out=tb_[:nk], in_=ta[:nk], func=AF.Sigmoid, scale=1.5957691216)
                    V("dve", "tensor_tensor", [Lb_, Lp], [Lyapt], out=yapt[:nk, m].rearrange("p t (g c) -> p g t c", g=GG),
                      in0=tb_[:nk].rearrange("p (g t c) -> p g t c", g=GG, t=8), in1=ps[:nk].rearrange("p (g t c) -> p g t c", g=GG, t=8), op=ALU.mult)
                    pb_, Lpb_ = psB.get()
                    for t in range(8):
                        V("pe", "transpose", [Lyapt, Lcb], [Lpb_], out=pb_[0:GG * 16, t * 128:t * 128 + nk], in_=yapt[:nk, m, t, :], identity=ident_b[:nk, :nk])
                    tok0 = k0 * 8
                    tl = tiles_overlapping(tiles, tok0, tok0 + nk * 8)
                    cq = (g0 * 16) // 128; p0 = (g0 * 16) % 128
                    V("dve", "tensor_copy", [Lpb_], [Lya[t] for t in tl], out=yaT[p0:p0 + GG * 16, cq, tok0:tok0 + nk * 8].rearrange("p (k j) -> p j k", j=8),
                      in_=pb_[0:GG * 16, :].rearrange("p (t k) -> p t k", t=8)[:, :, :nk])

            pairs = [(2 * i, 2 * i + 1) for i in range(NSUB // 2)]
            Ubuf = {0: stage_u(0), 1: stage_u(1)}
            for pi, pr in enumerate(pairs):
                for u, sq in enumerate(pr):
                    b = ctxs[u]
                    b.sq = sq; b.g0 = sq * GG; b.gs = slice(sq * GG, sq * GG + GG)
                    b.Uq, b.LU = Ubuf.pop(sq)
                if pi + 1 < len(pairs):
                    for sq2 in pairs[pi + 1]:
                        Ubuf[sq2] = stage_u(sq2)
                for step in (stepL, stepT, stepX, stepS, stepY):
                    for u in range(2):
                        step(ctxs[u])
            if hi == 1:
                DMA("sp", ssmp_out[:, l, :], ssm_p_sb[:, l, :], [Lssp], [])
                DMA("sp", ssms_out[:, l], ssm_s_sb[:, l], [Lsss], [])

        def phase_B(l, tiles, yaT, Lya, wo, Lwo):
            S.barrier()
            cbA = Carver(arena_b[:, 5120 + 8192:])
            wgl = cbA.take([128, 4, 512]); Lwgl = LT()
            wso = cbA.take([128, 4, 1024]); Lwso = LT()
            wga = cbA.take([128, 8, 1024]); Lwga = LT()
            mix = cbA.take([128, 8, 512]); Lmix = LT()
            ya2 = cbA.take([128, 4, 512]); Ly2 = LT()
            wload(wgl, w_glu[l].rearrange("(c p) n -> p c n", p=128), Lwgl)
            wload(wso, w_sso[l].rearrange("(c p) n -> p c n", p=128), Lwso)
            GA0 = 3584
            wload(wga, wview(w_in, l)[:, :, GA0:GA0 + 1024], Lwga)
            wload(wo, wview(w_o, l), Lwo)
            for ti, (t0, n) in enumerate(tiles):
                for oc in range(4):
                    ps, Lp = psF.get()
                    proj_fm(ps, Lp, wgl, Lwgl, oc * 128, 4, lambda kc: yaT[:, kc, t0:t0 + n], [Lya[ti]], n)
                    sg, Lsg = tb1k.get()
                    V("act", "activation", [Lp], [Lsg], out=sg[:, :n], in_=ps[:, :n], func=AF.Sigmoid)
                    V("dve", "tensor_tensor", [Lsg, Lya[ti]], [Ly2], out=ya2[:, oc, :n], in0=sg[:, :n], in1=yaT[:, oc, t0:t0 + n], op=ALU.mult)
                for oc in range(8):
                    ps, Lp = psF.get(); ps2, Lp2 = psF.get()
                    proj_fm(ps, Lp, wso, Lwso, oc * 128, 4, lambda kc: ya2[:, kc, :n], [Ly2], n)
                    proj_fm(ps2, Lp2, wga, Lwga, oc * 128, 8, lambda kc: hT[:, kc, t0:t0 + n], [Lh[ti]], n)
                    sg, Lsg = t2k.get()
                    V("act", "activation", [Lp2], [Lsg], out=sg[:, :n], in_=ps2[:, :n], func=AF.Sigmoid)
                    V("dve", "tensor_tensor", [Lsg, Lp], [Lmix], out=mix[:, oc, :n], in0=sg[:, :n], in1=ps[:, :n], op=ALU.mult)
                if "B" not in skip:
                    apply_wo(ti, t0, n, wo, Lwo, mix, Lmix)

        def phase_C(l, hi, tiles, oT, LoT):
            S.barrier()
            cfA = Carver(arena_f[:, :])
            cbY = Carver(arena_b[:, 0:5120])
            cbW = Carver(arena_b[:, 5120:5120 + 8192])
            cbA = Carver(arena_b[:, 5120 + 8192 + 9216:])
            r_st = cfA.take([128, 4, 256]); Lr_st = LT()
            orw_p = Pool_([(cfA.take([128, 4, 2, 128]), LT()) for i in range(1)])
            qf_p = Pool_([(cfA.take([128, 4, 256]), LT()) for i in range(1)])
            r0_p = Pool_([(cfA.take([128, 4, 256]), LT()) for i in range(2)])
            rn_p = Pool_([(cfA.take([128, 4, 256]), LT()) for i in range(2)])
            wq4 = cbA.take([128, 8, 2048]); Lwq4 = [LT() for _ in range(4)]
            r_bf = cbY.take([128, 4, 256]); Lr_bf = LT()
            qk_p = Pool_([(cbY.take([128, 4, 2, 128]), LT()) for i in range(2)])
            qkT_p = Pool_([(cbY.take([128, 4, 2, 128]), LT()) for i in range(1)])
            sc_p = Pool_([(cbY.take([128, 4, 128]), LT()) for i in range(2)])
            v_p = Pool_([(cbW.take([128, 4, 256]), LT()) for i in range(2)])
            sq_p = Pool_([(cbW.take([128, 1024]), LT()) for i in range(2)])
            r0b_p = Pool_([(cbW.take([128, 4, 256]), LT()) for i in range(2)])
            km_p = Pool_([(cbW.take([128, 4, 128]), LT()) for i in range(2)])
            goff = GOFF[hi]
            wv_ = wview(w_in, l)
            for hh in range(4):
                wload(wq4[:, :, hh * 512:hh * 512 + 128], wv_[:, :, 512 + hh * 128:512 + (hh + 1) * 128], Lwq4[hh])
                wload(wq4[:, :, hh * 512 + 128:hh * 512 + 256], wv_[:, :, 1024 + hh * 128:1024 + (hh + 1) * 128], Lwq4[hh])
                wload(wq4[:, :, hh * 512 + 256:hh * 512 + 512], wv_[:, :, 1536 + hh * 256:1536 + (hh + 1) * 256], Lwq4[hh])
            if hi == 0:
                V("dve", "memset", [], [Lr_st], r_st, 0.0)
            else:
                DMA("sp", r_st, rcar[l].rearrange("h d v -> d h v"), [Lrcar[l]], [Lr_st])
            V("act", "activation", [Lr_st], [Lr_bf], out=r_bf, in_=r_st, func=AF.Copy)
            psQ = psBig[:, :]
            LQ = [it[1] for it in psF.items]
            blocks = []
            for ti, (t0, n) in enumerate(tiles):
                for b in range(n // 128):
                    blocks.append((ti, t0 + b * 128))
            for (ti, t0) in blocks:
                tg = goff + t0
                is_s = (tg >= SEQ)
                blk = 16 if is_s else tg // 128
                kind = 1 if is_s else 0
                for hh in range(4):
                    for c in range(8):
                        V("pe", "matmul", [Lh[ti], Lwq4[hh]], [LQ[hh]], psQ[:, hh * 512:(hh + 1) * 512], lhsT=hT[:, c, t0:t0 + 128], rhs=wq4[:, c, hh * 512:(hh + 1) * 512],
                          start=(c == 0), stop=(c == 7))
                qf, Lqf = qf_p.get()
                vt, Lv = v_p.get()
                for hh in range(4):
                    V("act", "activation", [LQ[hh]], [Lqf], out=qf[:, hh, :], in_=psQ[:, hh * 512:hh * 512 + 256], func=AF.Copy)
                    V("act", "activation", [LQ[hh]], [Lv], out=vt[:, hh, :], in_=psQ[:, hh * 512 + 256:hh * 512 + 512], func=AF.Copy)
                x1 = qf.rearrange("p h (a f d) -> p h a f d", a=2, f=2)[:, :, :, 0, :]
                x2 = qf.rearrange("p h (a f d) -> p h a f d", a=2, f=2)[:, :, :, 1, :]
                cs_ = rope[:, blk, 0, :].unsqueeze(1).unsqueeze(1).to_broadcast([128, 4, 2, 64])
                sn_ = rope[:, blk, 1, :].unsqueeze(1).unsqueeze(1).to_broadcast([128, 4, 2, 64])
                pr = [t2k.get() for _ in range(4)]
                v4 = lambda a: a.rearrange("p (h a d) -> p h a d", h=4, a=2)
                V("dve", "tensor_tensor", [Lqf, Lrope], [pr[0][1]], out=v4(pr[0][0]), in0=x1, in1=cs_, op=ALU.mult)
                V("dve", "tensor_tensor", [Lqf, Lrope], [pr[1][1]], out=v4(pr[1][0]), in0=x2, in1=sn_, op=ALU.mult)
                V("dve", "tensor_tensor", [Lqf, Lrope], [pr[2][1]], out=v4(pr[2][0]), in0=x1, in1=sn_, op=ALU.mult)
                V("dve", "tensor_tensor", [Lqf, Lrope], [pr[3][1]], out=v4(pr[3][0]), in0=x2, in1=cs_, op=ALU.mult)
                V("dve", "tensor_tensor", [pr[0][1], pr[1][1]], [pr[0][1]], out=pr[0][0], in0=pr[0][0], in1=pr[1][0], op=ALU.subtract)
                V("dve", "tensor_tensor", [pr[2][1], pr[3][1]], [pr[2][1]], out=pr[2][0], in0=pr[2][0], in1=pr[3][0], op=ALU.add)
                qk, Lqk = qk_p.get()
                sct = qksc.rearrange("p (k h a) -> p k h a", k=2, h=4)[:, kind].unsqueeze(3).to_broadcast([128, 4, 2, 64])
                V("dve", "tensor_tensor", [pr[0][1], Lcf], [Lqk], out=qk[:, :, :, 0:64], in0=v4(pr[0][0]), in1=sct, op=ALU.mult)
                V("dve", "tensor_tensor", [pr[2][1], Lcf], [Lqk], out=qk[:, :, :, 64:128], in0=v4(pr[2][0]), in1=sct, op=ALU.mult)
                pb_, Lpb_ = psB.get()
                for hh in range(4):
                    for a in range(2):
                        V("pe", "transpose", [Lqk, Lcb], [Lpb_], out=pb_[:, (hh * 2 + a) * 128:(hh * 2 + a + 1) * 128], in_=qk[:, hh, a, :], identity=ident_b)
                qkT, LqkT = qkT_p.get()
                V("dve", "tensor_copy", [Lpb_], [LqkT], out=qkT.rearrange("p h a n -> p (h a n)"), in_=pb_[:, 0:1024])
                ps2, Lp2 = psF.get()
                for hh in range(4):
                    V("pe", "matmul", [LqkT], [Lp2], ps2[:, hh * 128:(hh + 1) * 128], lhsT=qkT[:, hh, 1, :], rhs=qkT[:, hh, 0, :], start=True, stop=True)
                sc, Lsc = sc_p.get()
                mk = (cmask_s if is_s else cmask_p).unsqueeze(1).to_broadcast([128, 4, 128])
                V("dve", "tensor_tensor", [Lp2, Lcf], [Lsc], out=sc, in0=ps2[:, :].rearrange("p (h n) -> p h n", h=4), in1=mk, op=ALU.mult)
                po = [psF.get(), psF.get()]
                orw, Lor = orw_p.get()
                for hh in range(4):
                    pso, Lpo = po[hh // 2]
                    for e_ in range(2):
                        o_ = pso[:, ((hh % 2) * 2 + e_) * 128:((hh % 2) * 2 + e_ + 1) * 128]
                        V("pe", "matmul", [Lv, Lsc], [Lpo], o_, lhsT=vt[:, hh, e_ * 128:(e_ + 1) * 128], rhs=sc[:, hh, :], start=True, stop=is_s)
                        if not is_s:
                            V("pe", "matmul", [Lr_bf, LqkT], [Lpo], o_, lhsT=r_bf[:, hh, e_ * 128:(e_ + 1) * 128], rhs=qkT[:, hh, 0, :], start=False, stop=True)
                orf = orw.rearrange("p h e n -> p (h e n)")
                for i2 in range(2):
                    V("act", "activation", [po[i2][1]], [Lor], out=orf[:, i2 * 512:(i2 + 1) * 512], in_=po[i2][0][:, :], func=AF.Copy)
                if not is_s:
                    pd = [psS.get(), psS.get()]
                    for hh in range(4):
                        psd, Lpd = pd[hh // 2]
                        V("pe", "matmul", [Lqk, Lv], [Lpd], psd[:, (hh % 2) * 256:(hh % 2 + 1) * 256], lhsT=qk[:, hh, 1, :], rhs=vt[:, hh, :], start=True, stop=True)
                    for i2 in range(2):
                        rv = r_st[:, i2 * 2:i2 * 2 + 2, :].rearrange("p h v -> p (h v)")
                        V("dve", "tensor_tensor", [pd[i2][1], Lr_st], [Lr_st], out=rv, in0=rv, in1=pd[i2][0][:, :], op=ALU.add)
                    gtab = gct.rearrange("p (k h) -> p k h", k=2)[:, 0].unsqueeze(2).to_broadcast([128, 4, 256])
                    V("dve", "tensor_tensor", [Lr_st, Lcf], [Lr_st], out=r_st, in0=r_st, in1=gtab, op=ALU.mult)
                    V("act", "activation", [Lr_st], [Lr_bf], out=r_bf, in_=r_st, func=AF.Copy)
                    if tg == 1024 - 128:
                        DMA("sp", rcar[l].rearrange("h d v -> d h v"), r_st, [Lr_st], [Lrcar[l]])
                    if tg == SEQ - 128:
                        DMA("sp", retp_out[l].rearrange("h d v -> d h v"), r_st, [Lr_st], [])
                else:
                    pin = [psF.get(), psF.get()]
                    gtab = gct.rearrange("p (k h) -> p k h", k=2)[:, 1].unsqueeze(2).to_broadcast([128, 4, 256])
                    def _ld_r0(sx):
                        r0x, Lr0x = r0_p.get()
                        DMA("sp", r0x, sret_in[l, sx].rearrange("h d v -> d h v"), [], [Lr0x])
                        return r0x, Lr0x
                    r0_next = _ld_r0(0)
                    for s_ in range(NS):
                        r0, Lr0 = r0_next
                        if s_ + 1 < NS:
                            r0_next = _ld_r0(s_ + 1)
                        r0b, Lr0b = r0b_p.get()
                        V("act", "activation", [Lr0], [Lr0b], out=r0b, in_=r0, func=AF.Copy)
                        for hh in range(4):
                            psi, Lpi = pin[hh // 2]
                            for e_ in range(2):
                                c0_ = ((hh % 2) * 2 + e_) * 128 + s_ * 8
                                V("pe", "matmul", [Lr0b, LqkT], [Lpi], psi[:, c0_:c0_ + 8], lhsT=r0b[:, hh, e_ * 128:(e_ + 1) * 128],
                                  rhs=qkT[:, hh, 0, s_ * 8:s_ * 8 + 8], start=True, stop=True)
                        km, Lkm = km_p.get()
                        V("dve", "tensor_scalar", [Lqk, Lcf], [Lkm], out=km, in0=qk[:, :, 1, :], scalar1=rowmask[:, s_:s_ + 1], scalar2=None, op0=ALU.mult)
                        pd = [psS.get(), psS.get()]
                        for hh in range(4):
                            psd, Lpd = pd[hh // 2]
                            V("pe", "matmul", [Lkm, Lv], [Lpd], psd[:, (hh % 2) * 256:(hh % 2 + 1) * 256], lhsT=km[:, hh, :], rhs=vt[:, hh, :], start=True, stop=True)
                        rn, Lrn = rn_p.get()
                        for i2 in range(2):
                            V("dve", "tensor_tensor", [pd[i2][1], Lr0], [Lrn], out=rn[:, i2 * 2:i2 * 2 + 2, :].rearrange("p h v -> p (h v)"),
                              in0=r0[:, i2 * 2:i2 * 2 + 2, :].rearrange("p h v -> p (h v)"), in1=pd[i2][0][:, :], op=ALU.add)
                        V("dve", "tensor_tensor", [Lrn, Lcf], [Lrn], out=rn, in0=rn, in1=gtab, op=ALU.mult)
                        DMA("pool", rets_out[l, s_].rearrange("h d v -> d h v"), rn, [Lrn], [])
                    for i2 in range(2):
                        tin, Ltin = t2k.get()
                        V("act", "activation", [pin[i2][1]], [Ltin], out=tin[:, :], in_=pin[i2][0][:, :], func=AF.Copy)
                        V("dve", "tensor_tensor", [Lor, Ltin], [Lor], out=orf[:, i2 * 512:(i2 + 1) * 512], in0=orf[:, i2 * 512:(i2 + 1) * 512], in1=tin[:, :], op=ALU.add)
                sq, Lsq = sq_p.get()
                V("dve", "tensor_tensor", [Lor], [Lsq], out=sq, in0=orf, in1=orf, op=ALU.mult)
                ps5, Lp5 = psF.get()
                for hh in range(4):
                    for e_ in range(2):
                        V("pe", "matmul", [Lsq, Lcb], [Lp5], ps5[:, hh * 128:(hh + 1) * 128], lhsT=ones_b, rhs=sq[:, (hh * 2 + e_) * 128:(hh * 2 + e_ + 1) * 128],
                          start=(e_ == 0), stop=(e_ == 1))
                rs, Lrs = t2k.get()
                V("act", "activation", [Lp5, Lcf], [Lrs], out=rs[:, :], in_=ps5[:, :], func=AF.Sqrt, bias=epsc[:, 0:1], scale=1.0 / 256)
                V("dve", "reciprocal", [Lrs], [Lrs], out=rs[:, :], in_=rs[:, :])
                V("dve", "tensor_tensor", [Lor, Lrs], [LoT[ti]], out=oT[:, :, t0:t0 + 128].rearrange("p (h e) n -> p h e n", h=4), in0=orw,
                  in1=rs[:, :].rearrange("p (h n) -> p h n", h=4).unsqueeze(2).to_broadcast([128, 4, 2, 128]), op=ALU.mult)

        def phase_D(l, tiles, oT, LoT, wo, Lwo):
            S.barrier()
            cbA = Carver(arena_b[:, 5120 + 8192 + 9216:])
            wX = cbA.take([128, 8, 1024]); LwX = LT()
            wY = cbA.take([128, 8, 1024]); LwY = LT()
            mix = arena_b[:, 0:4096].rearrange("p (c n) -> p c n", c=8); Lmix = LT()
            GR0 = 2560
            wload(wX, wview(w_in, l)[:, :, GR0:GR0 + 1024], LwX)
            wload(wY, wview(w_ro, l), LwY)
            wload(wo, wview(w_o, l), Lwo)
            for ti, (t0, n) in enumerate(tiles):
                for oc in range(8):
                    ps, Lp = psF.get()
                    proj_fm(ps, Lp, wX, LwX, oc * 128, 8, lambda kc: hT[:, kc, t0:t0 + n], [Lh[ti]], n)
                    sg, Lsg = tb1k.get()
                    V("act", "activation", [Lp], [Lsg], out=sg[:, :n], in_=ps[:, :n], func=AF.Silu)
                    o_ = oT[:, oc, t0:t0 + n]
                    V("dve", "tensor_tensor", [Lsg, LoT[ti]], [LoT[ti]], out=o_, in0=o_, in1=sg[:, :n], op=ALU.mult)
            GB0 = 4608
            wload(wX, wview(w_in, l)[:, :, GB0:GB0 + 1024], LwX)
            for ti, (t0, n) in enumerate(tiles):
                for oc in range(8):
                    ps, Lp = psF.get(); ps2, Lp2 = psF.get()
                    proj_fm(ps, Lp, wY, LwY, oc * 128, 8, lambda kc: oT[:, kc, t0:t0 + n], [LoT[ti]], n)
                    proj_fm(ps2, Lp2, wX, LwX, oc * 128, 8, lambda kc: hT[:, kc, t0:t0 + n], [Lh[ti]], n)
                    sg, Lsg = t2k.get()
                    V("act", "activation", [Lp2], [Lsg], out=sg[:, :n], in_=ps2[:, :n], func=AF.Sigmoid)
                    V("dve", "tensor_tensor", [Lsg, Lp], [Lmix], out=mix[:, oc, :n], in0=sg[:, :n], in1=ps[:, :n], op=ALU.mult)
                if "D" not in skip:
                    apply_wo(ti, t0, n, wo, Lwo, mix, Lmix)

        GF = 4
        EXTRA_K = [10]

        def ffn(l, hi, tiles, extra=None):
            S.barrier()
            cfA = Carver(arena_f[:, :]); cbA = Carver(arena_b[:, :])
            conv0 = cfA.take([128, NCH, NS, 2]); Lc0 = LT()
            convp_sb = cfA.take([128, NCH, 2]); Lcp = LT()
            convs_sb = cfA.take([128, NCH, NS, 2]); Lcs = LT()
            actT = cbA.take([128, GF, NH]); Lact = [LT() for _ in range(3)]
            wup_p = Pool_([(cbA.take([128, 8, 2 * GF * 128]), LT()) for i in range(2)])
            wdn_p = Pool_([(cbA.take([128, GF, D]), LT()) for i in range(2)])
            upb = [[(cbA.take([128, 514]), LT()) for a in range(2)] for i in range(GF)]
            Dg = cbA.take([128, GF, 2, 3, 128]); LDg = LT()
            ups = Pool_([(cbA.take([128, NS, 10]), LT()) for i in range(2)]) if hi == 1 else None
            cw, Lcw = small["convw"]; cbv, Lcbv = small["convb"]
            if hi == 1:
                DMA("sp", conv0, conv0_in[:, l], [], [Lc0])
            ngroups = (22 + GF - 1) // GF
            for gi in range(ngroups):
                c0 = gi * GF
                ng = min(GF, 22 - c0)
                wu_, Lwu_ = wup_p.get(); wd_, Lwd_ = wdn_p.get()
                wload(wu_[:, :, 0:ng * 128], wview(w_up, l)[:, :, c0 * 128:(c0 + ng) * 128], Lwu_)
                wload(wu_[:, :, GF * 128:GF * 128 + ng * 128], wview(w_up, l)[:, :, DFF + c0 * 128:DFF + (c0 + ng) * 128], Lwu_)
                wload(wd_[:, 0:ng, :], w_dn[l, c0 * 128:(c0 + ng) * 128, :].rearrange("(c p) n -> p c n", p=128), Lwd_)
                for cc in range(ng):
                    for a in range(2):
                        ch = c0 + cc + a * 22
                        for j in range(3):
                            V("dve", "tensor_scalar", [Lcw, Lcb], [LDg], out=Dg[:, cc, a, j, :], in0=ident_b, scalar1=cw[:, l, j, ch:ch + 1], scalar2=None, op0=ALU.mult)
                pend_conv = []
                pend_down = []
                resmap = {}

                def emit_up(ti, t0, n, cc, a):
                    is_s = (n == 128)
                    ch = c0 + cc + a * 22
                    ps, Lp = psF.get()
                    proj_fm(ps, Lp, wu_, Lwu_, a * GF * 128 + cc * 128, 8, lambda kc: hT[:, kc, t0:t0 + n], [Lh[ti]], n)
                    if not is_s:
                        ub, Lub = upb[cc][a]
                        if ti == 0:
                            if hi == 0:
                                V("dve", "memset", [], [Lub], ub[:, 0:2], 0.0)
                            else:
                                V("dve", "tensor_copy", [Lccar], [Lub], out=ub[:, 0:2], in_=convcar[:, l, ch, :])
                        else:
                            V("dve", "tensor_copy", [Lub], [Lub], out=ub[:, 0:2], in_=ub[:, 512:514])
                        V("act", "activation", [Lp], [Lub], out=ub[:, 2:514], in_=ps[:, :], func=AF.Copy)
                        if ti == 1:
                            if hi == 0:
                                V("act", "activation", [Lp], [Lccar], out=convcar[:, l, ch, :], in_=ps[:, 510:512], func=AF.Copy)
                            else:
                                V("act", "activation", [Lp], [Lcp], out=convp_sb[:, ch, :], in_=ps[:, 510:512], func=AF.Copy)
                        return (ub, Lub)
                    else:
                        us, Lus = ups.get()
                        V("dve", "tensor_copy", [Lc0], [Lus], out=us[:, :, 0:2], in_=conv0[:, ch])
                        V("act", "activation", [Lp], [Lus], out=us[:, :, 2:10], in_=ps[:, 0:128].rearrange("p (s j) -> p s j", j=8), func=AF.Copy)
                        V("act", "activation", [Lp], [Lcs], out=convs_sb[:, ch], in_=ps[:, 0:128].rearrange("p (s j) -> p s j", j=8)[:, :, 6:8], func=AF.Copy)
                        return (us, Lus)

                def emit_conv(ti, t0, n, cc, a, buf):
                    is_s = (n == 128)
                    ch = c0 + cc + a * 22
                    ub, Lub = buf
                    ps2, Lp2 = psF.get()
                    for j in range(3):
                        if not is_s:
                            V("pe", "matmul", [LDg, Lub], [Lp2], ps2[:, :], lhsT=Dg[:, cc, a, j, :], rhs=ub[:, j:j + 512], start=(j == 0), stop=(j == 2))
                        else:
                            V("pe", "matmul", [LDg, Lub], [Lp2], ps2[:, 0:128], lhsT=Dg[:, cc, a, j, :], rhs=ub[:, :, j:j + 8], start=(j == 0), stop=(j == 2))
                    resmap[(ti, cc, a)] = (ps2, Lp2, ch)
                    if a == 1:
                        (pv, Lpv, chv) = resmap.pop((ti, cc, 0)); (pg, Lpg, chg) = resmap.pop((ti, cc, 1))
                        sg, Lsg = t2k.get()
                        V("act", "activation", [Lpg, Lcbv], [Lsg], out=sg[:, :n], in_=pg[:, :n], func=AF.Silu, bias=cbv[:, l, chg:chg + 1])
                        V("dve", "scalar_tensor_tensor", [Lpv, Lsg, Lcbv], [Lact[ti]], out=actT[:, cc, t0:t0 + n], in0=pv[:, :n], scalar=cbv[:, l, chv:chv + 1], in1=sg[:, :n], op0=ALU.add, op1=ALU.mult)

                def emit_down(ti, t0, n):
                    for oc in range(8):
                        ps, Lp = psF.get()
                        for cc in range(ng):
                            V("pe", "matmul", [Lwd_, Lact[ti]], [Lp], ps[:, :n], lhsT=wd_[:, cc, oc * 128:(oc + 1) * 128], rhs=actT[:, cc, t0:t0 + n], start=(cc == 0), stop=(cc == ng - 1))
                        V("dve", "tensor_tensor", [Lp, Lx[ti]], [Lx[ti]], out=xT[:, oc, t0:t0 + n], in0=xT[:, oc, t0:t0 + n], in1=ps[:, :n], op=ALU.add)

                for ti, (t0, n) in enumerate(tiles):
                    ui = 0
                    for cc in range(ng):
                        for a in range(2):
                            buf = emit_up(ti, t0, n, cc, a)
                            if pend_conv:
                                emit_conv(*pend_conv.pop(0))
                            pend_conv.append((ti, t0, n, cc, a, buf))
                            if extra:
                                for _ in range(min(EXTRA_K[0], len(extra))):
                                    emit_captured(extra.pop(0))
                            if ui == 2 and pend_down:
                                emit_down(*pend_down.pop(0))
                            ui += 1
                    pend_down.append((ti, t0, n))
                while pend_conv:
                    emit_conv(*pend_conv.pop(0))
                while pend_down:
                    emit_down(*pend_down.pop(0))
            while extra:
                emit_captured(extra.pop(0))
            if hi == 1:
                DMA("sp", convp_out[:, l], convp_sb, [Lcp], [])
                DMA("sp", convs_out[:, l], convs_sb, [Lcs], [])
                S.final_wait("sp", [Lcp, Lcs])

        out_L = []
        for hi in range(2):
            tiles = HT[hi]
            goff = GOFF[hi]
            nh = sum(n for _, n in tiles)
            S.barrier()
            DMA("sp", xT[:, :, 0:nh], xT_in[:, :, goff:goff + nh], [], [Lx[i] for i in range(len(tiles))])
            yaT = arena_b[:, 0:4 * NH].rearrange("p (c n) -> p c n", c=4)
            wo = arena_b[:, 5120:5120 + 8192].rearrange("p (c n) -> p c n", c=8)
            oT = arena_b[:, 5120 + 8192:5120 + 8192 + 9216].rearrange("p (c n) -> p c n", c=8)
            for l in range(nlayers):
                Lya = [LT() for _ in range(3)]; Lwo = LT(); LoT = [LT() for _ in range(3)]
                norm_to_h(l, "gmix", tiles)
                if "a" not in skip:
                    if hi == 0 and l == 0:
                        S.barrier()
                        s5_setup(0)
                    s5_phase(l, hi, tiles, yaT, Lya)
                if "b" not in skip:
                    phase_B(l, tiles, yaT, Lya, wo, Lwo)
                if "c" not in skip:
                    phase_C(l, hi, tiles, oT, LoT)
                if "d" not in skip:
                    phase_D(l, tiles, oT, LoT, wo, Lwo)
                if "F" not in skip:
                    norm_to_h(l, "gffn", tiles)
                    extra = None
                    if hi == 0 and l + 1 < nlayers and "a" not in skip:
                        CAP[0] = []
                        s5_setup(l + 1)
                        extra = CAP[0]; CAP[0] = None
                        EXTRA_K[0] = len(extra) // 90 + 1
                    ffn(l, hi, tiles, extra)
            S.barrier()
            gt, Lg = small["gfin"]
            yo = arena_f[:, 0:4096].rearrange("p (c n) -> p c n", c=8); Lyo = LT()
            for ti, (t0, n) in enumerate(tiles):
                r, Lr = rms_rstd(ti, t0, n, 1.0 / D)
                for c in range(8):
                    V("dve", "scalar_tensor_tensor", [Lx[ti], Lg, Lr], [Lyo], out=yo[:, c, :n], in0=xT[:, c, t0:t0 + n], scalar=gt[:, c:c + 1], in1=r[:, :n], op0=ALU.mult, op1=ALU.mult)
                DMA("sp", yT_out[:, :, goff + t0:goff + t0 + n], yo[:, :, :n], [Lyo], [])
            out_L.append(Lyo)
            S.final_wait("sp", [Lyo])
        S.barrier()
        S.emit(block)
    return nc


def _mk_consts():
    cf = {}
    cf["ident"] = np.eye(128, dtype=np.float32)
    jc = np.arange(128) // 16
    tc_t = np.arange(128) // 16
    cf["tmask"] = (tc_t[None, :] >= jc[:, None]).astype(np.float32)
    ps = np.zeros((128, 128), np.float32)
    for p in range(64):
        ps[64 + p, p] = -1.0
        ps[p, 64 + p] = 1.0
    cf["pswap"] = ps
    nv = np.array([7, 6, 5, 4, 3, 2, 1, 0, -1, -2, -3, -4, -5, -6, -7, -8, 1, 2, 3, 4, 5, 6, 7, 8], np.float32)
    cf["nvec"] = np.broadcast_to(nv, (128, 24)).copy()
    cf["kvec"] = np.broadcast_to(np.arange(256, dtype=np.float32), (128, 256)).copy()
    top = (np.arange(128) < 64)
    cf["sgn"] = np.where(top, -1.0, 1.0).astype(np.float32)[:, None]
    cf["mre"] = top.astype(np.float32)[:, None]
    cf["nmre"] = -top.astype(np.float32)[:, None]
    cf["nmim"] = -(~top).astype(np.float32)[:, None]
    cf["eps"] = np.full((128, 1), EPS, np.float32)
    cf["halfpi"] = np.full((128, 1), math.pi / 2, np.float32)
    cf["zero"] = np.zeros((128, 1), np.float32)
    i = np.arange(128, dtype=np.float64)
    qs = np.zeros((128, 8)); ks = np.zeros((128, 8))
    for h in range(4):
        g = 1.0 - 2.0 ** (-5 - h)
        qs[:, h] = g ** (i + 1); ks[:, h] = (128 ** -0.5) * g ** (-(i + 1))
        qs[:, 4 + h] = g ** ((i % 8) + 1); ks[:, 4 + h] = (128 ** -0.5) * g ** (-((i % 8) + 1))
    cf["qsc"] = qs.astype(np.float32); cf["ksc"] = ks.astype(np.float32)
    qk_ = np.zeros((128, 2, 4, 2)); gc_ = np.zeros((128, 2, 4))
    for kd in range(2):
        for h in range(4):
            qk_[:, kd, h, 0] = qs[:, kd * 4 + h]; qk_[:, kd, h, 1] = ks[:, kd * 4 + h]
            gc_[:, kd, h] = (1.0 - 2.0 ** (-5 - h)) ** (128 if kd == 0 else 8)
    cf["qksc"] = qk_.reshape(128, 16).astype(np.float32); cf["gct"] = gc_.reshape(128, 8).astype(np.float32)
    rm = np.zeros((128, 16), np.float32)
    for s in range(16):
        rm[s * 8:(s + 1) * 8, s] = 1.0
    cf["rowmask"] = rm
    j = np.arange(128)
    cf["cmask_p"] = (j[:, None] <= j[None, :]).astype(np.float32)
    cf["cmask_s"] = ((j[:, None] <= j[None, :]) & ((j[:, None] // 8) == (j[None, :] // 8))).astype(np.float32)
    off = {}; o = 0; parts = []
    for k, v in cf.items():
        off[k] = (o, v.shape[1]); o += v.shape[1]; parts.append(v)
    cfa = np.ascontiguousarray(np.concatenate(parts, axis=1))
    cb = {"ident": np.eye(128, dtype=np.float32), "ones": np.ones((128, 128), np.float32)}
    offb = {}; o = 0; partsb = []
    for k, v in cb.items():
        offb[k] = (o, v.shape[1]); o += v.shape[1]; partsb.append(v)
    cba = np.ascontiguousarray(np.concatenate(partsb, axis=1)).astype(ml_dtypes.bfloat16)
    half = 64
    inv = (10000.0 ** (-np.arange(half, dtype=np.float32) / half)).astype(np.float32)
    rope = np.zeros((128, 17, 2, 64), np.float32)
    for b in range(17):
        if b < 16:
            pos = (b * 128 + np.arange(128)).astype(np.float32)
        else:
            pos = (PAST + (np.arange(128) % 8)).astype(np.float32)
        ang = (pos[:, None] * inv[None, :]).astype(np.float32)
        rope[:, b, 0, :] = np.cos(ang); rope[:, b, 1, :] = np.sin(ang)
    return cfa, off, cba, offb, rope


CF_ARR, CF_OFF, CB_ARR, CB_OFF, ROPE_ARR = _mk_consts()
CF_N = CF_ARR.shape[1]
CB_N = CB_ARR.shape[1]

_NC_CACHE = {}


def _stack(a, b):
    return np.ascontiguousarray(np.concatenate([a, b], axis=0))


def make_in_maps(inp):
    f = lambda a: np.ascontiguousarray(np.asarray(a, dtype=np.float32))
    shared = {}
    for k, src in [("w_in", "w_in"), ("w_glu", "w_glu"), ("w_ssm_out", "w_ssm_out"), ("w_ret_out", "w_ret_out"), ("w_o", "w_o"), ("w_up", "w_up"), ("w_down", "w_down")]:
        shared[k] = f(inp[src])
    pl = lambda v: np.ascontiguousarray(f(v).reshape(DEPTH, -1, 128).transpose(2, 0, 1))
    shared["gmix"] = pl(inp["norm_mix"]); shared["gffn"] = pl(inp["norm_ffn"])
    shared["gfin"] = np.ascontiguousarray(f(inp["norm_final"]).reshape(8, 128).T)
    shared["convw"] = np.ascontiguousarray(f(inp["conv_w"]).reshape(DEPTH, 3, NCH, 128).transpose(3, 0, 1, 2))
    shared["convb"] = np.ascontiguousarray(f(inp["conv_b"]).reshape(DEPTH, NCH, 128).transpose(2, 0, 1))
    lr = f(inp["ssm_lam_re"]).transpose(2, 0, 1)
    li = f(inp["ssm_lam_im"]).transpose(2, 0, 1)
    shared["lamr"] = _stack(lr, lr); shared["lami"] = _stack(li, li)
    shared["logdt"] = np.ascontiguousarray(np.broadcast_to(f(inp["ssm_log_dt"])[None], (128, DEPTH, G)))
    br = f(inp["ssm_b_re"]).transpose(2, 0, 1, 3)
    bi = f(inp["ssm_b_im"]).transpose(2, 0, 1, 3)
    shared["bS"] = _stack(br, bi); shared["bX"] = _stack(bi, br)
    cr = f(inp["ssm_c_re"]).transpose(3, 0, 1, 2)
    ci = f(inp["ssm_c_im"]).transpose(3, 0, 1, 2)
    shared["crD"] = _stack(cr, cr); shared["ciD"] = _stack(ci, ci)
    d = f(inp["ssm_d"]).reshape(DEPTH, G, 16)
    dc = d.transpose(2, 0, 1)
    shared["dcol"] = np.ascontiguousarray(np.tile(dc, (8, 1, 1)))
    shared["cf32"] = CF_ARR; shared["cbf16"] = CB_ARR; shared["rope"] = ROPE_ARR
    xp = f(inp["x_prompt"]); xs = f(inp["x_sample"])
    sre = f(inp["state_ssm_re"]); sim = f(inp["state_ssm_im"]); sret = f(inp["state_ret"]); scv = f(inp["state_conv"])
    maps = []
    for ci_ in range(NCORES):
        m = dict(shared)
        S0 = ci_ * NS
        xt = np.concatenate([xp[ci_], xs[S0:S0 + NS].reshape(NS * DS, D)], axis=0)
        m["xT_in"] = np.ascontiguousarray(xt.T.reshape(8, 128, TOK).transpose(1, 0, 2))
        a = sre[:, S0:S0 + NS].transpose(3, 0, 2, 1)
        b = sim[:, S0:S0 + NS].transpose(3, 0, 2, 1)
        m["s0S"] = _stack(a, b); m["s0X"] = _stack(b, a)
        cv = scv[:, S0:S0 + NS].reshape(DEPTH, NS, 2, NCH, 128).transpose(4, 0, 3, 1, 2)
        m["conv0"] = np.ascontiguousarray(cv)
        m["sret"] = np.ascontiguousarray(sret[:, S0:S0 + NS])
        maps.append(m)
    return maps


def kernel(**inputs):
    if "nc" not in _NC_CACHE:
        _NC_CACHE["nc"] = build()
    nc = _NC_CACHE["nc"]
    maps = make_in_maps(inputs)
    res = run_bass_kernel_spmd(nc, maps, core_ids=list(range(NCORES)))
    R = res.results
    if "dbg_out" in R[0]:
        _NC_CACHE["dbg"] = [np.asarray(r["dbg_out"]) for r in R]
    B = NCORES
    y_p = np.zeros((B, SEQ, D), np.float32); y_s = np.zeros((B * NS, DS, D), np.float32)
    sre_p = np.zeros((DEPTH, B, G, P), np.float32); sim_p = np.zeros_like(sre_p)
    ret_p = np.zeros((DEPTH, B, 4, 128, 256), np.float32)
    cv_p = np.zeros((DEPTH, B, 2, 2 * DFF), np.float32)
    sre_s = np.zeros((DEPTH, B * NS, G, P), np.float32); sim_s = np.zeros_like(sre_s)
    ret_s = np.zeros((DEPTH, B * NS, 4, 128, 256), np.float32)
    cv_s = np.zeros((DEPTH, B * NS, 2, 2 * DFF), np.float32)
    for c in range(B):
        r = R[c]
        yt = np.asarray(r["yT_out"]).transpose(1, 0, 2).reshape(D, TOK).T
        y_p[c] = yt[:SEQ]; y_s[c * NS:(c + 1) * NS] = yt[SEQ:].reshape(NS, DS, D)
        sp = np.asarray(r["ssmp_out"])
        sre_p[:, c] = sp[:64].transpose(1, 2, 0); sim_p[:, c] = sp[64:].transpose(1, 2, 0)
        ss = np.asarray(r["ssms_out"])
        sre_s[:, c * NS:(c + 1) * NS] = ss[:64].transpose(1, 3, 2, 0); sim_s[:, c * NS:(c + 1) * NS] = ss[64:].transpose(1, 3, 2, 0)
        ret_p[:, c] = np.asarray(r["retp_out"]); ret_s[:, c * NS:(c + 1) * NS] = np.asarray(r["rets_out"])
        cp = np.asarray(r["convp_out"])
        cv_p[:, c] = cp.transpose(1, 3, 2, 0).reshape(DEPTH, 2, 2 * DFF)
        cs = np.asarray(r["convs_out"])
        cv_s[:, c * NS:(c + 1) * NS] = cs.transpose(1, 3, 4, 2, 0).reshape(DEPTH, NS, 2, 2 * DFF)
    return (y_p, y_s, sre_p, sim_p, ret_p, cv_p, sre_s, sim_s, ret_s, cv_s)
```

```python
import math
from contextlib import ExitStack
import numpy as np
import ml_dtypes
import concourse.bass as bass
import concourse.mybir as mybir
from concourse.bass_utils import run_bass_kernel_spmd

F32 = mybir.dt.float32
BF16 = mybir.dt.bfloat16
ALU = mybir.AluOpType
AF = mybir.ActivationFunctionType

NCORES = 8
D = 1024
DEPTH = 4
SEQ = 2048
NS = 16
DS = 8
TOK = SEQ + NS * DS
G = 32
P = 64
DFF = 2816
NCH = 44
EPS = 1e-6
PAST = 16384
MAGIC = 12582912.0
TWO_PI = 2.0 * math.pi
LAYERS = DEPTH


class LT:
    __slots__ = ("w", "r", "key")

    def __init__(self):
        self.w = {}
        self.r = {}
        self.key = None


class Sched:
    ENGS = ("pe", "act", "dve", "pool", "sp")

    def __init__(self, nc, stack, n_dma):
        self.nc = nc
        self.sem = {}
        self.cnt = {}
        for e in self.ENGS:
            self.sem[e] = stack.enter_context(nc.semaphore("s_" + e))
            self.cnt[e] = 0
        self.n_dma = n_dma
        for i in range(n_dma):
            k = "d%d" % i
            self.sem[k] = stack.enter_context(nc.semaphore("s_" + k))
            self.cnt[k] = 0
        self.seen = {}
        self.prog = {e: [] for e in self.ENGS}
        self.rr = 0

    def _deps(self, eng, reads, writes):
        deps = {}

        def add(d, skip_same):
            for k, v in d.items():
                if skip_same and k == eng:
                    continue
                if deps.get(k, 0) < v:
                    deps[k] = v
        for t in reads:
            add(t.w, eng == "pe")
        for t in writes:
            add(t.w, True)
            add(t.r, True)
        waits = []
        for k, v in deps.items():
            if self.seen.get((eng, k), 0) >= v:
                continue
            self.seen[(eng, k)] = v
            waits.append((k, v))
        return waits

    def _mark(self, me, reads, writes):
        k, v = me
        for t in reads:
            if t.r.get(k, 0) < v:
                t.r[k] = v
        for t in writes:
            if t.w.get(k, 0) < v:
                t.w[k] = v

    def op(self, eng, fn, reads=(), writes=()):
        waits = self._deps(eng, reads, writes)
        self.cnt[eng] += 1
        self._mark((eng, self.cnt[eng]), reads, writes)
        self.prog[eng].append((waits, fn, (eng, 1)))

    def dma(self, q, fn, reads=(), writes=(), key=None):
        if key is None:
            lt = writes[0] if len(writes) else reads[0]
            if lt.key is None:
                lt.key = "d%d" % self.rr
                self.rr = (self.rr + 1) % self.n_dma
            key = lt.key
        waits = self._deps(q, reads, writes)
        self.cnt[key] += 16
        self._mark((key, self.cnt[key]), reads, writes)
        self.prog[q].append((waits, fn, (key, 16)))

    def barrier(self):
        for eng in self.ENGS:
            waits = []
            for k, v in self.cnt.items():
                if k == eng or v == 0:
                    continue
                if self.seen.get((eng, k), 0) >= v:
                    continue
                self.seen[(eng, k)] = v
                waits.append((k, v))
            if waits:
                self.prog[eng].append((waits, None, None))

    def final_wait(self, eng, tiles):
        waits = self._deps(eng, tiles, tiles)
        self.prog[eng].append((waits, None, None))

    def emit(self, block):
        def mk(ename):
            def body(e):
                for waits, fn, inc in self.prog[ename]:
                    for k, v in waits:
                        e.wait_ge(self.sem[k], v)
                    if fn is not None:
                        fn(e).then_inc(self.sem[inc[0]], inc[1])
            return body
        block.tensor(mk("pe"))
        block.scalar(mk("act"))
        block.vector(mk("dve"))
        block.gpsimd(mk("pool"))
        block.sync(mk("sp"))


class Carver:
    def __init__(self, ap2d):
        self.ap = ap2d
        self.off = 0
        self.n = ap2d.shape[1]

    def take(self, shape):
        n = 1
        for d in shape[1:]:
            n *= d
        assert self.off + n <= self.n, ("arena overflow", self.off, n, self.n)
        v = self.ap[:, self.off:self.off + n]
        self.off += n
        if len(shape) == 2:
            return v
        names = " ".join("d%d" % i for i in range(len(shape) - 1))
        kw = {"d%d" % i: shape[i + 1] for i in range(len(shape) - 1)}
        return v.rearrange("p (%s) -> p %s" % (names, names), **kw)


class Pool_:
    def __init__(self, items):
        self.items = items
        self.i = 0

    def get(self):
        it = self.items[self.i]
        self.i = (self.i + 1) % len(self.items)
        return it


def build(nlayers=LAYERS, skip=""):
    nc = bass.Bass("TRN2", target_bir_lowering=False)

    def IN(name, shape, dt=F32):
        return nc.dram_tensor(name, list(shape), dt, kind="ExternalInput").ap()

    def OUT(name, shape):
        return nc.dram_tensor(name, list(shape), F32, kind="ExternalOutput").ap()

    xT_in = IN("xT_in", [128, 8, TOK])
    w_in = IN("w_in", [DEPTH, D, 5632]); w_glu = IN("w_glu", [DEPTH, 512, 512]); w_sso = IN("w_ssm_out", [DEPTH, 512, D])
    w_ro = IN("w_ret_out", [DEPTH, D, D]); w_o = IN("w_o", [DEPTH, D, D]); w_up = IN("w_up", [DEPTH, D, 5632])
    w_dn = IN("w_down", [DEPTH, DFF, D])
    gmix = IN("gmix", [128, DEPTH, 8]); gffn = IN("gffn", [128, DEPTH, 8]); gfin = IN("gfin", [128, 8])
    convw = IN("convw", [128, DEPTH, 3, NCH]); convb = IN("convb", [128, DEPTH, NCH])
    lamr = IN("lamr", [128, DEPTH, G]); lami = IN("lami", [128, DEPTH, G]); logdt = IN("logdt", [128, DEPTH, G])
    bS_in = IN("bS", [128, DEPTH, G, 16]); bX_in = IN("bX", [128, DEPTH, G, 16])
    crD_in = IN("crD", [128, DEPTH, G, 16]); ciD_in = IN("ciD", [128, DEPTH, G, 16])
    dcol_in = IN("dcol", [128, DEPTH, G])
    s0S_in = IN("s0S", [128, DEPTH, G, NS]); s0X_in = IN("s0X", [128, DEPTH, G, NS])
    conv0_in = IN("conv0", [128, DEPTH, NCH, NS, 2])
    sret_in = IN("sret", [DEPTH, NS, 4, 128, 256])
    cf_in = IN("cf32", [128, CF_N]); cb_in = IN("cbf16", [128, CB_N], BF16)
    rope_in = IN("rope", [128, 17, 2, 64])

    yT_out = OUT("yT_out", [128, 8, TOK])
    ssmp_out = OUT("ssmp_out", [128, DEPTH, G]); ssms_out = OUT("ssms_out", [128, DEPTH, G, NS])
    retp_out = OUT("retp_out", [DEPTH, 4, 128, 256]); rets_out = OUT("rets_out", [DEPTH, NS, 4, 128, 256])
    convp_out = OUT("convp_out", [128, DEPTH, NCH, 2]); convs_out = OUT("convs_out", [128, DEPTH, NCH, NS, 2])
    dbg_out = OUT("dbg_out", [128, 1024]) if "G" in skip else None
    s5cb = nc.dram_tensor("s5cb", [DEPTH, 8, 128, 5 * 4 * 128], BF16, kind="Internal").ap()
    s5cf = nc.dram_tensor("s5cf", [DEPTH, 8, 128, 64 + 192], F32, kind="Internal").ap()
    rcar = nc.dram_tensor("rcar", [DEPTH, 4, 128, 256], F32, kind="Internal").ap()

    with ExitStack() as st:
        def sb(name, shape, dt=F32):
            return st.enter_context(nc.sbuf_tensor("sb_" + name, list(shape), dt))

        def pst(name, shape, dt=F32):
            return st.enter_context(nc.psum_tensor(name, list(shape), dt))

        S = Sched(nc, st, n_dma=24)
        block = st.enter_context(nc.Block())

        CAP = [None]

        def V(eng, method, reads, writes, *a, **kw):
            if CAP[0] is not None:
                CAP[0].append((eng, method, reads, writes, a, kw))
                return
            S.op(eng, lambda e: getattr(e, method)(*a, **kw), reads, writes)

        def DMA(q, out, in_, reads, writes, key=None):
            if CAP[0] is not None:
                CAP[0].append(("__dma__", q, out, in_, reads, writes))
                return
            S.dma(q, lambda e: e.dma_start(out=out, in_=in_), reads, writes, key)

        def emit_captured(item):
            if item[0] == "__dma__":
                _, q, out, in_, reads, writes = item
                DMA(q, out, in_, reads, writes)
            else:
                e_, m_, r_, w_, a_, kw_ = item
                V(e_, m_, r_, w_, *a_, **kw_)

        NH = 1152
        xT = sb("xT", [128, 8, NH]); Lx = [LT() for _ in range(3)]
        hT = sb("hT", [128, 8, NH], BF16); Lh = [LT() for _ in range(3)]
        HT = [[(0, 512), (512, 512)], [(0, 512), (512, 512), (1024, 128)]]
        GOFF = [0, 1024]

        cf = sb("cf", [128, CF_N]); Lcf = LT()
        cb = sb("cb", [128, CB_N], BF16); Lcb = LT()
        rope = sb("rope", [128, 17, 2, 64]); Lrope = LT()
        DMA("sp", cf[:], cf_in, [], [Lcf]); DMA("sp", cb[:], cb_in, [], [Lcb]); DMA("sp", rope[:], rope_in, [], [Lrope])

        def cfs(name):
            o, n = CF_OFF[name]
            return cf[:, o:o + n]

        def cbs(name):
            o, n = CB_OFF[name]
            return cb[:, o:o + n]
        ident_f = cfs("ident"); tmask = cfs("tmask"); pswap = cfs("pswap"); nvec = cfs("nvec"); kvec = cfs("kvec")
        sgn = cfs("sgn"); mre = cfs("mre"); nmre = cfs("nmre"); nmim = cfs("nmim"); epsc = cfs("eps")
        qsc = cfs("qsc"); ksc = cfs("ksc"); qksc = cfs("qksc"); gct = cfs("gct"); rowmask = cfs("rowmask"); cmask_p = cfs("cmask_p"); cmask_s = cfs("cmask_s")
        ident_b = cbs("ident"); ones_b = cbs("ones")

        small = {}
        for nm, src, shp in [("gmix", gmix, [128, DEPTH, 8]), ("gffn", gffn, [128, DEPTH, 8]), ("gfin", gfin, [128, 8]),
                             ("convw", convw, [128, DEPTH, 3, NCH]), ("convb", convb, [128, DEPTH, NCH]),
                             ("lamr", lamr, [128, DEPTH, G]), ("lami", lami, [128, DEPTH, G]), ("logdt", logdt, [128, DEPTH, G]),
                             ("dcol", dcol_in, [128, DEPTH, G])]:
            t = sb("sm_" + nm, shp); L = LT()
            DMA("sp", t[:], src, [], [L])
            small[nm] = (t, L)

        Wlast = sb("Wlast", [128, DEPTH, G]); Mclast = sb("Mclast", [128, DEPTH, G]); Mslast = sb("Mslast", [128, DEPTH, G]); Lcar = LT()
        convcar = sb("convcar", [128, DEPTH, NCH, 2]); Lccar = LT()
        ssm_p_sb = sb("ssm_p_sb", [128, DEPTH, G]); Lssp = LT()
        ssm_s_sb = sb("ssm_s_sb", [128, DEPTH, G, NS]); Lsss = LT()
        Lrcar = [LT() for _ in range(DEPTH)]

        psBig = pst("psbig", [128, 2048])
        psF = Pool_([(psBig[:, i * 512:(i + 1) * 512], LT()) for i in range(4)])
        psS = Pool_([(pst("pss%d" % i, [128, 512]), LT()) for i in range(2)])
        psB = Pool_([(pst("psb%d" % i, [128, 1024], BF16), LT()) for i in range(2)])
        t2k = Pool_([(sb("t2k%d" % i, [128, 512])[:], LT()) for i in range(5)])
        tb1k = Pool_([(sb("tb1k%d" % i, [128, 512], BF16)[:], LT()) for i in range(4)])
        rstd_p = Pool_([(sb("rstd%d" % i, [128, 512])[:], LT()) for i in range(2)])

        ARF_N = 7424
        ARB_N = 39700
        arena_f = sb("arena_f", [128, ARF_N]); arena_b = sb("arena_b", [128, ARB_N], BF16)

        def wload(dst, src, Ld):
            DMA("pool", dst, src, [], [Ld])

        def wview(w, l):
            return w[l].rearrange("(c p) n -> p c n", p=128)

        def rms_rstd(ti, t0, n, dscale):
            ps, Lp = psF.get()
            for c in range(8):
                sq, Lsq = tb1k.get()
                V("act", "activation", [Lx[ti]], [Lsq], out=sq[:, :n], in_=xT[:, c, t0:t0 + n], func=AF.Square)
                V("pe", "matmul", [Lsq, Lcb], [Lp], ps[:, :n], lhsT=ones_b, rhs=sq[:, :n], start=(c == 0), stop=(c == 7))
            r, Lr = rstd_p.get()
            V("act", "activation", [Lp, Lcf], [Lr], out=r[:, :n], in_=ps[:, :n], func=AF.Sqrt, bias=epsc[:, 0:1], scale=dscale)
            V("dve", "reciprocal", [Lr], [Lr], out=r[:, :n], in_=r[:, :n])
            return r, Lr

        def norm_to_h(l, which, tiles):
            gt, Lg = small[which]
            for ti, (t0, n) in enumerate(tiles):
                r, Lr = rms_rstd(ti, t0, n, 1.0 / D)
                for c in range(8):
                    eng = "dve"
                    V(eng, "scalar_tensor_tensor", [Lx[ti], Lg, Lr], [Lh[ti]], out=hT[:, c, t0:t0 + n], in0=xT[:, c, t0:t0 + n],
                      scalar=gt[:, l, c:c + 1], in1=r[:, :n], op0=ALU.mult, op1=ALU.mult)

        def proj_fm(ps, Lp, wt, Lw, col0, nk, act_fn, Lact, n):
            for kc in range(nk):
                V("pe", "matmul", [Lw] + Lact, [Lp], ps[:, :n], lhsT=wt[:, kc, col0:col0 + 128], rhs=act_fn(kc), start=(kc == 0), stop=(kc == nk - 1))

        def apply_wo(ti, t0, n, wo, Lwo, mix, Lmix):
            for oc in range(8):
                ps, Lp = psF.get()
                proj_fm(ps, Lp, wo, Lwo, oc * 128, 8, lambda kc: mix[:, kc, :n], [Lmix], n)
                V("dve", "tensor_tensor", [Lp, Lx[ti]], [Lx[ti]], out=xT[:, oc, t0:t0 + n], in0=xT[:, oc, t0:t0 + n], in1=ps[:, :n], op=ALU.add)

        def tiles_overlapping(tiles, a, b):
            return [ti for ti, (t0, n) in enumerate(tiles) if t0 < b and t0 + n > a]

        def range_reduce(dst, src, Lt, tmp, add_half_pi):
            if add_half_pi:
                V("dve", "tensor_scalar", [Lt], [Lt], out=dst, in0=src, scalar1=math.pi / 2, scalar2=None, op0=ALU.add)
                src = dst
            V("dve", "tensor_scalar", [Lt], [Lt], out=tmp, in0=src, scalar1=1.0 / TWO_PI, scalar2=MAGIC, op0=ALU.mult, op1=ALU.add)
            V("dve", "tensor_scalar", [Lt], [Lt], out=tmp, in0=tmp, scalar1=MAGIC, scalar2=TWO_PI, op0=ALU.subtract, op1=ALU.mult)
            V("dve", "tensor_tensor", [Lt], [Lt], out=dst, in0=src, in1=tmp, op=ALU.subtract)

        GG = 4
        NSUB = G // GG
        KP = 128
        S5C_L = {}

        def s5_setup(l):
            lamr_t, Ll1 = small["lamr"]; lami_t, Ll2 = small["lami"]; ldt_t, Ll3 = small["logdt"]; dcol_t, Ll4 = small["dcol"]
            Lsmall = [Ll1, Ll2, Ll3, Ll4, Lcf]
            cfA = Carver(arena_f[:, :]); cbA = Carver(arena_b[:, 36368:])
            s5f = [None] + [cfA.take([128, GG, 128]) for _ in range(5)]; Ls5f = [None] + [LT() for _ in range(5)]
            Pb = cbA.take([128, 3, GG, 128]); LPb = LT()
            Ab = cbA.take([128, 3, GG, 128]); LAb = LT()

            class SCtx:
                pass
            sctx = []
            for i in range(4):
                c = SCtx()
                c.s5sm = cfA.take([128, 16, GG]); c.Lsm = LT()
                c.Ep = cfA.take([128, 6, GG, 24]); c.LEp = LT()
                c.bsx = cfA.take([128, 3, GG, 16]); c.Lbsx = LT()
                c.bSt = cfA.take([128, 4, GG, 16]); c.LbSt = LT()
                c.tsm = cfA.take([128, GG, 32]); c.Ltsm = LT()
                sctx.append(c)

            def small_stage(c, sq):
                g0 = sq * GG
                gs = slice(g0, g0 + GG)
                s5sm, Lsm, Ep, LEp, bsx, Lbsx, bSt, LbSt = c.s5sm, c.Lsm, c.Ep, c.LEp, c.bsx, c.Lbsx, c.bSt, c.LbSt
                s5f = [c.tsm]; Ls5f = [c.Ltsm]
                sm = lambda i: s5sm[:, i, :]
                Er = Ep[:, 4]; Ei = Ep[:, 5]
                V("act", "activation", Lsmall, [Lsm], out=sm(0), in_=ldt_t[:, l, gs], func=AF.Exp)
                V("dve", "tensor_tensor", Lsmall + [Lsm], [Lsm], out=sm(1), in0=lamr_t[:, l, gs], in1=sm(0), op=ALU.mult)
                V("dve", "tensor_tensor", Lsmall + [Lsm], [Lsm], out=sm(2), in0=lami_t[:, l, gs], in1=sm(0), op=ALU.mult)
                nv = nvec.unsqueeze(1).to_broadcast([128, GG, 24])
                V("dve", "tensor_tensor", [Lsm, Lcf], [LEp], out=Ep[:, 0], in0=sm(2).unsqueeze(2).to_broadcast([128, GG, 24]), in1=nv, op=ALU.mult)
                V("dve", "tensor_tensor", [Lsm, Lcf], [LEp], out=Ep[:, 1], in0=sm(1).unsqueeze(2).to_broadcast([128, GG, 24]), in1=nv, op=ALU.mult)
                range_reduce(Ep[:, 2], Ep[:, 0], LEp, Ep[:, 3], False)
                V("act", "activation", [LEp], [LEp], out=Ep[:, 5], in_=Ep[:, 2], func=AF.Sin)
                range_reduce(Ep[:, 2], Ep[:, 0], LEp, Ep[:, 3], True)
                V("act", "activation", [LEp], [LEp], out=Ep[:, 4], in_=Ep[:, 2], func=AF.Sin)
                V("act", "activation", [LEp], [LEp], out=Ep[:, 1], in_=Ep[:, 1], func=AF.Exp)
                V("dve", "tensor_tensor", [LEp], [LEp], out=Ep[:, 4], in0=Ep[:, 4], in1=Ep[:, 1], op=ALU.mult)
                V("dve", "tensor_tensor", [LEp], [LEp], out=Ep[:, 5], in0=Ep[:, 5], in1=Ep[:, 1], op=ALU.mult)
                Er = Ep[:, 4]; Ei = Ep[:, 5]
                E1r = Er[:, :, 16]; E1i = Ei[:, :, 16]
                lr_ = lamr_t[:, l, gs]; li_ = lami_t[:, l, gs]
                RS = Lsmall + [Lsm, LEp]
                V("dve", "tensor_scalar", RS, [Lsm], out=sm(3), in0=E1r, scalar1=-1.0, scalar2=None, op0=ALU.add)
                V("dve", "tensor_tensor", RS, [Lsm], out=sm(4), in0=lr_, in1=lr_, op=ALU.mult)
                V("dve", "tensor_tensor", RS, [Lsm], out=sm(8), in0=li_, in1=li_, op=ALU.mult)
                V("dve", "tensor_tensor", RS, [Lsm], out=sm(4), in0=sm(4), in1=sm(8), op=ALU.add)
                V("dve", "reciprocal", RS, [Lsm], out=sm(5), in_=sm(4))
                V("dve", "tensor_tensor", RS, [Lsm], out=sm(6), in0=sm(3), in1=lr_, op=ALU.mult)
                V("dve", "tensor_tensor", RS, [Lsm], out=sm(8), in0=E1i, in1=li_, op=ALU.mult)
                V("dve", "tensor_tensor", RS, [Lsm], out=sm(6), in0=sm(6), in1=sm(8), op=ALU.add)
                V("dve", "tensor_tensor", RS, [Lsm], out=sm(6), in0=sm(6), in1=sm(5), op=ALU.mult)
                V("dve", "tensor_tensor", RS, [Lsm], out=sm(7), in0=E1i, in1=lr_, op=ALU.mult)
                V("dve", "tensor_tensor", RS, [Lsm], out=sm(8), in0=sm(3), in1=li_, op=ALU.mult)
                V("dve", "tensor_tensor", RS, [Lsm], out=sm(7), in0=sm(7), in1=sm(8), op=ALU.subtract)
                V("dve", "tensor_tensor", RS, [Lsm], out=sm(7), in0=sm(7), in1=sm(5), op=ALU.mult)
                V("act", "activation", RS, [Lsm], out=sm(10), in_=sm(1), func=AF.Exp, scale=8.0)
                V("dve", "tensor_scalar", RS, [Lsm], out=sm(11), in0=sm(2), scalar1=8.0, scalar2=None, op0=ALU.mult)
                bc16 = lambda a: a.unsqueeze(2).to_broadcast([128, GG, 16])
                RB = [LbSt, Lsm, Lcf]
                V("dve", "tensor_scalar", RB, [Lbsx], out=bsx[:, 0], in0=bSt[:, 1], scalar1=sgn[:, 0:1], scalar2=None, op0=ALU.mult)
                t0_ = s5f[0][:, :, 0:16]; t1_ = s5f[0][:, :, 16:32]
                V("dve", "tensor_tensor", RB, [Ls5f[0]], out=t0_, in0=bSt[:, 0], in1=bc16(sm(6)), op=ALU.mult)
                V("dve", "tensor_tensor", RB + [Lbsx], [Ls5f[0]], out=t1_, in0=bsx[:, 0], in1=bc16(sm(7)), op=ALU.mult)
                V("dve", "tensor_tensor", [Ls5f[0]], [Lbsx], out=bsx[:, 1], in0=t0_, in1=t1_, op=ALU.add)
                V("dve", "tensor_tensor", RB + [Lbsx], [Ls5f[0]], out=t0_, in0=bsx[:, 0], in1=bc16(sm(6)), op=ALU.mult)
                V("dve", "tensor_tensor", RB, [Ls5f[0]], out=t1_, in0=bSt[:, 0], in1=bc16(sm(7)), op=ALU.mult)
                V("dve", "tensor_tensor", [Ls5f[0]], [Lbsx], out=bsx[:, 2], in0=t0_, in1=t1_, op=ALU.subtract)

            def big_stage(c, sq):
                g0 = sq * GG
                gs = slice(g0, g0 + GG)
                s5sm, Lsm, Ep, LEp, bsx, Lbsx, bSt, LbSt = c.s5sm, c.Lsm, c.Ep, c.LEp, c.bsx, c.Lbsx, c.bSt, c.LbSt
                sm = lambda i: s5sm[:, i, :]
                Er = Ep[:, 4]; Ei = Ep[:, 5]
                Ls5c = LT()
                bc16 = lambda a: a.unsqueeze(2).to_broadcast([128, GG, 16])
                v4 = lambda a: a.rearrange("p g (j c) -> p g j c", c=16)
                bj = lambda a: a.unsqueeze(2).to_broadcast([128, GG, 8, 16])
                ej = lambda a: a.unsqueeze(3).to_broadcast([128, GG, 8, 16])
                bs_ = bsx[:, 1]; bx_ = bsx[:, 2]
                RP = [LEp, Lbsx]

                def cplx(dst_bf, e_r, e_i, sign_mode, Ldst):
                    a_, b_ = (e_r, e_i) if sign_mode == 0 else (e_i, e_r)
                    V("dve", "tensor_tensor", RP, [Ls5f[1]], out=v4(s5f[1]), in0=ej(a_), in1=bj(bs_), op=ALU.mult)
                    V("dve", "tensor_tensor", RP, [Ls5f[2]], out=v4(s5f[2]), in0=ej(b_), in1=bj(bx_), op=ALU.mult)
                    V("dve", "tensor_tensor", [Ls5f[1], Ls5f[2]], [Ldst], out=dst_bf, in0=s5f[1], in1=s5f[2],
                      op=(ALU.add if sign_mode == 0 else ALU.subtract))
                Pp, LPp = tb1k.get(); Ppt, LPpt = tb1k.get()
                g3 = lambda a: a.rearrange("p (g n) -> p g n", g=GG)
                cplx(g3(Pp), Er[:, :, 0:8], Ei[:, :, 0:8], 0, LPp)
                cplx(g3(Ppt), Er[:, :, 0:8], Ei[:, :, 0:8], 1, LPpt)
                cplx(Pb[:, 0], Er[:, :, 8:16], Ei[:, :, 8:16], 0, LPb)
                ECr = Er[:, :, 16:24]; ECi = Ei[:, :, 16:24]
                crD_ = bSt[:, 2]; ciD_ = bSt[:, 3]
                RC = [LEp, LbSt]
                V("dve", "tensor_tensor", RC, [Ls5f[1]], out=v4(s5f[1]), in0=ej(ECr), in1=bj(crD_), op=ALU.mult)
                V("dve", "tensor_tensor", RC, [Ls5f[2]], out=v4(s5f[2]), in0=ej(ECi), in1=bj(ciD_), op=ALU.mult)
                V("dve", "tensor_tensor", [Ls5f[1], Ls5f[2]], [Ls5f[3]], out=s5f[3], in0=s5f[1], in1=s5f[2], op=ALU.subtract)
                V("dve", "tensor_tensor", RC, [Ls5f[1]], out=v4(s5f[1]), in0=ej(ECi), in1=bj(crD_), op=ALU.mult)
                V("dve", "tensor_tensor", RC, [Ls5f[2]], out=v4(s5f[2]), in0=ej(ECr), in1=bj(ciD_), op=ALU.mult)
                V("dve", "tensor_tensor", [Ls5f[1], Ls5f[2]], [Ls5f[4]], out=s5f[4], in0=s5f[1], in1=s5f[2], op=ALU.add)
                V("dve", "tensor_scalar", [Ls5f[3], Lcf], [Ls5f[1]], out=s5f[1], in0=s5f[3], scalar1=mre[:, 0:1], scalar2=None, op0=ALU.mult)
                V("dve", "scalar_tensor_tensor", [Ls5f[4], Ls5f[1], Lcf], [LPb], out=Pb[:, 1], in0=s5f[4], scalar=nmim[:, 0:1], in1=s5f[1], op0=ALU.mult, op1=ALU.add)
                V("dve", "tensor_scalar", [Ls5f[4], Lcf], [Ls5f[2]], out=s5f[2], in0=s5f[4], scalar1=nmre[:, 0:1], scalar2=None, op0=ALU.mult)
                V("dve", "scalar_tensor_tensor", [Ls5f[3], Ls5f[2], Lcf], [LPb], out=Pb[:, 2], in0=s5f[3], scalar=nmim[:, 0:1], in1=s5f[2], op0=ALU.mult, op1=ALU.add)
                ps, Lp = psS.get()
                for g in range(GG):
                    V("pe", "matmul", [LPb], [Lp], ps[:, g * 128:(g + 1) * 128], lhsT=Pb[:, 0, g, :], rhs=Pb[:, 1, g, :], start=True, stop=True)
                V("dve", "tensor_tensor", [Lp, Lcf], [Ls5f[5]], out=s5f[5], in0=ps[:].rearrange("p (g n) -> p g n", g=GG),
                  in1=tmask.unsqueeze(1).to_broadcast([128, GG, 128]), op=ALU.mult)
                for g in range(GG):
                    V("dve", "scalar_tensor_tensor", [Ls5f[5], Lcf, Ll4], [LAb], out=Ab[:, 2, g, :], in0=ident_f, scalar=dcol_t[:, l, g0 + g:g0 + g + 1],
                      in1=s5f[5][:, g, :], op0=ALU.mult, op1=ALU.add)
                pb_, Lpb_ = psB.get()
                for i, (Px, LPx) in enumerate(((Pp, LPp), (Ppt, LPpt))):
                    for g in range(GG):
                        V("pe", "transpose", [LPx, Lcb], [Lpb_], out=pb_[:, (i * GG + g) * 128:(i * GG + g + 1) * 128], in_=Px[:, g * 128:(g + 1) * 128], identity=ident_b)
                V("act", "activation", [Lpb_], [LAb], out=Ab[:, 0:2].rearrange("p a g n -> p (a g n)"), in_=pb_[:, 0:2 * GG * 128], func=AF.Copy)
                DMA("sp", s5cb[l, sq, :, 0:2 * GG * 128], Pb[:, 1:3].rearrange("p a g n -> p (a g n)"), [LPb], [Ls5c])
                DMA("sp", s5cb[l, sq, :, 2 * GG * 128:5 * GG * 128], Ab.rearrange("p a g n -> p (a g n)"), [LAb], [Ls5c])
                DMA("sp", s5cf[l, sq, :, 0:64], s5sm.rearrange("p a g -> p (a g)"), [Lsm], [Ls5c])
                DMA("sp", s5cf[l, sq, :, 64:256], Ep[:, 4:6].rearrange("p a g n -> p (a g n)"), [LEp], [Ls5c])
                S5C_L[(l, sq)] = Ls5c

            for grp in range(NSUB // 4):
                subs = [grp * 4 + i for i in range(4)]
                lists = []
                for i, sq in enumerate(subs):
                    c = sctx[i]
                    for i2, src in enumerate([bS_in, bX_in, crD_in, ciD_in]):
                        DMA("sp", c.bSt[:, i2], src[:, l, sq * GG:(sq + 1) * GG, :], [], [c.LbSt])
                    prev_cap = CAP[0]
                    CAP[0] = []
                    small_stage(c, sq)
                    lists.append(CAP[0]); CAP[0] = prev_cap
                for k in range(max(len(x) for x in lists)):
                    for lst in lists:
                        if k < len(lst):
                            e_, m_, r_, w_, a_, kw_ = lst[k]
                            V(e_, m_, r_, w_, *a_, **kw_)
                for i, sq in enumerate(subs):
                    big_stage(sctx[i], sq)

        def s5_phase(l, hi, tiles, yaT, Lya):
            S.barrier()
            has_s = (hi == 1)
            KT = KP + (NS if has_s else 0)
            kbase = hi * KP
            PT = [(0, KP)] + ([(KP, NS)] if has_s else [])
            lamr_t, Ll1 = small["lamr"]; lami_t, Ll2 = small["lami"]; ldt_t, Ll3 = small["logdt"]; dcol_t, Ll4 = small["dcol"]
            Lsmall = [Ll1, Ll2, Ll3, Ll4, Lcf]
            cfA = Carver(arena_f[:, :]); cbA = Carver(arena_b[:, 5120:])

            class Ctx:
                pass
            ctxs = []
            for u in range(2):
                b = Ctx()
                b.s5sm = cfA.take([128, 16, GG]); b.Lsm = LT()
                b.Ep = cfA.take([128, 2, GG, 24]); b.LEp = LT()
                b.cosT = cfA.take([128, GG, KP]); b.sinT = cfA.take([128, GG, KP]); b.Ltab = LT()
                b.Yb = cfA.take([128, GG, KP]); b.LY = LT()
                b.Wb = cfA.take([128, GG, KP]); b.LW = LT()
                b.s0t = cfA.take([128, 2, GG, NS]); b.Ls0 = LT()
                b.sfin = cfA.take([128, 4, GG, NS]); b.Lsfin = LT()
                b.ctmp = cfA.take([128, GG, 2]); b.Lctmp = LT()
                b.PbC = cbA.take([128, 2, GG, 128]); b.LPb = LT()
                b.Ab = cbA.take([128, 3, GG, 128]); b.LAb = LT()
                b.Mc = cbA.take([128, GG, KP + NS]); b.Ms = cbA.take([128, GG, KP + NS]); b.LM = LT()
                b.yapt = cbA.take([128, 2, 8, GG * 16]); b.Lyapt = LT()
                ctxs.append(b)
            Uq_p = Pool_([(cbA.take([128, GG, KP + NS]), LT()) for i in range(4)])
            upt_p = Pool_([(cbA.take([128, 2, GG, 8, 16]), LT()) for i in range(4)])
            wu = Pool_([(cbA.take([128, 8, GG * 16]), LT()) for i in range(4)])

            def stage_u(sq):
                g0 = sq * GG
                wut, Lwu = wu.get()
                wload(wut, wview(w_in, l)[:, :, g0 * 16:g0 * 16 + GG * 16], Lwu)
                Uq, LU = Uq_p.get()
                upt, Lupt = upt_p.get()
                for m, (k0, nk) in enumerate(PT):
                    t0 = k0 * 8
                    tl = tiles_overlapping(tiles, t0, t0 + nk * 8)
                    ps, Lp = psF.get()
                    for j in range(8):
                        for c in range(8):
                            lh = hT[:, c, t0:t0 + nk * 8].rearrange("p (k j) -> p k j", j=8)[:, :, j]
                            V("pe", "matmul", [Lh[t] for t in tl] + [Lwu], [Lp], ps[:nk, j * 64:(j + 1) * 64], lhsT=lh, rhs=wut[:, c, :],
                              start=(c == 0), stop=(c == 7))
                    V("act", "activation", [Lp], [Lupt], out=upt[:nk, m].rearrange("p g j c -> p j g c"),
                      in_=ps[:nk, :].rearrange("p (j g c) -> p j g c", j=8, g=GG), func=AF.Copy)
                    pb_, Lpb_ = psB.get()
                    for g in range(GG):
                        V("pe", "transpose", [Lupt, Lcb], [Lpb_], out=pb_[:, g * 128:g * 128 + nk], in_=upt[:nk, m, g].rearrange("p j c -> p (j c)"),
                          identity=ident_b[:nk, :nk])
                    V("dve", "tensor_copy", [Lpb_], [LU], out=Uq[:, :, k0:k0 + nk], in_=pb_[:, 0:GG * 128].rearrange("p (g n) -> p g n", g=GG)[:, :, :nk])
                return Uq, LU


            f2 = lambda a: a.rearrange("p g k -> p (g k)")

            def stepL(b):
                sq = b.sq
                Lc_ = S5C_L[(l, sq)]
                DMA("sp", b.PbC.rearrange("p a g n -> p (a g n)"), s5cb[l, sq, :, 0:2 * GG * 128], [Lc_], [b.LPb])
                DMA("sp", b.Ab.rearrange("p a g n -> p (a g n)"), s5cb[l, sq, :, 2 * GG * 128:5 * GG * 128], [Lc_], [b.LAb])
                DMA("sp", b.s5sm.rearrange("p a g -> p (a g)"), s5cf[l, sq, :, 0:64], [Lc_], [b.Lsm])
                DMA("sp", b.Ep.rearrange("p a g n -> p (a g n)"), s5cf[l, sq, :, 64:256], [Lc_], [b.LEp])
                if has_s:
                    DMA("sp", b.s0t[:, 0], s0S_in[:, l, b.gs, :], [], [b.Ls0]); DMA("sp", b.s0t[:, 1], s0X_in[:, l, b.gs, :], [], [b.Ls0])

            def stepT(b):
                s5sm, Lsm, Yb, Wb, LY, sinT, cosT, Ltab = b.s5sm, b.Lsm, b.Yb, b.Wb, b.LY, b.sinT, b.cosT, b.Ltab
                kv = kvec[:, kbase:kbase + KP].unsqueeze(1).to_broadcast([128, GG, KP])
                V("dve", "tensor_tensor", [Lsm, Lcf], [LY], out=Yb, in0=s5sm[:, 11, :].unsqueeze(2).to_broadcast([128, GG, KP]), in1=kv, op=ALU.mult)
                range_reduce(Wb, Yb, LY, sinT, False)
                V("act", "activation", [LY], [Ltab], out=sinT, in_=Wb, func=AF.Sin)
                V("act", "activation", [LY], [LY], out=Yb, in_=Wb, func=AF.Abs)
                V("act", "activation", [LY, Lcf], [Ltab], out=cosT, in_=Yb, func=AF.Sin, scale=-1.0, bias=cfs("halfpi")[:, 0:1])

            def stepX(b):
                s5sm, Lsm, Yb, Wb, LY, LW, sinT, cosT, Ltab = b.s5sm, b.Lsm, b.Yb, b.Wb, b.LY, b.LW, b.sinT, b.cosT, b.Ltab
                Ab, LAb, Uq, LU = b.Ab, b.LAb, b.Uq, b.LU
                psx, Lpx = psF.get(); psxt, Lpxt = psF.get()
                for g in range(GG):
                    V("pe", "matmul", [LAb, LU], [Lpx], psx[:, g * KP:(g + 1) * KP], lhsT=Ab[:, 0, g, :], rhs=Uq[:, g, 0:KP], start=True, stop=True)
                    V("pe", "matmul", [LAb, LU], [Lpxt], psxt[:, g * KP:(g + 1) * KP], lhsT=Ab[:, 1, g, :], rhs=Uq[:, g, 0:KP], start=True, stop=True)
                V("dve", "tensor_tensor", [Lpx, Ltab], [LY], out=f2(Yb), in0=psx[:], in1=f2(cosT), op=ALU.mult)
                V("dve", "tensor_tensor", [Lpxt, Ltab], [LW], out=f2(Wb), in0=psxt[:], in1=f2(sinT), op=ALU.mult)
                V("dve", "tensor_tensor", [LY, LW], [LY], out=f2(Yb), in0=f2(Yb), in1=f2(Wb), op=ALU.add)
                if hi == 1:
                    V("dve", "tensor_tensor", [Lsm, Lcar], [b.Lctmp], out=b.ctmp[:, :, 0], in0=s5sm[:, 10, :], in1=Wlast[:, l, b.gs], op=ALU.mult)
                    V("dve", "tensor_tensor", [LY, b.Lctmp], [LY], out=Yb[:, :, 0], in0=Yb[:, :, 0], in1=b.ctmp[:, :, 0], op=ALU.add)

            def stepS(b):
                s5sm, Lsm, Yb, Wb, LY, LW, sinT, cosT, Ltab = b.s5sm, b.Lsm, b.Yb, b.Wb, b.LY, b.LW, b.sinT, b.cosT, b.Ltab
                Ab, LAb, Uq, LU, Mc, Ms, LM = b.Ab, b.LAb, b.Uq, b.LU, b.Mc, b.Ms, b.LM
                s0t, Ls0, sfin, Lsfin, LEp, gs = b.s0t, b.Ls0, b.sfin, b.Lsfin, b.LEp, b.gs
                Er = b.Ep[:, 0]; Ei = b.Ep[:, 1]
                for g in range(GG):
                    V("dve", "tensor_tensor_scan", [LY, Lsm, LW], [LW], out=Wb[:, g, :], data0=s5sm[:, 10, g:g + 1].to_broadcast([128, KP]), data1=Yb[:, g, :],
                      initial=0.0, op0=ALU.mult, op1=ALU.add)
                if hi == 0:
                    V("dve", "memset", [], [LM], Mc[:, :, 0:1], 0.0)
                    V("dve", "memset", [], [LM], Ms[:, :, 0:1], 0.0)
                else:
                    V("dve", "tensor_copy", [Lcar], [LM], out=Mc[:, :, 0], in_=Mclast[:, l, gs])
                    V("dve", "tensor_copy", [Lcar], [LM], out=Ms[:, :, 0], in_=Mslast[:, l, gs])
                V("dve", "tensor_tensor", [LW, Ltab], [LM], out=Mc[:, :, 1:KP], in0=Wb[:, :, 0:KP - 1], in1=cosT[:, :, 0:KP - 1], op=ALU.mult)
                V("dve", "tensor_tensor", [LW, Ltab], [LM], out=Ms[:, :, 1:KP], in0=Wb[:, :, 0:KP - 1], in1=sinT[:, :, 0:KP - 1], op=ALU.mult)
                if hi == 0:
                    V("dve", "tensor_copy", [LW], [Lcar], out=Wlast[:, l, gs], in_=Wb[:, :, KP - 1])
                    V("dve", "tensor_tensor", [LW, Ltab], [Lcar], out=Mclast[:, l, gs], in0=Wb[:, :, KP - 1], in1=cosT[:, :, KP - 1], op=ALU.mult)
                    V("dve", "tensor_tensor", [LW, Ltab], [Lcar], out=Mslast[:, l, gs], in0=Wb[:, :, KP - 1], in1=sinT[:, :, KP - 1], op=ALU.mult)
                else:
                    V("dve", "memset", [], [LM], Ms[:, :, KP:KT], 0.0)
                    V("act", "activation", [Ls0], [LM], out=Mc[:, :, KP:KT], in_=s0t[:, 0], func=AF.Copy)
                    V("dve", "tensor_tensor", [LW, Ltab], [Lsfin], out=sfin[:, 0, :, 0], in0=Wb[:, :, KP - 1], in1=cosT[:, :, KP - 1], op=ALU.mult)
                    V("dve", "tensor_tensor", [LW, Ltab], [Lsfin], out=sfin[:, 1, :, 0], in0=Wb[:, :, KP - 1], in1=sinT[:, :, KP - 1], op=ALU.mult)
                    ps, Lp = psF.get()
                    V("pe", "matmul", [Lsfin, Lcf], [Lp], ps[:, 0:GG], lhsT=pswap, rhs=sfin[:, 1, :, 0], start=True, stop=True)
                    V("dve", "tensor_tensor", [Lp, Lsfin], [Lssp], out=ssm_p_sb[:, l, gs], in0=ps[:, 0:GG], in1=sfin[:, 0, :, 0], op=ALU.add)
                    psx, Lpx = psF.get()
                    for g in range(GG):
                        V("pe", "matmul", [LAb, LU], [Lpx], psx[:, g * NS:(g + 1) * NS], lhsT=Ab[:, 0, g, :], rhs=Uq[:, g, KP:KT], start=True, stop=True)
                    Lr8 = Er[:, :, 23]; Li8 = Ei[:, :, 23]
                    bns = lambda a: a.unsqueeze(2).to_broadcast([128, GG, NS])
                    V("dve", "tensor_tensor", [Ls0, LEp], [Lsfin], out=sfin[:, 2], in0=s0t[:, 0], in1=bns(Lr8), op=ALU.mult)
                    V("dve", "scalar_tensor_tensor", [Ls0, LEp, Lcf], [Lsfin], out=sfin[:, 3], in0=s0t[:, 1], scalar=sgn[:, 0:1], in1=bns(Li8), op0=ALU.mult, op1=ALU.mult)
                    V("dve", "tensor_tensor", [Lsfin], [Lsfin], out=sfin[:, 2], in0=sfin[:, 2], in1=sfin[:, 3], op=ALU.add)
                    V("dve", "tensor_tensor", [Lsfin, Lpx], [Lsss], out=ssm_s_sb[:, l, gs, :], in0=sfin[:, 2], in1=psx[:, 0:GG * NS].rearrange("p (g s) -> p g s", g=GG), op=ALU.add)

            def stepY(b):
                Ab, LAb, Uq, LU, Mc, Ms, LM, PbC, LPb, yapt, Lyapt, g0 = b.Ab, b.LAb, b.Uq, b.LU, b.Mc, b.Ms, b.LM, b.PbC, b.LPb, b.yapt, b.Lyapt, b.g0
                for m, (k0, nk) in enumerate(PT):
                    ps, Lp = psF.get()
                    for g in range(GG):
                        o_ = ps[:nk, g * 128:(g + 1) * 128]
                        V("pe", "matmul", [LU, LAb], [Lp], o_, lhsT=Uq[:, g, k0:k0 + nk], rhs=Ab[:, 2, g, :], start=True, stop=False)
                        V("pe", "matmul", [LM, LPb], [Lp], o_, lhsT=Mc[:, g, k0:k0 + nk], rhs=PbC[:, 0, g, :], start=False, stop=False)
                        V("pe", "matmul", [LM, LPb], [Lp], o_, lhsT=Ms[:, g, k0:k0 + nk], rhs=PbC[:, 1, g, :], start=False, stop=True)
                    ta, La_ = t2k.get(); tb_, Lb_ = t2k.get()
                    V("act", "activation", [Lp], [La_], out=ta[:nk], in_=ps[:nk], func=AF.Square)
                    V("dve", "tensor_scalar", [La_], [La_], out=ta[:nk], in0=ta[:nk], scalar1=0.044715, scalar2=1.0, op0=ALU.mult, op1=ALU.add)
                    V("dve", "tensor_tensor", [La_, Lp], [La_], out=ta[:nk], in0=ta[:nk], in1=ps[:nk], op=ALU.mult)
                    V("act", "activation", [La_], [Lb_], out=tb_[:nk], in_=ta[:nk], func=AF.Sigmoid, scale=1.5957691216)
                    V("dve", "tensor_tensor", [Lb_, Lp], [Lyapt], out=yapt[:nk, m].rearrange("p t (g c) -> p g t c", g=GG),
                      in0=tb_[:nk].rearrange("p (g t c) -> p g t c", g=GG, t=8), in1=ps[:nk].rearrange("p (g t c) -> p g t c", g=GG, t=8), op=ALU.mult)
                    pb_, Lpb_ = psB.get()
                    for t in range(8):
                        V("pe", "transpose", [Lyapt, Lcb], [Lpb_], out=pb_[0:GG * 16, t * 128:t * 128 + nk], in_=yapt[:nk, m, t, :], identity=ident_b[:nk, :nk])
                    tok0 = k0 * 8
                    tl = tiles_overlapping(tiles, tok0, tok0 + nk * 8)
                    cq = (g0 * 16) // 128; p0 = (g0 * 16) % 128
                    V("dve", "tensor_copy", [Lpb_], [Lya[t] for t in tl], out=yaT[p0:p0 + GG * 16, cq, tok0:tok0 + nk * 8].rearrange("p (k j) -> p j k", j=8),
                      in_=pb_[0:GG * 16, :].rearrange("p (t k) -> p t k", t=8)[:, :, :nk])

            pairs = [(2 * i, 2 * i + 1) for i in range(NSUB // 2)]
            Ubuf = {0: stage_u(0), 1: stage_u(1)}
            for pi, pr in enumerate(pairs):
                for u, sq in enumerate(pr):
                    b = ctxs[u]
                    b.sq = sq; b.g0 = sq * GG; b.gs = slice(sq * GG, sq * GG + GG)
                    b.Uq, b.LU = Ubuf.pop(sq)
                if pi + 1 < len(pairs):
                    for sq2 in pairs[pi + 1]:
                        Ubuf[sq2] = stage_u(sq2)
                for step in (stepL, stepT, stepX, stepS, stepY):
                    for u in range(2):
                        step(ctxs[u])
            if hi == 1:
                DMA("sp", ssmp_out[:, l, :], ssm_p_sb[:, l, :], [Lssp], [])
                DMA("sp", ssms_out[:, l], ssm_s_sb[:, l], [Lsss], [])

        def phase_B(l, tiles, yaT, Lya, wo, Lwo):
            S.barrier()
            cbA = Carver(arena_b[:, 5120 + 8192:])
            wgl = cbA.take([128, 4, 512]); Lwgl = LT()
            wso = cbA.take([128, 4, 1024]); Lwso = LT()
            wga = cbA.take([128, 8, 1024]); Lwga = LT()
            mix = cbA.take([128, 8, 512]); Lmix = LT()
            ya2 = cbA.take([128, 4, 512]); Ly2 = LT()
            wload(wgl, w_glu[l].rearrange("(c p) n -> p c n", p=128), Lwgl)
            wload(wso, w_sso[l].rearrange("(c p) n -> p c n", p=128), Lwso)
            GA0 = 3584
            wload(wga, wview(w_in, l)[:, :, GA0:GA0 + 1024], Lwga)
            wload(wo, wview(w_o, l), Lwo)
            for ti, (t0, n) in enumerate(tiles):
                for oc in range(4):
                    ps, Lp = psF.get()
                    proj_fm(ps, Lp, wgl, Lwgl, oc * 128, 4, lambda kc: yaT[:, kc, t0:t0 + n], [Lya[ti]], n)
                    sg, Lsg = tb1k.get()
                    V("act", "activation", [Lp], [Lsg], out=sg[:, :n], in_=ps[:, :n], func=AF.Sigmoid)
                    V("dve", "tensor_tensor", [Lsg, Lya[ti]], [Ly2], out=ya2[:, oc, :n], in0=sg[:, :n], in1=yaT[:, oc, t0:t0 + n], op=ALU.mult)
                for oc in range(8):
                    ps, Lp = psF.get(); ps2, Lp2 = psF.get()
                    proj_fm(ps, Lp, wso, Lwso, oc * 128, 4, lambda kc: ya2[:, kc, :n], [Ly2], n)
                    proj_fm(ps2, Lp2, wga, Lwga, oc * 128, 8, lambda kc: hT[:, kc, t0:t0 + n], [Lh[ti]], n)
                    sg, Lsg = t2k.get()
                    V("act", "activation", [Lp2], [Lsg], out=sg[:, :n], in_=ps2[:, :n], func=AF.Sigmoid)
                    V("dve", "tensor_tensor", [Lsg, Lp], [Lmix], out=mix[:, oc, :n], in0=sg[:, :n], in1=ps[:, :n], op=ALU.mult)
                if "B" not in skip:
                    apply_wo(ti, t0, n, wo, Lwo, mix, Lmix)

        def phase_C(l, hi, tiles, oT, LoT):
            S.barrier()
            cfA = Carver(arena_f[:, :])
            cbY = Carver(arena_b[:, 0:5120])
            cbW = Carver(arena_b[:, 5120:5120 + 8192])
            cbA = Carver(arena_b[:, 5120 + 8192 + 9216:])
            r_st = cfA.take([128, 4, 256]); Lr_st = LT()
            orw_p = Pool_([(cfA.take([128, 4, 2, 128]), LT()) for i in range(1)])
            qf_p = Pool_([(cfA.take([128, 4, 256]), LT()) for i in range(1)])
            r0_p = Pool_([(cfA.take([128, 4, 256]), LT()) for i in range(2)])
            rn_p = Pool_([(cfA.take([128, 4, 256]), LT()) for i in range(2)])
            wq4 = cbA.take([128, 8, 2048]); Lwq4 = [LT() for _ in range(4)]
            r_bf = cbY.take([128, 4, 256]); Lr_bf = LT()
            qk_p = Pool_([(cbY.take([128, 4, 2, 128]), LT()) for i in range(2)])
            qkT_p = Pool_([(cbY.take([128, 4, 2, 128]), LT()) for i in range(1)])
            sc_p = Pool_([(cbY.take([128, 4, 128]), LT()) for i in range(2)])
            v_p = Pool_([(cbW.take([128, 4, 256]), LT()) for i in range(2)])
            sq_p = Pool_([(cbW.take([128, 1024]), LT()) for i in range(2)])
            r0b_p = Pool_([(cbW.take([128, 4, 256]), LT()) for i in range(2)])
            km_p = Pool_([(cbW.take([128, 4, 128]), LT()) for i in range(2)])
            goff = GOFF[hi]
            wv_ = wview(w_in, l)
            for hh in range(4):
                wload(wq4[:, :, hh * 512:hh * 512 + 128], wv_[:, :, 512 + hh * 128:512 + (hh + 1) * 128], Lwq4[hh])
                wload(wq4[:, :, hh * 512 + 128:hh * 512 + 256], wv_[:, :, 1024 + hh * 128:1024 + (hh + 1) * 128], Lwq4[hh])
                wload(wq4[:, :, hh * 512 + 256:hh * 512 + 512], wv_[:, :, 1536 + hh * 256:1536 + (hh + 1) * 256], Lwq4[hh])
            if hi == 0:
                V("dve", "memset", [], [Lr_st], r_st, 0.0)
            else:
                DMA("sp", r_st, rcar[l].rearrange("h d v -> d h v"), [Lrcar[l]], [Lr_st])
            V("act", "activation", [Lr_st], [Lr_bf], out=r_bf, in_=r_st, func=AF.Copy)
            psQ = psBig[:, :]
            LQ = [it[1] for it in psF.items]
            blocks = []
            for ti, (t0, n) in enumerate(tiles):
                for b in range(n // 128):
                    blocks.append((ti, t0 + b * 128))
            for (ti, t0) in blocks:
                tg = goff + t0
                is_s = (tg >= SEQ)
                blk = 16 if is_s else tg // 128
                kind = 1 if is_s else 0
                for hh in range(4):
                    for c in range(8):
                        V("pe", "matmul", [Lh[ti], Lwq4[hh]], [LQ[hh]], psQ[:, hh * 512:(hh + 1) * 512], lhsT=hT[:, c, t0:t0 + 128], rhs=wq4[:, c, hh * 512:(hh + 1) * 512],
                          start=(c == 0), stop=(c == 7))
                qf, Lqf = qf_p.get()
                vt, Lv = v_p.get()
                for hh in range(4):
                    V("act", "activation", [LQ[hh]], [Lqf], out=qf[:, hh, :], in_=psQ[:, hh * 512:hh * 512 + 256], func=AF.Copy)
                    V("act", "activation", [LQ[hh]], [Lv], out=vt[:, hh, :], in_=psQ[:, hh * 512 + 256:hh * 512 + 512], func=AF.Copy)
                x1 = qf.rearrange("p h (a f d) -> p h a f d", a=2, f=2)[:, :, :, 0, :]
                x2 = qf.rearrange("p h (a f d) -> p h a f d", a=2, f=2)[:, :, :, 1, :]
                cs_ = rope[:, blk, 0, :].unsqueeze(1).unsqueeze(1).to_broadcast([128, 4, 2, 64])
                sn_ = rope[:, blk, 1, :].unsqueeze(1).unsqueeze(1).to_broadcast([128, 4, 2, 64])
                pr = [t2k.get() for _ in range(4)]
                v4 = lambda a: a.rearrange("p (h a d) -> p h a d", h=4, a=2)
                V("dve", "tensor_tensor", [Lqf, Lrope], [pr[0][1]], out=v4(pr[0][0]), in0=x1, in1=cs_, op=ALU.mult)
                V("dve", "tensor_tensor", [Lqf, Lrope], [pr[1][1]], out=v4(pr[1][0]), in0=x2, in1=sn_, op=ALU.mult)
                V("dve", "tensor_tensor", [Lqf, Lrope], [pr[2][1]], out=v4(pr[2][0]), in0=x1, in1=sn_, op=ALU.mult)
                V("dve", "tensor_tensor", [Lqf, Lrope], [pr[3][1]], out=v4(pr[3][0]), in0=x2, in1=cs_, op=ALU.mult)
                V("dve", "tensor_tensor", [pr[0][1], pr[1][1]], [pr[0][1]], out=pr[0][0], in0=pr[0][0], in1=pr[1][0], op=ALU.subtract)
                V("dve", "tensor_tensor", [pr[2][1], pr[3][1]], [pr[2][1]], out=pr[2][0], in0=pr[2][0], in1=pr[3][0], op=ALU.add)
                qk, Lqk = qk_p.get()
                sct = qksc.rearrange("p (k h a) -> p k h a", k=2, h=4)[:, kind].unsqueeze(3).to_broadcast([128, 4, 2, 64])
                V("dve", "tensor_tensor", [pr[0][1], Lcf], [Lqk], out=qk[:, :, :, 0:64], in0=v4(pr[0][0]), in1=sct, op=ALU.mult)
                V("dve", "tensor_tensor", [pr[2][1], Lcf], [Lqk], out=qk[:, :, :, 64:128], in0=v4(pr[2][0]), in1=sct, op=ALU.mult)
                pb_, Lpb_ = psB.get()
                for hh in range(4):
                    for a in range(2):
                        V("pe", "transpose", [Lqk, Lcb], [Lpb_], out=pb_[:, (hh * 2 + a) * 128:(hh * 2 + a + 1) * 128], in_=qk[:, hh, a, :], identity=ident_b)
                qkT, LqkT = qkT_p.get()
                V("dve", "tensor_copy", [Lpb_], [LqkT], out=qkT.rearrange("p h a n -> p (h a n)"), in_=pb_[:, 0:1024])
                ps2, Lp2 = psF.get()
                for hh in range(4):
                    V("pe", "matmul", [LqkT], [Lp2], ps2[:, hh * 128:(hh + 1) * 128], lhsT=qkT[:, hh, 1, :], rhs=qkT[:, hh, 0, :], start=True, stop=True)
                sc, Lsc = sc_p.get()
                mk = (cmask_s if is_s else cmask_p).unsqueeze(1).to_broadcast([128, 4, 128])
                V("dve", "tensor_tensor", [Lp2, Lcf], [Lsc], out=sc, in0=ps2[:, :].rearrange("p (h n) -> p h n", h=4), in1=mk, op=ALU.mult)
                po = [psF.get(), psF.get()]
                orw, Lor = orw_p.get()
                for hh in range(4):
                    pso, Lpo = po[hh // 2]
                    for e_ in range(2):
                        o_ = pso[:, ((hh % 2) * 2 + e_) * 128:((hh % 2) * 2 + e_ + 1) * 128]
                        V("pe", "matmul", [Lv, Lsc], [Lpo], o_, lhsT=vt[:, hh, e_ * 128:(e_ + 1) * 128], rhs=sc[:, hh, :], start=True, stop=is_s)
                        if not is_s:
                            V("pe", "matmul", [Lr_bf, LqkT], [Lpo], o_, lhsT=r_bf[:, hh, e_ * 128:(e_ + 1) * 128], rhs=qkT[:, hh, 0, :], start=False, stop=True)
                orf = orw.rearrange("p h e n -> p (h e n)")
                for i2 in range(2):
                    V("act", "activation", [po[i2][1]], [Lor], out=orf[:, i2 * 512:(i2 + 1) * 512], in_=po[i2][0][:, :], func=AF.Copy)
                if not is_s:
                    pd = [psS.get(), psS.get()]
                    for hh in range(4):
                        psd, Lpd = pd[hh // 2]
                        V("pe", "matmul", [Lqk, Lv], [Lpd], psd[:, (hh % 2) * 256:(hh % 2 + 1) * 256], lhsT=qk[:, hh, 1, :], rhs=vt[:, hh, :], start=True, stop=True)
                    for i2 in range(2):
                        rv = r_st[:, i2 * 2:i2 * 2 + 2, :].rearrange("p h v -> p (h v)")
                        V("dve", "tensor_tensor", [pd[i2][1], Lr_st], [Lr_st], out=rv, in0=rv, in1=pd[i2][0][:, :], op=ALU.add)
                    gtab = gct.rearrange("p (k h) -> p k h", k=2)[:, 0].unsqueeze(2).to_broadcast([128, 4, 256])
                    V("dve", "tensor_tensor", [Lr_st, Lcf], [Lr_st], out=r_st, in0=r_st, in1=gtab, op=ALU.mult)
                    V("act", "activation", [Lr_st], [Lr_bf], out=r_bf, in_=r_st, func=AF.Copy)
                    if tg == 1024 - 128:
                        DMA("sp", rcar[l].rearrange("h d v -> d h v"), r_st, [Lr_st], [Lrcar[l]])
                    if tg == SEQ - 128:
                        DMA("sp", retp_out[l].rearrange("h d v -> d h v"), r_st, [Lr_st], [])
                else:
                    pin = [psF.get(), psF.get()]
                    gtab = gct.rearrange("p (k h) -> p k h", k=2)[:, 1].unsqueeze(2).to_broadcast([128, 4, 256])
                    def _ld_r0(sx):
                        r0x, Lr0x = r0_p.get()
                        DMA("sp", r0x, sret_in[l, sx].rearrange("h d v -> d h v"), [], [Lr0x])
                        return r0x, Lr0x
                    r0_next = _ld_r0(0)
                    for s_ in range(NS):
                        r0, Lr0 = r0_next
                        if s_ + 1 < NS:
                            r0_next = _ld_r0(s_ + 1)
                        r0b, Lr0b = r0b_p.get()
                        V("act", "activation", [Lr0], [Lr0b], out=r0b, in_=r0, func=AF.Copy)
                        for hh in range(4):
                            psi, Lpi = pin[hh // 2]
                            for e_ in range(2):
                                c0_ = ((hh % 2) * 2 + e_) * 128 + s_ * 8
                                V("pe", "matmul", [Lr0b, LqkT], [Lpi], psi[:, c0_:c0_ + 8], lhsT=r0b[:, hh, e_ * 128:(e_ + 1) * 128],
                                  rhs=qkT[:, hh, 0, s_ * 8:s_ * 8 + 8], start=True, stop=True)
                        km, Lkm = km_p.get()
                        V("dve", "tensor_scalar", [Lqk, Lcf], [Lkm], out=km, in0=qk[:, :, 1, :], scalar1=rowmask[:, s_:s_ + 1], scalar2=None, op0=ALU.mult)
                        pd = [psS.get(), psS.get()]
                        for hh in range(4):
                            psd, Lpd = pd[hh // 2]
                            V("pe", "matmul", [Lkm, Lv], [Lpd], psd[:, (hh % 2) * 256:(hh % 2 + 1) * 256], lhsT=km[:, hh, :], rhs=vt[:, hh, :], start=True, stop=True)
                        rn, Lrn = rn_p.get()
                        for i2 in range(2):
                            V("dve", "tensor_tensor", [pd[i2][1], Lr0], [Lrn], out=rn[:, i2 * 2:i2 * 2 + 2, :].rearrange("p h v -> p (h v)"),
                              in0=r0[:, i2 * 2:i2 * 2 + 2, :].rearrange("p h v -> p (h v)"), in1=pd[i2][0][:, :], op=ALU.add)
                        V("dve", "tensor_tensor", [Lrn, Lcf], [Lrn], out=rn, in0=rn, in1=gtab, op=ALU.mult)
                        DMA("pool", rets_out[l, s_].rearrange("h d v -> d h v"), rn, [Lrn], [])
                    for i2 in range(2):
                        tin, Ltin = t2k.get()
                        V("act", "activation", [pin[i2][1]], [Ltin], out=tin[:, :], in_=pin[i2][0][:, :], func=AF.Copy)
                        V("dve", "tensor_tensor", [Lor, Ltin], [Lor], out=orf[:, i2 * 512:(i2 + 1) * 512], in0=orf[:, i2 * 512:(i2 + 1) * 512], in1=tin[:, :], op=ALU.add)
                sq, Lsq = sq_p.get()
                V("dve", "tensor_tensor", [Lor], [Lsq], out=sq, in0=orf, in1=orf, op=ALU.mult)
                ps5, Lp5 = psF.get()
                for hh in range(4):
                    for e_ in range(2):
                        V("pe", "matmul", [Lsq, Lcb], [Lp5], ps5[:, hh * 128:(hh + 1) * 128], lhsT=ones_b, rhs=sq[:, (hh * 2 + e_) * 128:(hh * 2 + e_ + 1) * 128],
                          start=(e_ == 0), stop=(e_ == 1))
                rs, Lrs = t2k.get()
                V("act", "activation", [Lp5, Lcf], [Lrs], out=rs[:, :], in_=ps5[:, :], func=AF.Sqrt, bias=epsc[:, 0:1], scale=1.0 / 256)
                V("dve", "reciprocal", [Lrs], [Lrs], out=rs[:, :], in_=rs[:, :])
                V("dve", "tensor_tensor", [Lor, Lrs], [LoT[ti]], out=oT[:, :, t0:t0 + 128].rearrange("p (h e) n -> p h e n", h=4), in0=orw,
                  in1=rs[:, :].rearrange("p (h n) -> p h n", h=4).unsqueeze(2).to_broadcast([128, 4, 2, 128]), op=ALU.mult)

        def phase_D(l, tiles, oT, LoT, wo, Lwo):
            S.barrier()
            cbA = Carver(arena_b[:, 5120 + 8192 + 9216:])
            wX = cbA.take([128, 8, 1024]); LwX = LT()
            wY = cbA.take([128, 8, 1024]); LwY = LT()
            mix = arena_b[:, 0:4096].rearrange("p (c n) -> p c n", c=8); Lmix = LT()
            GR0 = 2560
            wload(wX, wview(w_in, l)[:, :, GR0:GR0 + 1024], LwX)
            wload(wY, wview(w_ro, l), LwY)
            wload(wo, wview(w_o, l), Lwo)
            for ti, (t0, n) in enumerate(tiles):
                for oc in range(8):
                    ps, Lp = psF.get()
                    proj_fm(ps, Lp, wX, LwX, oc * 128, 8, lambda kc: hT[:, kc, t0:t0 + n], [Lh[ti]], n)
                    sg, Lsg = tb1k.get()
                    V("act", "activation", [Lp], [Lsg], out=sg[:, :n], in_=ps[:, :n], func=AF.Silu)
                    o_ = oT[:, oc, t0:t0 + n]
                    V("dve", "tensor_tensor", [Lsg, LoT[ti]], [LoT[ti]], out=o_, in0=o_, in1=sg[:, :n], op=ALU.mult)
            GB0 = 4608
            wload(wX, wview(w_in, l)[:, :, GB0:GB0 + 1024], LwX)
            for ti, (t0, n) in enumerate(tiles):
                for oc in range(8):
                    ps, Lp = psF.get(); ps2, Lp2 = psF.get()
                    proj_fm(ps, Lp, wY, LwY, oc * 128, 8, lambda kc: oT[:, kc, t0:t0 + n], [LoT[ti]], n)
                    proj_fm(ps2, Lp2, wX, LwX, oc * 128, 8, lambda kc: hT[:, kc, t0:t0 + n], [Lh[ti]], n)
                    sg, Lsg = t2k.get()
                    V("act", "activation", [Lp2], [Lsg], out=sg[:, :n], in_=ps2[:, :n], func=AF.Sigmoid)
                    V("dve", "tensor_tensor", [Lsg, Lp], [Lmix], out=mix[:, oc, :n], in0=sg[:, :n], in1=ps[:, :n], op=ALU.mult)
                if "D" not in skip:
                    apply_wo(ti, t0, n, wo, Lwo, mix, Lmix)

        GF = 4
        EXTRA_K = [10]

        def ffn(l, hi, tiles, extra=None):
            S.barrier()
            cfA = Carver(arena_f[:, :]); cbA = Carver(arena_b[:, :])
            conv0 = cfA.take([128, NCH, NS, 2]); Lc0 = LT()
            convp_sb = cfA.take([128, NCH, 2]); Lcp = LT()
            convs_sb = cfA.take([128, NCH, NS, 2]); Lcs = LT()
            actT = cbA.take([128, GF, NH]); Lact = [LT() for _ in range(3)]
            wup_p = Pool_([(cbA.take([128, 8, 2 * GF * 128]), (LT(), LT())) for i in range(2)])
            wdn_p = Pool_([(cbA.take([128, GF, D]), LT()) for i in range(2)])
            upb = [[(cbA.take([128, 514]), LT()) for a in range(2)] for i in range(GF)]
            Dg = cbA.take([128, GF, 2, 3, 128]); LDg = LT()
            ups = Pool_([(cbA.take([128, NS, 10]), LT()) for i in range(2)]) if hi == 1 else None
            cw, Lcw = small["convw"]; cbv, Lcbv = small["convb"]
            if hi == 1:
                DMA("sp", conv0, conv0_in[:, l], [], [Lc0])
            ngroups = (22 + GF - 1) // GF
            for gi in range(ngroups):
                c0 = gi * GF
                ng = min(GF, 22 - c0)
                wu_, Lwu2 = wup_p.get(); wd_, Lwd_ = wdn_p.get()
                wload(wu_[:, :, 0:ng * 128], wview(w_up, l)[:, :, c0 * 128:(c0 + ng) * 128], Lwu2[0])
                wload(wu_[:, :, GF * 128:GF * 128 + ng * 128], wview(w_up, l)[:, :, DFF + c0 * 128:DFF + (c0 + ng) * 128], Lwu2[1])
                wload(wd_[:, 0:ng, :], w_dn[l, c0 * 128:(c0 + ng) * 128, :].rearrange("(c p) n -> p c n", p=128), Lwd_)
                for cc in range(ng):
                    for a in range(2):
                        ch = c0 + cc + a * 22
                        for j in range(3):
                            V("dve", "tensor_scalar", [Lcw, Lcb], [LDg], out=Dg[:, cc, a, j, :], in0=ident_b, scalar1=cw[:, l, j, ch:ch + 1], scalar2=None, op0=ALU.mult)
                pend_conv = []
                pend_down = []
                resmap = {}

                def emit_up(ti, t0, n, cc, a):
                    is_s = (n == 128)
                    ch = c0 + cc + a * 22
                    ps, Lp = psF.get()
                    proj_fm(ps, Lp, wu_, Lwu2[a], a * GF * 128 + cc * 128, 8, lambda kc: hT[:, kc, t0:t0 + n], [Lh[ti]], n)
                    if not is_s:
                        ub, Lub = upb[cc][a]
                        if ti == 0:
                            if hi == 0:
                                V("dve", "memset", [], [Lub], ub[:, 0:2], 0.0)
                            else:
                                V("dve", "tensor_copy", [Lccar], [Lub], out=ub[:, 0:2], in_=convcar[:, l, ch, :])
                        else:
                            V("dve", "tensor_copy", [Lub], [Lub], out=ub[:, 0:2], in_=ub[:, 512:514])
                        V("act", "activation", [Lp], [Lub], out=ub[:, 2:514], in_=ps[:, :], func=AF.Copy)
                        if ti == 1:
                            if hi == 0:
                                V("act", "activation", [Lp], [Lccar], out=convcar[:, l, ch, :], in_=ps[:, 510:512], func=AF.Copy)
                            else:
                                V("act", "activation", [Lp], [Lcp], out=convp_sb[:, ch, :], in_=ps[:, 510:512], func=AF.Copy)
                        return (ub, Lub)
                    else:
                        us, Lus = ups.get()
                        V("dve", "tensor_copy", [Lc0], [Lus], out=us[:, :, 0:2], in_=conv0[:, ch])
                        V("act", "activation", [Lp], [Lus], out=us[:, :, 2:10], in_=ps[:, 0:128].rearrange("p (s j) -> p s j", j=8), func=AF.Copy)
                        V("act", "activation", [Lp], [Lcs], out=convs_sb[:, ch], in_=ps[:, 0:128].rearrange("p (s j) -> p s j", j=8)[:, :, 6:8], func=AF.Copy)
                        return (us, Lus)

                def emit_conv(ti, t0, n, cc, a, buf):
                    is_s = (n == 128)
                    ch = c0 + cc + a * 22
                    ub, Lub = buf
                    ps2, Lp2 = psF.get()
                    for j in range(3):
                        if not is_s:
                            V("pe", "matmul", [LDg, Lub], [Lp2], ps2[:, :], lhsT=Dg[:, cc, a, j, :], rhs=ub[:, j:j + 512], start=(j == 0), stop=(j == 2))
                        else:
                            V("pe", "matmul", [LDg, Lub], [Lp2], ps2[:, 0:128], lhsT=Dg[:, cc, a, j, :], rhs=ub[:, :, j:j + 8], start=(j == 0), stop=(j == 2))
                    resmap[(ti, cc, a)] = (ps2, Lp2, ch)
                    if a == 1:
                        (pv, Lpv, chv) = resmap.pop((ti, cc, 0)); (pg, Lpg, chg) = resmap.pop((ti, cc, 1))
                        sg, Lsg = t2k.get()
                        V("act", "activation", [Lpg, Lcbv], [Lsg], out=sg[:, :n], in_=pg[:, :n], func=AF.Silu, bias=cbv[:, l, chg:chg + 1])
                        V("dve", "scalar_tensor_tensor", [Lpv, Lsg, Lcbv], [Lact[ti]], out=actT[:, cc, t0:t0 + n], in0=pv[:, :n], scalar=cbv[:, l, chv:chv + 1], in1=sg[:, :n], op0=ALU.add, op1=ALU.mult)

                def emit_down(ti, t0, n):
                    for oc in range(8):
                        ps, Lp = psF.get()
                        for cc in range(ng):
                            V("pe", "matmul", [Lwd_, Lact[ti]], [Lp], ps[:, :n], lhsT=wd_[:, cc, oc * 128:(oc + 1) * 128], rhs=actT[:, cc, t0:t0 + n], start=(cc == 0), stop=(cc == ng - 1))
                        V("dve", "tensor_tensor", [Lp, Lx[ti]], [Lx[ti]], out=xT[:, oc, t0:t0 + n], in0=xT[:, oc, t0:t0 + n], in1=ps[:, :n], op=ALU.add)

                for ti, (t0, n) in enumerate(tiles):
                    ui = 0
                    for cc in range(ng):
                        for a in range(2):
                            buf = emit_up(ti, t0, n, cc, a)
                            if pend_conv:
                                emit_conv(*pend_conv.pop(0))
                            pend_conv.append((ti, t0, n, cc, a, buf))
                            if extra:
                                for _ in range(min(EXTRA_K[0], len(extra))):
                                    emit_captured(extra.pop(0))
                            if ui == 2 and pend_down:
                                emit_down(*pend_down.pop(0))
                            ui += 1
                    pend_down.append((ti, t0, n))
                while pend_conv:
                    emit_conv(*pend_conv.pop(0))
                while pend_down:
                    emit_down(*pend_down.pop(0))
            while extra:
                emit_captured(extra.pop(0))
            if hi == 1:
                DMA("sp", convp_out[:, l], convp_sb, [Lcp], [])
                DMA("sp", convs_out[:, l], convs_sb, [Lcs], [])
                S.final_wait("sp", [Lcp, Lcs])

        out_L = []
        for hi in range(2):
            tiles = HT[hi]
            goff = GOFF[hi]
            nh = sum(n for _, n in tiles)
            S.barrier()
            DMA("sp", xT[:, :, 0:nh], xT_in[:, :, goff:goff + nh], [], [Lx[i] for i in range(len(tiles))])
            yaT = arena_b[:, 0:4 * NH].rearrange("p (c n) -> p c n", c=4)
            wo = arena_b[:, 5120:5120 + 8192].rearrange("p (c n) -> p c n", c=8)
            oT = arena_b[:, 5120 + 8192:5120 + 8192 + 9216].rearrange("p (c n) -> p c n", c=8)
            for l in range(nlayers):
                Lya = [LT() for _ in range(3)]; Lwo = LT(); LoT = [LT() for _ in range(3)]
                norm_to_h(l, "gmix", tiles)
                if "a" not in skip:
                    if hi == 0 and l == 0:
                        S.barrier()
                        s5_setup(0)
                    s5_phase(l, hi, tiles, yaT, Lya)
                if "b" not in skip:
                    phase_B(l, tiles, yaT, Lya, wo, Lwo)
                if "c" not in skip:
                    phase_C(l, hi, tiles, oT, LoT)
                if "d" not in skip:
                    phase_D(l, tiles, oT, LoT, wo, Lwo)
                if "F" not in skip:
                    norm_to_h(l, "gffn", tiles)
                    extra = None
                    if hi == 0 and l + 1 < nlayers and "a" not in skip:
                        CAP[0] = []
                        s5_setup(l + 1)
                        extra = CAP[0]; CAP[0] = None
                        EXTRA_K[0] = len(extra) // 90 + 1
                    ffn(l, hi, tiles, extra)
            S.barrier()
            gt, Lg = small["gfin"]
            yo = arena_f[:, 0:4096].rearrange("p (c n) -> p c n", c=8); Lyo = LT()
            for ti, (t0, n) in enumerate(tiles):
                r, Lr = rms_rstd(ti, t0, n, 1.0 / D)
                for c in range(8):
                    V("dve", "scalar_tensor_tensor", [Lx[ti], Lg, Lr], [Lyo], out=yo[:, c, :n], in0=xT[:, c, t0:t0 + n], scalar=gt[:, c:c + 1], in1=r[:, :n], op0=ALU.mult, op1=ALU.mult)
                DMA("sp", yT_out[:, :, goff + t0:goff + t0 + n], yo[:, :, :n], [Lyo], [])
            out_L.append(Lyo)
            S.final_wait("sp", [Lyo])
        S.barrier()
        S.emit(block)
    return nc


def _mk_consts():
    cf = {}
    cf["ident"] = np.eye(128, dtype=np.float32)
    jc = np.arange(128) // 16
    tc_t = np.arange(128) // 16
    cf["tmask"] = (tc_t[None, :] >= jc[:, None]).astype(np.float32)
    ps = np.zeros((128, 128), np.float32)
    for p in range(64):
        ps[64 + p, p] = -1.0
        ps[p, 64 + p] = 1.0
    cf["pswap"] = ps
    nv = np.array([7, 6, 5, 4, 3, 2, 1, 0, -1, -2, -3, -4, -5, -6, -7, -8, 1, 2, 3, 4, 5, 6, 7, 8], np.float32)
    cf["nvec"] = np.broadcast_to(nv, (128, 24)).copy()
    cf["kvec"] = np.broadcast_to(np.arange(256, dtype=np.float32), (128, 256)).copy()
    top = (np.arange(128) < 64)
    cf["sgn"] = np.where(top, -1.0, 1.0).astype(np.float32)[:, None]
    cf["mre"] = top.astype(np.float32)[:, None]
    cf["nmre"] = -top.astype(np.float32)[:, None]
    cf["nmim"] = -(~top).astype(np.float32)[:, None]
    cf["eps"] = np.full((128, 1), EPS, np.float32)
    cf["halfpi"] = np.full((128, 1), math.pi / 2, np.float32)
    cf["zero"] = np.zeros((128, 1), np.float32)
    i = np.arange(128, dtype=np.float64)
    qs = np.zeros((128, 8)); ks = np.zeros((128, 8))
    for h in range(4):
        g = 1.0 - 2.0 ** (-5 - h)
        qs[:, h] = g ** (i + 1); ks[:, h] = (128 ** -0.5) * g ** (-(i + 1))
        qs[:, 4 + h] = g ** ((i % 8) + 1); ks[:, 4 + h] = (128 ** -0.5) * g ** (-((i % 8) + 1))
    cf["qsc"] = qs.astype(np.float32); cf["ksc"] = ks.astype(np.float32)
    qk_ = np.zeros((128, 2, 4, 2)); gc_ = np.zeros((128, 2, 4))
    for kd in range(2):
        for h in range(4):
            qk_[:, kd, h, 0] = qs[:, kd * 4 + h]; qk_[:, kd, h, 1] = ks[:, kd * 4 + h]
            gc_[:, kd, h] = (1.0 - 2.0 ** (-5 - h)) ** (128 if kd == 0 else 8)
    cf["qksc"] = qk_.reshape(128, 16).astype(np.float32); cf["gct"] = gc_.reshape(128, 8).astype(np.float32)
    rm = np.zeros((128, 16), np.float32)
    for s in range(16):
        rm[s * 8:(s + 1) * 8, s] = 1.0
    cf["rowmask"] = rm
    j = np.arange(128)
    cf["cmask_p"] = (j[:, None] <= j[None, :]).astype(np.float32)
    cf["cmask_s"] = ((j[:, None] <= j[None, :]) & ((j[:, None] // 8) == (j[None, :] // 8))).astype(np.float32)
    off = {}; o = 0; parts = []
    for k, v in cf.items():
        off[k] = (o, v.shape[1]); o += v.shape[1]; parts.append(v)
    cfa = np.ascontiguousarray(np.concatenate(parts, axis=1))
    cb = {"ident": np.eye(128, dtype=np.float32), "ones": np.ones((128, 128), np.float32)}
    offb = {}; o = 0; partsb = []
    for k, v in cb.items():
        offb[k] = (o, v.shape[1]); o += v.shape[1]; partsb.append(v)
    cba = np.ascontiguousarray(np.concatenate(partsb, axis=1)).astype(ml_dtypes.bfloat16)
    half = 64
    inv = (10000.0 ** (-np.arange(half, dtype=np.float32) / half)).astype(np.float32)
    rope = np.zeros((128, 17, 2, 64), np.float32)
    for b in range(17):
        if b < 16:
            pos = (b * 128 + np.arange(128)).astype(np.float32)
        else:
            pos = (PAST + (np.arange(128) % 8)).astype(np.float32)
        ang = (pos[:, None] * inv[None, :]).astype(np.float32)
        rope[:, b, 0, :] = np.cos(ang); rope[:, b, 1, :] = np.sin(ang)
    return cfa, off, cba, offb, rope


CF_ARR, CF_OFF, CB_ARR, CB_OFF, ROPE_ARR = _mk_consts()
CF_N = CF_ARR.shape[1]
CB_N = CB_ARR.shape[1]

_NC_CACHE = {}


def _stack(a, b):
    return np.ascontiguousarray(np.concatenate([a, b], axis=0))


def make_in_maps(inp):
    f = lambda a: np.ascontiguousarray(np.asarray(a, dtype=np.float32))
    shared = {}
    for k, src in [("w_in", "w_in"), ("w_glu", "w_glu"), ("w_ssm_out", "w_ssm_out"), ("w_ret_out", "w_ret_out"), ("w_o", "w_o"), ("w_up", "w_up"), ("w_down", "w_down")]:
        shared[k] = f(inp[src])
    pl = lambda v: np.ascontiguousarray(f(v).reshape(DEPTH, -1, 128).transpose(2, 0, 1))
    shared["gmix"] = pl(inp["norm_mix"]); shared["gffn"] = pl(inp["norm_ffn"])
    shared["gfin"] = np.ascontiguousarray(f(inp["norm_final"]).reshape(8, 128).T)
    shared["convw"] = np.ascontiguousarray(f(inp["conv_w"]).reshape(DEPTH, 3, NCH, 128).transpose(3, 0, 1, 2))
    shared["convb"] = np.ascontiguousarray(f(inp["conv_b"]).reshape(DEPTH, NCH, 128).transpose(2, 0, 1))
    lr = f(inp["ssm_lam_re"]).transpose(2, 0, 1)
    li = f(inp["ssm_lam_im"]).transpose(2, 0, 1)
    shared["lamr"] = _stack(lr, lr); shared["lami"] = _stack(li, li)
    shared["logdt"] = np.ascontiguousarray(np.broadcast_to(f(inp["ssm_log_dt"])[None], (128, DEPTH, G)))
    br = f(inp["ssm_b_re"]).transpose(2, 0, 1, 3)
    bi = f(inp["ssm_b_im"]).transpose(2, 0, 1, 3)
    shared["bS"] = _stack(br, bi); shared["bX"] = _stack(bi, br)
    cr = f(inp["ssm_c_re"]).transpose(3, 0, 1, 2)
    ci = f(inp["ssm_c_im"]).transpose(3, 0, 1, 2)
    shared["crD"] = _stack(cr, cr); shared["ciD"] = _stack(ci, ci)
    d = f(inp["ssm_d"]).reshape(DEPTH, G, 16)
    dc = d.transpose(2, 0, 1)
    shared["dcol"] = np.ascontiguousarray(np.tile(dc, (8, 1, 1)))
    shared["cf32"] = CF_ARR; shared["cbf16"] = CB_ARR; shared["rope"] = ROPE_ARR
    xp = f(inp["x_prompt"]); xs = f(inp["x_sample"])
    sre = f(inp["state_ssm_re"]); sim = f(inp["state_ssm_im"]); sret = f(inp["state_ret"]); scv = f(inp["state_conv"])
    maps = []
    for ci_ in range(NCORES):
        m = dict(shared)
        S0 = ci_ * NS
        xt = np.concatenate([xp[ci_], xs[S0:S0 + NS].reshape(NS * DS, D)], axis=0)
        m["xT_in"] = np.ascontiguousarray(xt.T.reshape(8, 128, TOK).transpose(1, 0, 2))
        a = sre[:, S0:S0 + NS].transpose(3, 0, 2, 1)
        b = sim[:, S0:S0 + NS].transpose(3, 0, 2, 1)
        m["s0S"] = _stack(a, b); m["s0X"] = _stack(b, a)
        cv = scv[:, S0:S0 + NS].reshape(DEPTH, NS, 2, NCH, 128).transpose(4, 0, 3, 1, 2)
        m["conv0"] = np.ascontiguousarray(cv)
        m["sret"] = np.ascontiguousarray(sret[:, S0:S0 + NS])
        maps.append(m)
    return maps


def kernel(**inputs):
    if "nc" not in _NC_CACHE:
        _NC_CACHE["nc"] = build()
    nc = _NC_CACHE["nc"]
    maps = make_in_maps(inputs)
    res = run_bass_kernel_spmd(nc, maps, core_ids=list(range(NCORES)))
    R = res.results
    if "dbg_out" in R[0]:
        _NC_CACHE["dbg"] = [np.asarray(r["dbg_out"]) for r in R]
    B = NCORES
    y_p = np.zeros((B, SEQ, D), np.float32); y_s = np.zeros((B * NS, DS, D), np.float32)
    sre_p = np.zeros((DEPTH, B, G, P), np.float32); sim_p = np.zeros_like(sre_p)
    ret_p = np.zeros((DEPTH, B, 4, 128, 256), np.float32)
    cv_p = np.zeros((DEPTH, B, 2, 2 * DFF), np.float32)
    sre_s = np.zeros((DEPTH, B * NS, G, P), np.float32); sim_s = np.zeros_like(sre_s)
    ret_s = np.zeros((DEPTH, B * NS, 4, 128, 256), np.float32)
    cv_s = np.zeros((DEPTH, B * NS, 2, 2 * DFF), np.float32)
    for c in range(B):
        r = R[c]
        yt = np.asarray(r["yT_out"]).transpose(1, 0, 2).reshape(D, TOK).T
        y_p[c] = yt[:SEQ]; y_s[c * NS:(c + 1) * NS] = yt[SEQ:].reshape(NS, DS, D)
        sp = np.asarray(r["ssmp_out"])
        sre_p[:, c] = sp[:64].transpose(1, 2, 0); sim_p[:, c] = sp[64:].transpose(1, 2, 0)
        ss = np.asarray(r["ssms_out"])
        sre_s[:, c * NS:(c + 1) * NS] = ss[:64].transpose(1, 3, 2, 0); sim_s[:, c * NS:(c + 1) * NS] = ss[64:].transpose(1, 3, 2, 0)
        ret_p[:, c] = np.asarray(r["retp_out"]); ret_s[:, c * NS:(c + 1) * NS] = np.asarray(r["rets_out"])
        cp = np.asarray(r["convp_out"])
        cv_p[:, c] = cp.transpose(1, 3, 2, 0).reshape(DEPTH, 2, 2 * DFF)
        cs = np.asarray(r["convs_out"])
        cv_s[:, c * NS:(c + 1) * NS] = cs.transpose(1, 3, 4, 2, 0).reshape(DEPTH, NS, 2, 2 * DFF)
    return (y_p, y_s, sre_p, sim_p, ret_p, cv_p, sre_s, sim_s, ret_s, cv_s)
```

```python
import math
from contextlib import ExitStack
import numpy as np
import ml_dtypes
import concourse.bass as bass
import concourse.mybir as mybir
from concourse.bass_utils import run_bass_kernel_spmd

F32 = mybir.dt.float32
BF16 = mybir.dt.bfloat16
ALU = mybir.AluOpType
AF = mybir.ActivationFunctionType

NCORES = 8
D = 1024
DEPTH = 4
SEQ = 2048
NS = 16
DS = 8
TOK = SEQ + NS * DS
G = 32
P = 64
DFF = 2816
NCH = 44
EPS = 1e-6
PAST = 16384
MAGIC = 12582912.0
TWO_PI = 2.0 * math.pi
LAYERS = DEPTH


class LT:
    __slots__ = ("w", "r", "key")

    def __init__(self):
        self.w = {}
        self.r = {}
        self.key = None


class Sched:
    ENGS = ("pe", "act", "dve", "pool", "sp")

    def __init__(self, nc, stack, n_dma):
        self.nc = nc
        self.sem = {}
        self.cnt = {}
        for e in self.ENGS:
            self.sem[e] = stack.enter_context(nc.semaphore("s_" + e))
            self.cnt[e] = 0
        self.n_dma = n_dma
        for i in range(n_dma):
            k = "d%d" % i
            self.sem[k] = stack.enter_context(nc.semaphore("s_" + k))
            self.cnt[k] = 0
        self.seen = {}
        self.prog = {e: [] for e in self.ENGS}
        self.rr = 0

    def _deps(self, eng, reads, writes):
        deps = {}

        def add(d, skip_same):
            for k, v in d.items():
                if skip_same and k == eng:
                    continue
                if deps.get(k, 0) < v:
                    deps[k] = v
        for t in reads:
            add(t.w, eng == "pe")
        for t in writes:
            add(t.w, True)
            add(t.r, True)
        waits = []
        for k, v in deps.items():
            if self.seen.get((eng, k), 0) >= v:
                continue
            self.seen[(eng, k)] = v
            waits.append((k, v))
        return waits

    def _mark(self, me, reads, writes):
        k, v = me
        for t in reads:
            if t.r.get(k, 0) < v:
                t.r[k] = v
        for t in writes:
            if t.w.get(k, 0) < v:
                t.w[k] = v

    def op(self, eng, fn, reads=(), writes=()):
        waits = self._deps(eng, reads, writes)
        self.cnt[eng] += 1
        self._mark((eng, self.cnt[eng]), reads, writes)
        self.prog[eng].append((waits, fn, (eng, 1)))

    def dma(self, q, fn, reads=(), writes=(), key=None):
        if key is None:
            lt = writes[0] if len(writes) else reads[0]
            if lt.key is None:
                lt.key = "d%d" % self.rr
                self.rr = (self.rr + 1) % self.n_dma
            key = lt.key
        waits = self._deps(q, reads, writes)
        self.cnt[key] += 16
        self._mark((key, self.cnt[key]), reads, writes)
        self.prog[q].append((waits, fn, (key, 16)))

    def barrier(self):
        for eng in self.ENGS:
            waits = []
            for k, v in self.cnt.items():
                if k == eng or v == 0:
                    continue
                if self.seen.get((eng, k), 0) >= v:
                    continue
                self.seen[(eng, k)] = v
                waits.append((k, v))
            if waits:
                self.prog[eng].append((waits, None, None))

    def final_wait(self, eng, tiles):
        waits = self._deps(eng, tiles, tiles)
        self.prog[eng].append((waits, None, None))

    def emit(self, block):
        def mk(ename):
            def body(e):
                for waits, fn, inc in self.prog[ename]:
                    for k, v in waits:
                        e.wait_ge(self.sem[k], v)
                    if fn is not None:
                        fn(e).then_inc(self.sem[inc[0]], inc[1])
            return body
        block.tensor(mk("pe"))
        block.scalar(mk("act"))
        block.vector(mk("dve"))
        block.gpsimd(mk("pool"))
        block.sync(mk("sp"))


class Carver:
    def __init__(self, ap2d):
        self.ap = ap2d
        self.off = 0
        self.n = ap2d.shape[1]

    def take(self, shape):
        n = 1
        for d in shape[1:]:
            n *= d
        assert self.off + n <= self.n, ("arena overflow", self.off, n, self.n)
        v = self.ap[:, self.off:self.off + n]
        self.off += n
        if len(shape) == 2:
            return v
        names = " ".join("d%d" % i for i in range(len(shape) - 1))
        kw = {"d%d" % i: shape[i + 1] for i in range(len(shape) - 1)}
        return v.rearrange("p (%s) -> p %s" % (names, names), **kw)


class Pool_:
    def __init__(self, items):
        self.items = items
        self.i = 0

    def get(self):
        it = self.items[self.i]
        self.i = (self.i + 1) % len(self.items)
        return it


def build(nlayers=LAYERS, skip=""):
    nc = bass.Bass("TRN2", target_bir_lowering=False)

    def IN(name, shape, dt=F32):
        return nc.dram_tensor(name, list(shape), dt, kind="ExternalInput").ap()

    def OUT(name, shape):
        return nc.dram_tensor(name, list(shape), F32, kind="ExternalOutput").ap()

    xT_in = IN("xT_in", [128, 8, TOK])
    w_in = IN("w_in", [DEPTH, D, 5632]); w_glu = IN("w_glu", [DEPTH, 512, 512]); w_sso = IN("w_ssm_out", [DEPTH, 512, D])
    w_ro = IN("w_ret_out", [DEPTH, D, D]); w_o = IN("w_o", [DEPTH, D, D]); w_up = IN("w_up", [DEPTH, D, 5632])
    w_dn = IN("w_down", [DEPTH, DFF, D])
    gmix = IN("gmix", [128, DEPTH, 8]); gffn = IN("gffn", [128, DEPTH, 8]); gfin = IN("gfin", [128, 8])
    convw = IN("convw", [128, DEPTH, 3, NCH]); convb = IN("convb", [128, DEPTH, NCH])
    lamr = IN("lamr", [128, DEPTH, G]); lami = IN("lami", [128, DEPTH, G]); logdt = IN("logdt", [128, DEPTH, G])
    bS_in = IN("bS", [128, DEPTH, G, 16]); bX_in = IN("bX", [128, DEPTH, G, 16])
    crD_in = IN("crD", [128, DEPTH, G, 16]); ciD_in = IN("ciD", [128, DEPTH, G, 16])
    dcol_in = IN("dcol", [128, DEPTH, G])
    s0S_in = IN("s0S", [128, DEPTH, G, NS]); s0X_in = IN("s0X", [128, DEPTH, G, NS])
    conv0_in = IN("conv0", [128, DEPTH, NCH, NS, 2])
    sret_in = IN("sret", [DEPTH, NS, 4, 128, 256])
    cf_in = IN("cf32", [128, CF_N]); cb_in = IN("cbf16", [128, CB_N], BF16)
    rope_in = IN("rope", [128, 17, 2, 64])

    yT_out = OUT("yT_out", [128, 8, TOK])
    ssmp_out = OUT("ssmp_out", [128, DEPTH, G]); ssms_out = OUT("ssms_out", [128, DEPTH, G, NS])
    retp_out = OUT("retp_out", [DEPTH, 4, 128, 256]); rets_out = OUT("rets_out", [DEPTH, NS, 4, 128, 256])
    convp_out = OUT("convp_out", [128, DEPTH, NCH, 2]); convs_out = OUT("convs_out", [128, DEPTH, NCH, NS, 2])
    dbg_out = OUT("dbg_out", [128, 1024]) if "G" in skip else None
    s5cb = nc.dram_tensor("s5cb", [DEPTH, 8, 128, 5 * 4 * 128], BF16, kind="Internal").ap()
    s5cf = nc.dram_tensor("s5cf", [DEPTH, 8, 128, 64 + 192], F32, kind="Internal").ap()
    rcar = nc.dram_tensor("rcar", [DEPTH, 4, 128, 256], F32, kind="Internal").ap()

    with ExitStack() as st:
        def sb(name, shape, dt=F32):
            return st.enter_context(nc.sbuf_tensor("sb_" + name, list(shape), dt))

        def pst(name, shape, dt=F32):
            return st.enter_context(nc.psum_tensor(name, list(shape), dt))

        S = Sched(nc, st, n_dma=24)
        block = st.enter_context(nc.Block())

        CAP = [None]

        def V(eng, method, reads, writes, *a, **kw):
            if CAP[0] is not None:
                CAP[0].append((eng, method, reads, writes, a, kw))
                return
            S.op(eng, lambda e: getattr(e, method)(*a, **kw), reads, writes)

        def DMA(q, out, in_, reads, writes, key=None):
            if CAP[0] is not None:
                CAP[0].append(("__dma__", q, out, in_, reads, writes))
                return
            S.dma(q, lambda e: e.dma_start(out=out, in_=in_), reads, writes, key)

        def emit_captured(item):
            if item[0] == "__dma__":
                _, q, out, in_, reads, writes = item
                DMA(q, out, in_, reads, writes)
            else:
                e_, m_, r_, w_, a_, kw_ = item
                V(e_, m_, r_, w_, *a_, **kw_)

        NH = 1152
        xT = sb("xT", [128, 8, NH]); Lx = [LT() for _ in range(3)]
        hT = sb("hT", [128, 8, NH], BF16); Lh = [LT() for _ in range(3)]
        HT = [[(0, 512), (512, 512)], [(0, 512), (512, 512), (1024, 128)]]
        GOFF = [0, 1024]

        cf = sb("cf", [128, CF_N]); Lcf = LT()
        cb = sb("cb", [128, CB_N], BF16); Lcb = LT()
        rope = sb("rope", [128, 17, 2, 64]); Lrope = LT()
        DMA("sp", cf[:], cf_in, [], [Lcf]); DMA("sp", cb[:], cb_in, [], [Lcb]); DMA("sp", rope[:], rope_in, [], [Lrope])

        def cfs(name):
            o, n = CF_OFF[name]
            return cf[:, o:o + n]

        def cbs(name):
            o, n = CB_OFF[name]
            return cb[:, o:o + n]
        ident_f = cfs("ident"); tmask = cfs("tmask"); pswap = cfs("pswap"); nvec = cfs("nvec"); kvec = cfs("kvec")
        sgn = cfs("sgn"); mre = cfs("mre"); nmre = cfs("nmre"); nmim = cfs("nmim"); epsc = cfs("eps")
        qsc = cfs("qsc"); ksc = cfs("ksc"); qksc = cfs("qksc"); gct = cfs("gct"); rowmask = cfs("rowmask"); cmask_p = cfs("cmask_p"); cmask_s = cfs("cmask_s")
        ident_b = cbs("ident"); ones_b = cbs("ones")

        small = {}
        for nm, src, shp in [("gmix", gmix, [128, DEPTH, 8]), ("gffn", gffn, [128, DEPTH, 8]), ("gfin", gfin, [128, 8]),
                             ("convw", convw, [128, DEPTH, 3, NCH]), ("convb", convb, [128, DEPTH, NCH]),
                             ("lamr", lamr, [128, DEPTH, G]), ("lami", lami, [128, DEPTH, G]), ("logdt", logdt, [128, DEPTH, G]),
                             ("dcol", dcol_in, [128, DEPTH, G])]:
            t = sb("sm_" + nm, shp); L = LT()
            DMA("sp", t[:], src, [], [L])
            small[nm] = (t, L)

        Wlast = sb("Wlast", [128, DEPTH, G]); Mclast = sb("Mclast", [128, DEPTH, G]); Mslast = sb("Mslast", [128, DEPTH, G]); Lcar = LT()
        convcar = sb("convcar", [128, DEPTH, NCH, 2]); Lccar = LT()
        ssm_p_sb = sb("ssm_p_sb", [128, DEPTH, G]); Lssp = LT()
        ssm_s_sb = sb("ssm_s_sb", [128, DEPTH, G, NS]); Lsss = LT()
        Lrcar = [LT() for _ in range(DEPTH)]

        psBig = pst("psbig", [128, 2048])
        psF = Pool_([(psBig[:, i * 512:(i + 1) * 512], LT()) for i in range(4)])
        psS = Pool_([(pst("pss%d" % i, [128, 512]), LT()) for i in range(2)])
        psB = Pool_([(pst("psb%d" % i, [128, 1024], BF16), LT()) for i in range(2)])
        t2k = Pool_([(sb("t2k%d" % i, [128, 512])[:], LT()) for i in range(5)])
        tb1k = Pool_([(sb("tb1k%d" % i, [128, 512], BF16)[:], LT()) for i in range(4)])
        rstd_p = Pool_([(sb("rstd%d" % i, [128, 512])[:], LT()) for i in range(2)])

        ARF_N = 7424
        ARB_N = 39700
        arena_f = sb("arena_f", [128, ARF_N]); arena_b = sb("arena_b", [128, ARB_N], BF16)

        def wload(dst, src, Ld):
            DMA("pool", dst, src, [], [Ld])

        def wview(w, l):
            return w[l].rearrange("(c p) n -> p c n", p=128)

        def rms_rstd(ti, t0, n, dscale):
            ps, Lp = psF.get()
            for c in range(8):
                sq, Lsq = tb1k.get()
                V("act", "activation", [Lx[ti]], [Lsq], out=sq[:, :n], in_=xT[:, c, t0:t0 + n], func=AF.Square)
                V("pe", "matmul", [Lsq, Lcb], [Lp], ps[:, :n], lhsT=ones_b, rhs=sq[:, :n], start=(c == 0), stop=(c == 7))
            r, Lr = rstd_p.get()
            V("act", "activation", [Lp, Lcf], [Lr], out=r[:, :n], in_=ps[:, :n], func=AF.Sqrt, bias=epsc[:, 0:1], scale=dscale)
            V("dve", "reciprocal", [Lr], [Lr], out=r[:, :n], in_=r[:, :n])
            return r, Lr

        def norm_to_h(l, which, tiles):
            gt, Lg = small[which]
            for ti, (t0, n) in enumerate(tiles):
                r, Lr = rms_rstd(ti, t0, n, 1.0 / D)
                for c in range(8):
                    eng = "dve"
                    V(eng, "scalar_tensor_tensor", [Lx[ti], Lg, Lr], [Lh[ti]], out=hT[:, c, t0:t0 + n], in0=xT[:, c, t0:t0 + n],
                      scalar=gt[:, l, c:c + 1], in1=r[:, :n], op0=ALU.mult, op1=ALU.mult)

        def proj_fm(ps, Lp, wt, Lw, col0, nk, act_fn, Lact, n):
            for kc in range(nk):
                V("pe", "matmul", [Lw] + Lact, [Lp], ps[:, :n], lhsT=wt[:, kc, col0:col0 + 128], rhs=act_fn(kc), start=(kc == 0), stop=(kc == nk - 1))

        def apply_wo(ti, t0, n, wo, Lwo, mix, Lmix):
            for oc in range(8):
                ps, Lp = psF.get()
                proj_fm(ps, Lp, wo, Lwo, oc * 128, 8, lambda kc: mix[:, kc, :n], [Lmix], n)
                V("dve", "tensor_tensor", [Lp, Lx[ti]], [Lx[ti]], out=xT[:, oc, t0:t0 + n], in0=xT[:, oc, t0:t0 + n], in1=ps[:, :n], op=ALU.add)

        def tiles_overlapping(tiles, a, b):
            return [ti for ti, (t0, n) in enumerate(tiles) if t0 < b and t0 + n > a]

        def range_reduce(dst, src, Lt, tmp, add_half_pi):
            if add_half_pi:
                V("dve", "tensor_scalar", [Lt], [Lt], out=dst, in0=src, scalar1=math.pi / 2, scalar2=None, op0=ALU.add)
                src = dst
            V("dve", "tensor_scalar", [Lt], [Lt], out=tmp, in0=src, scalar1=1.0 / TWO_PI, scalar2=MAGIC, op0=ALU.mult, op1=ALU.add)
            V("dve", "tensor_scalar", [Lt], [Lt], out=tmp, in0=tmp, scalar1=MAGIC, scalar2=TWO_PI, op0=ALU.subtract, op1=ALU.mult)
            V("dve", "tensor_tensor", [Lt], [Lt], out=dst, in0=src, in1=tmp, op=ALU.subtract)

        GG = 4
        NSUB = G // GG
        KP = 128
        S5C_L = {}

        def s5_setup(l):
            lamr_t, Ll1 = small["lamr"]; lami_t, Ll2 = small["lami"]; ldt_t, Ll3 = small["logdt"]; dcol_t, Ll4 = small["dcol"]
            Lsmall = [Ll1, Ll2, Ll3, Ll4, Lcf]
            cfA = Carver(arena_f[:, :]); cbA = Carver(arena_b[:, 36368:])
            s5f = [None] + [cfA.take([128, GG, 128]) for _ in range(5)]; Ls5f = [None] + [LT() for _ in range(5)]
            Pb = cbA.take([128, 3, GG, 128]); LPb = LT()
            Ab = cbA.take([128, 3, GG, 128]); LAb = LT()

            class SCtx:
                pass
            sctx = []
            for i in range(4):
                c = SCtx()
                c.s5sm = cfA.take([128, 16, GG]); c.Lsm = LT()
                c.Ep = cfA.take([128, 6, GG, 24]); c.LEp = LT()
                c.bsx = cfA.take([128, 3, GG, 16]); c.Lbsx = LT()
                c.bSt = cfA.take([128, 4, GG, 16]); c.LbSt = LT()
                c.tsm = cfA.take([128, GG, 32]); c.Ltsm = LT()
                sctx.append(c)

            def small_stage(c, sq):
                g0 = sq * GG
                gs = slice(g0, g0 + GG)
                s5sm, Lsm, Ep, LEp, bsx, Lbsx, bSt, LbSt = c.s5sm, c.Lsm, c.Ep, c.LEp, c.bsx, c.Lbsx, c.bSt, c.LbSt
                s5f = [c.tsm]; Ls5f = [c.Ltsm]
                sm = lambda i: s5sm[:, i, :]
                Er = Ep[:, 4]; Ei = Ep[:, 5]
                V("act", "activation", Lsmall, [Lsm], out=sm(0), in_=ldt_t[:, l, gs], func=AF.Exp)
                V("dve", "tensor_tensor", Lsmall + [Lsm], [Lsm], out=sm(1), in0=lamr_t[:, l, gs], in1=sm(0), op=ALU.mult)
                V("dve", "tensor_tensor", Lsmall + [Lsm], [Lsm], out=sm(2), in0=lami_t[:, l, gs], in1=sm(0), op=ALU.mult)
                nv = nvec.unsqueeze(1).to_broadcast([128, GG, 24])
                V("dve", "tensor_tensor", [Lsm, Lcf], [LEp], out=Ep[:, 0], in0=sm(2).unsqueeze(2).to_broadcast([128, GG, 24]), in1=nv, op=ALU.mult)
                V("dve", "tensor_tensor", [Lsm, Lcf], [LEp], out=Ep[:, 1], in0=sm(1).unsqueeze(2).to_broadcast([128, GG, 24]), in1=nv, op=ALU.mult)
                range_reduce(Ep[:, 2], Ep[:, 0], LEp, Ep[:, 3], False)
                V("act", "activation", [LEp], [LEp], out=Ep[:, 5], in_=Ep[:, 2], func=AF.Sin)
                range_reduce(Ep[:, 2], Ep[:, 0], LEp, Ep[:, 3], True)
                V("act", "activation", [LEp], [LEp], out=Ep[:, 4], in_=Ep[:, 2], func=AF.Sin)
                V("act", "activation", [LEp], [LEp], out=Ep[:, 1], in_=Ep[:, 1], func=AF.Exp)
                V("dve", "tensor_tensor", [LEp], [LEp], out=Ep[:, 4], in0=Ep[:, 4], in1=Ep[:, 1], op=ALU.mult)
                V("dve", "tensor_tensor", [LEp], [LEp], out=Ep[:, 5], in0=Ep[:, 5], in1=Ep[:, 1], op=ALU.mult)
                Er = Ep[:, 4]; Ei = Ep[:, 5]
                E1r = Er[:, :, 16]; E1i = Ei[:, :, 16]
                lr_ = lamr_t[:, l, gs]; li_ = lami_t[:, l, gs]
                RS = Lsmall + [Lsm, LEp]
                V("dve", "tensor_scalar", RS, [Lsm], out=sm(3), in0=E1r, scalar1=-1.0, scalar2=None, op0=ALU.add)
                V("dve", "tensor_tensor", RS, [Lsm], out=sm(4), in0=lr_, in1=lr_, op=ALU.mult)
                V("dve", "tensor_tensor", RS, [Lsm], out=sm(8), in0=li_, in1=li_, op=ALU.mult)
                V("dve", "tensor_tensor", RS, [Lsm], out=sm(4), in0=sm(4), in1=sm(8), op=ALU.add)
                V("dve", "reciprocal", RS, [Lsm], out=sm(5), in_=sm(4))
                V("dve", "tensor_tensor", RS, [Lsm], out=sm(6), in0=sm(3), in1=lr_, op=ALU.mult)
                V("dve", "tensor_tensor", RS, [Lsm], out=sm(8), in0=E1i, in1=li_, op=ALU.mult)
                V("dve", "tensor_tensor", RS, [Lsm], out=sm(6), in0=sm(6), in1=sm(8), op=ALU.add)
                V("dve", "tensor_tensor", RS, [Lsm], out=sm(6), in0=sm(6), in1=sm(5), op=ALU.mult)
                V("dve", "tensor_tensor", RS, [Lsm], out=sm(7), in0=E1i, in1=lr_, op=ALU.mult)
                V("dve", "tensor_tensor", RS, [Lsm], out=sm(8), in0=sm(3), in1=li_, op=ALU.mult)
                V("dve", "tensor_tensor", RS, [Lsm], out=sm(7), in0=sm(7), in1=sm(8), op=ALU.subtract)
                V("dve", "tensor_tensor", RS, [Lsm], out=sm(7), in0=sm(7), in1=sm(5), op=ALU.mult)
                V("act", "activation", RS, [Lsm], out=sm(10), in_=sm(1), func=AF.Exp, scale=8.0)
                V("dve", "tensor_scalar", RS, [Lsm], out=sm(11), in0=sm(2), scalar1=8.0, scalar2=None, op0=ALU.mult)
                bc16 = lambda a: a.unsqueeze(2).to_broadcast([128, GG, 16])
                RB = [LbSt, Lsm, Lcf]
                V("dve", "tensor_scalar", RB, [Lbsx], out=bsx[:, 0], in0=bSt[:, 1], scalar1=sgn[:, 0:1], scalar2=None, op0=ALU.mult)
                t0_ = s5f[0][:, :, 0:16]; t1_ = s5f[0][:, :, 16:32]
                V("dve", "tensor_tensor", RB, [Ls5f[0]], out=t0_, in0=bSt[:, 0], in1=bc16(sm(6)), op=ALU.mult)
                V("dve", "tensor_tensor", RB + [Lbsx], [Ls5f[0]], out=t1_, in0=bsx[:, 0], in1=bc16(sm(7)), op=ALU.mult)
                V("dve", "tensor_tensor", [Ls5f[0]], [Lbsx], out=bsx[:, 1], in0=t0_, in1=t1_, op=ALU.add)
                V("dve", "tensor_tensor", RB + [Lbsx], [Ls5f[0]], out=t0_, in0=bsx[:, 0], in1=bc16(sm(6)), op=ALU.mult)
                V("dve", "tensor_tensor", RB, [Ls5f[0]], out=t1_, in0=bSt[:, 0], in1=bc16(sm(7)), op=ALU.mult)
                V("dve", "tensor_tensor", [Ls5f[0]], [Lbsx], out=bsx[:, 2], in0=t0_, in1=t1_, op=ALU.subtract)

            def big_stage(c, sq):
                g0 = sq * GG
                gs = slice(g0, g0 + GG)
                s5sm, Lsm, Ep, LEp, bsx, Lbsx, bSt, LbSt = c.s5sm, c.Lsm, c.Ep, c.LEp, c.bsx, c.Lbsx, c.bSt, c.LbSt
                sm = lambda i: s5sm[:, i, :]
                Er = Ep[:, 4]; Ei = Ep[:, 5]
                Ls5c = LT()
                bc16 = lambda a: a.unsqueeze(2).to_broadcast([128, GG, 16])
                v4 = lambda a: a.rearrange("p g (j c) -> p g j c", c=16)
                bj = lambda a: a.unsqueeze(2).to_broadcast([128, GG, 8, 16])
                ej = lambda a: a.unsqueeze(3).to_broadcast([128, GG, 8, 16])
                bs_ = bsx[:, 1]; bx_ = bsx[:, 2]
                RP = [LEp, Lbsx]

                def cplx(dst_bf, e_r, e_i, sign_mode, Ldst):
                    a_, b_ = (e_r, e_i) if sign_mode == 0 else (e_i, e_r)
                    V("dve", "tensor_tensor", RP, [Ls5f[1]], out=v4(s5f[1]), in0=ej(a_), in1=bj(bs_), op=ALU.mult)
                    V("dve", "tensor_tensor", RP, [Ls5f[2]], out=v4(s5f[2]), in0=ej(b_), in1=bj(bx_), op=ALU.mult)
                    V("dve", "tensor_tensor", [Ls5f[1], Ls5f[2]], [Ldst], out=dst_bf, in0=s5f[1], in1=s5f[2],
                      op=(ALU.add if sign_mode == 0 else ALU.subtract))
                Pp, LPp = tb1k.get(); Ppt, LPpt = tb1k.get()
                g3 = lambda a: a.rearrange("p (g n) -> p g n", g=GG)
                cplx(g3(Pp), Er[:, :, 0:8], Ei[:, :, 0:8], 0, LPp)
                cplx(g3(Ppt), Er[:, :, 0:8], Ei[:, :, 0:8], 1, LPpt)
                cplx(Pb[:, 0], Er[:, :, 8:16], Ei[:, :, 8:16], 0, LPb)
                ECr = Er[:, :, 16:24]; ECi = Ei[:, :, 16:24]
                crD_ = bSt[:, 2]; ciD_ = bSt[:, 3]
                RC = [LEp, LbSt]
                V("dve", "tensor_tensor", RC, [Ls5f[1]], out=v4(s5f[1]), in0=ej(ECr), in1=bj(crD_), op=ALU.mult)
                V("dve", "tensor_tensor", RC, [Ls5f[2]], out=v4(s5f[2]), in0=ej(ECi), in1=bj(ciD_), op=ALU.mult)
                V("dve", "tensor_tensor", [Ls5f[1], Ls5f[2]], [Ls5f[3]], out=s5f[3], in0=s5f[1], in1=s5f[2], op=ALU.subtract)
                V("dve", "tensor_tensor", RC, [Ls5f[1]], out=v4(s5f[1]), in0=ej(ECi), in1=bj(crD_), op=ALU.mult)
                V("dve", "tensor_tensor", RC, [Ls5f[2]], out=v4(s5f[2]), in0=ej(ECr), in1=bj(ciD_), op=ALU.mult)
                V("dve", "tensor_tensor", [Ls5f[1], Ls5f[2]], [Ls5f[4]], out=s5f[4], in0=s5f[1], in1=s5f[2], op=ALU.add)
                V("dve", "tensor_scalar", [Ls5f[3], Lcf], [Ls5f[1]], out=s5f[1], in0=s5f[3], scalar1=mre[:, 0:1], scalar2=None, op0=ALU.mult)
                V("dve", "scalar_tensor_tensor", [Ls5f[4], Ls5f[1], Lcf], [LPb], out=Pb[:, 1], in0=s5f[4], scalar=nmim[:, 0:1], in1=s5f[1], op0=ALU.mult, op1=ALU.add)
                V("dve", "tensor_scalar", [Ls5f[4], Lcf], [Ls5f[2]], out=s5f[2], in0=s5f[4], scalar1=nmre[:, 0:1], scalar2=None, op0=ALU.mult)
                V("dve", "scalar_tensor_tensor", [Ls5f[3], Ls5f[2], Lcf], [LPb], out=Pb[:, 2], in0=s5f[3], scalar=nmim[:, 0:1], in1=s5f[2], op0=ALU.mult, op1=ALU.add)
                ps, Lp = psS.get()
                for g in range(GG):
                    V("pe", "matmul", [LPb], [Lp], ps[:, g * 128:(g + 1) * 128], lhsT=Pb[:, 0, g, :], rhs=Pb[:, 1, g, :], start=True, stop=True)
                V("dve", "tensor_tensor", [Lp, Lcf], [Ls5f[5]], out=s5f[5], in0=ps[:].rearrange("p (g n) -> p g n", g=GG),
                  in1=tmask.unsqueeze(1).to_broadcast([128, GG, 128]), op=ALU.mult)
                for g in range(GG):
                    V("dve", "scalar_tensor_tensor", [Ls5f[5], Lcf, Ll4], [LAb], out=Ab[:, 2, g, :], in0=ident_f, scalar=dcol_t[:, l, g0 + g:g0 + g + 1],
                      in1=s5f[5][:, g, :], op0=ALU.mult, op1=ALU.add)
                pb_, Lpb_ = psB.get()
                for i, (Px, LPx) in enumerate(((Pp, LPp), (Ppt, LPpt))):
                    for g in range(GG):
                        V("pe", "transpose", [LPx, Lcb], [Lpb_], out=pb_[:, (i * GG + g) * 128:(i * GG + g + 1) * 128], in_=Px[:, g * 128:(g + 1) * 128], identity=ident_b)
                V("act", "activation", [Lpb_], [LAb], out=Ab[:, 0:2].rearrange("p a g n -> p (a g n)"), in_=pb_[:, 0:2 * GG * 128], func=AF.Copy)
                DMA("sp", s5cb[l, sq, :, 0:2 * GG * 128], Pb[:, 1:3].rearrange("p a g n -> p (a g n)"), [LPb], [Ls5c])
                DMA("sp", s5cb[l, sq, :, 2 * GG * 128:5 * GG * 128], Ab.rearrange("p a g n -> p (a g n)"), [LAb], [Ls5c])
                DMA("sp", s5cf[l, sq, :, 0:64], s5sm.rearrange("p a g -> p (a g)"), [Lsm], [Ls5c])
                DMA("sp", s5cf[l, sq, :, 64:256], Ep[:, 4:6].rearrange("p a g n -> p (a g n)"), [LEp], [Ls5c])
                S5C_L[(l, sq)] = Ls5c

            for grp in range(NSUB // 4):
                subs = [grp * 4 + i for i in range(4)]
                lists = []
                for i, sq in enumerate(subs):
                    c = sctx[i]
                    for i2, src in enumerate([bS_in, bX_in, crD_in, ciD_in]):
                        DMA("sp", c.bSt[:, i2], src[:, l, sq * GG:(sq + 1) * GG, :], [], [c.LbSt])
                    prev_cap = CAP[0]
                    CAP[0] = []
                    small_stage(c, sq)
                    lists.append(CAP[0]); CAP[0] = prev_cap
                for k in range(max(len(x) for x in lists)):
                    for lst in lists:
                        if k < len(lst):
                            e_, m_, r_, w_, a_, kw_ = lst[k]
                            V(e_, m_, r_, w_, *a_, **kw_)
                for i, sq in enumerate(subs):
                    big_stage(sctx[i], sq)

        def s5_phase(l, hi, tiles, yaT, Lya):
            S.barrier()
            has_s = (hi == 1)
            KT = KP + (NS if has_s else 0)
            kbase = hi * KP
            PT = [(0, KP)] + ([(KP, NS)] if has_s else [])
            lamr_t, Ll1 = small["lamr"]; lami_t, Ll2 = small["lami"]; ldt_t, Ll3 = small["logdt"]; dcol_t, Ll4 = small["dcol"]
            Lsmall = [Ll1, Ll2, Ll3, Ll4, Lcf]
            cfA = Carver(arena_f[:, :]); cbA = Carver(arena_b[:, 5120:])

            class Ctx:
                pass
            ctxs = []
            for u in range(2):
                b = Ctx()
                b.s5sm = cfA.take([128, 16, GG]); b.Lsm = LT()
                b.Ep = cfA.take([128, 2, GG, 24]); b.LEp = LT()
                b.cosT = cfA.take([128, GG, KP]); b.sinT = cfA.take([128, GG, KP]); b.Ltab = LT()
                b.Yb = cfA.take([128, GG, KP]); b.LY = LT()
                b.Wb = cfA.take([128, GG, KP]); b.LW = LT()
                b.s0t = cfA.take([128, 2, GG, NS]); b.Ls0 = LT()
                b.sfin = cfA.take([128, 4, GG, NS]); b.Lsfin = LT()
                b.ctmp = cfA.take([128, GG, 2]); b.Lctmp = LT()
                b.PbC = cbA.take([128, 2, GG, 128]); b.LPb = LT()
                b.Ab = cbA.take([128, 3, GG, 128]); b.LAb = LT()
                b.Mc = cbA.take([128, GG, KP + NS]); b.Ms = cbA.take([128, GG, KP + NS]); b.LM = LT()
                b.yapt = cbA.take([128, 2, 8, GG * 16]); b.Lyapt = LT()
                ctxs.append(b)
            Uq_p = Pool_([(cbA.take([128, GG, KP + NS]), LT()) for i in range(4)])
            upt_p = Pool_([(cbA.take([128, 2, GG, 8, 16]), LT()) for i in range(4)])
            wu = Pool_([(cbA.take([128, 8, GG * 16]), LT()) for i in range(4)])

            def stage_u(sq):
                g0 = sq * GG
                wut, Lwu = wu.get()
                wload(wut, wview(w_in, l)[:, :, g0 * 16:g0 * 16 + GG * 16], Lwu)
                Uq, LU = Uq_p.get()
                upt, Lupt = upt_p.get()
                for m, (k0, nk) in enumerate(PT):
                    t0 = k0 * 8
                    tl = tiles_overlapping(tiles, t0, t0 + nk * 8)
                    ps, Lp = psF.get()
                    for j in range(8):
                        for c in range(8):
                            lh = hT[:, c, t0:t0 + nk * 8].rearrange("p (k j) -> p k j", j=8)[:, :, j]
                            V("pe", "matmul", [Lh[t] for t in tl] + [Lwu], [Lp], ps[:nk, j * 64:(j + 1) * 64], lhsT=lh, rhs=wut[:, c, :],
                              start=(c == 0), stop=(c == 7))
                    V("act", "activation", [Lp], [Lupt], out=upt[:nk, m].rearrange("p g j c -> p j g c"),
                      in_=ps[:nk, :].rearrange("p (j g c) -> p j g c", j=8, g=GG), func=AF.Copy)
                    pb_, Lpb_ = psB.get()
                    for g in range(GG):
                        V("pe", "transpose", [Lupt, Lcb], [Lpb_], out=pb_[:, g * 128:g * 128 + nk], in_=upt[:nk, m, g].rearrange("p j c -> p (j c)"),
                          identity=ident_b[:nk, :nk])
                    V("dve", "tensor_copy", [Lpb_], [LU], out=Uq[:, :, k0:k0 + nk], in_=pb_[:, 0:GG * 128].rearrange("p (g n) -> p g n", g=GG)[:, :, :nk])
                return Uq, LU


            f2 = lambda a: a.rearrange("p g k -> p (g k)")

            def stepL(b):
                sq = b.sq
                Lc_ = S5C_L[(l, sq)]
                DMA("sp", b.PbC.rearrange("p a g n -> p (a g n)"), s5cb[l, sq, :, 0:2 * GG * 128], [Lc_], [b.LPb])
                DMA("sp", b.Ab.rearrange("p a g n -> p (a g n)"), s5cb[l, sq, :, 2 * GG * 128:5 * GG * 128], [Lc_], [b.LAb])
                DMA("sp", b.s5sm.rearrange("p a g -> p (a g)"), s5cf[l, sq, :, 0:64], [Lc_], [b.Lsm])
                DMA("sp", b.Ep.rearrange("p a g n -> p (a g n)"), s5cf[l, sq, :, 64:256], [Lc_], [b.LEp])
                if has_s:
                    DMA("sp", b.s0t[:, 0], s0S_in[:, l, b.gs, :], [], [b.Ls0]); DMA("sp", b.s0t[:, 1], s0X_in[:, l, b.gs, :], [], [b.Ls0])

            def stepT(b):
                s5sm, Lsm, Yb, Wb, LY, sinT, cosT, Ltab = b.s5sm, b.Lsm, b.Yb, b.Wb, b.LY, b.sinT, b.cosT, b.Ltab
                kv = kvec[:, kbase:kbase + KP].unsqueeze(1).to_broadcast([128, GG, KP])
                V("dve", "tensor_tensor", [Lsm, Lcf], [LY], out=Yb, in0=s5sm[:, 11, :].unsqueeze(2).to_broadcast([128, GG, KP]), in1=kv, op=ALU.mult)
                range_reduce(Wb, Yb, LY, sinT, False)
                V("act", "activation", [LY], [Ltab], out=sinT, in_=Wb, func=AF.Sin)
                V("act", "activation", [LY], [LY], out=Yb, in_=Wb, func=AF.Abs)
                V("act", "activation", [LY, Lcf], [Ltab], out=cosT, in_=Yb, func=AF.Sin, scale=-1.0, bias=cfs("halfpi")[:, 0:1])

            def stepX(b):
                s5sm, Lsm, Yb, Wb, LY, LW, sinT, cosT, Ltab = b.s5sm, b.Lsm, b.Yb, b.Wb, b.LY, b.LW, b.sinT, b.cosT, b.Ltab
                Ab, LAb, Uq, LU = b.Ab, b.LAb, b.Uq, b.LU
                psx, Lpx = psF.get(); psxt, Lpxt = psF.get()
                for g in range(GG):
                    V("pe", "matmul", [LAb, LU], [Lpx], psx[:, g * KP:(g + 1) * KP], lhsT=Ab[:, 0, g, :], rhs=Uq[:, g, 0:KP], start=True, stop=True)
                    V("pe", "matmul", [LAb, LU], [Lpxt], psxt[:, g * KP:(g + 1) * KP], lhsT=Ab[:, 1, g, :], rhs=Uq[:, g, 0:KP], start=True, stop=True)
                V("dve", "tensor_tensor", [Lpx, Ltab], [LY], out=f2(Yb), in0=psx[:], in1=f2(cosT), op=ALU.mult)
                V("dve", "tensor_tensor", [Lpxt, Ltab], [LW], out=f2(Wb), in0=psxt[:], in1=f2(sinT), op=ALU.mult)
                V("dve", "tensor_tensor", [LY, LW], [LY], out=f2(Yb), in0=f2(Yb), in1=f2(Wb), op=ALU.add)
                if hi == 1:
                    V("dve", "tensor_tensor", [Lsm, Lcar], [b.Lctmp], out=b.ctmp[:, :, 0], in0=s5sm[:, 10, :], in1=Wlast[:, l, b.gs], op=ALU.mult)
                    V("dve", "tensor_tensor", [LY, b.Lctmp], [LY], out=Yb[:, :, 0], in0=Yb[:, :, 0], in1=b.ctmp[:, :, 0], op=ALU.add)

            def stepS(b):
                s5sm, Lsm, Yb, Wb, LY, LW, sinT, cosT, Ltab = b.s5sm, b.Lsm, b.Yb, b.Wb, b.LY, b.LW, b.sinT, b.cosT, b.Ltab
                Ab, LAb, Uq, LU, Mc, Ms, LM = b.Ab, b.LAb, b.Uq, b.LU, b.Mc, b.Ms, b.LM
                s0t, Ls0, sfin, Lsfin, LEp, gs = b.s0t, b.Ls0, b.sfin, b.Lsfin, b.LEp, b.gs
                Er = b.Ep[:, 0]; Ei = b.Ep[:, 1]
                for g in range(GG):
                    V("dve", "tensor_tensor_scan", [LY, Lsm, LW], [LW], out=Wb[:, g, :], data0=s5sm[:, 10, g:g + 1].to_broadcast([128, KP]), data1=Yb[:, g, :],
                      initial=0.0, op0=ALU.mult, op1=ALU.add)
                if hi == 0:
                    V("dve", "memset", [], [LM], Mc[:, :, 0:1], 0.0)
                    V("dve", "memset", [], [LM], Ms[:, :, 0:1], 0.0)
                else:
                    V("dve", "tensor_copy", [Lcar], [LM], out=Mc[:, :, 0], in_=Mclast[:, l, gs])
                    V("dve", "tensor_copy", [Lcar], [LM], out=Ms[:, :, 0], in_=Mslast[:, l, gs])
                V("dve", "tensor_tensor", [LW, Ltab], [LM], out=Mc[:, :, 1:KP], in0=Wb[:, :, 0:KP - 1], in1=cosT[:, :, 0:KP - 1], op=ALU.mult)
                V("dve", "tensor_tensor", [LW, Ltab], [LM], out=Ms[:, :, 1:KP], in0=Wb[:, :, 0:KP - 1], in1=sinT[:, :, 0:KP - 1], op=ALU.mult)
                if hi == 0:
                    V("dve", "tensor_copy", [LW], [Lcar], out=Wlast[:, l, gs], in_=Wb[:, :, KP - 1])
                    V("dve", "tensor_tensor", [LW, Ltab], [Lcar], out=Mclast[:, l, gs], in0=Wb[:, :, KP - 1], in1=cosT[:, :, KP - 1], op=ALU.mult)
                    V("dve", "tensor_tensor", [LW, Ltab], [Lcar], out=Mslast[:, l, gs], in0=Wb[:, :, KP - 1], in1=sinT[:, :, KP - 1], op=ALU.mult)
                else:
                    V("dve", "memset", [], [LM], Ms[:, :, KP:KT], 0.0)
                    V("act", "activation", [Ls0], [LM], out=Mc[:, :, KP:KT], in_=s0t[:, 0], func=AF.Copy)
                    V("dve", "tensor_tensor", [LW, Ltab], [Lsfin], out=sfin[:, 0, :, 0], in0=Wb[:, :, KP - 1], in1=cosT[:, :, KP - 1], op=ALU.mult)
                    V("dve", "tensor_tensor", [LW, Ltab], [Lsfin], out=sfin[:, 1, :, 0], in0=Wb[:, :, KP - 1], in1=sinT[:, :, KP - 1], op=ALU.mult)
                    ps, Lp = psF.get()
                    V("pe", "matmul", [Lsfin, Lcf], [Lp], ps[:, 0:GG], lhsT=pswap, rhs=sfin[:, 1, :, 0], start=True, stop=True)
                    V("dve", "tensor_tensor", [Lp, Lsfin], [Lssp], out=ssm_p_sb[:, l, gs], in0=ps[:, 0:GG], in1=sfin[:, 0, :, 0], op=ALU.add)
                    psx, Lpx = psF.get()
                    for g in range(GG):
                        V("pe", "matmul", [LAb, LU], [Lpx], psx[:, g * NS:(g + 1) * NS], lhsT=Ab[:, 0, g, :], rhs=Uq[:, g, KP:KT], start=True, stop=True)
                    Lr8 = Er[:, :, 23]; Li8 = Ei[:, :, 23]
                    bns = lambda a: a.unsqueeze(2).to_broadcast([128, GG, NS])
                    V("dve", "tensor_tensor", [Ls0, LEp], [Lsfin], out=sfin[:, 2], in0=s0t[:, 0], in1=bns(Lr8), op=ALU.mult)
                    V("dve", "scalar_tensor_tensor", [Ls0, LEp, Lcf], [Lsfin], out=sfin[:, 3], in0=s0t[:, 1], scalar=sgn[:, 0:1], in1=bns(Li8), op0=ALU.mult, op1=ALU.mult)
                    V("dve", "tensor_tensor", [Lsfin], [Lsfin], out=sfin[:, 2], in0=sfin[:, 2], in1=sfin[:, 3], op=ALU.add)
                    V("dve", "tensor_tensor", [Lsfin, Lpx], [Lsss], out=ssm_s_sb[:, l, gs, :], in0=sfin[:, 2], in1=psx[:, 0:GG * NS].rearrange("p (g s) -> p g s", g=GG), op=ALU.add)

            def stepY(b):
                Ab, LAb, Uq, LU, Mc, Ms, LM, PbC, LPb, yapt, Lyapt, g0 = b.Ab, b.LAb, b.Uq, b.LU, b.Mc, b.Ms, b.LM, b.PbC, b.LPb, b.yapt, b.Lyapt, b.g0
                for m, (k0, nk) in enumerate(PT):
                    ps, Lp = psF.get()
                    for g in range(GG):
                        o_ = ps[:nk, g * 128:(g + 1) * 128]
                        V("pe", "matmul", [LU, LAb], [Lp], o_, lhsT=Uq[:, g, k0:k0 + nk], rhs=Ab[:, 2, g, :], start=True, stop=False)
                        V("pe", "matmul", [LM, LPb], [Lp], o_, lhsT=Mc[:, g, k0:k0 + nk], rhs=PbC[:, 0, g, :], start=False, stop=False)
                        V("pe", "matmul", [LM, LPb], [Lp], o_, lhsT=Ms[:, g, k0:k0 + nk], rhs=PbC[:, 1, g, :], start=False, stop=True)
                    ta, La_ = t2k.get(); tb_, Lb_ = t2k.get()
                    V("act", "activation", [Lp], [La_], out=ta[:nk], in_=ps[:nk], func=AF.Square)
                    V("dve", "tensor_scalar", [La_], [La_], out=ta[:nk], in0=ta[:nk], scalar1=0.044715, scalar2=1.0, op0=ALU.mult, op1=ALU.add)
                    V("dve", "tensor_tensor", [La_, Lp], [La_], out=ta[:nk], in0=ta[:nk], in1=ps[:nk], op=ALU.mult)
                    V("act", "activation", [La_], [Lb_], out=tb_[:nk], in_=ta[:nk], func=AF.Sigmoid, scale=1.5957691216)
                    V("dve", "tensor_tensor", [Lb_, Lp], [Lyapt], out=yapt[:nk, m].rearrange("p t (g c) -> p g t c", g=GG),
                      in0=tb_[:nk].rearrange("p (g t c) -> p g t c", g=GG, t=8), in1=ps[:nk].rearrange("p (g t c) -> p g t c", g=GG, t=8), op=ALU.mult)
                    pb_, Lpb_ = psB.get()
                    for t in range(8):
                        V("pe", "transpose", [Lyapt, Lcb], [Lpb_], out=pb_[0:GG * 16, t * 128:t * 128 + nk], in_=yapt[:nk, m, t, :], identity=ident_b[:nk, :nk])
                    tok0 = k0 * 8
                    tl = tiles_overlapping(tiles, tok0, tok0 + nk * 8)
                    cq = (g0 * 16) // 128; p0 = (g0 * 16) % 128
                    V("dve", "tensor_copy", [Lpb_], [Lya[t] for t in tl], out=yaT[p0:p0 + GG * 16, cq, tok0:tok0 + nk * 8].rearrange("p (k j) -> p j k", j=8),
                      in_=pb_[0:GG * 16, :].rearrange("p (t k) -> p t k", t=8)[:, :, :nk])

            pairs = [(2 * i, 2 * i + 1) for i in range(NSUB // 2)]
            Ubuf = {0: stage_u(0), 1: stage_u(1)}
            for pi, pr in enumerate(pairs):
                for u, sq in enumerate(pr):
                    b = ctxs[u]
                    b.sq = sq; b.g0 = sq * GG; b.gs = slice(sq * GG, sq * GG + GG)
                    b.Uq, b.LU = Ubuf.pop(sq)
                if pi + 1 < len(pairs):
                    for sq2 in pairs[pi + 1]:
                        Ubuf[sq2] = stage_u(sq2)
                for step in (stepL, stepT, stepX, stepS, stepY):
                    for u in range(2):
                        step(ctxs[u])
            if hi == 1:
                DMA("sp", ssmp_out[:, l, :], ssm_p_sb[:, l, :], [Lssp], [])
                DMA("sp", ssms_out[:, l], ssm_s_sb[:, l], [Lsss], [])

        def phase_B(l, tiles, yaT, Lya, wo, Lwo):
            S.barrier()
            cbA = Carver(arena_b[:, 5120 + 8192:])
            wgl = cbA.take([128, 4, 512]); Lwgl = LT()
            wso = cbA.take([128, 4, 1024]); Lwso = LT()
            wga = cbA.take([128, 8, 1024]); Lwga = LT()
            mix = cbA.take([128, 8, 512]); Lmix = LT()
            ya2 = cbA.take([128, 4, 512]); Ly2 = LT()
            wload(wgl, w_glu[l].rearrange("(c p) n -> p c n", p=128), Lwgl)
            wload(wso, w_sso[l].rearrange("(c p) n -> p c n", p=128), Lwso)
            GA0 = 3584
            wload(wga, wview(w_in, l)[:, :, GA0:GA0 + 1024], Lwga)
            wload(wo, wview(w_o, l), Lwo)
            for ti, (t0, n) in enumerate(tiles):
                for oc in range(4):
                    ps, Lp = psF.get()
                    proj_fm(ps, Lp, wgl, Lwgl, oc * 128, 4, lambda kc: yaT[:, kc, t0:t0 + n], [Lya[ti]], n)
                    sg, Lsg = tb1k.get()
                    V("act", "activation", [Lp], [Lsg], out=sg[:, :n], in_=ps[:, :n], func=AF.Sigmoid)
                    V("dve", "tensor_tensor", [Lsg, Lya[ti]], [Ly2], out=ya2[:, oc, :n], in0=sg[:, :n], in1=yaT[:, oc, t0:t0 + n], op=ALU.mult)
                for oc in range(8):
                    ps, Lp = psF.get(); ps2, Lp2 = psF.get()
                    proj_fm(ps, Lp, wso, Lwso, oc * 128, 4, lambda kc: ya2[:, kc, :n], [Ly2], n)
                    proj_fm(ps2, Lp2, wga, Lwga, oc * 128, 8, lambda kc: hT[:, kc, t0:t0 + n], [Lh[ti]], n)
                    sg, Lsg = t2k.get()
                    V("act", "activation", [Lp2], [Lsg], out=sg[:, :n], in_=ps2[:, :n], func=AF.Sigmoid)
                    V("dve", "tensor_tensor", [Lsg, Lp], [Lmix], out=mix[:, oc, :n], in0=sg[:, :n], in1=ps[:, :n], op=ALU.mult)
                if "B" not in skip:
                    apply_wo(ti, t0, n, wo, Lwo, mix, Lmix)

        def phase_C(l, hi, tiles, oT, LoT):
            S.barrier()
            cfA = Carver(arena_f[:, :])
            cbY = Carver(arena_b[:, 0:5120])
            cbW = Carver(arena_b[:, 5120:5120 + 8192])
            cbA = Carver(arena_b[:, 5120 + 8192 + 9216:])
            r_st = cfA.take([128, 4, 256]); Lr_st = LT()
            orw_p = Pool_([(cfA.take([128, 4, 2, 128]), LT()) for i in range(1)])
            qf_p = Pool_([(cfA.take([128, 4, 256]), LT()) for i in range(1)])
            r0_p = Pool_([(cfA.take([128, 4, 256]), LT()) for i in range(2)])
            rn_p = Pool_([(cfA.take([128, 4, 256]), LT()) for i in range(2)])
            wq4 = cbA.take([128, 8, 2048]); Lwq4 = [LT() for _ in range(4)]
            r_bf = cbY.take([128, 4, 256]); Lr_bf = LT()
            qk_p = Pool_([(cbY.take([128, 4, 2, 128]), LT()) for i in range(2)])
            qkT_p = Pool_([(cbY.take([128, 4, 2, 128]), LT()) for i in range(1)])
            sc_p = Pool_([(cbY.take([128, 4, 128]), LT()) for i in range(2)])
            v_p = Pool_([(cbW.take([128, 4, 256]), LT()) for i in range(2)])
            sq_p = Pool_([(cbW.take([128, 1024]), LT()) for i in range(2)])
            r0b_p = Pool_([(cbW.take([128, 4, 256]), LT()) for i in range(2)])
            km_p = Pool_([(cbW.take([128, 4, 128]), LT()) for i in range(2)])
            goff = GOFF[hi]
            wv_ = wview(w_in, l)
            for hh in range(4):
                wload(wq4[:, :, hh * 512:hh * 512 + 128], wv_[:, :, 512 + hh * 128:512 + (hh + 1) * 128], Lwq4[hh])
                wload(wq4[:, :, hh * 512 + 128:hh * 512 + 256], wv_[:, :, 1024 + hh * 128:1024 + (hh + 1) * 128], Lwq4[hh])
                wload(wq4[:, :, hh * 512 + 256:hh * 512 + 512], wv_[:, :, 1536 + hh * 256:1536 + (hh + 1) * 256], Lwq4[hh])
            if hi == 0:
                V("dve", "memset", [], [Lr_st], r_st, 0.0)
            else:
                DMA("sp", r_st, rcar[l].rearrange("h d v -> d h v"), [Lrcar[l]], [Lr_st])
            V("act", "activation", [Lr_st], [Lr_bf], out=r_bf, in_=r_st, func=AF.Copy)
            psQ = psBig[:, :]
            LQ = [it[1] for it in psF.items]
            blocks = []
            for ti, (t0, n) in enumerate(tiles):
                for b in range(n // 128):
                    blocks.append((ti, t0 + b * 128))
            for (ti, t0) in blocks:
                tg = goff + t0
                is_s = (tg >= SEQ)
                blk = 16 if is_s else tg // 128
                kind = 1 if is_s else 0
                for hh in range(4):
                    for c in range(8):
                        V("pe", "matmul", [Lh[ti], Lwq4[hh]], [LQ[hh]], psQ[:, hh * 512:(hh + 1) * 512], lhsT=hT[:, c, t0:t0 + 128], rhs=wq4[:, c, hh * 512:(hh + 1) * 512],
                          start=(c == 0), stop=(c == 7))
                qf, Lqf = qf_p.get()
                vt, Lv = v_p.get()
                for hh in range(4):
                    V("act", "activation", [LQ[hh]], [Lqf], out=qf[:, hh, :], in_=psQ[:, hh * 512:hh * 512 + 256], func=AF.Copy)
                    V("act", "activation", [LQ[hh]], [Lv], out=vt[:, hh, :], in_=psQ[:, hh * 512 + 256:hh * 512 + 512], func=AF.Copy)
                x1 = qf.rearrange("p h (a f d) -> p h a f d", a=2, f=2)[:, :, :, 0, :]
                x2 = qf.rearrange("p h (a f d) -> p h a f d", a=2, f=2)[:, :, :, 1, :]
                cs_ = rope[:, blk, 0, :].unsqueeze(1).unsqueeze(1).to_broadcast([128, 4, 2, 64])
                sn_ = rope[:, blk, 1, :].unsqueeze(1).unsqueeze(1).to_broadcast([128, 4, 2, 64])
                pr = [t2k.get() for _ in range(4)]
                v4 = lambda a: a.rearrange("p (h a d) -> p h a d", h=4, a=2)
                V("dve", "tensor_tensor", [Lqf, Lrope], [pr[0][1]], out=v4(pr[0][0]), in0=x1, in1=cs_, op=ALU.mult)
                V("dve", "tensor_tensor", [Lqf, Lrope], [pr[1][1]], out=v4(pr[1][0]), in0=x2, in1=sn_, op=ALU.mult)
                V("dve", "tensor_tensor", [Lqf, Lrope], [pr[2][1]], out=v4(pr[2][0]), in0=x1, in1=sn_, op=ALU.mult)
                V("dve", "tensor_tensor", [Lqf, Lrope], [pr[3][1]], out=v4(pr[3][0]), in0=x2, in1=cs_, op=ALU.mult)
                V("dve", "tensor_tensor", [pr[0][1], pr[1][1]], [pr[0][1]], out=pr[0][0], in0=pr[0][0], in1=pr[1][0], op=ALU.subtract)
                V("dve", "tensor_tensor", [pr[2][1], pr[3][1]], [pr[2][1]], out=pr[2][0], in0=pr[2][0], in1=pr[3][0], op=ALU.add)
                qk, Lqk = qk_p.get()
                sct = qksc.rearrange("p (k h a) -> p k h a", k=2, h=4)[:, kind].unsqueeze(3).to_broadcast([128, 4, 2, 64])
                V("dve", "tensor_tensor", [pr[0][1], Lcf], [Lqk], out=qk[:, :, :, 0:64], in0=v4(pr[0][0]), in1=sct, op=ALU.mult)
                V("dve", "tensor_tensor", [pr[2][1], Lcf], [Lqk], out=qk[:, :, :, 64:128], in0=v4(pr[2][0]), in1=sct, op=ALU.mult)
                pb_, Lpb_ = psB.get()
                for hh in range(4):
                    for a in range(2):
                        V("pe", "transpose", [Lqk, Lcb], [Lpb_], out=pb_[:, (hh * 2 + a) * 128:(hh * 2 + a + 1) * 128], in_=qk[:, hh, a, :], identity=ident_b)
                qkT, LqkT = qkT_p.get()
                V("dve", "tensor_copy", [Lpb_], [LqkT], out=qkT.rearrange("p h a n -> p (h a n)"), in_=pb_[:, 0:1024])
                ps2, Lp2 = psF.get()
                for hh in range(4):
                    V("pe", "matmul", [LqkT], [Lp2], ps2[:, hh * 128:(hh + 1) * 128], lhsT=qkT[:, hh, 1, :], rhs=qkT[:, hh, 0, :], start=True, stop=True)
                sc, Lsc = sc_p.get()
                mk = (cmask_s if is_s else cmask_p).unsqueeze(1).to_broadcast([128, 4, 128])
                V("dve", "tensor_tensor", [Lp2, Lcf], [Lsc], out=sc, in0=ps2[:, :].rearrange("p (h n) -> p h n", h=4), in1=mk, op=ALU.mult)
                po = [psF.get(), psF.get()]
                orw, Lor = orw_p.get()
                for hh in range(4):
                    pso, Lpo = po[hh // 2]
                    for e_ in range(2):
                        o_ = pso[:, ((hh % 2) * 2 + e_) * 128:((hh % 2) * 2 + e_ + 1) * 128]
                        V("pe", "matmul", [Lv, Lsc], [Lpo], o_, lhsT=vt[:, hh, e_ * 128:(e_ + 1) * 128], rhs=sc[:, hh, :], start=True, stop=is_s)
                        if not is_s:
                            V("pe", "matmul", [Lr_bf, LqkT], [Lpo], o_, lhsT=r_bf[:, hh, e_ * 128:(e_ + 1) * 128], rhs=qkT[:, hh, 0, :], start=False, stop=True)
                orf = orw.rearrange("p h e n -> p (h e n)")
                for i2 in range(2):
                    V("act", "activation", [po[i2][1]], [Lor], out=orf[:, i2 * 512:(i2 + 1) * 512], in_=po[i2][0][:, :], func=AF.Copy)
                if not is_s:
                    pd = [psS.get(), psS.get()]
                    for hh in range(4):
                        psd, Lpd = pd[hh // 2]
                        V("pe", "matmul", [Lqk, Lv], [Lpd], psd[:, (hh % 2) * 256:(hh % 2 + 1) * 256], lhsT=qk[:, hh, 1, :], rhs=vt[:, hh, :], start=True, stop=True)
                    for i2 in range(2):
                        rv = r_st[:, i2 * 2:i2 * 2 + 2, :].rearrange("p h v -> p (h v)")
                        V("dve", "tensor_tensor", [pd[i2][1], Lr_st], [Lr_st], out=rv, in0=rv, in1=pd[i2][0][:, :], op=ALU.add)
                    gtab = gct.rearrange("p (k h) -> p k h", k=2)[:, 0].unsqueeze(2).to_broadcast([128, 4, 256])
                    V("dve", "tensor_tensor", [Lr_st, Lcf], [Lr_st], out=r_st, in0=r_st, in1=gtab, op=ALU.mult)
                    V("act", "activation", [Lr_st], [Lr_bf], out=r_bf, in_=r_st, func=AF.Copy)
                    if tg == 1024 - 128:
                        DMA("sp", rcar[l].rearrange("h d v -> d h v"), r_st, [Lr_st], [Lrcar[l]])
                    if tg == SEQ - 128:
                        DMA("sp", retp_out[l].rearrange("h d v -> d h v"), r_st, [Lr_st], [])
                else:
                    pin = [psF.get(), psF.get()]
                    gtab = gct.rearrange("p (k h) -> p k h", k=2)[:, 1].unsqueeze(2).to_broadcast([128, 4, 256])
                    def _ld_r0(sx):
                        r0x, Lr0x = r0_p.get()
                        DMA("sp", r0x, sret_in[l, sx].rearrange("h d v -> d h v"), [], [Lr0x])
                        return r0x, Lr0x
                    r0_next = _ld_r0(0)
                    for s_ in range(NS):
                        r0, Lr0 = r0_next
                        if s_ + 1 < NS:
                            r0_next = _ld_r0(s_ + 1)
                        r0b, Lr0b = r0b_p.get()
                        V("act", "activation", [Lr0], [Lr0b], out=r0b, in_=r0, func=AF.Copy)
                        for hh in range(4):
                            psi, Lpi = pin[hh // 2]
                            for e_ in range(2):
                                c0_ = ((hh % 2) * 2 + e_) * 128 + s_ * 8
                                V("pe", "matmul", [Lr0b, LqkT], [Lpi], psi[:, c0_:c0_ + 8], lhsT=r0b[:, hh, e_ * 128:(e_ + 1) * 128],
                                  rhs=qkT[:, hh, 0, s_ * 8:s_ * 8 + 8], start=True, stop=True)
                        km, Lkm = km_p.get()
                        V("dve", "tensor_scalar", [Lqk, Lcf], [Lkm], out=km, in0=qk[:, :, 1, :], scalar1=rowmask[:, s_:s_ + 1], scalar2=None, op0=ALU.mult)
                        pd = [psS.get(), psS.get()]
                        for hh in range(4):
                            psd, Lpd = pd[hh // 2]
                            V("pe", "matmul", [Lkm, Lv], [Lpd], psd[:, (hh % 2) * 256:(hh % 2 + 1) * 256], lhsT=km[:, hh, :], rhs=vt[:, hh, :], start=True, stop=True)
                        rn, Lrn = rn_p.get()
                        for i2 in range(2):
                            V("dve", "tensor_tensor", [pd[i2][1], Lr0], [Lrn], out=rn[:, i2 * 2:i2 * 2 + 2, :].rearrange("p h v -> p (h v)"),
                              in0=r0[:, i2 * 2:i2 * 2 + 2, :].rearrange("p h v -> p (h v)"), in1=pd[i2][0][:, :], op=ALU.add)
                        V("dve", "tensor_tensor", [Lrn, Lcf], [Lrn], out=rn, in0=rn, in1=gtab, op=ALU.mult)
                        DMA("pool", rets_out[l, s_].rearrange("h d v -> d h v"), rn, [Lrn], [])
                    for i2 in range(2):
                        tin, Ltin = t2k.get()
                        V("act", "activation", [pin[i2][1]], [Ltin], out=tin[:, :], in_=pin[i2][0][:, :], func=AF.Copy)
                        V("dve", "tensor_tensor", [Lor, Ltin], [Lor], out=orf[:, i2 * 512:(i2 + 1) * 512], in0=orf[:, i2 * 512:(i2 + 1) * 512], in1=tin[:, :], op=ALU.add)
                sq, Lsq = sq_p.get()
                V("dve", "tensor_tensor", [Lor], [Lsq], out=sq, in0=orf, in1=orf, op=ALU.mult)
                ps5, Lp5 = psF.get()
                for hh in range(4):
                    for e_ in range(2):
                        V("pe", "matmul", [Lsq, Lcb], [Lp5], ps5[:, hh * 128:(hh + 1) * 128], lhsT=ones_b, rhs=sq[:, (hh * 2 + e_) * 128:(hh * 2 + e_ + 1) * 128],
                          start=(e_ == 0), stop=(e_ == 1))
                rs, Lrs = t2k.get()
                V("act", "activation", [Lp5, Lcf], [Lrs], out=rs[:, :], in_=ps5[:, :], func=AF.Sqrt, bias=epsc[:, 0:1], scale=1.0 / 256)
                V("dve", "reciprocal", [Lrs], [Lrs], out=rs[:, :], in_=rs[:, :])
                V("dve", "tensor_tensor", [Lor, Lrs], [LoT[ti]], out=oT[:, :, t0:t0 + 128].rearrange("p (h e) n -> p h e n", h=4), in0=orw,
                  in1=rs[:, :].rearrange("p (h n) -> p h n", h=4).unsqueeze(2).to_broadcast([128, 4, 2, 128]), op=ALU.mult)

        def phase_D(l, tiles, oT, LoT, wo, Lwo):
            S.barrier()
            cbA = Carver(arena_b[:, 5120 + 8192 + 9216:])
            wX = cbA.take([128, 8, 1024]); LwX = LT()
            wY = cbA.take([128, 8, 1024]); LwY = LT()
            mix = arena_b[:, 0:4096].rearrange("p (c n) -> p c n", c=8); Lmix = LT()
            GR0 = 2560
            GB0 = 4608
            wload(wo, wview(w_in, l)[:, :, GR0:GR0 + 1024], Lwo)
            wload(wY, wview(w_ro, l), LwY)
            wload(wX, wview(w_in, l)[:, :, GB0:GB0 + 1024], LwX)
            for ti, (t0, n) in enumerate(tiles):
                for oc in range(8):
                    ps, Lp = psF.get()
                    proj_fm(ps, Lp, wo, Lwo, oc * 128, 8, lambda kc: hT[:, kc, t0:t0 + n], [Lh[ti]], n)
                    sg, Lsg = tb1k.get()
                    V("act", "activation", [Lp], [Lsg], out=sg[:, :n], in_=ps[:, :n], func=AF.Silu)
                    o_ = oT[:, oc, t0:t0 + n]
                    V("dve", "tensor_tensor", [Lsg, LoT[ti]], [LoT[ti]], out=o_, in0=o_, in1=sg[:, :n], op=ALU.mult)
            wload(wo, wview(w_o, l), Lwo)
            for ti, (t0, n) in enumerate(tiles):
                for oc in range(8):
                    ps, Lp = psF.get(); ps2, Lp2 = psF.get()
                    proj_fm(ps, Lp, wY, LwY, oc * 128, 8, lambda kc: oT[:, kc, t0:t0 + n], [LoT[ti]], n)
                    proj_fm(ps2, Lp2, wX, LwX, oc * 128, 8, lambda kc: hT[:, kc, t0:t0 + n], [Lh[ti]], n)
                    sg, Lsg = t2k.get()
                    V("act", "activation", [Lp2], [Lsg], out=sg[:, :n], in_=ps2[:, :n], func=AF.Sigmoid)
                    V("dve", "tensor_tensor", [Lsg, Lp], [Lmix], out=mix[:, oc, :n], in0=sg[:, :n], in1=ps[:, :n], op=ALU.mult)
                if "D" not in skip:
                    apply_wo(ti, t0, n, wo, Lwo, mix, Lmix)

        GF = 4
        EXTRA_K = [10]

        def ffn(l, hi, tiles, extra=None):
            S.barrier()
            cfA = Carver(arena_f[:, :]); cbA = Carver(arena_b[:, :])
            conv0 = cfA.take([128, NCH, NS, 2]); Lc0 = LT()
            convp_sb = cfA.take([128, NCH, 2]); Lcp = LT()
            convs_sb = cfA.take([128, NCH, NS, 2]); Lcs = LT()
            actT = cbA.take([128, GF, NH]); Lact = [LT() for _ in range(3)]
            wup_p = Pool_([(cbA.take([128, 8, 2 * GF * 128]), (LT(), LT())) for i in range(2)])
            wdn_p = Pool_([(cbA.take([128, GF, D]), LT()) for i in range(2)])
            upb = [[(cbA.take([128, 514]), LT()) for a in range(2)] for i in range(GF)]
            Dg = cbA.take([128, GF, 2, 3, 128]); LDg = LT()
            ups = Pool_([(cbA.take([128, NS, 10]), LT()) for i in range(2)]) if hi == 1 else None
            cw, Lcw = small["convw"]; cbv, Lcbv = small["convb"]
            if hi == 1:
                DMA("sp", conv0, conv0_in[:, l], [], [Lc0])
            ngroups = (22 + GF - 1) // GF
            for gi in range(ngroups):
                c0 = gi * GF
                ng = min(GF, 22 - c0)
                wu_, Lwu2 = wup_p.get(); wd_, Lwd_ = wdn_p.get()
                wload(wu_[:, :, 0:ng * 128], wview(w_up, l)[:, :, c0 * 128:(c0 + ng) * 128], Lwu2[0])
                wload(wu_[:, :, GF * 128:GF * 128 + ng * 128], wview(w_up, l)[:, :, DFF + c0 * 128:DFF + (c0 + ng) * 128], Lwu2[1])
                wload(wd_[:, 0:ng, :], w_dn[l, c0 * 128:(c0 + ng) * 128, :].rearrange("(c p) n -> p c n", p=128), Lwd_)
                for cc in range(ng):
                    for a in range(2):
                        ch = c0 + cc + a * 22
                        for j in range(3):
                            V("dve", "tensor_scalar", [Lcw, Lcb], [LDg], out=Dg[:, cc, a, j, :], in0=ident_b, scalar1=cw[:, l, j, ch:ch + 1], scalar2=None, op0=ALU.mult)
                pend_conv = []
                pend_down = []
                resmap = {}

                def emit_up(ti, t0, n, cc, a):
                    is_s = (n == 128)
                    ch = c0 + cc + a * 22
                    ps, Lp = psF.get()
                    proj_fm(ps, Lp, wu_, Lwu2[a], a * GF * 128 + cc * 128, 8, lambda kc: hT[:, kc, t0:t0 + n], [Lh[ti]], n)
                    if not is_s:
                        ub, Lub = upb[cc][a]
                        if ti == 0:
                            if hi == 0:
                                V("dve", "memset", [], [Lub], ub[:, 0:2], 0.0)
                            else:
                                V("dve", "tensor_copy", [Lccar], [Lub], out=ub[:, 0:2], in_=convcar[:, l, ch, :])
                        else:
                            V("dve", "tensor_copy", [Lub], [Lub], out=ub[:, 0:2], in_=ub[:, 512:514])
                        V("act", "activation", [Lp], [Lub], out=ub[:, 2:514], in_=ps[:, :], func=AF.Copy)
                        if ti == 1:
                            if hi == 0:
                                V("act", "activation", [Lp], [Lccar], out=convcar[:, l, ch, :], in_=ps[:, 510:512], func=AF.Copy)
                            else:
                                V("act", "activation", [Lp], [Lcp], out=convp_sb[:, ch, :], in_=ps[:, 510:512], func=AF.Copy)
                        return (ub, Lub)
                    else:
                        us, Lus = ups.get()
                        V("dve", "tensor_copy", [Lc0], [Lus], out=us[:, :, 0:2], in_=conv0[:, ch])
                        V("act", "activation", [Lp], [Lus], out=us[:, :, 2:10], in_=ps[:, 0:128].rearrange("p (s j) -> p s j", j=8), func=AF.Copy)
                        V("act", "activation", [Lp], [Lcs], out=convs_sb[:, ch], in_=ps[:, 0:128].rearrange("p (s j) -> p s j", j=8)[:, :, 6:8], func=AF.Copy)
                        return (us, Lus)

                def emit_conv(ti, t0, n, cc, a, buf):
                    is_s = (n == 128)
                    ch = c0 + cc + a * 22
                    ub, Lub = buf
                    ps2, Lp2 = psF.get()
                    for j in range(3):
                        if not is_s:
                            V("pe", "matmul", [LDg, Lub], [Lp2], ps2[:, :], lhsT=Dg[:, cc, a, j, :], rhs=ub[:, j:j + 512], start=(j == 0), stop=(j == 2))
                        else:
                            V("pe", "matmul", [LDg, Lub], [Lp2], ps2[:, 0:128], lhsT=Dg[:, cc, a, j, :], rhs=ub[:, :, j:j + 8], start=(j == 0), stop=(j == 2))
                    resmap[(ti, cc, a)] = (ps2, Lp2, ch)
                    if a == 1:
                        (pv, Lpv, chv) = resmap.pop((ti, cc, 0)); (pg, Lpg, chg) = resmap.pop((ti, cc, 1))
                        sg, Lsg = t2k.get()
                        V("act", "activation", [Lpg, Lcbv], [Lsg], out=sg[:, :n], in_=pg[:, :n], func=AF.Silu, bias=cbv[:, l, chg:chg + 1])
                        V("dve", "scalar_tensor_tensor", [Lpv, Lsg, Lcbv], [Lact[ti]], out=actT[:, cc, t0:t0 + n], in0=pv[:, :n], scalar=cbv[:, l, chv:chv + 1], in1=sg[:, :n], op0=ALU.add, op1=ALU.mult)

                def emit_down(ti, t0, n):
                    for oc in range(8):
                        ps, Lp = psF.get()
                        for cc in range(ng):
                            V("pe", "matmul", [Lwd_, Lact[ti]], [Lp], ps[:, :n], lhsT=wd_[:, cc, oc * 128:(oc + 1) * 128], rhs=actT[:, cc, t0:t0 + n], start=(cc == 0), stop=(cc == ng - 1))
                        V("dve", "tensor_tensor", [Lp, Lx[ti]], [Lx[ti]], out=xT[:, oc, t0:t0 + n], in0=xT[:, oc, t0:t0 + n], in1=ps[:, :n], op=ALU.add)

                for ti, (t0, n) in enumerate(tiles):
                    ui = 0
                    for cc in range(ng):
                        for a in range(2):
                            buf = emit_up(ti, t0, n, cc, a)
                            if pend_conv:
                                emit_conv(*pend_conv.pop(0))
                            pend_conv.append((ti, t0, n, cc, a, buf))
                            if extra:
                                for _ in range(min(EXTRA_K[0], len(extra))):
                                    emit_captured(extra.pop(0))
                            if ui == 2 and pend_down:
                                emit_down(*pend_down.pop(0))
                            ui += 1
                    pend_down.append((ti, t0, n))
                while pend_conv:
                    emit_conv(*pend_conv.pop(0))
                while pend_down:
                    emit_down(*pend_down.pop(0))
            while extra:
                emit_captured(extra.pop(0))
            if hi == 1:
                DMA("sp", convp_out[:, l], convp_sb, [Lcp], [])
                DMA("sp", convs_out[:, l], convs_sb, [Lcs], [])
                S.final_wait("sp", [Lcp, Lcs])

        out_L = []
        for hi in range(2):
            tiles = HT[hi]
            goff = GOFF[hi]
            nh = sum(n for _, n in tiles)
            S.barrier()
            DMA("sp", xT[:, :, 0:nh], xT_in[:, :, goff:goff + nh], [], [Lx[i] for i in range(len(tiles))])
            yaT = arena_b[:, 0:4 * NH].rearrange("p (c n) -> p c n", c=4)
            wo = arena_b[:, 5120:5120 + 8192].rearrange("p (c n) -> p c n", c=8)
            oT = arena_b[:, 5120 + 8192:5120 + 8192 + 9216].rearrange("p (c n) -> p c n", c=8)
            for l in range(nlayers):
                Lya = [LT() for _ in range(3)]; Lwo = LT(); LoT = [LT() for _ in range(3)]
                norm_to_h(l, "gmix", tiles)
                if "a" not in skip:
                    if hi == 0 and l == 0:
                        S.barrier()
                        s5_setup(0)
                    s5_phase(l, hi, tiles, yaT, Lya)
                if "b" not in skip:
                    phase_B(l, tiles, yaT, Lya, wo, Lwo)
                if "c" not in skip:
                    phase_C(l, hi, tiles, oT, LoT)
                if "d" not in skip:
                    phase_D(l, tiles, oT, LoT, wo, Lwo)
                if "F" not in skip:
                    norm_to_h(l, "gffn", tiles)
                    extra = None
                    if hi == 0 and l + 1 < nlayers and "a" not in skip:
                        CAP[0] = []
                        s5_setup(l + 1)
                        extra = CAP[0]; CAP[0] = None
                        EXTRA_K[0] = len(extra) // 90 + 1
                    ffn(l, hi, tiles, extra)
            S.barrier()
            gt, Lg = small["gfin"]
            yo = arena_f[:, 0:4096].rearrange("p (c n) -> p c n", c=8); Lyo = LT()
            for ti, (t0, n) in enumerate(tiles):
                r, Lr = rms_rstd(ti, t0, n, 1.0 / D)
                for c in range(8):
                    V("dve", "scalar_tensor_tensor", [Lx[ti], Lg, Lr], [Lyo], out=yo[:, c, :n], in0=xT[:, c, t0:t0 + n], scalar=gt[:, c:c + 1], in1=r[:, :n], op0=ALU.mult, op1=ALU.mult)
                DMA("sp", yT_out[:, :, goff + t0:goff + t0 + n], yo[:, :, :n], [Lyo], [])
            out_L.append(Lyo)
            S.final_wait("sp", [Lyo])
        S.barrier()
        S.emit(block)
    return nc


def _mk_consts():
    cf = {}
    cf["ident"] = np.eye(128, dtype=np.float32)
    jc = np.arange(128) // 16
    tc_t = np.arange(128) // 16
    cf["tmask"] = (tc_t[None, :] >= jc[:, None]).astype(np.float32)
    ps = np.zeros((128, 128), np.float32)
    for p in range(64):
        ps[64 + p, p] = -1.0
        ps[p, 64 + p] = 1.0
    cf["pswap"] = ps
    nv = np.array([7, 6, 5, 4, 3, 2, 1, 0, -1, -2, -3, -4, -5, -6, -7, -8, 1, 2, 3, 4, 5, 6, 7, 8], np.float32)
    cf["nvec"] = np.broadcast_to(nv, (128, 24)).copy()
    cf["kvec"] = np.broadcast_to(np.arange(256, dtype=np.float32), (128, 256)).copy()
    top = (np.arange(128) < 64)
    cf["sgn"] = np.where(top, -1.0, 1.0).astype(np.float32)[:, None]
    cf["mre"] = top.astype(np.float32)[:, None]
    cf["nmre"] = -top.astype(np.float32)[:, None]
    cf["nmim"] = -(~top).astype(np.float32)[:, None]
    cf["eps"] = np.full((128, 1), EPS, np.float32)
    cf["halfpi"] = np.full((128, 1), math.pi / 2, np.float32)
    cf["zero"] = np.zeros((128, 1), np.float32)
    i = np.arange(128, dtype=np.float64)
    qs = np.zeros((128, 8)); ks = np.zeros((128, 8))
    for h in range(4):
        g = 1.0 - 2.0 ** (-5 - h)
        qs[:, h] = g ** (i + 1); ks[:, h] = (128 ** -0.5) * g ** (-(i + 1))
        qs[:, 4 + h] = g ** ((i % 8) + 1); ks[:, 4 + h] = (128 ** -0.5) * g ** (-((i % 8) + 1))
    cf["qsc"] = qs.astype(np.float32); cf["ksc"] = ks.astype(np.float32)
    qk_ = np.zeros((128, 2, 4, 2)); gc_ = np.zeros((128, 2, 4))
    for kd in range(2):
        for h in range(4):
            qk_[:, kd, h, 0] = qs[:, kd * 4 + h]; qk_[:, kd, h, 1] = ks[:, kd * 4 + h]
            gc_[:, kd, h] = (1.0 - 2.0 ** (-5 - h)) ** (128 if kd == 0 else 8)
    cf["qksc"] = qk_.reshape(128, 16).astype(np.float32); cf["gct"] = gc_.reshape(128, 8).astype(np.float32)
    rm = np.zeros((128, 16), np.float32)
    for s in range(16):
        rm[s * 8:(s + 1) * 8, s] = 1.0
    cf["rowmask"] = rm
    j = np.arange(128)
    cf["cmask_p"] = (j[:, None] <= j[None, :]).astype(np.float32)
    cf["cmask_s"] = ((j[:, None] <= j[None, :]) & ((j[:, None] // 8) == (j[None, :] // 8))).astype(np.float32)
    off = {}; o = 0; parts = []
    for k, v in cf.items():
        off[k] = (o, v.shape[1]); o += v.shape[1]; parts.append(v)
    cfa = np.ascontiguousarray(np.concatenate(parts, axis=1))
    cb = {"ident": np.eye(128, dtype=np.float32), "ones": np.ones((128, 128), np.float32)}
    offb = {}; o = 0; partsb = []
    for k, v in cb.items():
        offb[k] = (o, v.shape[1]); o += v.shape[1]; partsb.append(v)
    cba = np.ascontiguousarray(np.concatenate(partsb, axis=1)).astype(ml_dtypes.bfloat16)
    half = 64
    inv = (10000.0 ** (-np.arange(half, dtype=np.float32) / half)).astype(np.float32)
    rope = np.zeros((128, 17, 2, 64), np.float32)
    for b in range(17):
        if b < 16:
            pos = (b * 128 + np.arange(128)).astype(np.float32)
        else:
            pos = (PAST + (np.arange(128) % 8)).astype(np.float32)
        ang = (pos[:, None] * inv[None, :]).astype(np.float32)
        rope[:, b, 0, :] = np.cos(ang); rope[:, b, 1, :] = np.sin(ang)
    return cfa, off, cba, offb, rope


CF_ARR, CF_OFF, CB_ARR, CB_OFF, ROPE_ARR = _mk_consts()
CF_N = CF_ARR.shape[1]
CB_N = CB_ARR.shape[1]

_NC_CACHE = {}


def _stack(a, b):
    return np.ascontiguousarray(np.concatenate([a, b], axis=0))


def make_in_maps(inp):
    f = lambda a: np.ascontiguousarray(np.asarray(a, dtype=np.float32))
    shared = {}
    for k, src in [("w_in", "w_in"), ("w_glu", "w_glu"), ("w_ssm_out", "w_ssm_out"), ("w_ret_out", "w_ret_out"), ("w_o", "w_o"), ("w_up", "w_up"), ("w_down", "w_down")]:
        shared[k] = f(inp[src])
    pl = lambda v: np.ascontiguousarray(f(v).reshape(DEPTH, -1, 128).transpose(2, 0, 1))
    shared["gmix"] = pl(inp["norm_mix"]); shared["gffn"] = pl(inp["norm_ffn"])
    shared["gfin"] = np.ascontiguousarray(f(inp["norm_final"]).reshape(8, 128).T)
    shared["convw"] = np.ascontiguousarray(f(inp["conv_w"]).reshape(DEPTH, 3, NCH, 128).transpose(3, 0, 1, 2))
    shared["convb"] = np.ascontiguousarray(f(inp["conv_b"]).reshape(DEPTH, NCH, 128).transpose(2, 0, 1))
    lr = f(inp["ssm_lam_re"]).transpose(2, 0, 1)
    li = f(inp["ssm_lam_im"]).transpose(2, 0, 1)
    shared["lamr"] = _stack(lr, lr); shared["lami"] = _stack(li, li)
    shared["logdt"] = np.ascontiguousarray(np.broadcast_to(f(inp["ssm_log_dt"])[None], (128, DEPTH, G)))
    br = f(inp["ssm_b_re"]).transpose(2, 0, 1, 3)
    bi = f(inp["ssm_b_im"]).transpose(2, 0, 1, 3)
    shared["bS"] = _stack(br, bi); shared["bX"] = _stack(bi, br)
    cr = f(inp["ssm_c_re"]).transpose(3, 0, 1, 2)
    ci = f(inp["ssm_c_im"]).transpose(3, 0, 1, 2)
    shared["crD"] = _stack(cr, cr); shared["ciD"] = _stack(ci, ci)
    d = f(inp["ssm_d"]).reshape(DEPTH, G, 16)
    dc = d.transpose(2, 0, 1)
    shared["dcol"] = np.ascontiguousarray(np.tile(dc, (8, 1, 1)))
    shared["cf32"] = CF_ARR; shared["cbf16"] = CB_ARR; shared["rope"] = ROPE_ARR
    xp = f(inp["x_prompt"]); xs = f(inp["x_sample"])
    sre = f(inp["state_ssm_re"]); sim = f(inp["state_ssm_im"]); sret = f(inp["state_ret"]); scv = f(inp["state_conv"])
    maps = []
    for ci_ in range(NCORES):
        m = dict(shared)
        S0 = ci_ * NS
        xt = np.concatenate([xp[ci_], xs[S0:S0 + NS].reshape(NS * DS, D)], axis=0)
        m["xT_in"] = np.ascontiguousarray(xt.T.reshape(8, 128, TOK).transpose(1, 0, 2))
        a = sre[:, S0:S0 + NS].transpose(3, 0, 2, 1)
        b = sim[:, S0:S0 + NS].transpose(3, 0, 2, 1)
        m["s0S"] = _stack(a, b); m["s0X"] = _stack(b, a)
        cv = scv[:, S0:S0 + NS].reshape(DEPTH, NS, 2, NCH, 128).transpose(4, 0, 3, 1, 2)
        m["conv0"] = np.ascontiguousarray(cv)
        m["sret"] = np.ascontiguousarray(sret[:, S0:S0 + NS])
        maps.append(m)
    return maps


def kernel(**inputs):
    if "nc" not in _NC_CACHE:
        _NC_CACHE["nc"] = build()
    nc = _NC_CACHE["nc"]
    maps = make_in_maps(inputs)
    res = run_bass_kernel_spmd(nc, maps, core_ids=list(range(NCORES)))
    R = res.results
    if "dbg_out" in R[0]:
        _NC_CACHE["dbg"] = [np.asarray(r["dbg_out"]) for r in R]
    B = NCORES
    y_p = np.zeros((B, SEQ, D), np.float32); y_s = np.zeros((B * NS, DS, D), np.float32)
    sre_p = np.zeros((DEPTH, B, G, P), np.float32); sim_p = np.zeros_like(sre_p)
    ret_p = np.zeros((DEPTH, B, 4, 128, 256), np.float32)
    cv_p = np.zeros((DEPTH, B, 2, 2 * DFF), np.float32)
    sre_s = np.zeros((DEPTH, B * NS, G, P), np.float32); sim_s = np.zeros_like(sre_s)
    ret_s = np.zeros((DEPTH, B * NS, 4, 128, 256), np.float32)
    cv_s = np.zeros((DEPTH, B * NS, 2, 2 * DFF), np.float32)
    for c in range(B):
        r = R[c]
        yt = np.asarray(r["yT_out"]).transpose(1, 0, 2).reshape(D, TOK).T
        y_p[c] = yt[:SEQ]; y_s[c * NS:(c + 1) * NS] = yt[SEQ:].reshape(NS, DS, D)
        sp = np.asarray(r["ssmp_out"])
        sre_p[:, c] = sp[:64].transpose(1, 2, 0); sim_p[:, c] = sp[64:].transpose(1, 2, 0)
        ss = np.asarray(r["ssms_out"])
        sre_s[:, c * NS:(c + 1) * NS] = ss[:64].transpose(1, 3, 2, 0); sim_s[:, c * NS:(c + 1) * NS] = ss[64:].transpose(1, 3, 2, 0)
        ret_p[:, c] = np.asarray(r["retp_out"]); ret_s[:, c * NS:(c + 1) * NS] = np.asarray(r["rets_out"])
        cp = np.asarray(r["convp_out"])
        cv_p[:, c] = cp.transpose(1, 3, 2, 0).reshape(DEPTH, 2, 2 * DFF)
        cs = np.asarray(r["convs_out"])
        cv_s[:, c * NS:(c + 1) * NS] = cs.transpose(1, 3, 4, 2, 0).reshape(DEPTH, NS, 2, 2 * DFF)
    return (y_p, y_s, sre_p, sim_p, ret_p, cv_p, sre_s, sim_s, ret_s, cv_s)
```

```python
import math
from contextlib import ExitStack
import numpy as np
import ml_dtypes
import concourse.bass as bass
import concourse.mybir as mybir
from concourse.bass_utils import run_bass_kernel_spmd

F32 = mybir.dt.float32
BF16 = mybir.dt.bfloat16
ALU = mybir.AluOpType
AF = mybir.ActivationFunctionType

NCORES = 8
D = 1024
DEPTH = 4
SEQ = 2048
NS = 16
DS = 8
TOK = SEQ + NS * DS
G = 32
P = 64
DFF = 2816
NCH = 44
EPS = 1e-6
PAST = 16384
MAGIC = 12582912.0
TWO_PI = 2.0 * math.pi
LAYERS = DEPTH


class LT:
    __slots__ = ("w", "r", "key")

    def __init__(self):
        self.w = {}
        self.r = {}
        self.key = None


class Sched:
    ENGS = ("pe", "act", "dve", "pool", "sp")

    def __init__(self, nc, stack, n_dma):
        self.nc = nc
        self.sem = {}
        self.cnt = {}
        for e in self.ENGS:
            self.sem[e] = stack.enter_context(nc.semaphore("s_" + e))
            self.cnt[e] = 0
        self.n_dma = n_dma
        for i in range(n_dma):
            k = "d%d" % i
            self.sem[k] = stack.enter_context(nc.semaphore("s_" + k))
            self.cnt[k] = 0
        self.seen = {}
        self.prog = {e: [] for e in self.ENGS}
        self.rr = 0

    def _deps(self, eng, reads, writes):
        deps = {}

        def add(d, skip_same):
            for k, v in d.items():
                if skip_same and k == eng:
                    continue
                if deps.get(k, 0) < v:
                    deps[k] = v
        for t in reads:
            add(t.w, eng == "pe")
        for t in writes:
            add(t.w, True)
            add(t.r, True)
        waits = []
        for k, v in deps.items():
            if self.seen.get((eng, k), 0) >= v:
                continue
            self.seen[(eng, k)] = v
            waits.append((k, v))
        return waits

    def _mark(self, me, reads, writes):
        k, v = me
        for t in reads:
            if t.r.get(k, 0) < v:
                t.r[k] = v
        for t in writes:
            if t.w.get(k, 0) < v:
                t.w[k] = v

    def op(self, eng, fn, reads=(), writes=()):
        waits = self._deps(eng, reads, writes)
        self.cnt[eng] += 1
        self._mark((eng, self.cnt[eng]), reads, writes)
        self.prog[eng].append((waits, fn, (eng, 1)))

    def dma(self, q, fn, reads=(), writes=(), key=None):
        if key is None:
            lt = writes[0] if len(writes) else reads[0]
            if lt.key is None:
                lt.key = "d%d" % self.rr
                self.rr = (self.rr + 1) % self.n_dma
            key = lt.key
        waits = self._deps(q, reads, writes)
        self.cnt[key] += 16
        self._mark((key, self.cnt[key]), reads, writes)
        self.prog[q].append((waits, fn, (key, 16)))

    def barrier(self):
        for eng in self.ENGS:
            waits = []
            for k, v in self.cnt.items():
                if k == eng or v == 0:
                    continue
                if self.seen.get((eng, k), 0) >= v:
                    continue
                self.seen[(eng, k)] = v
                waits.append((k, v))
            if waits:
                self.prog[eng].append((waits, None, None))

    def final_wait(self, eng, tiles):
        waits = self._deps(eng, tiles, tiles)
        self.prog[eng].append((waits, None, None))

    def emit(self, block):
        def mk(ename):
            def body(e):
                for waits, fn, inc in self.prog[ename]:
                    for k, v in waits:
                        e.wait_ge(self.sem[k], v)
                    if fn is not None:
                        fn(e).then_inc(self.sem[inc[0]], inc[1])
            return body
        block.tensor(mk("pe"))
        block.scalar(mk("act"))
        block.vector(mk("dve"))
        block.gpsimd(mk("pool"))
        block.sync(mk("sp"))


class Carver:
    def __init__(self, ap2d):
        self.ap = ap2d
        self.off = 0
        self.n = ap2d.shape[1]

    def take(self, shape):
        n = 1
        for d in shape[1:]:
            n *= d
        assert self.off + n <= self.n, ("arena overflow", self.off, n, self.n)
        v = self.ap[:, self.off:self.off + n]
        self.off += n
        if len(shape) == 2:
            return v
        names = " ".join("d%d" % i for i in range(len(shape) - 1))
        kw = {"d%d" % i: shape[i + 1] for i in range(len(shape) - 1)}
        return v.rearrange("p (%s) -> p %s" % (names, names), **kw)


class Pool_:
    def __init__(self, items):
        self.items = items
        self.i = 0

    def get(self):
        it = self.items[self.i]
        self.i = (self.i + 1) % len(self.items)
        return it


def build(nlayers=LAYERS, skip=""):
    nc = bass.Bass("TRN2", target_bir_lowering=False)

    def IN(name, shape, dt=F32):
        return nc.dram_tensor(name, list(shape), dt, kind="ExternalInput").ap()

    def OUT(name, shape):
        return nc.dram_tensor(name, list(shape), F32, kind="ExternalOutput").ap()

    xT_in = IN("xT_in", [128, 8, TOK])
    w_in = IN("w_in", [DEPTH, D, 5632]); w_glu = IN("w_glu", [DEPTH, 512, 512]); w_sso = IN("w_ssm_out", [DEPTH, 512, D])
    w_ro = IN("w_ret_out", [DEPTH, D, D]); w_o = IN("w_o", [DEPTH, D, D]); w_up = IN("w_up", [DEPTH, D, 5632])
    w_dn = IN("w_down", [DEPTH, DFF, D])
    gmix = IN("gmix", [128, DEPTH, 8]); gffn = IN("gffn", [128, DEPTH, 8]); gfin = IN("gfin", [128, 8])
    convw = IN("convw", [128, DEPTH, 3, NCH]); convb = IN("convb", [128, DEPTH, NCH])
    lamr = IN("lamr", [128, DEPTH, G]); lami = IN("lami", [128, DEPTH, G]); logdt = IN("logdt", [128, DEPTH, G])
    bS_in = IN("bS", [128, DEPTH, G, 16]); bX_in = IN("bX", [128, DEPTH, G, 16])
    crD_in = IN("crD", [128, DEPTH, G, 16]); ciD_in = IN("ciD", [128, DEPTH, G, 16])
    dcol_in = IN("dcol", [128, DEPTH, G])
    s0S_in = IN("s0S", [128, DEPTH, G, NS]); s0X_in = IN("s0X", [128, DEPTH, G, NS])
    conv0_in = IN("conv0", [128, DEPTH, NCH, NS, 2])
    sret_in = IN("sret", [DEPTH, NS, 4, 128, 256])
    cf_in = IN("cf32", [128, CF_N]); cb_in = IN("cbf16", [128, CB_N], BF16)
    rope_in = IN("rope", [128, 17, 2, 64])

    yT_out = OUT("yT_out", [128, 8, TOK])
    ssmp_out = OUT("ssmp_out", [128, DEPTH, G]); ssms_out = OUT("ssms_out", [128, DEPTH, G, NS])
    retp_out = OUT("retp_out", [DEPTH, 4, 128, 256]); rets_out = OUT("rets_out", [DEPTH, NS, 4, 128, 256])
    convp_out = OUT("convp_out", [128, DEPTH, NCH, 2]); convs_out = OUT("convs_out", [128, DEPTH, NCH, NS, 2])
    dbg_out = OUT("dbg_out", [128, 1024]) if "G" in skip else None
    s5cb = nc.dram_tensor("s5cb", [DEPTH, 8, 128, 5 * 4 * 128], BF16, kind="Internal").ap()
    s5cf = nc.dram_tensor("s5cf", [DEPTH, 8, 128, 64 + 192], F32, kind="Internal").ap()
    rcar = nc.dram_tensor("rcar", [DEPTH, 4, 128, 256], F32, kind="Internal").ap()

    with ExitStack() as st:
        def sb(name, shape, dt=F32):
            return st.enter_context(nc.sbuf_tensor("sb_" + name, list(shape), dt))

        def pst(name, shape, dt=F32):
            return st.enter_context(nc.psum_tensor(name, list(shape), dt))

        S = Sched(nc, st, n_dma=24)
        block = st.enter_context(nc.Block())

        CAP = [None]

        def V(eng, method, reads, writes, *a, **kw):
            if CAP[0] is not None:
                CAP[0].append((eng, method, reads, writes, a, kw))
                return
            S.op(eng, lambda e: getattr(e, method)(*a, **kw), reads, writes)

        def DMA(q, out, in_, reads, writes, key=None):
            if CAP[0] is not None:
                CAP[0].append(("__dma__", q, out, in_, reads, writes))
                return
            S.dma(q, lambda e: e.dma_start(out=out, in_=in_), reads, writes, key)

        def emit_captured(item):
            if item[0] == "__dma__":
                _, q, out, in_, reads, writes = item
                DMA(q, out, in_, reads, writes)
            else:
                e_, m_, r_, w_, a_, kw_ = item
                V(e_, m_, r_, w_, *a_, **kw_)

        NH = 1152
        xT = sb("xT", [128, 8, NH]); Lx = [LT() for _ in range(3)]
        hT = sb("hT", [128, 8, NH], BF16); Lh = [LT() for _ in range(3)]
        HT = [[(0, 512), (512, 512)], [(0, 512), (512, 512), (1024, 128)]]
        GOFF = [0, 1024]

        cf = sb("cf", [128, CF_N]); Lcf = LT()
        cb = sb("cb", [128, CB_N], BF16); Lcb = LT()
        rope = sb("rope", [128, 17, 2, 64]); Lrope = LT()
        DMA("sp", cf[:], cf_in, [], [Lcf]); DMA("sp", cb[:], cb_in, [], [Lcb]); DMA("sp", rope[:], rope_in, [], [Lrope])

        def cfs(name):
            o, n = CF_OFF[name]
            return cf[:, o:o + n]

        def cbs(name):
            o, n = CB_OFF[name]
            return cb[:, o:o + n]
        ident_f = cfs("ident"); tmask = cfs("tmask"); pswap = cfs("pswap"); nvec = cfs("nvec"); kvec = cfs("kvec")
        sgn = cfs("sgn"); mre = cfs("mre"); nmre = cfs("nmre"); nmim = cfs("nmim"); epsc = cfs("eps")
        qsc = cfs("qsc"); ksc = cfs("ksc"); qksc = cfs("qksc"); gct = cfs("gct"); rowmask = cfs("rowmask"); cmask_p = cfs("cmask_p"); cmask_s = cfs("cmask_s")
        ident_b = cbs("ident"); ones_b = cbs("ones")

        small = {}
        for nm, src, shp in [("gmix", gmix, [128, DEPTH, 8]), ("gffn", gffn, [128, DEPTH, 8]), ("gfin", gfin, [128, 8]),
                             ("convw", convw, [128, DEPTH, 3, NCH]), ("convb", convb, [128, DEPTH, NCH]),
                             ("lamr", lamr, [128, DEPTH, G]), ("lami", lami, [128, DEPTH, G]), ("logdt", logdt, [128, DEPTH, G]),
                             ("dcol", dcol_in, [128, DEPTH, G])]:
            t = sb("sm_" + nm, shp); L = LT()
            DMA("sp", t[:], src, [], [L])
            small[nm] = (t, L)

        Wlast = sb("Wlast", [128, DEPTH, G]); Mclast = sb("Mclast", [128, DEPTH, G]); Mslast = sb("Mslast", [128, DEPTH, G]); Lcar = LT()
        convcar = sb("convcar", [128, DEPTH, NCH, 2]); Lccar = LT()
        ssm_p_sb = sb("ssm_p_sb", [128, DEPTH, G]); Lssp = LT()
        ssm_s_sb = sb("ssm_s_sb", [128, DEPTH, G, NS]); Lsss = LT()
        Lrcar = [LT() for _ in range(DEPTH)]

        psBig = pst("psbig", [128, 2048])
        psF = Pool_([(psBig[:, i * 512:(i + 1) * 512], LT()) for i in range(4)])
        psS = Pool_([(pst("pss%d" % i, [128, 512]), LT()) for i in range(2)])
        psB = Pool_([(pst("psb%d" % i, [128, 1024], BF16), LT()) for i in range(2)])
        t2k = Pool_([(sb("t2k%d" % i, [128, 512])[:], LT()) for i in range(5)])
        tb1k = Pool_([(sb("tb1k%d" % i, [128, 512], BF16)[:], LT()) for i in range(4)])
        rstd_p = Pool_([(sb("rstd%d" % i, [128, 512])[:], LT()) for i in range(2)])

        ARF_N = 7424
        ARB_N = 39700
        arena_f = sb("arena_f", [128, ARF_N]); arena_b = sb("arena_b", [128, ARB_N], BF16)

        def wload(dst, src, Ld):
            DMA("pool", dst, src, [], [Ld])

        def wview(w, l):
            return w[l].rearrange("(c p) n -> p c n", p=128)

        def rms_rstd(ti, t0, n, dscale):
            ps, Lp = psF.get()
            for c in range(8):
                sq, Lsq = tb1k.get()
                V("act", "activation", [Lx[ti]], [Lsq], out=sq[:, :n], in_=xT[:, c, t0:t0 + n], func=AF.Square)
                V("pe", "matmul", [Lsq, Lcb], [Lp], ps[:, :n], lhsT=ones_b, rhs=sq[:, :n], start=(c == 0), stop=(c == 7))
            r, Lr = rstd_p.get()
            V("act", "activation", [Lp, Lcf], [Lr], out=r[:, :n], in_=ps[:, :n], func=AF.Sqrt, bias=epsc[:, 0:1], scale=dscale)
            V("dve", "reciprocal", [Lr], [Lr], out=r[:, :n], in_=r[:, :n])
            return r, Lr

        def norm_to_h(l, which, tiles):
            gt, Lg = small[which]
            for ti, (t0, n) in enumerate(tiles):
                r, Lr = rms_rstd(ti, t0, n, 1.0 / D)
                for c in range(8):
                    eng = "dve"
                    V(eng, "scalar_tensor_tensor", [Lx[ti], Lg, Lr], [Lh[ti]], out=hT[:, c, t0:t0 + n], in0=xT[:, c, t0:t0 + n],
                      scalar=gt[:, l, c:c + 1], in1=r[:, :n], op0=ALU.mult, op1=ALU.mult)

        def proj_fm(ps, Lp, wt, Lw, col0, nk, act_fn, Lact, n):
            for kc in range(nk):
                V("pe", "matmul", [Lw] + Lact, [Lp], ps[:, :n], lhsT=wt[:, kc, col0:col0 + 128], rhs=act_fn(kc), start=(kc == 0), stop=(kc == nk - 1))

        def apply_wo(ti, t0, n, wo, Lwo, mix, Lmix):
            for oc in range(8):
                ps, Lp = psF.get()
                proj_fm(ps, Lp, wo, Lwo, oc * 128, 8, lambda kc: mix[:, kc, :n], [Lmix], n)
                V("dve", "tensor_tensor", [Lp, Lx[ti]], [Lx[ti]], out=xT[:, oc, t0:t0 + n], in0=xT[:, oc, t0:t0 + n], in1=ps[:, :n], op=ALU.add)

        def tiles_overlapping(tiles, a, b):
            return [ti for ti, (t0, n) in enumerate(tiles) if t0 < b and t0 + n > a]

        def range_reduce(dst, src, Lt, tmp, add_half_pi):
            if add_half_pi:
                V("dve", "tensor_scalar", [Lt], [Lt], out=dst, in0=src, scalar1=math.pi / 2, scalar2=None, op0=ALU.add)
                src = dst
            V("dve", "tensor_scalar", [Lt], [Lt], out=tmp, in0=src, scalar1=1.0 / TWO_PI, scalar2=MAGIC, op0=ALU.mult, op1=ALU.add)
            V("dve", "tensor_scalar", [Lt], [Lt], out=tmp, in0=tmp, scalar1=MAGIC, scalar2=TWO_PI, op0=ALU.subtract, op1=ALU.mult)
            V("dve", "tensor_tensor", [Lt], [Lt], out=dst, in0=src, in1=tmp, op=ALU.subtract)

        GG = 4
        NSUB = G // GG
        KP = 128
        S5C_L = {}

        def s5_setup(l):
            lamr_t, Ll1 = small["lamr"]; lami_t, Ll2 = small["lami"]; ldt_t, Ll3 = small["logdt"]; dcol_t, Ll4 = small["dcol"]
            Lsmall = [Ll1, Ll2, Ll3, Ll4, Lcf]
            cfA = Carver(arena_f[:, :]); cbA = Carver(arena_b[:, 36368:])
            s5f = [None] + [cfA.take([128, GG, 128]) for _ in range(5)]; Ls5f = [None] + [LT() for _ in range(5)]
            Pb = cbA.take([128, 3, GG, 128]); LPb = LT()
            Ab = cbA.take([128, 3, GG, 128]); LAb = LT()

            class SCtx:
                pass
            sctx = []
            for i in range(4):
                c = SCtx()
                c.s5sm = cfA.take([128, 16, GG]); c.Lsm = LT()
                c.Ep = cfA.take([128, 6, GG, 24]); c.LEp = LT()
                c.bsx = cfA.take([128, 3, GG, 16]); c.Lbsx = LT()
                c.bSt = cfA.take([128, 4, GG, 16]); c.LbSt = LT()
                c.tsm = cfA.take([128, GG, 32]); c.Ltsm = LT()
                sctx.append(c)

            def small_stage(c, sq):
                g0 = sq * GG
                gs = slice(g0, g0 + GG)
                s5sm, Lsm, Ep, LEp, bsx, Lbsx, bSt, LbSt = c.s5sm, c.Lsm, c.Ep, c.LEp, c.bsx, c.Lbsx, c.bSt, c.LbSt
                s5f = [c.tsm]; Ls5f = [c.Ltsm]
                sm = lambda i: s5sm[:, i, :]
                Er = Ep[:, 4]; Ei = Ep[:, 5]
                V("act", "activation", Lsmall, [Lsm], out=sm(0), in_=ldt_t[:, l, gs], func=AF.Exp)
                V("dve", "tensor_tensor", Lsmall + [Lsm], [Lsm], out=sm(1), in0=lamr_t[:, l, gs], in1=sm(0), op=ALU.mult)
                V("dve", "tensor_tensor", Lsmall + [Lsm], [Lsm], out=sm(2), in0=lami_t[:, l, gs], in1=sm(0), op=ALU.mult)
                nv = nvec.unsqueeze(1).to_broadcast([128, GG, 24])
                V("dve", "tensor_tensor", [Lsm, Lcf], [LEp], out=Ep[:, 0], in0=sm(2).unsqueeze(2).to_broadcast([128, GG, 24]), in1=nv, op=ALU.mult)
                V("dve", "tensor_tensor", [Lsm, Lcf], [LEp], out=Ep[:, 1], in0=sm(1).unsqueeze(2).to_broadcast([128, GG, 24]), in1=nv, op=ALU.mult)
                range_reduce(Ep[:, 2], Ep[:, 0], LEp, Ep[:, 3], False)
                V("act", "activation", [LEp], [LEp], out=Ep[:, 5], in_=Ep[:, 2], func=AF.Sin)
                range_reduce(Ep[:, 2], Ep[:, 0], LEp, Ep[:, 3], True)
                V("act", "activation", [LEp], [LEp], out=Ep[:, 4], in_=Ep[:, 2], func=AF.Sin)
                V("act", "activation", [LEp], [LEp], out=Ep[:, 1], in_=Ep[:, 1], func=AF.Exp)
                V("dve", "tensor_tensor", [LEp], [LEp], out=Ep[:, 4], in0=Ep[:, 4], in1=Ep[:, 1], op=ALU.mult)
                V("dve", "tensor_tensor", [LEp], [LEp], out=Ep[:, 5], in0=Ep[:, 5], in1=Ep[:, 1], op=ALU.mult)
                Er = Ep[:, 4]; Ei = Ep[:, 5]
                E1r = Er[:, :, 16]; E1i = Ei[:, :, 16]
                lr_ = lamr_t[:, l, gs]; li_ = lami_t[:, l, gs]
                RS = Lsmall + [Lsm, LEp]
                V("dve", "tensor_scalar", RS, [Lsm], out=sm(3), in0=E1r, scalar1=-1.0, scalar2=None, op0=ALU.add)
                V("dve", "tensor_tensor", RS, [Lsm], out=sm(4), in0=lr_, in1=lr_, op=ALU.mult)
                V("dve", "tensor_tensor", RS, [Lsm], out=sm(8), in0=li_, in1=li_, op=ALU.mult)
                V("dve", "tensor_tensor", RS, [Lsm], out=sm(4), in0=sm(4), in1=sm(8), op=ALU.add)
                V("dve", "reciprocal", RS, [Lsm], out=sm(5), in_=sm(4))
                V("dve", "tensor_tensor", RS, [Lsm], out=sm(6), in0=sm(3), in1=lr_, op=ALU.mult)
                V("dve", "tensor_tensor", RS, [Lsm], out=sm(8), in0=E1i, in1=li_, op=ALU.mult)
                V("dve", "tensor_tensor", RS, [Lsm], out=sm(6), in0=sm(6), in1=sm(8), op=ALU.add)
                V("dve", "tensor_tensor", RS, [Lsm], out=sm(6), in0=sm(6), in1=sm(5), op=ALU.mult)
                V("dve", "tensor_tensor", RS, [Lsm], out=sm(7), in0=E1i, in1=lr_, op=ALU.mult)
                V("dve", "tensor_tensor", RS, [Lsm], out=sm(8), in0=sm(3), in1=li_, op=ALU.mult)
                V("dve", "tensor_tensor", RS, [Lsm], out=sm(7), in0=sm(7), in1=sm(8), op=ALU.subtract)
                V("dve", "tensor_tensor", RS, [Lsm], out=sm(7), in0=sm(7), in1=sm(5), op=ALU.mult)
                V("act", "activation", RS, [Lsm], out=sm(10), in_=sm(1), func=AF.Exp, scale=8.0)
                V("dve", "tensor_scalar", RS, [Lsm], out=sm(11), in0=sm(2), scalar1=8.0, scalar2=None, op0=ALU.mult)
                bc16 = lambda a: a.unsqueeze(2).to_broadcast([128, GG, 16])
                RB = [LbSt, Lsm, Lcf]
                V("dve", "tensor_scalar", RB, [Lbsx], out=bsx[:, 0], in0=bSt[:, 1], scalar1=sgn[:, 0:1], scalar2=None, op0=ALU.mult)
                t0_ = s5f[0][:, :, 0:16]; t1_ = s5f[0][:, :, 16:32]
                V("dve", "tensor_tensor", RB, [Ls5f[0]], out=t0_, in0=bSt[:, 0], in1=bc16(sm(6)), op=ALU.mult)
                V("dve", "tensor_tensor", RB + [Lbsx], [Ls5f[0]], out=t1_, in0=bsx[:, 0], in1=bc16(sm(7)), op=ALU.mult)
                V("dve", "tensor_tensor", [Ls5f[0]], [Lbsx], out=bsx[:, 1], in0=t0_, in1=t1_, op=ALU.add)
                V("dve", "tensor_tensor", RB + [Lbsx], [Ls5f[0]], out=t0_, in0=bsx[:, 0], in1=bc16(sm(6)), op=ALU.mult)
                V("dve", "tensor_tensor", RB, [Ls5f[0]], out=t1_, in0=bSt[:, 0], in1=bc16(sm(7)), op=ALU.mult)
                V("dve", "tensor_tensor", [Ls5f[0]], [Lbsx], out=bsx[:, 2], in0=t0_, in1=t1_, op=ALU.subtract)

            def big_stage(c, sq):
                g0 = sq * GG
                gs = slice(g0, g0 + GG)
                s5sm, Lsm, Ep, LEp, bsx, Lbsx, bSt, LbSt = c.s5sm, c.Lsm, c.Ep, c.LEp, c.bsx, c.Lbsx, c.bSt, c.LbSt
                sm = lambda i: s5sm[:, i, :]
                Er = Ep[:, 4]; Ei = Ep[:, 5]
                Ls5c = LT()
                bc16 = lambda a: a.unsqueeze(2).to_broadcast([128, GG, 16])
                v4 = lambda a: a.rearrange("p g (j c) -> p g j c", c=16)
                bj = lambda a: a.unsqueeze(2).to_broadcast([128, GG, 8, 16])
                ej = lambda a: a.unsqueeze(3).to_broadcast([128, GG, 8, 16])
                bs_ = bsx[:, 1]; bx_ = bsx[:, 2]
                RP = [LEp, Lbsx]

                def cplx(dst_bf, e_r, e_i, sign_mode, Ldst):
                    a_, b_ = (e_r, e_i) if sign_mode == 0 else (e_i, e_r)
                    V("dve", "tensor_tensor", RP, [Ls5f[1]], out=v4(s5f[1]), in0=ej(a_), in1=bj(bs_), op=ALU.mult)
                    V("dve", "tensor_tensor", RP, [Ls5f[2]], out=v4(s5f[2]), in0=ej(b_), in1=bj(bx_), op=ALU.mult)
                    V("dve", "tensor_tensor", [Ls5f[1], Ls5f[2]], [Ldst], out=dst_bf, in0=s5f[1], in1=s5f[2],
                      op=(ALU.add if sign_mode == 0 else ALU.subtract))
                Pp, LPp = tb1k.get(); Ppt, LPpt = tb1k.get()
                g3 = lambda a: a.rearrange("p (g n) -> p g n", g=GG)
                cplx(g3(Pp), Er[:, :, 0:8], Ei[:, :, 0:8], 0, LPp)
                cplx(g3(Ppt), Er[:, :, 0:8], Ei[:, :, 0:8], 1, LPpt)
                cplx(Pb[:, 0], Er[:, :, 8:16], Ei[:, :, 8:16], 0, LPb)
                ECr = Er[:, :, 16:24]; ECi = Ei[:, :, 16:24]
                crD_ = bSt[:, 2]; ciD_ = bSt[:, 3]
                RC = [LEp, LbSt]
                V("dve", "tensor_tensor", RC, [Ls5f[1]], out=v4(s5f[1]), in0=ej(ECr), in1=bj(crD_), op=ALU.mult)
                V("dve", "tensor_tensor", RC, [Ls5f[2]], out=v4(s5f[2]), in0=ej(ECi), in1=bj(ciD_), op=ALU.mult)
                V("dve", "tensor_tensor", [Ls5f[1], Ls5f[2]], [Ls5f[3]], out=s5f[3], in0=s5f[1], in1=s5f[2], op=ALU.subtract)
                V("dve", "tensor_tensor", RC, [Ls5f[1]], out=v4(s5f[1]), in0=ej(ECi), in1=bj(crD_), op=ALU.mult)
                V("dve", "tensor_tensor", RC, [Ls5f[2]], out=v4(s5f[2]), in0=ej(ECr), in1=bj(ciD_), op=ALU.mult)
                V("dve", "tensor_tensor", [Ls5f[1], Ls5f[2]], [Ls5f[4]], out=s5f[4], in0=s5f[1], in1=s5f[2], op=ALU.add)
                V("dve", "tensor_scalar", [Ls5f[3], Lcf], [Ls5f[1]], out=s5f[1], in0=s5f[3], scalar1=mre[:, 0:1], scalar2=None, op0=ALU.mult)
                V("dve", "scalar_tensor_tensor", [Ls5f[4], Ls5f[1], Lcf], [LPb], out=Pb[:, 1], in0=s5f[4], scalar=nmim[:, 0:1], in1=s5f[1], op0=ALU.mult, op1=ALU.add)
                V("dve", "tensor_scalar", [Ls5f[4], Lcf], [Ls5f[2]], out=s5f[2], in0=s5f[4], scalar1=nmre[:, 0:1], scalar2=None, op0=ALU.mult)
                V("dve", "scalar_tensor_tensor", [Ls5f[3], Ls5f[2], Lcf], [LPb], out=Pb[:, 2], in0=s5f[3], scalar=nmim[:, 0:1], in1=s5f[2], op0=ALU.mult, op1=ALU.add)
                ps, Lp = psS.get()
                for g in range(GG):
                    V("pe", "matmul", [LPb], [Lp], ps[:, g * 128:(g + 1) * 128], lhsT=Pb[:, 0, g, :], rhs=Pb[:, 1, g, :], start=True, stop=True)
                V("dve", "tensor_tensor", [Lp, Lcf], [Ls5f[5]], out=s5f[5], in0=ps[:].rearrange("p (g n) -> p g n", g=GG),
                  in1=tmask.unsqueeze(1).to_broadcast([128, GG, 128]), op=ALU.mult)
                for g in range(GG):
                    V("dve", "scalar_tensor_tensor", [Ls5f[5], Lcf, Ll4], [LAb], out=Ab[:, 2, g, :], in0=ident_f, scalar=dcol_t[:, l, g0 + g:g0 + g + 1],
                      in1=s5f[5][:, g, :], op0=ALU.mult, op1=ALU.add)
                pb_, Lpb_ = psB.get()
                for i, (Px, LPx) in enumerate(((Pp, LPp), (Ppt, LPpt))):
                    for g in range(GG):
                        V("pe", "transpose", [LPx, Lcb], [Lpb_], out=pb_[:, (i * GG + g) * 128:(i * GG + g + 1) * 128], in_=Px[:, g * 128:(g + 1) * 128], identity=ident_b)
                V("act", "activation", [Lpb_], [LAb], out=Ab[:, 0:2].rearrange("p a g n -> p (a g n)"), in_=pb_[:, 0:2 * GG * 128], func=AF.Copy)
                DMA("sp", s5cb[l, sq, :, 0:2 * GG * 128], Pb[:, 1:3].rearrange("p a g n -> p (a g n)"), [LPb], [Ls5c])
                DMA("sp", s5cb[l, sq, :, 2 * GG * 128:5 * GG * 128], Ab.rearrange("p a g n -> p (a g n)"), [LAb], [Ls5c])
                DMA("sp", s5cf[l, sq, :, 0:64], s5sm.rearrange("p a g -> p (a g)"), [Lsm], [Ls5c])
                DMA("sp", s5cf[l, sq, :, 64:256], Ep[:, 4:6].rearrange("p a g n -> p (a g n)"), [LEp], [Ls5c])
                S5C_L[(l, sq)] = Ls5c

            for grp in range(NSUB // 4):
                subs = [grp * 4 + i for i in range(4)]
                lists = []
                for i, sq in enumerate(subs):
                    c = sctx[i]
                    for i2, src in enumerate([bS_in, bX_in, crD_in, ciD_in]):
                        DMA("sp", c.bSt[:, i2], src[:, l, sq * GG:(sq + 1) * GG, :], [], [c.LbSt])
                    prev_cap = CAP[0]
                    CAP[0] = []
                    small_stage(c, sq)
                    lists.append(CAP[0]); CAP[0] = prev_cap
                for k in range(max(len(x) for x in lists)):
                    for lst in lists:
                        if k < len(lst):
                            e_, m_, r_, w_, a_, kw_ = lst[k]
                            V(e_, m_, r_, w_, *a_, **kw_)
                for i, sq in enumerate(subs):
                    big_stage(sctx[i], sq)

        def s5_phase(l, hi, tiles, yaT, Lya, prefetch=None):
            S.barrier()
            has_s = (hi == 1)
            KT = KP + (NS if has_s else 0)
            kbase = hi * KP
            PT = [(0, KP)] + ([(KP, NS)] if has_s else [])
            lamr_t, Ll1 = small["lamr"]; lami_t, Ll2 = small["lami"]; ldt_t, Ll3 = small["logdt"]; dcol_t, Ll4 = small["dcol"]
            Lsmall = [Ll1, Ll2, Ll3, Ll4, Lcf]
            cfA = Carver(arena_f[:, :]); cbA = Carver(arena_b[:, 5120:])

            class Ctx:
                pass
            ctxs = []
            for u in range(2):
                b = Ctx()
                b.s5sm = cfA.take([128, 16, GG]); b.Lsm = LT()
                b.Ep = cfA.take([128, 2, GG, 24]); b.LEp = LT()
                b.cosT = cfA.take([128, GG, KP]); b.sinT = cfA.take([128, GG, KP]); b.Ltab = LT()
                b.Yb = cfA.take([128, GG, KP]); b.LY = LT()
                b.Wb = cfA.take([128, GG, KP]); b.LW = LT()
                b.s0t = cfA.take([128, 2, GG, NS]); b.Ls0 = LT()
                b.sfin = cfA.take([128, 4, GG, NS]); b.Lsfin = LT()
                b.ctmp = cfA.take([128, GG, 2]); b.Lctmp = LT()
                b.PbC = cbA.take([128, 2, GG, 128]); b.LPb = LT()
                b.Ab = cbA.take([128, 3, GG, 128]); b.LAb = LT()
                b.Mc = cbA.take([128, GG, KP + NS]); b.Ms = cbA.take([128, GG, KP + NS]); b.LM = LT()
                b.yapt = cbA.take([128, 2, 8, GG * 16]); b.Lyapt = LT()
                ctxs.append(b)
            Uq_p = Pool_([(cbA.take([128, GG, KP + NS]), LT()) for i in range(4)])
            upt_p = Pool_([(cbA.take([128, 2, GG, 8, 16]), LT()) for i in range(4)])
            wu = Pool_([(cbA.take([128, 8, GG * 16]), LT()) for i in range(4)])

            def stage_u(sq):
                g0 = sq * GG
                wut, Lwu = wu.get()
                wload(wut, wview(w_in, l)[:, :, g0 * 16:g0 * 16 + GG * 16], Lwu)
                Uq, LU = Uq_p.get()
                upt, Lupt = upt_p.get()
                for m, (k0, nk) in enumerate(PT):
                    t0 = k0 * 8
                    tl = tiles_overlapping(tiles, t0, t0 + nk * 8)
                    ps, Lp = psF.get()
                    for j in range(8):
                        for c in range(8):
                            lh = hT[:, c, t0:t0 + nk * 8].rearrange("p (k j) -> p k j", j=8)[:, :, j]
                            V("pe", "matmul", [Lh[t] for t in tl] + [Lwu], [Lp], ps[:nk, j * 64:(j + 1) * 64], lhsT=lh, rhs=wut[:, c, :],
                              start=(c == 0), stop=(c == 7))
                    V("act", "activation", [Lp], [Lupt], out=upt[:nk, m].rearrange("p g j c -> p j g c"),
                      in_=ps[:nk, :].rearrange("p (j g c) -> p j g c", j=8, g=GG), func=AF.Copy)
                    pb_, Lpb_ = psB.get()
                    for g in range(GG):
                        V("pe", "transpose", [Lupt, Lcb], [Lpb_], out=pb_[:, g * 128:g * 128 + nk], in_=upt[:nk, m, g].rearrange("p j c -> p (j c)"),
                          identity=ident_b[:nk, :nk])
                    V("dve", "tensor_copy", [Lpb_], [LU], out=Uq[:, :, k0:k0 + nk], in_=pb_[:, 0:GG * 128].rearrange("p (g n) -> p g n", g=GG)[:, :, :nk])
                return Uq, LU


            f2 = lambda a: a.rearrange("p g k -> p (g k)")

            def stepL(b):
                sq = b.sq
                Lc_ = S5C_L[(l, sq)]
                DMA("sp", b.PbC.rearrange("p a g n -> p (a g n)"), s5cb[l, sq, :, 0:2 * GG * 128], [Lc_], [b.LPb])
                DMA("sp", b.Ab.rearrange("p a g n -> p (a g n)"), s5cb[l, sq, :, 2 * GG * 128:5 * GG * 128], [Lc_], [b.LAb])
                DMA("sp", b.s5sm.rearrange("p a g -> p (a g)"), s5cf[l, sq, :, 0:64], [Lc_], [b.Lsm])
                DMA("sp", b.Ep.rearrange("p a g n -> p (a g n)"), s5cf[l, sq, :, 64:256], [Lc_], [b.LEp])
                if has_s:
                    DMA("sp", b.s0t[:, 0], s0S_in[:, l, b.gs, :], [], [b.Ls0]); DMA("sp", b.s0t[:, 1], s0X_in[:, l, b.gs, :], [], [b.Ls0])

            def stepT(b):
                s5sm, Lsm, Yb, Wb, LY, sinT, cosT, Ltab = b.s5sm, b.Lsm, b.Yb, b.Wb, b.LY, b.sinT, b.cosT, b.Ltab
                kv = kvec[:, kbase:kbase + KP].unsqueeze(1).to_broadcast([128, GG, KP])
                V("dve", "tensor_tensor", [Lsm, Lcf], [LY], out=Yb, in0=s5sm[:, 11, :].unsqueeze(2).to_broadcast([128, GG, KP]), in1=kv, op=ALU.mult)
                range_reduce(Wb, Yb, LY, sinT, False)
                V("act", "activation", [LY], [Ltab], out=sinT, in_=Wb, func=AF.Sin)
                V("act", "activation", [LY], [LY], out=Yb, in_=Wb, func=AF.Abs)
                V("act", "activation", [LY, Lcf], [Ltab], out=cosT, in_=Yb, func=AF.Sin, scale=-1.0, bias=cfs("halfpi")[:, 0:1])

            def stepX(b):
                s5sm, Lsm, Yb, Wb, LY, LW, sinT, cosT, Ltab = b.s5sm, b.Lsm, b.Yb, b.Wb, b.LY, b.LW, b.sinT, b.cosT, b.Ltab
                Ab, LAb, Uq, LU = b.Ab, b.LAb, b.Uq, b.LU
                psx, Lpx = psF.get(); psxt, Lpxt = psF.get()
                for g in range(GG):
                    V("pe", "matmul", [LAb, LU], [Lpx], psx[:, g * KP:(g + 1) * KP], lhsT=Ab[:, 0, g, :], rhs=Uq[:, g, 0:KP], start=True, stop=True)
                    V("pe", "matmul", [LAb, LU], [Lpxt], psxt[:, g * KP:(g + 1) * KP], lhsT=Ab[:, 1, g, :], rhs=Uq[:, g, 0:KP], start=True, stop=True)
                V("dve", "tensor_tensor", [Lpx, Ltab], [LY], out=f2(Yb), in0=psx[:], in1=f2(cosT), op=ALU.mult)
                V("dve", "tensor_tensor", [Lpxt, Ltab], [LW], out=f2(Wb), in0=psxt[:], in1=f2(sinT), op=ALU.mult)
                V("dve", "tensor_tensor", [LY, LW], [LY], out=f2(Yb), in0=f2(Yb), in1=f2(Wb), op=ALU.add)
                if hi == 1:
                    V("dve", "tensor_tensor", [Lsm, Lcar], [b.Lctmp], out=b.ctmp[:, :, 0], in0=s5sm[:, 10, :], in1=Wlast[:, l, b.gs], op=ALU.mult)
                    V("dve", "tensor_tensor", [LY, b.Lctmp], [LY], out=Yb[:, :, 0], in0=Yb[:, :, 0], in1=b.ctmp[:, :, 0], op=ALU.add)

            def stepS(b):
                s5sm, Lsm, Yb, Wb, LY, LW, sinT, cosT, Ltab = b.s5sm, b.Lsm, b.Yb, b.Wb, b.LY, b.LW, b.sinT, b.cosT, b.Ltab
                Ab, LAb, Uq, LU, Mc, Ms, LM = b.Ab, b.LAb, b.Uq, b.LU, b.Mc, b.Ms, b.LM
                s0t, Ls0, sfin, Lsfin, LEp, gs = b.s0t, b.Ls0, b.sfin, b.Lsfin, b.LEp, b.gs
                Er = b.Ep[:, 0]; Ei = b.Ep[:, 1]
                for g in range(GG):
                    V("dve", "tensor_tensor_scan", [LY, Lsm, LW], [LW], out=Wb[:, g, :], data0=s5sm[:, 10, g:g + 1].to_broadcast([128, KP]), data1=Yb[:, g, :],
                      initial=0.0, op0=ALU.mult, op1=ALU.add)
                if hi == 0:
                    V("dve", "memset", [], [LM], Mc[:, :, 0:1], 0.0)
                    V("dve", "memset", [], [LM], Ms[:, :, 0:1], 0.0)
                else:
                    V("dve", "tensor_copy", [Lcar], [LM], out=Mc[:, :, 0], in_=Mclast[:, l, gs])
                    V("dve", "tensor_copy", [Lcar], [LM], out=Ms[:, :, 0], in_=Mslast[:, l, gs])
                V("dve", "tensor_tensor", [LW, Ltab], [LM], out=Mc[:, :, 1:KP], in0=Wb[:, :, 0:KP - 1], in1=cosT[:, :, 0:KP - 1], op=ALU.mult)
                V("dve", "tensor_tensor", [LW, Ltab], [LM], out=Ms[:, :, 1:KP], in0=Wb[:, :, 0:KP - 1], in1=sinT[:, :, 0:KP - 1], op=ALU.mult)
                if hi == 0:
                    V("dve", "tensor_copy", [LW], [Lcar], out=Wlast[:, l, gs], in_=Wb[:, :, KP - 1])
                    V("dve", "tensor_tensor", [LW, Ltab], [Lcar], out=Mclast[:, l, gs], in0=Wb[:, :, KP - 1], in1=cosT[:, :, KP - 1], op=ALU.mult)
                    V("dve", "tensor_tensor", [LW, Ltab], [Lcar], out=Mslast[:, l, gs], in0=Wb[:, :, KP - 1], in1=sinT[:, :, KP - 1], op=ALU.mult)
                else:
                    V("dve", "memset", [], [LM], Ms[:, :, KP:KT], 0.0)
                    V("act", "activation", [Ls0], [LM], out=Mc[:, :, KP:KT], in_=s0t[:, 0], func=AF.Copy)
                    V("dve", "tensor_tensor", [LW, Ltab], [Lsfin], out=sfin[:, 0, :, 0], in0=Wb[:, :, KP - 1], in1=cosT[:, :, KP - 1], op=ALU.mult)
                    V("dve", "tensor_tensor", [LW, Ltab], [Lsfin], out=sfin[:, 1, :, 0], in0=Wb[:, :, KP - 1], in1=sinT[:, :, KP - 1], op=ALU.mult)
                    ps, Lp = psF.get()
                    V("pe", "matmul", [Lsfin, Lcf], [Lp], ps[:, 0:GG], lhsT=pswap, rhs=sfin[:, 1, :, 0], start=True, stop=True)
                    V("dve", "tensor_tensor", [Lp, Lsfin], [Lssp], out=ssm_p_sb[:, l, gs], in0=ps[:, 0:GG], in1=sfin[:, 0, :, 0], op=ALU.add)
                    psx, Lpx = psF.get()
                    for g in range(GG):
                        V("pe", "matmul", [LAb, LU], [Lpx], psx[:, g * NS:(g + 1) * NS], lhsT=Ab[:, 0, g, :], rhs=Uq[:, g, KP:KT], start=True, stop=True)
                    Lr8 = Er[:, :, 23]; Li8 = Ei[:, :, 23]
                    bns = lambda a: a.unsqueeze(2).to_broadcast([128, GG, NS])
                    V("dve", "tensor_tensor", [Ls0, LEp], [Lsfin], out=sfin[:, 2], in0=s0t[:, 0], in1=bns(Lr8), op=ALU.mult)
                    V("dve", "scalar_tensor_tensor", [Ls0, LEp, Lcf], [Lsfin], out=sfin[:, 3], in0=s0t[:, 1], scalar=sgn[:, 0:1], in1=bns(Li8), op0=ALU.mult, op1=ALU.mult)
                    V("dve", "tensor_tensor", [Lsfin], [Lsfin], out=sfin[:, 2], in0=sfin[:, 2], in1=sfin[:, 3], op=ALU.add)
                    V("dve", "tensor_tensor", [Lsfin, Lpx], [Lsss], out=ssm_s_sb[:, l, gs, :], in0=sfin[:, 2], in1=psx[:, 0:GG * NS].rearrange("p (g s) -> p g s", g=GG), op=ALU.add)

            def stepY(b):
                Ab, LAb, Uq, LU, Mc, Ms, LM, PbC, LPb, yapt, Lyapt, g0 = b.Ab, b.LAb, b.Uq, b.LU, b.Mc, b.Ms, b.LM, b.PbC, b.LPb, b.yapt, b.Lyapt, b.g0
                for m, (k0, nk) in enumerate(PT):
                    ps, Lp = psF.get()
                    for g in range(GG):
                        o_ = ps[:nk, g * 128:(g + 1) * 128]
                        V("pe", "matmul", [LU, LAb], [Lp], o_, lhsT=Uq[:, g, k0:k0 + nk], rhs=Ab[:, 2, g, :], start=True, stop=False)
                        V("pe", "matmul", [LM, LPb], [Lp], o_, lhsT=Mc[:, g, k0:k0 + nk], rhs=PbC[:, 0, g, :], start=False, stop=False)
                        V("pe", "matmul", [LM, LPb], [Lp], o_, lhsT=Ms[:, g, k0:k0 + nk], rhs=PbC[:, 1, g, :], start=False, stop=True)
                    ta, La_ = t2k.get(); tb_, Lb_ = t2k.get()
                    V("act", "activation", [Lp], [La_], out=ta[:nk], in_=ps[:nk], func=AF.Square)
                    V("dve", "tensor_scalar", [La_], [La_], out=ta[:nk], in0=ta[:nk], scalar1=0.044715, scalar2=1.0, op0=ALU.mult, op1=ALU.add)
                    V("dve", "tensor_tensor", [La_, Lp], [La_], out=ta[:nk], in0=ta[:nk], in1=ps[:nk], op=ALU.mult)
                    V("act", "activation", [La_], [Lb_], out=tb_[:nk], in_=ta[:nk], func=AF.Sigmoid, scale=1.5957691216)
                    V("dve", "tensor_tensor", [Lb_, Lp], [Lyapt], out=yapt[:nk, m].rearrange("p t (g c) -> p g t c", g=GG),
                      in0=tb_[:nk].rearrange("p (g t c) -> p g t c", g=GG, t=8), in1=ps[:nk].rearrange("p (g t c) -> p g t c", g=GG, t=8), op=ALU.mult)
                    pb_, Lpb_ = psB.get()
                    for t in range(8):
                        V("pe", "transpose", [Lyapt, Lcb], [Lpb_], out=pb_[0:GG * 16, t * 128:t * 128 + nk], in_=yapt[:nk, m, t, :], identity=ident_b[:nk, :nk])
                    tok0 = k0 * 8
                    tl = tiles_overlapping(tiles, tok0, tok0 + nk * 8)
                    cq = (g0 * 16) // 128; p0 = (g0 * 16) % 128
                    V("dve", "tensor_copy", [Lpb_], [Lya[t] for t in tl], out=yaT[p0:p0 + GG * 16, cq, tok0:tok0 + nk * 8].rearrange("p (k j) -> p j k", j=8),
                      in_=pb_[0:GG * 16, :].rearrange("p (t k) -> p t k", t=8)[:, :, :nk])

            pairs = [(2 * i, 2 * i + 1) for i in range(NSUB // 2)]
            Ubuf = {0: stage_u(0), 1: stage_u(1)}
            if prefetch is not None:
                prefetch()
            for pi, pr in enumerate(pairs):
                for u, sq in enumerate(pr):
                    b = ctxs[u]
                    b.sq = sq; b.g0 = sq * GG; b.gs = slice(sq * GG, sq * GG + GG)
                    b.Uq, b.LU = Ubuf.pop(sq)
                if pi + 1 < len(pairs):
                    for sq2 in pairs[pi + 1]:
                        Ubuf[sq2] = stage_u(sq2)
                for step in (stepL, stepT, stepX, stepS, stepY):
                    for u in range(2):
                        step(ctxs[u])
            if hi == 1:
                DMA("sp", ssmp_out[:, l, :], ssm_p_sb[:, l, :], [Lssp], [])
                DMA("sp", ssms_out[:, l], ssm_s_sb[:, l], [Lsss], [])

        def phase_B(l, tiles, yaT, Lya, wo, Lwo, pw):
            S.barrier()
            cbA = Carver(arena_b[:, 5120 + 8192:23040])
            (wga, Lwga), (wso, Lwso), (wgl, Lwgl) = pw
            mix = cbA.take([128, 8, 512]); Lmix = LT()
            ya2 = cbA.take([128, 4, 512]); Ly2 = LT()
            wload(wo, wview(w_o, l), Lwo)
            for ti, (t0, n) in enumerate(tiles):
                for oc in range(4):
                    ps, Lp = psF.get()
                    proj_fm(ps, Lp, wgl, Lwgl, oc * 128, 4, lambda kc: yaT[:, kc, t0:t0 + n], [Lya[ti]], n)
                    sg, Lsg = tb1k.get()
                    V("act", "activation", [Lp], [Lsg], out=sg[:, :n], in_=ps[:, :n], func=AF.Sigmoid)
                    V("dve", "tensor_tensor", [Lsg, Lya[ti]], [Ly2], out=ya2[:, oc, :n], in0=sg[:, :n], in1=yaT[:, oc, t0:t0 + n], op=ALU.mult)
                for oc in range(8):
                    ps, Lp = psF.get(); ps2, Lp2 = psF.get()
                    proj_fm(ps, Lp, wso, Lwso, oc * 128, 4, lambda kc: ya2[:, kc, :n], [Ly2], n)
                    proj_fm(ps2, Lp2, wga, Lwga, oc * 128, 8, lambda kc: hT[:, kc, t0:t0 + n], [Lh[ti]], n)
                    sg, Lsg = t2k.get()
                    V("act", "activation", [Lp2], [Lsg], out=sg[:, :n], in_=ps2[:, :n], func=AF.Sigmoid)
                    V("dve", "tensor_tensor", [Lsg, Lp], [Lmix], out=mix[:, oc, :n], in0=sg[:, :n], in1=ps[:, :n], op=ALU.mult)
                if "B" not in skip:
                    apply_wo(ti, t0, n, wo, Lwo, mix, Lmix)

        def phase_C(l, hi, tiles, oT, LoT):
            S.barrier()
            cfA = Carver(arena_f[:, :])
            cbY = Carver(arena_b[:, 0:5120])
            cbW = Carver(arena_b[:, 5120:5120 + 8192])
            cbA = Carver(arena_b[:, 5120 + 8192 + 9216:])
            r_st = cfA.take([128, 4, 256]); Lr_st = LT()
            orw_p = Pool_([(cfA.take([128, 4, 2, 128]), LT()) for i in range(1)])
            qf_p = Pool_([(cfA.take([128, 4, 256]), LT()) for i in range(1)])
            r0_p = Pool_([(cfA.take([128, 4, 256]), LT()) for i in range(2)])
            rn_p = Pool_([(cfA.take([128, 4, 256]), LT()) for i in range(2)])
            wq4 = cbA.take([128, 8, 2048]); Lwq4 = [LT() for _ in range(4)]
            r_bf = cbY.take([128, 4, 256]); Lr_bf = LT()
            qk_p = Pool_([(cbY.take([128, 4, 2, 128]), LT()) for i in range(2)])
            qkT_p = Pool_([(cbY.take([128, 4, 2, 128]), LT()) for i in range(1)])
            sc_p = Pool_([(cbY.take([128, 4, 128]), LT()) for i in range(2)])
            v_p = Pool_([(cbW.take([128, 4, 256]), LT()) for i in range(2)])
            sq_p = Pool_([(cbW.take([128, 1024]), LT()) for i in range(2)])
            r0b_p = Pool_([(cbW.take([128, 4, 256]), LT()) for i in range(2)])
            km_p = Pool_([(cbW.take([128, 4, 128]), LT()) for i in range(2)])
            goff = GOFF[hi]
            wv_ = wview(w_in, l)
            for hh in range(4):
                wload(wq4[:, :, hh * 512:hh * 512 + 128], wv_[:, :, 512 + hh * 128:512 + (hh + 1) * 128], Lwq4[hh])
                wload(wq4[:, :, hh * 512 + 128:hh * 512 + 256], wv_[:, :, 1024 + hh * 128:1024 + (hh + 1) * 128], Lwq4[hh])
                wload(wq4[:, :, hh * 512 + 256:hh * 512 + 512], wv_[:, :, 1536 + hh * 256:1536 + (hh + 1) * 256], Lwq4[hh])
            if hi == 0:
                V("dve", "memset", [], [Lr_st], r_st, 0.0)
            else:
                DMA("sp", r_st, rcar[l].rearrange("h d v -> d h v"), [Lrcar[l]], [Lr_st])
            V("act", "activation", [Lr_st], [Lr_bf], out=r_bf, in_=r_st, func=AF.Copy)
            psQ = psBig[:, :]
            LQ = [it[1] for it in psF.items]
            blocks = []
            for ti, (t0, n) in enumerate(tiles):
                for b in range(n // 128):
                    blocks.append((ti, t0 + b * 128))
            for (ti, t0) in blocks:
                tg = goff + t0
                is_s = (tg >= SEQ)
                blk = 16 if is_s else tg // 128
                kind = 1 if is_s else 0
                for hh in range(4):
                    for c in range(8):
                        V("pe", "matmul", [Lh[ti], Lwq4[hh]], [LQ[hh]], psQ[:, hh * 512:(hh + 1) * 512], lhsT=hT[:, c, t0:t0 + 128], rhs=wq4[:, c, hh * 512:(hh + 1) * 512],
                          start=(c == 0), stop=(c == 7))
                qf, Lqf = qf_p.get()
                vt, Lv = v_p.get()
                for hh in range(4):
                    V("act", "activation", [LQ[hh]], [Lqf], out=qf[:, hh, :], in_=psQ[:, hh * 512:hh * 512 + 256], func=AF.Copy)
                    V("act", "activation", [LQ[hh]], [Lv], out=vt[:, hh, :], in_=psQ[:, hh * 512 + 256:hh * 512 + 512], func=AF.Copy)
                x1 = qf.rearrange("p h (a f d) -> p h a f d", a=2, f=2)[:, :, :, 0, :]
                x2 = qf.rearrange("p h (a f d) -> p h a f d", a=2, f=2)[:, :, :, 1, :]
                cs_ = rope[:, blk, 0, :].unsqueeze(1).unsqueeze(1).to_broadcast([128, 4, 2, 64])
                sn_ = rope[:, blk, 1, :].unsqueeze(1).unsqueeze(1).to_broadcast([128, 4, 2, 64])
                pr = [t2k.get() for _ in range(4)]
                v4 = lambda a: a.rearrange("p (h a d) -> p h a d", h=4, a=2)
                V("dve", "tensor_tensor", [Lqf, Lrope], [pr[0][1]], out=v4(pr[0][0]), in0=x1, in1=cs_, op=ALU.mult)
                V("dve", "tensor_tensor", [Lqf, Lrope], [pr[1][1]], out=v4(pr[1][0]), in0=x2, in1=sn_, op=ALU.mult)
                V("dve", "tensor_tensor", [Lqf, Lrope], [pr[2][1]], out=v4(pr[2][0]), in0=x1, in1=sn_, op=ALU.mult)
                V("dve", "tensor_tensor", [Lqf, Lrope], [pr[3][1]], out=v4(pr[3][0]), in0=x2, in1=cs_, op=ALU.mult)
                V("dve", "tensor_tensor", [pr[0][1], pr[1][1]], [pr[0][1]], out=pr[0][0], in0=pr[0][0], in1=pr[1][0], op=ALU.subtract)
                V("dve", "tensor_tensor", [pr[2][1], pr[3][1]], [pr[2][1]], out=pr[2][0], in0=pr[2][0], in1=pr[3][0], op=ALU.add)
                qk, Lqk = qk_p.get()
                sct = qksc.rearrange("p (k h a) -> p k h a", k=2, h=4)[:, kind].unsqueeze(3).to_broadcast([128, 4, 2, 64])
                V("dve", "tensor_tensor", [pr[0][1], Lcf], [Lqk], out=qk[:, :, :, 0:64], in0=v4(pr[0][0]), in1=sct, op=ALU.mult)
                V("dve", "tensor_tensor", [pr[2][1], Lcf], [Lqk], out=qk[:, :, :, 64:128], in0=v4(pr[2][0]), in1=sct, op=ALU.mult)
                pb_, Lpb_ = psB.get()
                for hh in range(4):
                    for a in range(2):
                        V("pe", "transpose", [Lqk, Lcb], [Lpb_], out=pb_[:, (hh * 2 + a) * 128:(hh * 2 + a + 1) * 128], in_=qk[:, hh, a, :], identity=ident_b)
                qkT, LqkT = qkT_p.get()
                V("dve", "tensor_copy", [Lpb_], [LqkT], out=qkT.rearrange("p h a n -> p (h a n)"), in_=pb_[:, 0:1024])
                ps2, Lp2 = psF.get()
                for hh in range(4):
                    V("pe", "matmul", [LqkT], [Lp2], ps2[:, hh * 128:(hh + 1) * 128], lhsT=qkT[:, hh, 1, :], rhs=qkT[:, hh, 0, :], start=True, stop=True)
                sc, Lsc = sc_p.get()
                mk = (cmask_s if is_s else cmask_p).unsqueeze(1).to_broadcast([128, 4, 128])
                V("dve", "tensor_tensor", [Lp2, Lcf], [Lsc], out=sc, in0=ps2[:, :].rearrange("p (h n) -> p h n", h=4), in1=mk, op=ALU.mult)
                po = [psF.get(), psF.get()]
                orw, Lor = orw_p.get()
                for hh in range(4):
                    pso, Lpo = po[hh // 2]
                    for e_ in range(2):
                        o_ = pso[:, ((hh % 2) * 2 + e_) * 128:((hh % 2) * 2 + e_ + 1) * 128]
                        V("pe", "matmul", [Lv, Lsc], [Lpo], o_, lhsT=vt[:, hh, e_ * 128:(e_ + 1) * 128], rhs=sc[:, hh, :], start=True, stop=is_s)
                        if not is_s:
                            V("pe", "matmul", [Lr_bf, LqkT], [Lpo], o_, lhsT=r_bf[:, hh, e_ * 128:(e_ + 1) * 128], rhs=qkT[:, hh, 0, :], start=False, stop=True)
                orf = orw.rearrange("p h e n -> p (h e n)")
                for i2 in range(2):
                    V("act", "activation", [po[i2][1]], [Lor], out=orf[:, i2 * 512:(i2 + 1) * 512], in_=po[i2][0][:, :], func=AF.Copy)
                if not is_s:
                    pd = [psS.get(), psS.get()]
                    for hh in range(4):
                        psd, Lpd = pd[hh // 2]
                        V("pe", "matmul", [Lqk, Lv], [Lpd], psd[:, (hh % 2) * 256:(hh % 2 + 1) * 256], lhsT=qk[:, hh, 1, :], rhs=vt[:, hh, :], start=True, stop=True)
                    for i2 in range(2):
                        rv = r_st[:, i2 * 2:i2 * 2 + 2, :].rearrange("p h v -> p (h v)")
                        V("dve", "tensor_tensor", [pd[i2][1], Lr_st], [Lr_st], out=rv, in0=rv, in1=pd[i2][0][:, :], op=ALU.add)
                    gtab = gct.rearrange("p (k h) -> p k h", k=2)[:, 0].unsqueeze(2).to_broadcast([128, 4, 256])
                    V("dve", "tensor_tensor", [Lr_st, Lcf], [Lr_st], out=r_st, in0=r_st, in1=gtab, op=ALU.mult)
                    V("act", "activation", [Lr_st], [Lr_bf], out=r_bf, in_=r_st, func=AF.Copy)
                    if tg == 1024 - 128:
                        DMA("sp", rcar[l].rearrange("h d v -> d h v"), r_st, [Lr_st], [Lrcar[l]])
                    if tg == SEQ - 128:
                        DMA("sp", retp_out[l].rearrange("h d v -> d h v"), r_st, [Lr_st], [])
                else:
                    pin = [psF.get(), psF.get()]
                    gtab = gct.rearrange("p (k h) -> p k h", k=2)[:, 1].unsqueeze(2).to_broadcast([128, 4, 256])
                    def _ld_r0(sx):
                        r0x, Lr0x = r0_p.get()
                        DMA("sp", r0x, sret_in[l, sx].rearrange("h d v -> d h v"), [], [Lr0x])
                        return r0x, Lr0x
                    r0_next = _ld_r0(0)
                    for s_ in range(NS):
                        r0, Lr0 = r0_next
                        if s_ + 1 < NS:
                            r0_next = _ld_r0(s_ + 1)
                        r0b, Lr0b = r0b_p.get()
                        V("act", "activation", [Lr0], [Lr0b], out=r0b, in_=r0, func=AF.Copy)
                        for hh in range(4):
                            psi, Lpi = pin[hh // 2]
                            for e_ in range(2):
                                c0_ = ((hh % 2) * 2 + e_) * 128 + s_ * 8
                                V("pe", "matmul", [Lr0b, LqkT], [Lpi], psi[:, c0_:c0_ + 8], lhsT=r0b[:, hh, e_ * 128:(e_ + 1) * 128],
                                  rhs=qkT[:, hh, 0, s_ * 8:s_ * 8 + 8], start=True, stop=True)
                        km, Lkm = km_p.get()
                        V("dve", "tensor_scalar", [Lqk, Lcf], [Lkm], out=km, in0=qk[:, :, 1, :], scalar1=rowmask[:, s_:s_ + 1], scalar2=None, op0=ALU.mult)
                        pd = [psS.get(), psS.get()]
                        for hh in range(4):
                            psd, Lpd = pd[hh // 2]
                            V("pe", "matmul", [Lkm, Lv], [Lpd], psd[:, (hh % 2) * 256:(hh % 2 + 1) * 256], lhsT=km[:, hh, :], rhs=vt[:, hh, :], start=True, stop=True)
                        rn, Lrn = rn_p.get()
                        for i2 in range(2):
                            V("dve", "tensor_tensor", [pd[i2][1], Lr0], [Lrn], out=rn[:, i2 * 2:i2 * 2 + 2, :].rearrange("p h v -> p (h v)"),
                              in0=r0[:, i2 * 2:i2 * 2 + 2, :].rearrange("p h v -> p (h v)"), in1=pd[i2][0][:, :], op=ALU.add)
                        V("dve", "tensor_tensor", [Lrn, Lcf], [Lrn], out=rn, in0=rn, in1=gtab, op=ALU.mult)
                        DMA("pool", rets_out[l, s_].rearrange("h d v -> d h v"), rn, [Lrn], [])
                    for i2 in range(2):
                        tin, Ltin = t2k.get()
                        V("act", "activation", [pin[i2][1]], [Ltin], out=tin[:, :], in_=pin[i2][0][:, :], func=AF.Copy)
                        V("dve", "tensor_tensor", [Lor, Ltin], [Lor], out=orf[:, i2 * 512:(i2 + 1) * 512], in0=orf[:, i2 * 512:(i2 + 1) * 512], in1=tin[:, :], op=ALU.add)
                sq, Lsq = sq_p.get()
                V("dve", "tensor_tensor", [Lor], [Lsq], out=sq, in0=orf, in1=orf, op=ALU.mult)
                ps5, Lp5 = psF.get()
                for hh in range(4):
                    for e_ in range(2):
                        V("pe", "matmul", [Lsq, Lcb], [Lp5], ps5[:, hh * 128:(hh + 1) * 128], lhsT=ones_b, rhs=sq[:, (hh * 2 + e_) * 128:(hh * 2 + e_ + 1) * 128],
                          start=(e_ == 0), stop=(e_ == 1))
                rs, Lrs = t2k.get()
                V("act", "activation", [Lp5, Lcf], [Lrs], out=rs[:, :], in_=ps5[:, :], func=AF.Sqrt, bias=epsc[:, 0:1], scale=1.0 / 256)
                V("dve", "reciprocal", [Lrs], [Lrs], out=rs[:, :], in_=rs[:, :])
                V("dve", "tensor_tensor", [Lor, Lrs], [LoT[ti]], out=oT[:, :, t0:t0 + 128].rearrange("p (h e) n -> p h e n", h=4), in0=orw,
                  in1=rs[:, :].rearrange("p (h n) -> p h n", h=4).unsqueeze(2).to_broadcast([128, 4, 2, 128]), op=ALU.mult)

        def phase_D(l, tiles, oT, LoT, wo, Lwo):
            S.barrier()
            cbA = Carver(arena_b[:, 5120 + 8192 + 9216:])
            wX = cbA.take([128, 8, 1024]); LwX = LT()
            wY = cbA.take([128, 8, 1024]); LwY = LT()
            mix = arena_b[:, 0:4096].rearrange("p (c n) -> p c n", c=8); Lmix = LT()
            GR0 = 2560
            GB0 = 4608
            wload(wo, wview(w_in, l)[:, :, GR0:GR0 + 1024], Lwo)
            wload(wY, wview(w_ro, l), LwY)
            wload(wX, wview(w_in, l)[:, :, GB0:GB0 + 1024], LwX)
            for ti, (t0, n) in enumerate(tiles):
                for oc in range(8):
                    ps, Lp = psF.get()
                    proj_fm(ps, Lp, wo, Lwo, oc * 128, 8, lambda kc: hT[:, kc, t0:t0 + n], [Lh[ti]], n)
                    sg, Lsg = tb1k.get()
                    V("act", "activation", [Lp], [Lsg], out=sg[:, :n], in_=ps[:, :n], func=AF.Silu)
                    o_ = oT[:, oc, t0:t0 + n]
                    V("dve", "tensor_tensor", [Lsg, LoT[ti]], [LoT[ti]], out=o_, in0=o_, in1=sg[:, :n], op=ALU.mult)
            wload(wo, wview(w_o, l), Lwo)
            for ti, (t0, n) in enumerate(tiles):
                for oc in range(8):
                    ps, Lp = psF.get(); ps2, Lp2 = psF.get()
                    proj_fm(ps, Lp, wY, LwY, oc * 128, 8, lambda kc: oT[:, kc, t0:t0 + n], [LoT[ti]], n)
                    proj_fm(ps2, Lp2, wX, LwX, oc * 128, 8, lambda kc: hT[:, kc, t0:t0 + n], [Lh[ti]], n)
                    sg, Lsg = t2k.get()
                    V("act", "activation", [Lp2], [Lsg], out=sg[:, :n], in_=ps2[:, :n], func=AF.Sigmoid)
                    V("dve", "tensor_tensor", [Lsg, Lp], [Lmix], out=mix[:, oc, :n], in0=sg[:, :n], in1=ps[:, :n], op=ALU.mult)
                if "D" not in skip:
                    apply_wo(ti, t0, n, wo, Lwo, mix, Lmix)

        GF = 4
        EXTRA_K = [10]

        def ffn(l, hi, tiles, extra=None):
            S.barrier()
            cfA = Carver(arena_f[:, :]); cbA = Carver(arena_b[:, :])
            conv0 = cfA.take([128, NCH, NS, 2]); Lc0 = LT()
            convp_sb = cfA.take([128, NCH, 2]); Lcp = LT()
            convs_sb = cfA.take([128, NCH, NS, 2]); Lcs = LT()
            actT = cbA.take([128, GF, NH]); Lact = [LT() for _ in range(3)]
            wup_p = Pool_([(cbA.take([128, 8, 2 * GF * 128]), (LT(), LT())) for i in range(2)])
            wdn_p = Pool_([(cbA.take([128, GF, D]), LT()) for i in range(2)])
            upb = [[(cbA.take([128, 514]), LT()) for a in range(2)] for i in range(GF)]
            Dg = cbA.take([128, GF, 2, 3, 128]); LDg = LT()
            ups = Pool_([(cbA.take([128, NS, 10]), LT()) for i in range(2)]) if hi == 1 else None
            cw, Lcw = small["convw"]; cbv, Lcbv = small["convb"]
            if hi == 1:
                DMA("sp", conv0, conv0_in[:, l], [], [Lc0])
            ngroups = (22 + GF - 1) // GF
            for gi in range(ngroups):
                c0 = gi * GF
                ng = min(GF, 22 - c0)
                wu_, Lwu2 = wup_p.get(); wd_, Lwd_ = wdn_p.get()
                wload(wu_[:, :, 0:ng * 128], wview(w_up, l)[:, :, c0 * 128:(c0 + ng) * 128], Lwu2[0])
                wload(wu_[:, :, GF * 128:GF * 128 + ng * 128], wview(w_up, l)[:, :, DFF + c0 * 128:DFF + (c0 + ng) * 128], Lwu2[1])
                wload(wd_[:, 0:ng, :], w_dn[l, c0 * 128:(c0 + ng) * 128, :].rearrange("(c p) n -> p c n", p=128), Lwd_)
                for cc in range(ng):
                    for a in range(2):
                        ch = c0 + cc + a * 22
                        for j in range(3):
                            V("dve", "tensor_scalar", [Lcw, Lcb], [LDg], out=Dg[:, cc, a, j, :], in0=ident_b, scalar1=cw[:, l, j, ch:ch + 1], scalar2=None, op0=ALU.mult)
                pend_conv = []
                pend_down = []
                resmap = {}

                def emit_up(ti, t0, n, cc, a):
                    is_s = (n == 128)
                    ch = c0 + cc + a * 22
                    ps, Lp = psF.get()
                    proj_fm(ps, Lp, wu_, Lwu2[a], a * GF * 128 + cc * 128, 8, lambda kc: hT[:, kc, t0:t0 + n], [Lh[ti]], n)
                    if not is_s:
                        ub, Lub = upb[cc][a]
                        if ti == 0:
                            if hi == 0:
                                V("dve", "memset", [], [Lub], ub[:, 0:2], 0.0)
                            else:
                                V("dve", "tensor_copy", [Lccar], [Lub], out=ub[:, 0:2], in_=convcar[:, l, ch, :])
                        else:
                            V("dve", "tensor_copy", [Lub], [Lub], out=ub[:, 0:2], in_=ub[:, 512:514])
                        V("act", "activation", [Lp], [Lub], out=ub[:, 2:514], in_=ps[:, :], func=AF.Copy)
                        if ti == 1:
                            if hi == 0:
                                V("act", "activation", [Lp], [Lccar], out=convcar[:, l, ch, :], in_=ps[:, 510:512], func=AF.Copy)
                            else:
                                V("act", "activation", [Lp], [Lcp], out=convp_sb[:, ch, :], in_=ps[:, 510:512], func=AF.Copy)
                        return (ub, Lub)
                    else:
                        us, Lus = ups.get()
                        V("dve", "tensor_copy", [Lc0], [Lus], out=us[:, :, 0:2], in_=conv0[:, ch])
                        V("act", "activation", [Lp], [Lus], out=us[:, :, 2:10], in_=ps[:, 0:128].rearrange("p (s j) -> p s j", j=8), func=AF.Copy)
                        V("act", "activation", [Lp], [Lcs], out=convs_sb[:, ch], in_=ps[:, 0:128].rearrange("p (s j) -> p s j", j=8)[:, :, 6:8], func=AF.Copy)
                        return (us, Lus)

                def emit_conv(ti, t0, n, cc, a, buf):
                    is_s = (n == 128)
                    ch = c0 + cc + a * 22
                    ub, Lub = buf
                    ps2, Lp2 = psF.get()
                    for j in range(3):
                        if not is_s:
                            V("pe", "matmul", [LDg, Lub], [Lp2], ps2[:, :], lhsT=Dg[:, cc, a, j, :], rhs=ub[:, j:j + 512], start=(j == 0), stop=(j == 2))
                        else:
                            V("pe", "matmul", [LDg, Lub], [Lp2], ps2[:, 0:128], lhsT=Dg[:, cc, a, j, :], rhs=ub[:, :, j:j + 8], start=(j == 0), stop=(j == 2))
                    resmap[(ti, cc, a)] = (ps2, Lp2, ch)
                    if a == 1:
                        (pv, Lpv, chv) = resmap.pop((ti, cc, 0)); (pg, Lpg, chg) = resmap.pop((ti, cc, 1))
                        sg, Lsg = t2k.get()
                        V("act", "activation", [Lpg, Lcbv], [Lsg], out=sg[:, :n], in_=pg[:, :n], func=AF.Silu, bias=cbv[:, l, chg:chg + 1])
                        V("dve", "scalar_tensor_tensor", [Lpv, Lsg, Lcbv], [Lact[ti]], out=actT[:, cc, t0:t0 + n], in0=pv[:, :n], scalar=cbv[:, l, chv:chv + 1], in1=sg[:, :n], op0=ALU.add, op1=ALU.mult)

                def emit_down(ti, t0, n):
                    for oc in range(8):
                        ps, Lp = psF.get()
                        for cc in range(ng):
                            V("pe", "matmul", [Lwd_, Lact[ti]], [Lp], ps[:, :n], lhsT=wd_[:, cc, oc * 128:(oc + 1) * 128], rhs=actT[:, cc, t0:t0 + n], start=(cc == 0), stop=(cc == ng - 1))
                        V("dve", "tensor_tensor", [Lp, Lx[ti]], [Lx[ti]], out=xT[:, oc, t0:t0 + n], in0=xT[:, oc, t0:t0 + n], in1=ps[:, :n], op=ALU.add)

                for ti, (t0, n) in enumerate(tiles):
                    ui = 0
                    for cc in range(ng):
                        for a in range(2):
                            buf = emit_up(ti, t0, n, cc, a)
                            if pend_conv:
                                emit_conv(*pend_conv.pop(0))
                            pend_conv.append((ti, t0, n, cc, a, buf))
                            if extra:
                                for _ in range(min(EXTRA_K[0], len(extra))):
                                    emit_captured(extra.pop(0))
                            if ui == 2 and pend_down:
                                emit_down(*pend_down.pop(0))
                            ui += 1
                    pend_down.append((ti, t0, n))
                while pend_conv:
                    emit_conv(*pend_conv.pop(0))
                while pend_down:
                    emit_down(*pend_down.pop(0))
            while extra:
                emit_captured(extra.pop(0))
            if hi == 1:
                DMA("sp", convp_out[:, l], convp_sb, [Lcp], [])
                DMA("sp", convs_out[:, l], convs_sb, [Lcs], [])
                S.final_wait("sp", [Lcp, Lcs])

        out_L = []
        for hi in range(2):
            tiles = HT[hi]
            goff = GOFF[hi]
            nh = sum(n for _, n in tiles)
            S.barrier()
            DMA("sp", xT[:, :, 0:nh], xT_in[:, :, goff:goff + nh], [], [Lx[i] for i in range(len(tiles))])
            yaT = arena_b[:, 0:4 * NH].rearrange("p (c n) -> p c n", c=4)
            wo = arena_b[:, 5120:5120 + 8192].rearrange("p (c n) -> p c n", c=8)
            oT = arena_b[:, 5120 + 8192:5120 + 8192 + 9216].rearrange("p (c n) -> p c n", c=8)
            for l in range(nlayers):
                Lya = [LT() for _ in range(3)]; Lwo = LT(); LoT = [LT() for _ in range(3)]
                norm_to_h(l, "gmix", tiles)
                pw = ((arena_b[:, 23040:31232].rearrange("p (c n) -> p c n", c=8), LT()),
                      (arena_b[:, 31232:35328].rearrange("p (c n) -> p c n", c=4), LT()),
                      (arena_b[:, 35328:37376].rearrange("p (c n) -> p c n", c=4), LT()))

                def _prefB(l=l, pw=pw):
                    wload(pw[2][0], w_glu[l].rearrange("(c p) n -> p c n", p=128), pw[2][1])
                    wload(pw[1][0], w_sso[l].rearrange("(c p) n -> p c n", p=128), pw[1][1])
                    wload(pw[0][0], wview(w_in, l)[:, :, 3584:3584 + 1024], pw[0][1])
                if "a" not in skip:
                    if hi == 0 and l == 0:
                        S.barrier()
                        s5_setup(0)
                    s5_phase(l, hi, tiles, yaT, Lya, _prefB)
                else:
                    _prefB()
                if "b" not in skip:
                    phase_B(l, tiles, yaT, Lya, wo, Lwo, pw)
                if "c" not in skip:
                    phase_C(l, hi, tiles, oT, LoT)
                if "d" not in skip:
                    phase_D(l, tiles, oT, LoT, wo, Lwo)
                if "F" not in skip:
                    norm_to_h(l, "gffn", tiles)
                    extra = None
                    if hi == 0 and l + 1 < nlayers and "a" not in skip:
                        CAP[0] = []
                        s5_setup(l + 1)
                        extra = CAP[0]; CAP[0] = None
                        EXTRA_K[0] = len(extra) // 90 + 1
                    ffn(l, hi, tiles, extra)
            S.barrier()
            gt, Lg = small["gfin"]
            yo = arena_f[:, 0:4096].rearrange("p (c n) -> p c n", c=8); Lyo = LT()
            for ti, (t0, n) in enumerate(tiles):
                r, Lr = rms_rstd(ti, t0, n, 1.0 / D)
                for c in range(8):
                    V("dve", "scalar_tensor_tensor", [Lx[ti], Lg, Lr], [Lyo], out=yo[:, c, :n], in0=xT[:, c, t0:t0 + n], scalar=gt[:, c:c + 1], in1=r[:, :n], op0=ALU.mult, op1=ALU.mult)
                DMA("sp", yT_out[:, :, goff + t0:goff + t0 + n], yo[:, :, :n], [Lyo], [])
            out_L.append(Lyo)
            S.final_wait("sp", [Lyo])
        S.barrier()
        S.emit(block)
    return nc


def _mk_consts():
    cf = {}
    cf["ident"] = np.eye(128, dtype=np.float32)
    jc = np.arange(128) // 16
    tc_t = np.arange(128) // 16
    cf["tmask"] = (tc_t[None, :] >= jc[:, None]).astype(np.float32)
    ps = np.zeros((128, 128), np.float32)
    for p in range(64):
        ps[64 + p, p] = -1.0
        ps[p, 64 + p] = 1.0
    cf["pswap"] = ps
    nv = np.array([7, 6, 5, 4, 3, 2, 1, 0, -1, -2, -3, -4, -5, -6, -7, -8, 1, 2, 3, 4, 5, 6, 7, 8], np.float32)
    cf["nvec"] = np.broadcast_to(nv, (128, 24)).copy()
    cf["kvec"] = np.broadcast_to(np.arange(256, dtype=np.float32), (128, 256)).copy()
    top = (np.arange(128) < 64)
    cf["sgn"] = np.where(top, -1.0, 1.0).astype(np.float32)[:, None]
    cf["mre"] = top.astype(np.float32)[:, None]
    cf["nmre"] = -top.astype(np.float32)[:, None]
    cf["nmim"] = -(~top).astype(np.float32)[:, None]
    cf["eps"] = np.full((128, 1), EPS, np.float32)
    cf["halfpi"] = np.full((128, 1), math.pi / 2, np.float32)
    cf["zero"] = np.zeros((128, 1), np.float32)
    i = np.arange(128, dtype=np.float64)
    qs = np.zeros((128, 8)); ks = np.zeros((128, 8))
    for h in range(4):
        g = 1.0 - 2.0 ** (-5 - h)
        qs[:, h] = g ** (i + 1); ks[:, h] = (128 ** -0.5) * g ** (-(i + 1))
        qs[:, 4 + h] = g ** ((i % 8) + 1); ks[:, 4 + h] = (128 ** -0.5) * g ** (-((i % 8) + 1))
    cf["qsc"] = qs.astype(np.float32); cf["ksc"] = ks.astype(np.float32)
    qk_ = np.zeros((128, 2, 4, 2)); gc_ = np.zeros((128, 2, 4))
    for kd in range(2):
        for h in range(4):
            qk_[:, kd, h, 0] = qs[:, kd * 4 + h]; qk_[:, kd, h, 1] = ks[:, kd * 4 + h]
            gc_[:, kd, h] = (1.0 - 2.0 ** (-5 - h)) ** (128 if kd == 0 else 8)
    cf["qksc"] = qk_.reshape(128, 16).astype(np.float32); cf["gct"] = gc_.reshape(128, 8).astype(np.float32)
    rm = np.zeros((128, 16), np.float32)
    for s in range(16):
        rm[s * 8:(s + 1) * 8, s] = 1.0
    cf["rowmask"] = rm
    j = np.arange(128)
    cf["cmask_p"] = (j[:, None] <= j[None, :]).astype(np.float32)
    cf["cmask_s"] = ((j[:, None] <= j[None, :]) & ((j[:, None] // 8) == (j[None, :] // 8))).astype(np.float32)
    off = {}; o = 0; parts = []
    for k, v in cf.items():
        off[k] = (o, v.shape[1]); o += v.shape[1]; parts.append(v)
    cfa = np.ascontiguousarray(np.concatenate(parts, axis=1))
    cb = {"ident": np.eye(128, dtype=np.float32), "ones": np.ones((128, 128), np.float32)}
    offb = {}; o = 0; partsb = []
    for k, v in cb.items():
        offb[k] = (o, v.shape[1]); o += v.shape[1]; partsb.append(v)
    cba = np.ascontiguousarray(np.concatenate(partsb, axis=1)).astype(ml_dtypes.bfloat16)
    half = 64
    inv = (10000.0 ** (-np.arange(half, dtype=np.float32) / half)).astype(np.float32)
    rope = np.zeros((128, 17, 2, 64), np.float32)
    for b in range(17):
        if b < 16:
            pos = (b * 128 + np.arange(128)).astype(np.float32)
        else:
            pos = (PAST + (np.arange(128) % 8)).astype(np.float32)
        ang = (pos[:, None] * inv[None, :]).astype(np.float32)
        rope[:, b, 0, :] = np.cos(ang); rope[:, b, 1, :] = np.sin(ang)
    return cfa, off, cba, offb, rope


CF_ARR, CF_OFF, CB_ARR, CB_OFF, ROPE_ARR = _mk_consts()
CF_N = CF_ARR.shape[1]
CB_N = CB_ARR.shape[1]

_NC_CACHE = {}


def _stack(a, b):
    return np.ascontiguousarray(np.concatenate([a, b], axis=0))


def make_in_maps(inp):
    f = lambda a: np.ascontiguousarray(np.asarray(a, dtype=np.float32))
    shared = {}
    for k, src in [("w_in", "w_in"), ("w_glu", "w_glu"), ("w_ssm_out", "w_ssm_out"), ("w_ret_out", "w_ret_out"), ("w_o", "w_o"), ("w_up", "w_up"), ("w_down", "w_down")]:
        shared[k] = f(inp[src])
    pl = lambda v: np.ascontiguousarray(f(v).reshape(DEPTH, -1, 128).transpose(2, 0, 1))
    shared["gmix"] = pl(inp["norm_mix"]); shared["gffn"] = pl(inp["norm_ffn"])
    shared["gfin"] = np.ascontiguousarray(f(inp["norm_final"]).reshape(8, 128).T)
    shared["convw"] = np.ascontiguousarray(f(inp["conv_w"]).reshape(DEPTH, 3, NCH, 128).transpose(3, 0, 1, 2))
    shared["convb"] = np.ascontiguousarray(f(inp["conv_b"]).reshape(DEPTH, NCH, 128).transpose(2, 0, 1))
    lr = f(inp["ssm_lam_re"]).transpose(2, 0, 1)
    li = f(inp["ssm_lam_im"]).transpose(2, 0, 1)
    shared["lamr"] = _stack(lr, lr); shared["lami"] = _stack(li, li)
    shared["logdt"] = np.ascontiguousarray(np.broadcast_to(f(inp["ssm_log_dt"])[None], (128, DEPTH, G)))
    br = f(inp["ssm_b_re"]).transpose(2, 0, 1, 3)
    bi = f(inp["ssm_b_im"]).transpose(2, 0, 1, 3)
    shared["bS"] = _stack(br, bi); shared["bX"] = _stack(bi, br)
    cr = f(inp["ssm_c_re"]).transpose(3, 0, 1, 2)
    ci = f(inp["ssm_c_im"]).transpose(3, 0, 1, 2)
    shared["crD"] = _stack(cr, cr); shared["ciD"] = _stack(ci, ci)
    d = f(inp["ssm_d"]).reshape(DEPTH, G, 16)
    dc = d.transpose(2, 0, 1)
    shared["dcol"] = np.ascontiguousarray(np.tile(dc, (8, 1, 1)))
    shared["cf32"] = CF_ARR; shared["cbf16"] = CB_ARR; shared["rope"] = ROPE_ARR
    xp = f(inp["x_prompt"]); xs = f(inp["x_sample"])
    sre = f(inp["state_ssm_re"]); sim = f(inp["state_ssm_im"]); sret = f(inp["state_ret"]); scv = f(inp["state_conv"])
    maps = []
    for ci_ in range(NCORES):
        m = dict(shared)
        S0 = ci_ * NS
        xt = np.concatenate([xp[ci_], xs[S0:S0 + NS].reshape(NS * DS, D)], axis=0)
        m["xT_in"] = np.ascontiguousarray(xt.T.reshape(8, 128, TOK).transpose(1, 0, 2))
        a = sre[:, S0:S0 + NS].transpose(3, 0, 2, 1)
        b = sim[:, S0:S0 + NS].transpose(3, 0, 2, 1)
        m["s0S"] = _stack(a, b); m["s0X"] = _stack(b, a)
        cv = scv[:, S0:S0 + NS].reshape(DEPTH, NS, 2, NCH, 128).transpose(4, 0, 3, 1, 2)
        m["conv0"] = np.ascontiguousarray(cv)
        m["sret"] = np.ascontiguousarray(sret[:, S0:S0 + NS])
        maps.append(m)
    return maps


def kernel(**inputs):
    if "nc" not in _NC_CACHE:
        _NC_CACHE["nc"] = build()
    nc = _NC_CACHE["nc"]
    maps = make_in_maps(inputs)
    res = run_bass_kernel_spmd(nc, maps, core_ids=list(range(NCORES)))
    R = res.results
    if "dbg_out" in R[0]:
        _NC_CACHE["dbg"] = [np.asarray(r["dbg_out"]) for r in R]
    B = NCORES
    y_p = np.zeros((B, SEQ, D), np.float32); y_s = np.zeros((B * NS, DS, D), np.float32)
    sre_p = np.zeros((DEPTH, B, G, P), np.float32); sim_p = np.zeros_like(sre_p)
    ret_p = np.zeros((DEPTH, B, 4, 128, 256), np.float32)
    cv_p = np.zeros((DEPTH, B, 2, 2 * DFF), np.float32)
    sre_s = np.zeros((DEPTH, B * NS, G, P), np.float32); sim_s = np.zeros_like(sre_s)
    ret_s = np.zeros((DEPTH, B * NS, 4, 128, 256), np.float32)
    cv_s = np.zeros((DEPTH, B * NS, 2, 2 * DFF), np.float32)
    for c in range(B):
        r = R[c]
        yt = np.asarray(r["yT_out"]).transpose(1, 0, 2).reshape(D, TOK).T
        y_p[c] = yt[:SEQ]; y_s[c * NS:(c + 1) * NS] = yt[SEQ:].reshape(NS, DS, D)
        sp = np.asarray(r["ssmp_out"])
        sre_p[:, c] = sp[:64].transpose(1, 2, 0); sim_p[:, c] = sp[64:].transpose(1, 2, 0)
        ss = np.asarray(r["ssms_out"])
        sre_s[:, c * NS:(c + 1) * NS] = ss[:64].transpose(1, 3, 2, 0); sim_s[:, c * NS:(c + 1) * NS] = ss[64:].transpose(1, 3, 2, 0)
        ret_p[:, c] = np.asarray(r["retp_out"]); ret_s[:, c * NS:(c + 1) * NS] = np.asarray(r["rets_out"])
        cp = np.asarray(r["convp_out"])
        cv_p[:, c] = cp.transpose(1, 3, 2, 0).reshape(DEPTH, 2, 2 * DFF)
        cs = np.asarray(r["convs_out"])
        cv_s[:, c * NS:(c + 1) * NS] = cs.transpose(1, 3, 4, 2, 0).reshape(DEPTH, NS, 2, 2 * DFF)
    return (y_p, y_s, sre_p, sim_p, ret_p, cv_p, sre_s, sim_s, ret_s, cv_s)
```
